# Optimizing a Trainium2 kernel written in Bass

```python
import math
import jax, jax.numpy as jnp
from jax import lax
import numpy as np

D_MODEL = 2048
BATCH = 8
SEQ = 2048
DEPTH = 2

CHUNK = 64
Q_BLOCK = 128
N_MIXERS = 2
N_RWKV_LAYERS = (DEPTH + N_MIXERS - 1) // N_MIXERS
N_MLA_LAYERS = DEPTH // N_MIXERS
D_FF = 5504
RMS_EPS = 1e-6
RWKV_HEAD = 64
RWKV_HEADS = D_MODEL // RWKV_HEAD
DECAY_LORA = 96
AAA_LORA = 96
GATE_LORA = 256
GN_EPS = 64e-5
MLA_HEADS = 16
Q_LORA = 512
KV_LORA = 512
NOPE_DIM = 128
ROPE_DIM = 64
QK_DIM = NOPE_DIM + ROPE_DIM
V_DIM = 128
ROPE_THETA = 10000.0

kernel_name = 'hybrid_rwkv7_mla_macaron_trunk'


def rmsnorm(x, g, eps=RMS_EPS):
    xf = x.astype(jnp.float32)
    y = xf * lax.rsqrt(jnp.mean(xf * xf, axis=-1, keepdims=True) + eps)
    return (y * g.astype(jnp.float32)).astype(x.dtype)


def swiglu(h, w13, w2):
    gate, up = jnp.split(h @ w13, 2, axis=-1)
    return (jax.nn.silu(gate) * up) @ w2


def rwkv7_scan(r, w, k, v, a, b):
    B, T, H, N = r.shape

    def step(S, inp):
        r_t, w_t, k_t, v_t, a_t, b_t = inp
        sa = jnp.einsum('bhvk,bhk->bhv', S, a_t)
        S = (S * w_t[:, :, None, :]
             + sa[..., None] * b_t[:, :, None, :]
             + v_t[..., None] * k_t[:, :, None, :])
        y_t = jnp.einsum('bhvk,bhk->bhv', S, r_t)
        return S, y_t

    xs = tuple(jnp.moveaxis(t, 1, 0) for t in (r, w, k, v, a, b))
    S0 = jnp.zeros((B, H, N, N), jnp.float32)
    _, y = lax.scan(step, S0, xs)
    return jnp.moveaxis(y, 0, 1)


def rwkv7_time_mix(h, mu, w_rkv, w0, w1, w2, a0, a1, a2, g1, g2,
                   k_k, k_a, r_k, ln_w, ln_b, w_o):
    B, T, C = h.shape
    H, N = RWKV_HEADS, RWKV_HEAD
    f32 = jnp.float32
    x_prev = jnp.pad(h, ((0, 0), (1, 0), (0, 0)))[:, :T]
    xx = x_prev - h
    xr, xw, xk, xv, xa, xg = (h + xx * mu[j] for j in range(6))
    r, k, v = jnp.einsum('cbtd,cde->cbte', jnp.stack([xr, xk, xv]), w_rkv)
    w_log = -jax.nn.softplus(-(w0 + jnp.tanh(xw @ w1) @ w2)) - 0.5
    decay = jnp.exp(-jnp.exp(w_log.astype(f32)))
    a = jax.nn.sigmoid(a0 + (xa @ a1) @ a2)
    g = jax.nn.sigmoid(xg @ g1) @ g2
    heads = lambda t: t.reshape(B, T, H, N).astype(f32)
    kk = heads(k * k_k)
    kk = kk / jnp.maximum(jnp.linalg.norm(kk, axis=-1, keepdims=True), 1e-12)
    k = k * (1.0 + (a - 1.0) * k_a)
    r_h, k_h, v_h, a_h = heads(r), heads(k), heads(v), heads(a)
    y = rwkv7_scan(r_h, heads(decay), k_h, v_h, -kk, kk * a_h)
    mean = jnp.mean(y, axis=-1, keepdims=True)
    var = jnp.mean(jnp.square(y - mean), axis=-1, keepdims=True)
    y = ((y - mean) * lax.rsqrt(var + GN_EPS)).reshape(B, T, C)
    y = y * ln_w.astype(f32) + ln_b.astype(f32)
    bonus = jnp.sum(r_h * k_h * r_k.astype(f32), axis=-1, keepdims=True) * v_h
    y = y + bonus.reshape(B, T, C)
    return (y.astype(h.dtype) * g) @ w_o


def rope(x, cos, sin):
    x1, x2 = jnp.split(x, 2, axis=-1)
    return jnp.concatenate([x1 * cos - x2 * sin, x2 * cos + x1 * sin], axis=-1)


def chunk_causal_attention(q, k, v):
    B, T, H, Dqk = q.shape
    Dv = v.shape[-1]
    nb = T // Q_BLOCK
    scale = 1.0 / math.sqrt(Dqk)
    qb = jnp.moveaxis(q.reshape(B, nb, Q_BLOCK, H, Dqk), 1, 0)
    key_chunk = jnp.arange(T) // CHUNK

    def one_block(args):
        q_i, blk = args
        q_chunk = (blk * Q_BLOCK + jnp.arange(Q_BLOCK)) // CHUNK
        s = jnp.einsum('bqhd,bkhd->bhqk', q_i, k,
                       preferred_element_type=jnp.float32) * scale
        mask = key_chunk[None, :] <= q_chunk[:, None]
        s = jnp.where(mask[None, None], s, jnp.finfo(jnp.float32).min)
        p = jax.nn.softmax(s, axis=-1)
        return jnp.einsum('bhqk,bkhd->bqhd', p.astype(v.dtype), v)

    o = lax.map(one_block, (qb, jnp.arange(nb)))
    return jnp.moveaxis(o, 0, 1).reshape(B, T, H, Dv)


def mla_mix(h, cos, sin, w_down, q_a_norm, kv_a_norm, w_uq, w_ukv,
            q_norm, k_norm, w_o):
    B, T, _ = h.shape
    c = h @ w_down
    c_q, c_kv, k_pe = jnp.split(c, [Q_LORA, Q_LORA + KV_LORA], axis=-1)
    c_q = rmsnorm(c_q, q_a_norm)
    c_kv = rmsnorm(c_kv, kv_a_norm)
    q = jnp.einsum('btl,lhd->bthd', c_q, w_uq)
    kv = jnp.einsum('btl,lhd->bthd', c_kv, w_ukv)
    k_nope, v = jnp.split(kv, [NOPE_DIM], axis=-1)
    k_pe = jnp.broadcast_to(k_pe[:, :, None, :], (B, T, MLA_HEADS, ROPE_DIM))
    k = jnp.concatenate([k_nope, k_pe], axis=-1)
    q = rmsnorm(q, q_norm)
    k = rmsnorm(k, k_norm)
    q = jnp.concatenate([q[..., :NOPE_DIM], rope(q[..., NOPE_DIM:], cos, sin)], axis=-1)
    k = jnp.concatenate([k[..., :NOPE_DIM], rope(k[..., NOPE_DIM:], cos, sin)], axis=-1)
    o = chunk_causal_attention(q, k, v)
    return jnp.einsum('bthd,hde->bte', o, w_o)


def setup_inputs(seed: int = 0) -> dict:
    key = jax.random.key(seed)
    ks = jax.random.split(key, 40)
    f32 = jnp.float32
    D, F, NR, NM = D_MODEL, D_FF, N_RWKV_LAYERS, N_MLA_LAYERS
    H, N = RWKV_HEADS, RWKV_HEAD

    def nrm(k, shape, scale):
        return jax.random.normal(k, shape, f32) * scale

    def gain(k, shape):
        return 1.0 + 0.02 * jax.random.normal(k, shape, f32)

    x = nrm(ks[0], (BATCH, SEQ, D), 1.0)
    start = jax.random.randint(ks[1], (BATCH, 1), 0, 4096, dtype=jnp.int32)
    positions = start + jnp.arange(SEQ, dtype=jnp.int32)[None, :]
    return {
        'x': x,
        'positions': positions,
        'ffn_norm': gain(ks[2], (DEPTH, 2, D)),
        'ffn_w13': nrm(ks[3], (DEPTH, 2, D, 2 * F), D ** -0.5),
        'ffn_w2': nrm(ks[4], (DEPTH, 2, F, D), F ** -0.5),
        'mix_norm': gain(ks[5], (DEPTH, D)),
        'rwkv_mu': jax.random.uniform(ks[6], (NR, 6, D), f32),
        'rwkv_w_rkv': nrm(ks[7], (NR, 3, D, D), D ** -0.5),
        'rwkv_w0': -6.5 + 5.0 * jax.random.uniform(ks[8], (NR, D), f32),
        'rwkv_w1': nrm(ks[9], (NR, D, DECAY_LORA), D ** -0.5),
        'rwkv_w2': nrm(ks[10], (NR, DECAY_LORA, D), 0.5 * DECAY_LORA ** -0.5),
        'rwkv_a0': nrm(ks[11], (NR, D), 0.1),
        'rwkv_a1': nrm(ks[12], (NR, D, AAA_LORA), D ** -0.5),
        'rwkv_a2': nrm(ks[13], (NR, AAA_LORA, D), 0.5 * AAA_LORA ** -0.5),
        'rwkv_g1': nrm(ks[14], (NR, D, GATE_LORA), D ** -0.5),
        'rwkv_g2': nrm(ks[15], (NR, GATE_LORA, D), GATE_LORA ** -0.5),
        'rwkv_k_k': 0.85 + 0.05 * jax.random.normal(ks[16], (NR, D), f32),
        'rwkv_k_a': gain(ks[17], (NR, D)),
        'rwkv_r_k': nrm(ks[18], (NR, H, N), 0.1),
        'rwkv_ln_w': gain(ks[19], (NR, D)),
        'rwkv_ln_b': nrm(ks[20], (NR, D), 0.01),
        'rwkv_w_o': nrm(ks[21], (NR, D, D), D ** -0.5),
        'mla_w_down': nrm(ks[22], (NM, D, Q_LORA + KV_LORA + ROPE_DIM), D ** -0.5),
        'mla_q_a_norm': gain(ks[23], (NM, Q_LORA)),
        'mla_kv_a_norm': gain(ks[24], (NM, KV_LORA)),
        'mla_w_uq': nrm(ks[25], (NM, Q_LORA, MLA_HEADS, QK_DIM), Q_LORA ** -0.5),
        'mla_w_ukv': nrm(ks[26], (NM, KV_LORA, MLA_HEADS, NOPE_DIM + V_DIM), KV_LORA ** -0.5),
        'mla_q_norm': gain(ks[27], (NM, QK_DIM)),
        'mla_k_norm': gain(ks[28], (NM, QK_DIM)),
        'mla_w_o': nrm(ks[29], (NM, MLA_HEADS, V_DIM, D), (MLA_HEADS * V_DIM) ** -0.5),
    }


def reference(x, positions, ffn_norm, ffn_w13, ffn_w2, mix_norm,
              rwkv_mu, rwkv_w_rkv, rwkv_w0, rwkv_w1, rwkv_w2, rwkv_a0, rwkv_a1, rwkv_a2,
              rwkv_g1, rwkv_g2, rwkv_k_k, rwkv_k_a, rwkv_r_k, rwkv_ln_w, rwkv_ln_b, rwkv_w_o,
              mla_w_down, mla_q_a_norm, mla_kv_a_norm, mla_w_uq, mla_w_ukv,
              mla_q_norm, mla_k_norm, mla_w_o):
    inv_freq = ROPE_THETA ** (-jnp.arange(0, ROPE_DIM, 2, dtype=jnp.float32) / ROPE_DIM)
    ang = positions.astype(jnp.float32)[..., None] * inv_freq
    cos = jnp.cos(ang)[:, :, None, :].astype(x.dtype)
    sin = jnp.sin(ang)[:, :, None, :].astype(x.dtype)

    for i in range(DEPTH):
        j = i // N_MIXERS
        x = x + 0.5 * swiglu(rmsnorm(x, ffn_norm[i, 0]), ffn_w13[i, 0], ffn_w2[i, 0])
        h = rmsnorm(x, mix_norm[i])
        if i % N_MIXERS == 0:
            y = rwkv7_time_mix(h, rwkv_mu[j], rwkv_w_rkv[j], rwkv_w0[j], rwkv_w1[j],
                               rwkv_w2[j], rwkv_a0[j], rwkv_a1[j], rwkv_a2[j],
                               rwkv_g1[j], rwkv_g2[j], rwkv_k_k[j], rwkv_k_a[j],
                               rwkv_r_k[j], rwkv_ln_w[j], rwkv_ln_b[j], rwkv_w_o[j])
        else:
            y = mla_mix(h, cos, sin, mla_w_down[j], mla_q_a_norm[j], mla_kv_a_norm[j],
                        mla_w_uq[j], mla_w_ukv[j], mla_q_norm[j], mla_k_norm[j],
                        mla_w_o[j])
        x = x + y
        x = x + 0.5 * swiglu(rmsnorm(x, ffn_norm[i, 1]), ffn_w13[i, 1], ffn_w2[i, 1])
    return x
```

```python
import numpy as np
import concourse.bass as bass
import concourse.mybir as mybir
from concourse.bass_utils import run_bass_kernel_spmd

F32 = mybir.dt.float32
BF16 = mybir.dt.bfloat16
I32 = mybir.dt.int32
AF = mybir.ActivationFunctionType
ALU = mybir.AluOpType
AX = mybir.AxisListType

T = 2048
D = 2048
FF = 5504
NCH = 16
NFC = 43
RMS_EPS = 1e-6
SB_BYTES = 207872


class Tok:
    __slots__ = ("w", "r", "excl")

    def __init__(self, excl=False):
        self.w = None
        self.r = {}
        self.excl = excl


class DSem:
    def __init__(self, h, idx):
        self.h = h
        self.idx = idx
        self.count = 0


class Prog:
    CE = ("pe", "act", "dve", "pool")

    def __init__(self, nc, esems, dsems):
        self.nc = nc
        self.code = {e: [] for e in ("pe", "act", "dve", "pool", "sp")}
        self.esem = esems
        self.ecnt = {e: 0 for e in self.CE}
        self.seen = {e: {} for e in self.code}
        self.free_dsems = [DSem(h, i) for i, h in enumerate(dsems)]
        self.all_dsems = list(self.free_dsems)
        self.ninstr = 0

    def dsem(self):
        return self.free_dsems.pop()

    def _need(self, eng, waits, ev):
        if ev is None:
            return
        kind, s, v = ev
        if kind == "d":
            v = s.count
            key = ("d", s.idx)
        else:
            if s == eng and eng == "pe":
                return
            key = ("e", s)
        if self.seen[eng].get(key, 0) >= v:
            return
        if waits.get(key, (None, 0))[1] < v:
            waits[key] = (s, v)

    def op(self, eng, fn, reads=(), writes=(), inc=True, dsem=None):
        if any(t.excl for t in reads):
            writes = tuple(writes) + tuple(t for t in reads if t.excl)
            reads = tuple(t for t in reads if not t.excl)
        waits = {}
        for t in reads:
            self._need(eng, waits, t.w)
        for t in writes:
            if t.w is not None and not (t.w[0] == "e" and t.w[1] == eng):
                self._need(eng, waits, t.w)
            for ev in t.r.values():
                if not (ev[0] == "e" and ev[1] == eng):
                    self._need(eng, waits, ev)
        wl = []
        for key, (s, v) in waits.items():
            self.seen[eng][key] = v
            wl.append((s.h if key[0] == "d" else self.esem[s], v))
        if dsem is not None:
            dsem.count += 16
            ev = ("d", dsem, dsem.count)
            incspec = (dsem.h, 16)
            rkey = ("d", dsem.idx)
        else:
            if inc:
                self.ecnt[eng] += 1
                ev = ("e", eng, self.ecnt[eng])
                incspec = (self.esem[eng], 1)
            else:
                ev = ("e", eng, self.ecnt[eng] + 1)
                incspec = None
            rkey = ("e", eng)
        for t in reads:
            t.r[rkey] = ev
        for t in writes:
            t.w = ev
            t.r = {}
        self.code[eng].append((wl, fn, incspec))
        self.ninstr += 1

    def barrier(self):
        evs = [("e", e, self.ecnt[e]) for e in self.CE if self.ecnt[e] > 0]
        evs += [("d", d, d.count) for d in self.all_dsems if d.count > 0]
        for eng in self.code:
            waits = {}
            for ev in evs:
                if ev[0] == "e" and ev[1] == eng and eng == "pe":
                    continue
                self._need(eng, waits, ev)
            wl = []
            for key, (s, v) in waits.items():
                self.seen[eng][key] = v
                wl.append((s.h if key[0] == "d" else self.esem[s], v))
            if wl:
                self.code[eng].append((wl, None, None))

    def emit(self, block):
        def mk(name):
            def body(e):
                for wl, fn, incspec in self.code[name]:
                    for h, v in wl:
                        e.wait_ge(h, v)
                    if fn is None:
                        continue
                    ins = fn(e)
                    if incspec is not None:
                        ins.then_inc(incspec[0], incspec[1])
            return body

        block.tensor(mk("pe"))
        block.scalar(mk("act"))
        block.vector(mk("dve"))
        block.gpsimd(mk("pool"))
        block.sync(mk("sp"))

    def mm(self, out, lhsT, rhs, start, stop, reads, writes, inc=None):
        self.op("pe", lambda e: e.matmul(out, lhsT, rhs, start=start, stop=stop),
                reads, writes, inc=(stop if inc is None else inc))

    def tr(self, out, in_, ident, reads, writes, inc=True):
        self.op("pe", lambda e: e.transpose(out, in_, ident), reads, writes, inc=inc)

    def dma(self, q, out, in_, dsem, reads, writes):
        self.op(q, lambda e: e.dma_start(out=out, in_=in_), reads, writes, dsem=dsem)

    def actf(self, out, in_, func, reads, writes, bias=None, scale=None, eng="act"):
        kw = {}
        if bias is not None:
            kw["bias"] = bias
        if scale is not None:
            kw["scale"] = scale
        self.op("act", lambda e: e.activation(out, in_, func, **kw), reads, writes)

    def copy(self, eng, out, in_, reads, writes):
        if eng == "act":
            self.op("act", lambda e: e.copy(out, in_), reads, writes)
        else:
            self.op(eng, lambda e: e.tensor_copy(out, in_), reads, writes)

    def tt(self, eng, out, in0, in1, op, reads, writes):
        self.op(eng, lambda e: e.tensor_tensor(out, in0, in1, op), reads, writes)

    def ts(self, eng, out, in0, s1, s2, op0, op1, reads, writes):
        if s2 is None:
            self.op(eng, lambda e: e.tensor_scalar(out, in0, s1, None, op0), reads, writes)
        else:
            self.op(eng, lambda e: e.tensor_scalar(out, in0, s1, s2, op0, op1), reads, writes)

    def stt(self, eng, out, in0, scalar, in1, op0, op1, reads, writes):
        self.op(eng, lambda e: e.scalar_tensor_tensor(out, in0, scalar, in1, op0, op1), reads, writes)

    def memset(self, eng, ap, val, writes):
        self.op(eng, lambda e: e.memset(ap, val), (), writes)


class Arena:
    def __init__(self, t32, nbytes):
        self.v = {F32: t32, BF16: t32.bitcast(BF16), I32: t32.bitcast(I32)}
        self.cap = nbytes
        self.top = 0
        self.marks = []

    def alloc(self, dtype, n, parts=128, p0=0):
        sz = 2 if dtype == BF16 else 4
        off = (self.top + 63) // 64 * 64
        self.top = off + n * sz
        assert self.top <= self.cap, f"SBUF arena overflow {self.top} > {self.cap}"
        return self.v[dtype][p0:p0 + parts, off // sz: off // sz + n]

    def mark(self):
        self.marks.append(self.top)

    def release(self):
        self.top = self.marks.pop()


class Ctx:
    pass


def phase_tin(cx, x_tm, xs_dst):
    P, A = cx.P, cx.A
    A.mark()
    xin = [A.alloc(F32, D) for _ in range(2)]
    xo = [A.alloc(F32, NCH * 128) for _ in range(2)]
    t_in = [Tok() for _ in range(2)]
    t_o = [Tok() for _ in range(2)]
    ds_in = [P.dsem() for _ in range(2)]
    ds_o = [P.dsem() for _ in range(2)]
    t_ps = [Tok(True) for _ in range(2)]
    dst_v = xs_dst.rearrange("c p t -> p c t")
    for tb in range(T // 128):
        s = tb % 2
        P.dma("sp", xin[s], x_tm[tb * 128:(tb + 1) * 128, :], ds_in[s], (), (t_in[s],))
        for q in range(4):
            b = (tb * 4 + q) % 2
            bank = cx.bank(b)
            for i in range(4):
                c = q * 4 + i
                P.tr(bank[:, i * 128:(i + 1) * 128], xin[s][:, c * 128:(c + 1) * 128], cx.ident,
                     (t_in[s],), (t_ps[b],), inc=(i == 3))
            eng = "dve" if q % 2 == 0 else "act"
            P.copy(eng, xo[s][:, q * 512:(q + 1) * 512], bank, (t_ps[b],), (t_o[s],))
        P.dma("sp", dst_v[:, :, tb * 128:(tb + 1) * 128],
              xo[s].rearrange("p (c t) -> p c t", c=NCH), ds_o[s], (t_o[s],), cx.xs_tok(None, tb // 4))
    P.barrier()
    for d in ds_in + ds_o:
        P.free_dsems.append(d)
    A.release()


def phase_tout(cx, xs_src, out_tm):
    P, A = cx.P, cx.A
    A.mark()
    xin = [A.alloc(F32, NCH * 128) for _ in range(2)]
    xo = [A.alloc(F32, D) for _ in range(2)]
    t_in = [Tok() for _ in range(2)]
    t_o = [Tok() for _ in range(2)]
    ds_in = [P.dsem() for _ in range(2)]
    ds_o = [P.dsem() for _ in range(2)]
    t_ps = [Tok(True) for _ in range(2)]
    src_v = xs_src.rearrange("c p t -> p c t")
    for tb in range(T // 128):
        s = tb % 2
        P.dma("sp", xin[s].rearrange("p (c t) -> p c t", c=NCH), src_v[:, :, tb * 128:(tb + 1) * 128],
              ds_in[s], cx.xs_tok(None, tb // 4), (t_in[s],))
        for q in range(4):
            b = (tb * 4 + q) % 2
            bank = cx.bank(b)
            for i in range(4):
                c = q * 4 + i
                P.tr(bank[:, i * 128:(i + 1) * 128], xin[s][:, c * 128:(c + 1) * 128], cx.ident,
                     (t_in[s],), (t_ps[b],), inc=(i == 3))
            eng = "dve" if q % 2 == 0 else "act"
            P.copy(eng, xo[s][:, q * 512:(q + 1) * 512], bank, (t_ps[b],), (t_o[s],))
        P.dma("sp", out_tm[tb * 128:(tb + 1) * 128, :], xo[s], ds_o[s], (t_o[s],), (cx.t_out,))
    P.barrier()
    for d in ds_in + ds_o:
        P.free_dsems.append(d)
    A.release()


def rmsnorm_tile(cx, xt, t_xt, gcol, hT_out, t_h, ntok, sqb, t_sq, rstd, t_rstd, bank, t_bank):
    P = cx.P
    xv = xt.rearrange("p (c t) -> p c t", c=NCH)
    for c in range(NCH):
        s = c % len(sqb)
        P.actf(sqb[s][:, :ntok], xv[:, c, :], AF.Square, (t_xt,), (t_sq[s],))
        P.mm(bank[:, :ntok], cx.ones_bf, sqb[s][:, :ntok], c == 0, c == NCH - 1,
             (t_sq[s], cx.t_const), (t_bank,), inc=True)
    P.ts("dve", rstd[:, :ntok], bank[:, :ntok], RMS_EPS, None, ALU.add, None, (t_bank,), (t_rstd,))
    P.actf(rstd[:, :ntok], rstd[:, :ntok], AF.Ln, (t_rstd,), (t_rstd,))
    P.actf(rstd[:, :ntok], rstd[:, :ntok], AF.Exp, (t_rstd,), (t_rstd,), scale=-0.5)
    for c in range(NCH):
        P.stt("dve", hT_out(c), xv[:, c, :], gcol[:, c:c + 1], rstd[:, :ntok], ALU.mult, ALU.mult,
              (t_xt, t_rstd, cx.t_const), (t_h,))


def phase_ffn(cx, xs_src, xs_dst, w13, w2, gcol):
    P, A = cx.P, cx.A
    A.mark()
    HALF = 1024
    NTT = HALF // 512
    hT = A.alloc(BF16, NCH * HALF)
    hTv = hT.rearrange("p (c t) -> p c t", c=NCH)
    actT = A.alloc(BF16, NFC * HALF)
    actTv = actT.rearrange("p (j t) -> p j t", j=NFC)
    t_h = Tok()
    t_act = [Tok() for _ in range(NFC)]
    sqb = [A.alloc(BF16, 512) for _ in range(3)]
    t_sq = [Tok() for _ in range(3)]
    rstd = A.alloc(F32, 512)
    t_rstd = Tok()
    sg = [A.alloc(F32, 512) for _ in range(2)]
    t_sg = [Tok() for _ in range(2)]
    xold = [A.alloc(F32, 512) for _ in range(2)]
    t_xold = [Tok() for _ in range(2)]
    ds_xold = [P.dsem() for _ in range(2)]
    xnew = [A.alloc(F32, 512) for _ in range(2)]
    t_xnew = [Tok() for _ in range(2)]
    ds_xnew = [P.dsem() for _ in range(2)]
    tops = []
    A.mark()
    xt = [A.alloc(F32, NCH * 512) for _ in range(2)]
    tops.append(A.top); A.release(); A.mark()
    w13s = [A.alloc(F32, 2 * NCH * 128) for _ in range(2)]
    w13b = [A.alloc(BF16, 2 * NCH * 128) for _ in range(2)]
    tops.append(A.top); A.release(); A.mark()
    w2s = [A.alloc(F32, NFC * 128) for _ in range(2)]
    w2b = [A.alloc(BF16, NFC * 128) for _ in range(2)]
    tops.append(A.top); A.release()
    A.top = max(tops)
    t_reg = [Tok() for _ in range(2)]
    t_regb = [Tok() for _ in range(2)]
    t_regb2 = [Tok() for _ in range(2)]
    ds_stage = [P.dsem() for _ in range(2)]
    src_v = xs_src.rearrange("c p t -> p c t")
    dst_v = xs_dst.rearrange("c p t -> p c t")
    w13v = w13.rearrange("(c p) f -> p c f", p=128)
    w2v = w2.rearrange("(j p) e -> p j e", p=128)
    t_bA = Tok(True)
    t_bB = [Tok(True) for _ in range(4)]
    t_bC = [Tok(True) for _ in range(2)]
    for th in range(T // HALF):
        tok0 = th * HALF
        for tt in range(NTT):
            s = tt % 2
            P.dma("sp", xt[s].rearrange("p (c t) -> p c t", c=NCH),
                  src_v[:, :, tok0 + tt * 512: tok0 + (tt + 1) * 512], ds_stage[s], cx.xs_tok(None, th * NTT + tt), (t_reg[s],))
            rmsnorm_tile(cx, xt[s], t_reg[s], gcol, lambda c, tt=tt: hTv[:, c, tt * 512:(tt + 1) * 512], t_h,
                         512, sqb, t_sq, rstd, t_rstd, cx.bank(6), t_bA)
        P.barrier()
        for j in range(NFC):
            s = j % 2
            stg = w13s[s].rearrange("p (g c f) -> p g c f", g=2, c=NCH)
            stb = w13b[s].rearrange("p (g c f) -> p g c f", g=2, c=NCH)
            P.dma("sp", stg[:, 0], w13v[:, :, j * 128:(j + 1) * 128], ds_stage[s], (), (t_reg[s],))
            P.dma("sp", stg[:, 1], w13v[:, :, FF + j * 128: FF + (j + 1) * 128], ds_stage[s], (), (t_reg[s],))
            P.copy("dve", stb[:, 0], stg[:, 0], (t_reg[s],), (t_regb[s],))
            P.copy("act", stb[:, 1], stg[:, 1], (t_reg[s],), (t_regb2[s],))
            for tt in range(NTT):
                bi = (j * NTT + tt) % 2
                bg, bu = cx.bank(2 * bi), cx.bank(2 * bi + 1)
                tg, tu = t_bB[2 * bi], t_bB[2 * bi + 1]
                rhs_t = slice(tt * 512, (tt + 1) * 512)
                for c in range(NCH):
                    P.mm(bg, stb[:, 0, c, :], hTv[:, c, rhs_t], c == 0, c == NCH - 1, (t_regb[s], t_h), (tg,))
                for c in range(NCH):
                    P.mm(bu, stb[:, 1, c, :], hTv[:, c, rhs_t], c == 0, c == NCH - 1, (t_regb2[s], t_h), (tu,))
                P.actf(sg[bi], bg, AF.Silu, (tg,), (t_sg[bi],))
                P.tt("dve", actTv[:, j, rhs_t], sg[bi], bu, ALU.mult, (t_sg[bi], tu), (t_act[j],))
        P.barrier()
        for e in range(NCH):
            s = e % 2
            stg = w2s[s].rearrange("p (j e) -> p j e", j=NFC)
            stb = w2b[s].rearrange("p (j e) -> p j e", j=NFC)
            P.dma("sp", stg[:, 0:22, :], w2v[:, 0:22, e * 128:(e + 1) * 128], ds_stage[s], (), (t_reg[s],))
            P.dma("sp", stg[:, 22:NFC, :], w2v[:, 22:NFC, e * 128:(e + 1) * 128], ds_stage[s], (), (t_reg[s],))
            P.copy("dve", stb[:, 0:22, :], stg[:, 0:22, :], (t_reg[s],), (t_regb[s],))
            P.copy("act", stb[:, 22:NFC, :], stg[:, 22:NFC, :], (t_reg[s],), (t_regb2[s],))
            for tt in range(NTT):
                bi = (e * NTT + tt) % 2
                bk, tb_ = cx.bank(4 + bi), t_bC[bi]
                rhs_t = slice(tt * 512, (tt + 1) * 512)
                tsl = slice(tok0 + tt * 512, tok0 + (tt + 1) * 512)
                P.dma("act", xold[bi], src_v[:, e, tsl], ds_xold[bi], cx.xs_tok(e, th * NTT + tt), (t_xold[bi],))
                for j in range(NFC):
                    P.mm(bk, stb[:, j, :], actTv[:, j, rhs_t], j == 0, j == NFC - 1,
                         (t_regb[s] if j < 22 else t_regb2[s], t_act[j]), (tb_,))
                P.stt("dve", xnew[bi], bk, 0.5, xold[bi], ALU.mult, ALU.add, (tb_, t_xold[bi]), (t_xnew[bi],))
                P.dma("pool", dst_v[:, e, tsl], xnew[bi], ds_xnew[bi], (t_xnew[bi],), cx.xs_tok(e, th * NTT + tt))
        P.barrier()
    for d in ds_xold + ds_xnew + ds_stage:
        P.free_dsems.append(d)
    A.release()


def norm_from_bank(cx, rstd, t_rstd, bank, t_bank, n, mean_scale, eps, parts=128):
    P = cx.P
    P.ts("dve", rstd[:parts, :n], bank[:parts, :n], mean_scale, eps, ALU.mult, ALU.add, (t_bank,), (t_rstd,))
    P.actf(rstd[:parts, :n], rstd[:parts, :n], AF.Ln, (t_rstd,), (t_rstd,))
    P.actf(rstd[:parts, :n], rstd[:parts, :n], AF.Exp, (t_rstd,), (t_rstd,), scale=-0.5)


MLA_H = 16
STOP = 0
DBG = False
SM_SCALE = 1.0 / float(np.sqrt(192.0))


def phase_mla(cx, xs_src, xs_dst, wd, wuq, wukv, wo, gcol, mcols_d, pos_d):
    P, A = cx.P, cx.A
    A.mark()
    src_v = xs_src.rearrange("c p t -> p c t")
    dst_v = xs_dst.rearrange("c p t -> p c t")
    mc = A.alloc(F32, 16)
    t_mc = Tok()
    dsm = P.dsem()
    P.dma("sp", mc, mcols_d, dsm, (), (t_mc,))
    cqn = A.alloc(BF16, 4 * T); cqnv = cqn.rearrange("p (c t) -> p c t", c=4)
    ckvn = A.alloc(BF16, 4 * T); ckvnv = ckvn.rearrange("p (c t) -> p c t", c=4)
    kpe = A.alloc(F32, T)
    kpesw = A.alloc(F32, T)
    t_cqn, t_ckvn, t_kpe = Tok(), Tok(), Tok()
    rstd = A.alloc(F32, 512); t_rstd = Tok()
    sqb = [A.alloc(BF16, 512) for _ in range(3)]; t_sq = [Tok() for _ in range(3)]
    A.mark()
    wdb = A.alloc(BF16, NCH * 1152); wdbv = wdb.rearrange("p (c f) -> p c f", c=NCH)
    t_wdb = Tok()
    stg = [A.alloc(F32, NCH * 128) for _ in range(2)]; t_stg = [Tok() for _ in range(2)]
    ds_stg = [P.dsem() for _ in range(2)]
    wdv = wd.rearrange("(c p) f -> p c f", p=128)
    for i in range(9):
        s = i % 2
        P.dma("sp", stg[s].rearrange("p (c f) -> p c f", c=NCH), wdv[:, :, i * 128:(i + 1) * 128], ds_stg[s], (), (t_stg[s],))
        P.copy("pool", wdbv[:, :, i * 128:(i + 1) * 128], stg[s].rearrange("p (c f) -> p c f", c=NCH), (t_stg[s],), (t_wdb,))
    if STOP == 11:
        P.barrier(); A.release(); A.release(); return
    xt = A.alloc(F32, NCH * 512); t_xt = Tok(); ds_xt = P.dsem()
    hT = A.alloc(BF16, NCH * 512); hTv = hT.rearrange("p (c t) -> p c t", c=NCH); t_h = Tok()
    cT = A.alloc(F32, 8 * 512); cTv = cT.rearrange("p (c t) -> p c t", c=8); t_cT = Tok()
    t_b = [Tok(True) for _ in range(8)]
    for tt in range(4):
        tsl = slice(tt * 512, (tt + 1) * 512)
        P.dma("sp", xt.rearrange("p (c t) -> p c t", c=NCH), src_v[:, :, tsl], ds_xt, cx.xs_tok(None, tt), (t_xt,))
        rmsnorm_tile(cx, xt, t_xt, gcol, lambda c: hTv[:, c, :], t_h, 512, sqb, t_sq, rstd, t_rstd, cx.bank(6), t_b[6])
        for oc in range(10):
            if STOP == 12 or (STOP == 13 and oc >= 8):
                break
            b = oc % 2
            bank = cx.bank(b)
            if oc < 8:
                for c in range(NCH):
                    P.mm(bank, wdbv[:, c, oc * 128:(oc + 1) * 128], hTv[:, c, :], c == 0, c == NCH - 1, (t_wdb, t_h), (t_b[b],))
                eng = "act" if oc % 2 == 0 else "dve"
                P.copy(eng, cTv[:, oc, :], bank, (t_b[b],), (t_cT,))
            else:
                c0 = 1024 + (oc - 8) * 64
                for c in range(NCH):
                    P.mm(bank[0:64, :], wdbv[:, c, c0:c0 + 64], hTv[:, c, :], c == 0, c == NCH - 1, (t_wdb, t_h), (t_b[b],))
                dstb = kpe if oc == 8 else kpesw
                P.copy("act", dstb[0:64, tsl], bank[0:64, :], (t_b[b],), (t_kpe,))
        for which in range(2):
            if STOP in (12, 13, 14):
                break
            for c in range(4):
                s = c % 3
                P.actf(sqb[s], cTv[:, which * 4 + c, :], AF.Square, (t_cT,), (t_sq[s],))
                P.mm(cx.bank(6), cx.ones1_bf, sqb[s], c == 0, c == 3, (t_sq[s], cx.t_const), (t_b[6],), inc=True)
            norm_from_bank(cx, rstd, t_rstd, cx.bank(6), t_b[6], 512, 1.0 / 512.0, RMS_EPS)
            dstv, tk = (cqnv, t_cqn) if which == 0 else (ckvnv, t_ckvn)
            for c in range(4):
                P.stt("dve", dstv[:, c, tsl], cTv[:, which * 4 + c, :], mc[:, which * 4 + c: which * 4 + c + 1], rstd,
                      ALU.mult, ALU.mult, (t_cT, t_rstd, t_mc), (tk,))
    P.barrier()
    A.release()
    for d in ds_stg + [ds_xt]:
        P.free_dsems.append(d)
    if STOP == 1:
        A.release(); return
    Cq = A.alloc(F32, T); Sq = A.alloc(F32, T)
    t_tab = Tok()
    kperot = A.alloc(F32, T); sqkpe = A.alloc(BF16, T); t_kr = Tok()
    OT = A.alloc(BF16, MLA_H * T); OTv = OT.rearrange("p (h t) -> p h t", h=MLA_H); t_OT = Tok()
    A.mark()
    posi = A.alloc(I32, T); posf = A.alloc(F32, T); ang = posf
    t_pos, t_ang = Tok(), Tok()
    t_ang = t_pos
    Ck = A.alloc(F32, T); Sk = A.alloc(F32, T)
    tmp = A.alloc(F32, T); t_tmp = Tok()
    P.dma("sp", posi[0:64, :], pos_d, dsm, (), (t_pos,))
    P.copy("dve", posf[0:64, :], posi[0:64, :], (t_pos,), (t_pos,))
    TWO_PI = float(2.0 * np.pi)
    PI = float(np.pi)
    P.ts("dve", ang[0:64, :], posf[0:64, :], mc[0:64, 14:15], None, ALU.mult, None, (t_pos, t_mc), (t_ang,))
    ki = posi
    C1 = 6.28125
    C2 = float(2.0 * np.pi - 6.28125)

    def sin_of(out, shift):
        P.ts("dve", tmp[0:64, :], ang[0:64, :], shift, 1.0 / TWO_PI, ALU.add, ALU.mult, (t_ang,), (t_tmp,))
        P.copy("dve", ki[0:64, :], tmp[0:64, :], (t_tmp,), (t_ki,))
        P.copy("dve", tmp[0:64, :], ki[0:64, :], (t_ki,), (t_tmp,))
        P.ts("dve", out, ang[0:64, :], shift, None, ALU.add, None, (t_ang,), (t_tab,))
        P.stt("dve", out, tmp[0:64, :], -C1, out, ALU.mult, ALU.add, (t_tmp, t_tab), (t_tab,))
        P.stt("dve", out, tmp[0:64, :], -C2, out, ALU.mult, ALU.add, (t_tmp, t_tab), (t_tab,))
        P.ts("dve", tmp[0:64, :], out, PI, TWO_PI, ALU.is_gt, ALU.mult, (t_tab,), (t_tmp,))
        P.tt("dve", out, out, tmp[0:64, :], ALU.subtract, (t_tab, t_tmp), (t_tab,))
        P.ts("dve", out, out, -PI, PI, ALU.max, ALU.min, (t_tab,), (t_tab,))
        P.actf(out, out, AF.Sin, (t_tab,), (t_tab,))

    t_ki = Tok()
    sin_of(Sq[0:64, :], 0.0)
    sin_of(Cq[0:64, :], 0.5 * PI)
    P.ts("dve", Sq[0:64, :], Sq[0:64, :], mc[0:64, 15:16], None, ALU.mult, None, (t_tab, t_mc), (t_tab,))
    P.ts("dve", Ck[0:64, :], Cq[0:64, :], mc[0:64, 12:13], None, ALU.mult, None, (t_tab, t_mc), (t_tab,))
    P.ts("dve", Sk[0:64, :], Sq[0:64, :], mc[0:64, 13:14], None, ALU.mult, None, (t_tab, t_mc), (t_tab,))
    P.ts("dve", Cq[0:64, :], Cq[0:64, :], mc[0:64, 10:11], None, ALU.mult, None, (t_tab, t_mc), (t_tab,))
    P.ts("dve", Sq[0:64, :], Sq[0:64, :], mc[0:64, 11:12], None, ALU.mult, None, (t_tab, t_mc), (t_tab,))
    P.tt("dve", kperot[0:64, :], kpe[0:64, :], Ck[0:64, :], ALU.mult, (t_kpe, t_tab), (t_kr,))
    P.tt("dve", tmp[0:64, :], kpesw[0:64, :], Sk[0:64, :], ALU.mult, (t_kpe, t_tab), (t_tmp,))
    P.tt("dve", kperot[0:64, :], kperot[0:64, :], tmp[0:64, :], ALU.add, (t_kr, t_tmp), (t_kr,))
    P.memset("pool", sqkpe, 0.0, (t_kr,))
    P.actf(sqkpe[0:64, :], kpe[0:64, :], AF.Square, (t_kpe,), (t_kr,))
    P.barrier()
    A.release()
    if STOP == 2:
        A.release(); return
    A.mark()
    wq_s = [A.alloc(F32, 4 * 256) for _ in range(2)]; wq_b = [A.alloc(BF16, 4 * 256) for _ in range(2)]
    wk_s = [A.alloc(F32, 4 * 256) for _ in range(2)]; wk_b = [A.alloc(BF16, 4 * 256) for _ in range(2)]
    t_wqs = [Tok() for _ in range(2)]; t_wqb = [Tok() for _ in range(2)]
    t_wks = [Tok() for _ in range(2)]; t_wkb = [Tok() for _ in range(2)]
    ds_w = [P.dsem() for _ in range(2)]
    wuqv = wuq.rearrange("(c p) h f -> p c h f", p=128)
    wukvv = wukv.rearrange("(c p) h f -> p c h f", p=128)
    qn = A.alloc(BF16, T); qr = A.alloc(BF16, T); kn = A.alloc(BF16, T); kr = A.alloc(BF16, T)
    t_q, t_k = Tok(), Tok()
    P.memset("pool", qr, 0.0, (t_q,))
    P.memset("pool", kr, 0.0, (t_k,))
    P.memset("pool", sqb[1], 0.0, (t_sq[1],))
    Vb = A.alloc(BF16, 16 * 128); Vv = Vb.rearrange("p (s d) -> p s d", s=16); t_V = Tok()
    pT = [A.alloc(BF16, 512) for _ in range(3)]; t_pT = [Tok() for _ in range(3)]
    rs = A.alloc(F32, 512); t_rs = Tok()
    t1 = A.alloc(F32, 512); t2 = A.alloc(F32, 512); t_t1, t_t2 = Tok(), Tok()
    t_b = [Tok(True) for _ in range(8)]
    pj = 0
    sc = 0
    for h in range(MLA_H):
        s = h % 2
        P.dma("sp", wq_s[s].rearrange("p (c f) -> p c f", c=4), wuqv[:, :, h, :], ds_w[s], (), (t_wqs[s],))
        P.dma("sp", wk_s[s].rearrange("p (c f) -> p c f", c=4), wukvv[:, :, h, :], ds_w[s], (), (t_wks[s],))
        P.copy("pool", wq_b[s], wq_s[s], (t_wqs[s],), (t_wqb[s],))
        P.copy("pool", wk_b[s], wk_s[s], (t_wks[s],), (t_wkb[s],))
        wqv = wq_b[s].rearrange("p (c f) -> p c f", c=4)
        wkv = wk_b[s].rearrange("p (c f) -> p c f", c=4)
        for g in range(4):
            b = 6 + (pj % 2); pj += 1
            for i in range(4):
                st = g * 4 + i
                for c in range(4):
                    P.mm(cx.bank(b)[:, i * 128:(i + 1) * 128], ckvnv[:, c, st * 128:(st + 1) * 128], wkv[:, c, 128:256],
                         c == 0, c == 3, (t_ckvn, t_wkb[s]), (t_b[b],), inc=(c == 3 and i == 3))
            P.copy("act", Vb[:, g * 512:(g + 1) * 512], cx.bank(b), (t_b[b],), (t_V,))
        if STOP == 31:
            continue
        for tt in range(4):
            tsl = slice(tt * 512, (tt + 1) * 512)
            for c in range(4):
                P.mm(cx.bank(6)[0:64, :], wqv[:, c, 128:192], cqnv[:, c, tsl], c == 0, c == 3, (t_wqb[s], t_cqn), (t_b[6],))
            P.actf(sqb[1][0:64, :], cx.bank(6)[0:64, :], AF.Square, (t_b[6],), (t_sq[1],))
            P.tt("dve", t1[0:64, :], cx.bank(6)[0:64, :], Cq[0:64, tsl], ALU.mult, (t_b[6], t_tab), (t_t1,))
            for c in range(4):
                P.mm(cx.bank(7)[0:64, :], wqv[:, c, 192:256], cqnv[:, c, tsl], c == 0, c == 3, (t_wqb[s], t_cqn), (t_b[7],))
            P.tt("dve", t2[0:64, :], cx.bank(7)[0:64, :], Sq[0:64, tsl], ALU.mult, (t_b[7], t_tab), (t_t2,))
            P.tt("dve", t1[0:64, :], t1[0:64, :], t2[0:64, :], ALU.add, (t_t1, t_t2), (t_t1,))
            for c in range(4):
                P.mm(cx.bank(6), wqv[:, c, 0:128], cqnv[:, c, tsl], c == 0, c == 3, (t_wqb[s], t_cqn), (t_b[6],))
            P.actf(sqb[0], cx.bank(6), AF.Square, (t_b[6],), (t_sq[0],))
            P.mm(cx.bank(5), cx.ones1_bf, sqb[0], True, False, (t_sq[0], cx.t_const), (t_b[5],), inc=True)
            P.mm(cx.bank(5), cx.ones1_bf, sqb[1], False, True, (t_sq[1], cx.t_const), (t_b[5],), inc=True)
            norm_from_bank(cx, rstd, t_rstd, cx.bank(5), t_b[5], 512, 1.0 / 192.0, RMS_EPS)
            P.stt("dve", qn[:, tsl], cx.bank(6), mc[:, 8:9], rstd, ALU.mult, ALU.mult, (t_b[6], t_rstd, t_mc), (t_q,))
            P.tt("dve", qr[0:64, tsl], t1[0:64, :], rstd[0:64, :], ALU.mult, (t_t1, t_rstd), (t_q,))
            for c in range(4):
                P.mm(cx.bank(7), wkv[:, c, 0:128], ckvnv[:, c, tsl], c == 0, c == 3, (t_wkb[s], t_ckvn), (t_b[7],))
            P.actf(sqb[2], cx.bank(7), AF.Square, (t_b[7],), (t_sq[2],))
            P.mm(cx.bank(5), cx.ones1_bf, sqb[2], True, False, (t_sq[2], cx.t_const), (t_b[5],), inc=True)
            P.mm(cx.bank(5), cx.ones1_bf, sqkpe[:, tsl], False, True, (t_kr, cx.t_const), (t_b[5],), inc=True)
            norm_from_bank(cx, rstd, t_rstd, cx.bank(5), t_b[5], 512, 1.0 / 192.0, RMS_EPS)
            P.stt("dve", kn[:, tsl], cx.bank(7), mc[:, 9:10], rstd, ALU.mult, ALU.mult, (t_b[7], t_rstd, t_mc), (t_k,))
            P.tt("dve", kr[0:64, tsl], kperot[0:64, tsl], rstd[0:64, :], ALU.mult, (t_kr, t_rstd), (t_k,))
        if STOP == 32:
            continue
        for qt in range(4):
            qb0 = qt * 4
            bO, bS = 3, 4
            nkt = qb0 + 4
            for kt in range(nkt):
                c0 = max(0, kt - qb0) * 128
                n = 512 - c0
                qsl = slice(qt * 512 + c0, (qt + 1) * 512)
                ksl = slice(kt * 128, (kt + 1) * 128)
                b = sc % 3; sc += 1
                ip = b
                P.mm(cx.bank(b)[:, 0:n], kn[:, ksl], qn[:, qsl], True, False, (t_k, t_q), (t_b[b],), inc=False)
                P.mm(cx.bank(b)[:, 0:n], kr[:, ksl], qr[:, qsl], False, True, (t_k, t_q), (t_b[b],))
                P.actf(pT[ip][:, 0:n], cx.bank(b)[:, 0:n], AF.Exp, (t_b[b],), (t_pT[ip],), scale=SM_SCALE)
                if kt >= qb0:
                    P.memset("pool", pT[ip][64:128, 0:64], 0.0, (t_pT[ip],))
                P.mm(cx.bank(bO)[:, c0:512], Vv[:, kt, :], pT[ip][:, 0:n], kt == 0, kt == nkt - 1, (t_V, t_pT[ip]), (t_b[bO],), inc=False)
                P.mm(cx.bank(bS)[:, c0:512], cx.ones1_bf, pT[ip][:, 0:n], kt == 0, kt == nkt - 1, (cx.t_const, t_pT[ip]), (t_b[bS],), inc=True)
            P.actf(rs, cx.bank(bS), AF.Ln, (t_b[bS],), (t_rs,))
            P.actf(rs, rs, AF.Exp, (t_rs,), (t_rs,), scale=-1.0)
            P.tt("dve", OTv[:, h, qt * 512:(qt + 1) * 512], cx.bank(bO), rs, ALU.mult, (t_b[bO], t_rs), (t_OT,))
    P.barrier()
    A.release()
    if STOP == 3:
        A.release(); return
    wo_s = [A.alloc(F32, MLA_H * 128) for _ in range(2)]; wo_b = [A.alloc(BF16, MLA_H * 128) for _ in range(2)]
    t_wos = [Tok() for _ in range(2)]; t_wob = [Tok() for _ in range(2)]
    xold = [A.alloc(F32, 512) for _ in range(2)]; t_xold = [Tok() for _ in range(2)]; ds_xold = [P.dsem() for _ in range(2)]
    xnew = [A.alloc(F32, 512) for _ in range(2)]; t_xnew = [Tok() for _ in range(2)]; ds_xnew = [P.dsem() for _ in range(2)]
    wov = wo.rearrange("h p e -> p h e")
    k = 0
    for e in range(NCH):
        s = e % 2
        P.dma("sp", wo_s[s].rearrange("p (h e) -> p h e", h=MLA_H), wov[:, :, e * 128:(e + 1) * 128], ds_w[s], (), (t_wos[s],))
        P.copy("pool", wo_b[s], wo_s[s], (t_wos[s],), (t_wob[s],))
        wb = wo_b[s].rearrange("p (h e) -> p h e", h=MLA_H)
        for tt in range(4):
            bi = k % 2; k += 1
            b = 6 + bi
            tsl = slice(tt * 512, (tt + 1) * 512)
            P.dma("act", xold[bi], src_v[:, e, tsl], ds_xold[bi], cx.xs_tok(e, tt), (t_xold[bi],))
            for h in range(MLA_H):
                P.mm(cx.bank(b), wb[:, h, :], OTv[:, h, tsl], h == 0, h == MLA_H - 1, (t_wob[s], t_OT), (t_b[b],))
            P.tt("dve", xnew[bi], cx.bank(b), xold[bi], ALU.add, (t_b[b], t_xold[bi]), (t_xnew[bi],))
            P.dma("pool", dst_v[:, e, tsl], xnew[bi], ds_xnew[bi], (t_xnew[bi],), cx.xs_tok(e, tt))
    P.barrier()
    for d in ds_w + ds_xold + ds_xnew + [dsm]:
        P.free_dsems.append(d)
    A.release()


RW_L = 64
RW_NCK = T // RW_L
DEC_C = float(np.exp(-0.5))


def linear_fm(cx, actv, t_act, w_ap, n_in_chunks, out_cols, evac, bank0=0, M=128):
    P, A = cx.P, cx.A
    A.mark()
    stg = [A.alloc(F32, n_in_chunks * 128) for _ in range(2)]
    wb = [A.alloc(BF16, n_in_chunks * 128) for _ in range(2)]
    t_s = [Tok() for _ in range(2)]; t_w = [Tok() for _ in range(2)]; t_w2 = [Tok() for _ in range(2)]
    ds = [P.dsem() for _ in range(2)]
    t_bk = [Tok(True) for _ in range(2)]
    wv = w_ap.rearrange("(c p) f -> p c f", p=128)
    k = 0
    for oi, (c0, m) in enumerate(out_cols):
        s = oi % 2
        sv = stg[s].rearrange("p (c f) -> p c f", c=n_in_chunks)
        bv = wb[s].rearrange("p (c f) -> p c f", c=n_in_chunks)
        P.dma("sp", sv[:, :, 0:m], wv[:, :, c0:c0 + m], ds[s], (), (t_s[s],))
        hc = n_in_chunks // 2
        P.copy("dve", bv[:, 0:hc, 0:m], sv[:, 0:hc, 0:m], (t_s[s],), (t_w[s],))
        P.copy("act", bv[:, hc:, 0:m], sv[:, hc:, 0:m], (t_s[s],), (t_w2[s],))
        for tt in range(4):
            bi = k % 2; k += 1
            bank = cx.bank(bank0 + bi)
            for c in range(n_in_chunks):
                P.mm(bank[0:m, :], bv[:, c, 0:m], actv[:, c, tt * 512:(tt + 1) * 512], c == 0, c == n_in_chunks - 1,
                     (t_w[s] if c < hc else t_w2[s], t_act), (t_bk[bi],))
            evac(oi, tt, bank, t_bk[bi])
    P.barrier()
    for d in ds:
        P.free_dsems.append(d)
    A.release()


def phase_rwkv(cx, xs_src, xs_dst, W, gcol, rc_d):
    P, A = cx.P, cx.A
    A.mark()
    src_v = xs_src.rearrange("c p t -> p c t")
    dst_v = xs_dst.rearrange("c p t -> p c t")
    rc = A.alloc(F32, 13 * 16); t_rc = Tok(); dsm = P.dsem()
    P.dma("sp", rc, rc_d, dsm, (), (t_rc,))
    omka = A.alloc(F32, 16)
    P.ts("dve", omka, rc[:, 9 * 16:10 * 16], -1.0, 1.0, ALU.mult, ALU.add, (t_rc,), (t_rc,))
    col = lambda k, e: rc[:, k * 16 + e: k * 16 + e + 1]
    t_xm = cx.t_xmix
    A.mark()
    TT = 256
    xt = [A.alloc(F32, NCH * TT) for _ in range(2)]; t_xt = [Tok() for _ in range(2)]; ds_xt = [P.dsem() for _ in range(2)]
    hb = A.alloc(F32, NCH * (TT + 1)); hbv = hb.rearrange("p (c t) -> p c t", c=NCH); t_hb = Tok()
    xx = A.alloc(F32, NCH * TT); xxv = xx.rearrange("p (c t) -> p c t", c=NCH); t_xx = Tok()
    xm = [A.alloc(BF16, NCH * TT) for _ in range(6)]; t_xmb = [Tok() for _ in range(6)]; ds_xm = [P.dsem() for _ in range(6)]
    sqb = [A.alloc(BF16, 512) for _ in range(3)]; t_sq = [Tok() for _ in range(3)]
    rstd = A.alloc(F32, 512); t_rstd = Tok()
    t_bn = Tok(True)
    P.memset("dve", hbv[:, :, 0:1], 0.0, (t_hb,))
    for ti in range(T // TT):
        s = ti % 2
        tsl = slice(ti * TT, (ti + 1) * TT)
        P.dma("sp", xt[s].rearrange("p (c t) -> p c t", c=NCH), src_v[:, :, tsl], ds_xt[s], cx.xs_tok(None, ti // 2), (t_xt[s],))
        if ti > 0:
            P.copy("dve", hbv[:, :, 0:1], hbv[:, :, TT:TT + 1], (t_hb,), (t_hb,))
        rmsnorm_tile(cx, xt[s], t_xt[s], gcol, lambda c: hbv[:, c, 1:TT + 1], t_hb, TT, sqb, t_sq, rstd, t_rstd,
                     cx.bank(6), t_bn)
        P.tt("dve", xxv, hbv[:, :, 0:TT], hbv[:, :, 1:TT + 1], ALU.subtract, (t_hb,), (t_xx,))
        for j in range(6):
            xmv = xm[j].rearrange("p (c t) -> p c t", c=NCH)
            for c in range(NCH):
                P.stt("dve", xmv[:, c, :], xxv[:, c, :], col(j, c), hbv[:, c, 1:TT + 1], ALU.mult, ALU.add,
                      (t_xx, t_hb, t_rc), (t_xmb[j],))
            P.dma("sp", cx.xmix[j].rearrange("c p t -> p c t")[:, :, tsl], xmv, ds_xm[j], (t_xmb[j],), (t_xm,))
    P.barrier()
    for d in ds_xt + ds_xm:
        P.free_dsems.append(d)
    A.release()
    A.mark()
    lw = A.alloc(BF16, T); la = A.alloc(BF16, T); lg = A.alloc(BF16, 2 * T); lgv = lg.rearrange("p (c t) -> p c t", c=2)
    t_l = Tok()
    acts = [A.alloc(BF16, NCH * T) for _ in range(2)]
    t_acts = [Tok() for _ in range(2)]; ds_act = [P.dsem() for _ in range(2)]
    ot = [A.alloc(F32, 512) for _ in range(4)]; t_ot = [Tok() for _ in range(4)]; ds_ot = [P.dsem() for _ in range(4)]
    cnt = [0]
    order = [(0, "r"), (2, "k"), (3, "v"), (1, "lw"), (4, "la"), (5, "lg")]

    def load_act(j):
        av = acts[j % 2].rearrange("p (c t) -> p c t", c=NCH)
        for c4 in range(4):
            P.dma("sp", av[:, c4 * 4:(c4 + 1) * 4, :], cx.xmix[order[j][0]].rearrange("c p t -> p c t")[:, c4 * 4:(c4 + 1) * 4, :],
                  ds_act[j % 2], (t_xm,), (t_acts[j % 2],))

    def store_tile(ji, oi, tt, src_engine_copy):
        i = cnt[0] % 4; cnt[0] += 1
        src_engine_copy(ot[i], t_ot[i], i)
        P.dma("pool", cx.rkv[ji].rearrange("c p t -> p c t")[:, oi, tt * 512:(tt + 1) * 512], ot[i], ds_ot[i],
              (t_ot[i],), (cx.t_rkv,))

    load_act(0)
    for j, (src_j, kind) in enumerate(order):
        if j + 1 < len(order):
            load_act(j + 1)
        actv = acts[j % 2].rearrange("p (c t) -> p c t", c=NCH)
        t_act = t_acts[j % 2]
        if kind in ("r", "k", "v"):
            ji = "rkv".index(kind)

            def ev(oi, tt, bank, tb, ji=ji):
                def cp(dst, tk, i):
                    P.copy("act" if i % 2 == 0 else "dve", dst, bank, (tb,), (tk,))
                store_tile(ji, oi, tt, cp)
            linear_fm(cx, actv, t_act, W["w_rkv"][ji], NCH, [(e * 128, 128) for e in range(NCH)], ev)
        elif kind == "lw":
            def ev(oi, tt, bank, tb):
                P.actf(lw[0:96, tt * 512:(tt + 1) * 512], bank[0:96, :], AF.Tanh, (tb,), (t_l,))
            linear_fm(cx, actv, t_act, W["w1"], NCH, [(0, 96)], ev)
        elif kind == "la":
            def ev(oi, tt, bank, tb):
                P.copy("act", la[0:96, tt * 512:(tt + 1) * 512], bank[0:96, :], (tb,), (t_l,))
            linear_fm(cx, actv, t_act, W["a1"], NCH, [(0, 96)], ev)
        else:
            def ev(oi, tt, bank, tb):
                P.actf(lgv[:, oi, tt * 512:(tt + 1) * 512], bank, AF.Sigmoid, (tb,), (t_l,))
            linear_fm(cx, actv, t_act, W["g1"], NCH, [(0, 128), (128, 128)], ev)
    A.mark()
    w2b = A.alloc(BF16, D); a2b = A.alloc(BF16, D); g2b = A.alloc(BF16, 2 * D); g2bv = g2b.rearrange("p (c f) -> p c f", c=2)
    t_lw2 = Tok()
    st32 = A.alloc(F32, 2 * D)
    P.dma("sp", st32[0:96, 0:D], W["w2"], dsm, (), (t_lw2,))
    P.copy("dve", w2b[0:96, :], st32[0:96, 0:D], (t_lw2,), (t_lw2,))
    P.dma("sp", st32[0:96, 0:D], W["a2"], dsm, (t_lw2,), (t_lw2,))
    P.copy("dve", a2b[0:96, :], st32[0:96, 0:D], (t_lw2,), (t_lw2,))
    P.dma("sp", st32.rearrange("p (c f) -> p c f", c=2), W["g2"].rearrange("(c p) f -> p c f", p=128), dsm, (t_lw2,), (t_lw2,))
    P.copy("dve", g2b, st32, (t_lw2,), (t_lw2,))
    t_b2 = [Tok(True) for _ in range(3)]
    for e in range(NCH):
        esl = slice(e * 128, (e + 1) * 128)
        for tt in range(4):
            tsl = slice(tt * 512, (tt + 1) * 512)
            P.mm(cx.bank(0), w2b[0:96, esl], lw[0:96, tsl], True, True, (t_lw2, t_l), (t_b2[0],))
            store_tile(3, e, tt, lambda dst, tk, i, e=e: P.actf(dst, cx.bank(0), AF.Sigmoid, (t_b2[0], t_rc), (tk,), bias=col(6, e)))
            P.mm(cx.bank(1), a2b[0:96, esl], la[0:96, tsl], True, True, (t_lw2, t_l), (t_b2[1],))
            store_tile(4, e, tt, lambda dst, tk, i, e=e: P.actf(dst, cx.bank(1), AF.Sigmoid, (t_b2[1], t_rc), (tk,), bias=col(7, e)))
            for c in range(2):
                P.mm(cx.bank(2), g2bv[:, c, esl], lgv[:, c, tsl], c == 0, c == 1, (t_lw2, t_l), (t_b2[2],))
            store_tile(5, e, tt, lambda dst, tk, i: P.copy("dve", dst, cx.bank(2), (t_b2[2],), (tk,)))
    P.barrier()
    A.release()
    P.free_dsems.extend(ds_act + ds_ot)
    A.release()
    if STOP == 52:
        A.release(); return
    t_yg = Tok()
    A.mark()
    msk = A.alloc(F32, 3 * 512); t_k = Tok()
    P.dma("sp", msk, cx.rw_masks_d, dsm, (), (t_k,))
    ML_s, MU_s, MU_i = msk[:, 0:512], msk[:, 512:1024], msk[:, 1024:1536]
    id4 = A.alloc(BF16, 512)
    for q in range(4):
        P.copy("dve", id4[:, q * 128:(q + 1) * 128], cx.ident, (cx.t_const,), (t_k,))
    idb = id4[:, 0:128]
    bones = A.alloc(F32, 128)
    P.memset("pool", bones, 0.0, (t_k,))
    P.memset("pool", bones[0:64, 0:64], 1.0, (t_k,))
    P.memset("pool", bones[64:128, 64:128], 1.0, (t_k,))
    rmask = A.alloc(BF16, T)
    P.memset("pool", rmask, 1.0, (t_k,))
    P.memset("pool", rmask.rearrange("p (c t) -> p c t", t=RW_L)[:, :, 0:1], 0.0, (t_k,))
    xop = [A.alloc(BF16, RW_NCK * 128) for _ in range(7)]
    t_xop = Tok()
    for x_ in xop:
        P.memset("pool", x_, 0.0, (t_xop,))
    RTx, KTx, BTx, KHx, BHx, ATx, Vx = xop
    gam = A.alloc(F32, RW_NCK); t_gam = Tok()
    bon = A.alloc(F32, T); t_bon = Tok()
    psb = cx.ps_bf

    def xview(xo, h):
        return xo.rearrange("p (c i) -> p c i", i=128)[h * 64:(h + 1) * 64, :, h * 64:(h + 1) * 64]

    def hview(ap, h):
        return ap.rearrange("p (c t) -> p c t", t=RW_L)[h * 64:(h + 1) * 64, :, :]

    def ch(xo, c):
        return xo[:, c * 128:(c + 1) * 128]

    def q4(ap, q):
        return ap[:, q * 128:(q + 1) * 128]

    NG = RW_NCK // 4
    NSLOT = 3
    for e in range(NCH):
        A.mark()
        r_, k_, v_, a_, cum, sg_, tA, tB, tC = [A.alloc(F32, T) for _ in range(9)]
        t_r, t_kk, t_v, t_a, t_cum, t_sg, t_tA, t_tB, t_tC = [Tok() for _ in range(9)]
        t_bk = [Tok(True) for _ in range(8)]
        for ji, (dst, tk) in enumerate([(r_, t_r), (k_, t_kk), (v_, t_v), (sg_, t_sg), (a_, t_a)]):
            P.dma("sp", dst, cx.rkv[ji][e], dsm, (cx.t_rkv,), (tk,))
        P.op("dve", lambda en, cum=cum, sg_=sg_: en.tensor_tensor_scan(cum, rmask, sg_, 0.0, ALU.mult, ALU.add),
             (t_sg, t_k), (t_cum,))
        cumv = cum.rearrange("p (c t) -> p c t", t=RW_L)
        P.ts("dve", tA, k_, col(8, e), None, ALU.mult, None, (t_kk, t_rc), (t_tA,))
        P.actf(tB, tA, AF.Square, (t_tA,), (t_tB,))
        for tt in range(4):
            tsl = slice(tt * 512, (tt + 1) * 512)
            P.mm(cx.bank(3), bones, tB[:, tsl], True, True, (t_k, t_tB), (t_bk[3],))
            P.ts("dve", tC[:, tsl], cx.bank(3), 1e-18, None, ALU.max, None, (t_bk[3],), (t_tC,))
        P.actf(tC, tC, AF.Ln, (t_tC,), (t_tC,))
        P.actf(tC, tC, AF.Exp, (t_tC,), (t_tC,), scale=-0.5)
        P.tt("dve", tA, tA, tC, ALU.mult, (t_tA, t_tC), (t_tA,))
        P.tt("dve", tB, tA, a_, ALU.mult, (t_tA, t_a), (t_tB,))
        P.ts("dve", tC, a_, col(9, e), omka[:, e:e + 1], ALU.mult, ALU.add, (t_a, t_rc), (t_tC,))
        P.tt("dve", k_, k_, tC, ALU.mult, (t_kk, t_tC), (t_kk,))
        P.stt("dve", tC, r_, col(10, e), k_, ALU.mult, ALU.mult, (t_r, t_kk, t_rc), (t_tC,))
        for tt in range(4):
            tsl = slice(tt * 512, (tt + 1) * 512)
            P.mm(cx.bank(4), bones, tC[:, tsl], True, True, (t_k, t_tC), (t_bk[4],))
            P.tt("dve", bon[:, tsl], cx.bank(4), v_[:, tsl], ALU.mult, (t_bk[4], t_v), (t_bon,))
        P.actf(tC, cum, AF.Exp, (t_cum,), (t_tC,), scale=-DEC_C)
        P.copy("dve", gam, cumv[:, :, RW_L - 1], (t_cum,), (t_gam,))
        P.actf(gam, gam, AF.Exp, (t_gam,), (t_gam,), scale=-DEC_C)
        for h in range(2):
            P.tt("dve", xview(RTx, h), hview(r_, h), hview(tC, h), ALU.mult, (t_r, t_tC), (t_xop,))
        P.tt("dve", r_, cum, sg_, ALU.subtract, (t_cum, t_sg, t_r), (t_r,))
        P.actf(r_, r_, AF.Exp, (t_r,), (t_r,), scale=-DEC_C)
        for h in range(2):
            P.stt("dve", xview(ATx, h), hview(tA, h), -1.0, hview(r_, h), ALU.mult, ALU.mult, (t_tA, t_r), (t_xop,))
        P.actf(tC, cum, AF.Exp, (t_cum,), (t_tC,), scale=DEC_C)
        for h in range(2):
            P.tt("dve", xview(KTx, h), hview(k_, h), hview(tC, h), ALU.mult, (t_kk, t_tC), (t_xop,))
            P.tt("dve", xview(BTx, h), hview(tB, h), hview(tC, h), ALU.mult, (t_tB, t_tC), (t_xop,))
        P.tt("dve", r_.rearrange("p (c t) -> p c t", t=RW_L), cumv[:, :, RW_L - 1:RW_L].to_broadcast([128, RW_NCK, RW_L]),
             cumv, ALU.subtract, (t_cum, t_r), (t_r,))
        P.actf(r_, r_, AF.Exp, (t_r,), (t_r,), scale=-DEC_C)
        for h in range(2):
            P.tt("dve", xview(KHx, h), hview(k_, h), hview(r_, h), ALU.mult, (t_kk, t_r), (t_xop,))
            P.tt("dve", xview(BHx, h), hview(tB, h), hview(r_, h), ALU.mult, (t_tB, t_r), (t_xop,))
            P.copy("act", xview(Vx, h), hview(v_, h), (t_v,), (t_xop,))
        P.barrier()
        A.release()
        A.mark()
        GT = A.alloc(BF16, RW_NCK * 128); Hh = A.alloc(F32, RW_NCK * 128); Rb = A.alloc(BF16, RW_NCK * 128)
        y0 = A.alloc(F32, T); Sall = A.alloc(BF16, RW_NCK * 128); yT = A.alloc(F32, T)
        t_GT = [Tok() for _ in range(NG)]; t_H = [Tok() for _ in range(NG)]; t_Rb = [Tok() for _ in range(NG)]
        t_y0 = [Tok() for _ in range(NG)]
        t_S, t_y = Tok(), Tok()
        gT = A.alloc(F32, T); t_g = Tok()
        P.dma("sp", gT, cx.rkv[5][e], dsm, (cx.t_rkv,), (t_g,))
        t_bk = [Tok(True) for _ in range(8)]
        slots = []
        for sl in range(NSLOT):
            d_ = {}
            d_["TMb"] = [A.alloc(BF16, 512) for _ in range(3)]
            d_["WUin"] = A.alloc(BF16, 4 * 256); d_["WU"] = A.alloc(BF16, 4 * 256)
            d_["Nb"] = [A.alloc(BF16, 512) for _ in range(2)]; d_["Qb"] = [A.alloc(BF16, 512) for _ in range(2)]
            d_["Pb"] = [A.alloc(BF16, 512) for _ in range(2)]
            d_["M"] = [A.alloc(BF16, 512) for _ in range(3)]
            d_["tok"] = {k: Tok() for k in ("TM", "WUin", "WU", "N", "Q", "P", "M")}
            d_["banks"] = (2 * sl, 2 * sl + 1)
            slots.append(d_)
        ev_i = [0]

        def group_steps(g, sd):
            cs = [g * 4 + q for q in range(4)]
            TMb, WUin, WU, Nb, Qb, Pb = sd["TMb"], sd["WUin"], sd["WU"], sd["Nb"], sd["Qb"], sd["Pb"]
            Mak, Mrb, Mrk = sd["M"]
            tk = sd["tok"]
            WUinv = WUin.rearrange("p (q f) -> p q f", q=4)
            WUv = WU.rearrange("p (q f) -> p q f", q=4)
            bi = [0]

            def nb():
                bi[0] ^= 1
                return sd["banks"][bi[0]]

            def eng2():
                ev_i[0] += 1
                return "dve" if ev_i[0] % 2 == 0 else "act"
            for half, ops_ in enumerate([(BHx, KHx), (Vx, ATx)]):
                b = nb()
                for oi, xo in enumerate(ops_):
                    for q in range(4):
                        P.tr(psb[:, b * 1024 + (oi * 4 + q) * 128: b * 1024 + (oi * 4 + q + 1) * 128], ch(xo, cs[q]), idb,
                             (t_xop, t_k), (t_bk[b],), inc=(oi == 1 and q == 3))
                if half == 0:
                    P.copy("act", TMb[0], psb[:, b * 1024: b * 1024 + 512], (t_bk[b],), (tk["TM"],))
                    P.copy("dve", TMb[1], psb[:, b * 1024 + 512: b * 1024 + 1024], (t_bk[b],), (tk["TM"],))
                else:
                    P.copy("act", TMb[2], psb[:, b * 1024: b * 1024 + 512], (t_bk[b],), (tk["TM"],))
                    P.copy("dve", WUinv[:, :, 0:128], psb[:, b * 1024 + 512: b * 1024 + 1024].rearrange("p (q f) -> p q f", q=4),
                           (t_bk[b],), (tk["WUin"],))
                yield
            b = nb()
            for q in range(4):
                P.mm(q4(cx.bank(b), q), ch(ATx, cs[q]), ch(BTx, cs[q]), True, True, (t_xop,), (t_bk[b],), inc=(q == 3))
            P.tt("dve", Nb[0], cx.bank(b), ML_s, ALU.mult, (t_bk[b], t_k), (tk["N"],))
            yield
            b = nb()
            for q in range(4):
                P.mm(q4(cx.bank(b), q), ch(BTx, cs[q]), ch(ATx, cs[q]), True, True, (t_xop,), (t_bk[b],), inc=(q == 3))
            P.tt("dve", Qb[0], cx.bank(b), MU_s, ALU.mult, (t_bk[b], t_k), (tk["Q"],))
            P.tt("pool", Pb[0], Qb[0], id4, ALU.add, (tk["Q"], t_k), (tk["P"],))
            yield
            for (lx, rx, mk, dst) in ((KTx, ATx, MU_s, Mak), (BTx, RTx, MU_i, Mrb), (KTx, RTx, MU_i, Mrk)):
                b = nb()
                for q in range(4):
                    P.mm(q4(cx.bank(b), q), ch(lx, cs[q]), ch(rx, cs[q]), True, True, (t_xop,), (t_bk[b],), inc=(q == 3))
                P.tt("dve", dst, cx.bank(b), mk, ALU.mult, (t_bk[b], t_k), (tk["M"],))
                yield
            pi = 0
            for j in range(1, 6):
                i0, i1 = (j - 1) % 2, j % 2
                b = nb()
                for q in range(4):
                    P.mm(q4(cx.bank(b), q), q4(Qb[i0], q), q4(Nb[i0], q), True, True, (tk["N"], tk["Q"]), (t_bk[b],), inc=(q == 3))
                if j < 5:
                    b2 = nb()
                    for q in range(4):
                        P.mm(q4(cx.bank(b2), q), q4(Nb[i0], q), q4(Qb[i0], q), True, True, (tk["N"], tk["Q"]), (t_bk[b2],), inc=(q == 3))
                P.copy("act", Nb[i1], cx.bank(b), (t_bk[b],), (tk["N"],))
                if j < 5:
                    P.copy(eng2(), Qb[i1], cx.bank(b2), (t_bk[b2],), (tk["Q"],))
                yield
                b = nb()
                for q in range(4):
                    P.mm(q4(cx.bank(b), q), idb, q4(Pb[pi], q), True, False, (t_k, tk["P"]), (t_bk[b],), inc=False)
                    P.mm(q4(cx.bank(b), q), q4(Nb[i1], q), q4(Pb[pi], q), False, True, (tk["N"], tk["P"]), (t_bk[b],), inc=(q == 3))
                P.copy(eng2(), Pb[1 - pi], cx.bank(b), (t_bk[b],), (tk["P"],))
                pi = 1 - pi
                yield
            TiT = Pb[pi]
            b = nb()
            for q in range(4):
                P.mm(q4(cx.bank(b), q), q4(Mak, q), q4(TMb[2], q), True, True, (tk["M"], tk["TM"]), (t_bk[b],), inc=(q == 3))
            P.copy("act", WUinv[:, :, 128:256], cx.bank(b).rearrange("p (q f) -> p q f", q=4), (t_bk[b],), (tk["WUin"],))
            yield
            for hf in range(2):
                b = nb()
                for qq in range(2):
                    q = hf * 2 + qq
                    P.mm(cx.bank(b)[:, qq * 256:(qq + 1) * 256], q4(TiT, q), WUinv[:, q, :], True, True, (tk["P"], tk["WUin"]),
                         (t_bk[b],), inc=(qq == 1))
                P.copy("act" if hf == 0 else "dve", WU[:, hf * 512:(hf + 1) * 512], cx.bank(b), (t_bk[b],), (tk["WU"],))
            yield
            b = nb()
            for q in range(4):
                P.mm(q4(cx.bank(b), q), WUv[:, q, 0:128], q4(TMb[0], q), True, True, (tk["WU"], tk["TM"]), (t_bk[b],), inc=(q == 3))
            P.copy("act", GT[:, g * 512:(g + 1) * 512], cx.bank(b), (t_bk[b],), (t_GT[g],))
            b = nb()
            for q in range(4):
                P.mm(q4(cx.bank(b), q), q4(TMb[0], q), WUv[:, q, 128:256], True, False, (tk["WU"], tk["TM"]), (t_bk[b],), inc=False)
                P.mm(q4(cx.bank(b), q), q4(TMb[1], q), q4(TMb[2], q), False, True, (tk["TM"],), (t_bk[b],), inc=(q == 3))
            P.copy("dve", Hh[:, g * 512:(g + 1) * 512], cx.bank(b), (t_bk[b],), (t_H[g],))
            yield
            b = nb()
            for q in range(4):
                P.mm(q4(cx.bank(b), q), idb, ch(RTx, cs[q]), True, False, (t_k, t_xop), (t_bk[b],), inc=False)
                P.mm(q4(cx.bank(b), q), WUv[:, q, 0:128], q4(Mrb, q), False, True, (tk["WU"], tk["M"]), (t_bk[b],), inc=(q == 3))
            P.copy("act", Rb[:, g * 512:(g + 1) * 512], cx.bank(b), (t_bk[b],), (t_Rb[g],))
            b = nb()
            for q in range(4):
                P.mm(q4(cx.bank(b), q), WUv[:, q, 128:256], q4(Mrb, q), True, False, (tk["WU"], tk["M"]), (t_bk[b],), inc=False)
                P.mm(q4(cx.bank(b), q), q4(TMb[2], q), q4(Mrk, q), False, True, (tk["TM"], tk["M"]), (t_bk[b],), inc=(q == 3))
            for h in range(2):
                bv = cx.bank(b).rearrange("p (q i) -> p q i", q=4)[h * 64:(h + 1) * 64, :, h * 64:(h + 1) * 64]
                P.copy("act", hview(y0, h)[:, g * 4:(g + 1) * 4, :], bv, (t_bk[b],), (t_y0[g],))
            yield

        pending = list(range(NG))
        active = []
        free_slots = list(range(NSLOT))
        while pending or active:
            while pending and free_slots:
                sl = free_slots.pop(0)
                active.append((group_steps(pending.pop(0), slots[sl]), sl))
            for item in list(active):
                gen, sl = item
                try:
                    next(gen)
                except StopIteration:
                    active.remove(item)
                    free_slots.append(sl)
        Sf = [A.alloc(F32, 128) for _ in range(2)]; tAq = [A.alloc(F32, 128) for _ in range(2)]
        t_Sf, t_tAq = [Tok() for _ in range(2)], [Tok() for _ in range(2)]
        P.memset("pool", Sall[:, 0:128], 0.0, (t_S,))
        P.copy("dve", tAq[0], ch(Hh, 0), (t_H[0],), (t_tAq[0],))
        for c in range(RW_NCK - 1):
            si = c % 2
            b = 6 + (c % 2)
            P.mm(cx.bank(b)[:, 0:128], ch(GT, c), ch(Sall, c), True, True, (t_GT[c // 4], t_S), (t_bk[b],))
            P.tt("dve", ch(Sall, c + 1), cx.bank(b)[:, 0:128], tAq[si], ALU.add, (t_bk[b], t_tAq[si]), (t_S,))
            if c + 1 < RW_NCK - 1:
                P.tt("dve", Sf[si], cx.bank(b)[:, 0:128], tAq[si], ALU.add, (t_bk[b], t_tAq[si]), (t_Sf[si],))
                P.stt("dve", tAq[1 - si], Sf[si], gam[:, c + 1:c + 2], ch(Hh, c + 1), ALU.mult, ALU.add,
                      (t_Sf[si], t_gam, t_H[(c + 1) // 4]), (t_tAq[1 - si],))
        for g in range(NG):
            b = 6 + (g % 2)
            for q in range(4):
                c = g * 4 + q
                P.mm(q4(cx.bank(b), q), ch(Sall, c), ch(Rb, c), True, True, (t_S, t_Rb[g]), (t_bk[b],), inc=(q == 3))
            for h in range(2):
                bv = cx.bank(b).rearrange("p (q i) -> p q i", q=4)[h * 64:(h + 1) * 64, :, h * 64:(h + 1) * 64]
                P.tt("dve", hview(yT, h)[:, g * 4:(g + 1) * 4, :], bv, hview(y0, h)[:, g * 4:(g + 1) * 4, :], ALU.add,
                     (t_bk[b], t_y0[g]), (t_y,))
        if cx.dbg_y is not None:
            P.dma("sp", cx.dbg_y[e], yT, dsm, (t_y,), (cx.t_out,))
        mean = A.alloc(F32, 512); t_mean = Tok()
        ygb = A.alloc(BF16, T); t_ygb = Tok()
        t_y0p = Tok()
        for tt in range(4):
            tsl = slice(tt * 512, (tt + 1) * 512)
            P.mm(cx.bank(0), bones, yT[:, tsl], True, True, (t_k, t_y), (t_bk[0],))
            P.stt("dve", yT[:, tsl], cx.bank(0), -1.0 / 64.0, yT[:, tsl], ALU.mult, ALU.add, (t_bk[0], t_y), (t_y,))
            P.actf(y0[:, tsl], yT[:, tsl], AF.Square, (t_y,) + tuple(t_y0), (t_y0p,))
            P.mm(cx.bank(1), bones, y0[:, tsl], True, True, (t_k, t_y0p), (t_bk[1],))
            norm_from_bank(cx, mean, t_mean, cx.bank(1), t_bk[1], 512, 1.0 / 64.0, 64e-5)
            P.tt("dve", yT[:, tsl], yT[:, tsl], mean, ALU.mult, (t_y, t_mean), (t_y,))
            P.ts("dve", yT[:, tsl], yT[:, tsl], col(11, e), col(12, e), ALU.mult, ALU.add, (t_y, t_rc), (t_y,))
            P.tt("dve", yT[:, tsl], yT[:, tsl], bon[:, tsl], ALU.add, (t_y, t_bon), (t_y,))
            P.tt("dve", ygb[:, tsl], yT[:, tsl], gT[:, tsl], ALU.mult, (t_y, t_g), (t_ygb,))
        P.dma("sp", cx.ygs[e], ygb, dsm, (t_ygb,), (t_yg,))
        P.barrier()
        A.release()
    A.release()
    xold = [A.alloc(F32, 512) for _ in range(2)]; t_xold = [Tok() for _ in range(2)]; ds_xold = [P.dsem() for _ in range(2)]
    xnew = [A.alloc(F32, 512) for _ in range(2)]; t_xnew = [Tok() for _ in range(2)]; ds_xnew = [P.dsem() for _ in range(2)]
    kk_ = [0]

    def ev_o(oi, tt, bank, tb):
        bi = kk_[0] % 2; kk_[0] += 1
        tsl = slice(tt * 512, (tt + 1) * 512)
        P.dma("act", xold[bi], src_v[:, oi, tsl], ds_xold[bi], cx.xs_tok(oi, tt), (t_xold[bi],))
        P.tt("dve", xnew[bi], bank, xold[bi], ALU.add, (tb, t_xold[bi]), (t_xnew[bi],))
        P.dma("pool", dst_v[:, oi, tsl], xnew[bi], ds_xnew[bi], (t_xnew[bi],), cx.xs_tok(oi, tt))
    ygT = A.alloc(BF16, NCH * T); ygv = ygT.rearrange("p (c t) -> p c t", c=NCH)
    for c4 in range(4):
        P.dma("sp", ygv[:, c4 * 4:(c4 + 1) * 4, :], cx.ygs.rearrange("c p t -> p c t")[:, c4 * 4:(c4 + 1) * 4, :], dsm, (t_yg,), (t_yg,))
    linear_fm(cx, ygv, t_yg, W["w_o"], NCH, [(e * 128, 128) for e in range(NCH)], ev_o, bank0=2)
    P.barrier()
    P.free_dsems.extend(ds_xold + ds_xnew + [dsm])
    A.release()


def pack_cols(vecs):
    return np.ascontiguousarray(
        np.concatenate([np.asarray(v, np.float32).reshape(NCH, 128).T for v in vecs], axis=1))


def build(phases):
    nc = bass.Bass("TRN2", target_bir_lowering=False)
    names = [p[0] for p in phases]
    dram = {}

    def din(name, shape, dt=F32):
        dram[name] = nc.dram_tensor(name, list(shape), dt, kind="ExternalInput").ap()
        return dram[name]

    ins = []
    if "tin" in names:
        x_tm = din("x", [T, D]); ins.append("x")
    else:
        xs_in = din("xs_in", [NCH, 128, T]); ins.append("xs_in")
    if "tout" in names:
        out_ap = nc.dram_tensor("out", [T, D], F32, kind="ExternalOutput").ap()
        out_name = "out"
    else:
        out_ap = nc.dram_tensor("xs_out", [NCH, 128, T], F32, kind="ExternalOutput").ap()
        out_name = "xs_out"
    ident_d = din("ident", [128, 128]); ins.append("ident")
    ncols = 16 * 8
    cols_d = din("cols", [128, ncols]); ins.append("cols")
    for p in phases:
        if p[0] == "ffn":
            l, s = p[1], p[2]
            din(f"w13_{l}{s}", [D, 2 * FF]); ins.append(f"w13_{l}{s}")
            din(f"w2_{l}{s}", [FF, D]); ins.append(f"w2_{l}{s}")
        if p[0] == "rwkv":
            din("rw_w_rkv", [3, D, D]); din("rw_w1", [D, 96]); din("rw_w2", [96, D]); din("rw_a1", [D, 96])
            din("rw_a2", [96, D]); din("rw_g1", [D, 256]); din("rw_g2", [256, D]); din("rw_w_o", [D, D])
            din("rw_cols", [128, 13 * 16]); din("rw_masks", [128, 1536])
            ins.extend(["rw_w_rkv", "rw_w1", "rw_w2", "rw_a1", "rw_a2", "rw_g1", "rw_g2", "rw_w_o", "rw_cols", "rw_masks"])
        if p[0] == "mla":
            din("mla_wd", [D, 1152]); din("mla_wuq", [512, 16, 256]); din("mla_wukv", [512, 16, 256])
            din("mla_wo", [16, 128, D]); din("mla_cols", [128, 16]); din("mla_pos", [64, T], I32)
            ins.extend(["mla_wd", "mla_wuq", "mla_wukv", "mla_wo", "mla_cols", "mla_pos"])
    xs_a = nc.dram_tensor("xs_a", [NCH, 128, T], F32, kind="Internal").ap()
    has_rw = "rwkv" in names
    if has_rw:
        xmix_d = nc.dram_tensor("xmix", [6, NCH, 128, T], BF16, kind="Internal").ap()
        rkv_d = nc.dram_tensor("rkv", [6, NCH, 128, T], F32, kind="Internal").ap()
        ygs_d = nc.dram_tensor("ygs", [NCH, 128, T], BF16, kind="Internal").ap()
        dbg_d = nc.dram_tensor("dbg_y", [NCH, 128, T], F32, kind="ExternalOutput").ap() if DBG else None

    from contextlib import ExitStack
    with ExitStack() as es:
        sb = es.enter_context(nc.sbuf_tensor("sb", [128, SB_BYTES // 4], F32))
        ps = es.enter_context(nc.psum_tensor("ps", [128, 4096], F32))
        esems = {e: es.enter_context(nc.semaphore("s_" + e)) for e in Prog.CE}
        dsems = [es.enter_context(nc.semaphore(f"d{i}")) for i in range(40)]
        block = es.enter_context(nc.Block())
        P = Prog(nc, esems, dsems)
        A = Arena(sb, SB_BYTES)
        cx = Ctx()
        cx.P, cx.A, cx.nc = P, A, nc
        cx.bank = lambda b: ps[:, b * 512:(b + 1) * 512]
        cx.t_out, cx.t_const = Tok(), Tok()
        xs_toks = [[Tok() for _ in range(T // 512)] for _ in range(NCH)]

        def xs_tok(c=None, tt=None):
            cs = range(NCH) if c is None else [c]
            ts_ = range(T // 512) if tt is None else [tt]
            return tuple(xs_toks[ci][ti] for ci in cs for ti in ts_)
        cx.xs_tok = xs_tok
        cx.ps_bf = ps.bitcast(BF16)
        if has_rw:
            cx.xmix, cx.rkv, cx.ygs, cx.dbg_y = xmix_d, rkv_d, ygs_d, dbg_d
            cx.t_xmix, cx.t_rkv = Tok(), Tok()
            cx.rw_masks_d = dram["rw_masks"]
        cx.ident = A.alloc(F32, 128)
        cx.cols = A.alloc(F32, ncols)
        cx.ones_bf = A.alloc(BF16, 128)
        dsc = P.dsem()
        P.dma("sp", cx.ident, ident_d, dsc, (), (cx.t_const,))
        P.dma("sp", cx.cols, cols_d, dsc, (), (cx.t_const,))
        P.memset("pool", cx.ones_bf, 1.0 / D, (cx.t_const,))
        cx.ones1_bf = A.alloc(BF16, 128)
        P.memset("pool", cx.ones1_bf, 1.0, (cx.t_const,))
        P.barrier()
        cur = None if "tin" in names else xs_in
        n_ph = len(phases)
        for i, p in enumerate(phases):
            last = (i == n_ph - 1)
            if p[0] == "tin":
                dst = out_ap if last else xs_a
                phase_tin(cx, x_tm, dst)
                cur = dst
            elif p[0] == "tout":
                phase_tout(cx, cur, out_ap)
            elif p[0] == "ffn":
                l, s = p[1], p[2]
                nxt_is_out = last
                dst = out_ap if nxt_is_out else xs_a
                if cur is not xs_a and dst is xs_a:
                    pass
                k = (l * 2 + s)
                phase_ffn(cx, cur, dst, dram[f"w13_{l}{s}"], dram[f"w2_{l}{s}"], cx.cols[:, k * 16:(k + 1) * 16])
                cur = dst
            elif p[0] == "rwkv":
                dst = out_ap if last else xs_a
                Wd = {k: dram["rw_" + k] for k in ("w_rkv", "w1", "w2", "a1", "a2", "g1", "g2", "w_o")}
                phase_rwkv(cx, cur, dst, Wd, cx.cols[:, 4 * 16:5 * 16], dram["rw_cols"])
                cur = dst
            elif p[0] == "mla":
                dst = out_ap if last else xs_a
                phase_mla(cx, cur, dst, dram["mla_wd"], dram["mla_wuq"], dram["mla_wukv"], dram["mla_wo"],
                          cx.cols[:, 5 * 16:6 * 16], dram["mla_cols"], dram["mla_pos"])
                cur = dst
            else:
                raise ValueError(p)
        P.barrier()
        P.emit(block)
    return nc, ins, out_name, P


def host_consts(inputs):
    ident = np.eye(128, dtype=np.float32)
    fn = inputs["ffn_norm"]
    cols = pack_cols([fn[0, 0], fn[0, 1], fn[1, 0], fn[1, 1],
                      inputs["mix_norm"][0], inputs["mix_norm"][1], np.zeros(D), np.zeros(D)])
    return ident, cols


ROPE_PERM = np.concatenate([np.arange(32, 64), np.arange(0, 32)])


def mla_host(inputs, b):
    wd = inputs["mla_w_down"][0]
    wd_ext = np.ascontiguousarray(np.concatenate([wd, wd[:, 1024 + ROPE_PERM]], axis=1))
    wuq = inputs["mla_w_uq"][0]
    wuq_ext = np.ascontiguousarray(np.concatenate([wuq, wuq[:, :, 128 + ROPE_PERM]], axis=2))
    qn, kn = inputs["mla_q_norm"][0], inputs["mla_k_norm"][0]
    mc = np.zeros((128, 16), np.float32)
    mc[:, 0:4] = inputs["mla_q_a_norm"][0].reshape(4, 128).T
    mc[:, 4:8] = inputs["mla_kv_a_norm"][0].reshape(4, 128).T
    mc[:, 8] = qn[0:128]
    mc[:, 9] = kn[0:128]
    mc[0:64, 10] = qn[128:192]
    mc[0:64, 11] = qn[128 + ROPE_PERM]
    mc[0:64, 12] = kn[128:192]
    mc[0:64, 13] = kn[128 + ROPE_PERM]
    inv_freq = (np.float32(10000.0) ** (-np.arange(0, 64, 2, dtype=np.float32) / np.float32(64))).astype(np.float32)
    mc[0:64, 14] = np.concatenate([inv_freq, inv_freq])
    mc[0:32, 15] = -1.0
    mc[32:64, 15] = 1.0
    pos = np.ascontiguousarray(np.broadcast_to(inputs["positions"][b][None, :], (64, T))).astype(np.int32)
    return {"mla_wd": wd_ext, "mla_wuq": wuq_ext, "mla_wukv": np.ascontiguousarray(inputs["mla_w_ukv"][0]),
            "mla_wo": np.ascontiguousarray(inputs["mla_w_o"][0]), "mla_cols": mc, "mla_pos": pos}


def rwkv_host(inputs):
    g = lambda k: inputs["rwkv_" + k][0]
    vecs = [g("mu")[j] for j in range(6)] + [g("w0"), g("a0"), g("k_k"), g("k_a"), g("r_k").reshape(-1), g("ln_w"), g("ln_b")]
    idx = np.arange(128)
    same = (idx[:, None] // 64) == (idx[None, :] // 64)
    ti, tj = idx[:, None] % 64, idx[None, :] % 64
    ML_s = (same & (ti > tj)).astype(np.float32)
    MU_s = (same & (ti < tj)).astype(np.float32)
    MU_i = (same & (ti <= tj)).astype(np.float32)
    masks = np.ascontiguousarray(np.concatenate([np.tile(m, (1, 4)) for m in (ML_s, MU_s, MU_i)], axis=1))
    return {"rw_w_rkv": np.ascontiguousarray(g("w_rkv")), "rw_w1": g("w1"), "rw_w2": g("w2"), "rw_a1": g("a1"), "rw_a2": g("a2"),
            "rw_g1": g("g1"), "rw_g2": g("g2"), "rw_w_o": g("w_o"), "rw_cols": pack_cols(vecs), "rw_masks": masks}


PHASES = [("tin",), ("ffn", 0, 0), ("rwkv",), ("ffn", 0, 1), ("ffn", 1, 0), ("mla",), ("ffn", 1, 1), ("tout",)]


def make_feeds(inputs, b, shared=None):
    if shared is None:
        shared = {}
        ident, cols = host_consts(inputs)
        shared["ident"] = ident
        shared["cols"] = cols
        for l in range(2):
            for s_ in range(2):
                shared[f"w13_{l}{s_}"] = np.ascontiguousarray(np.asarray(inputs["ffn_w13"][l, s_], np.float32))
                shared[f"w2_{l}{s_}"] = np.ascontiguousarray(np.asarray(inputs["ffn_w2"][l, s_], np.float32))
        shared.update(rwkv_host(inputs))
        m = mla_host(inputs, 0)
        m.pop("mla_pos")
        shared.update(m)
    feeds = dict(shared)
    feeds["x"] = np.ascontiguousarray(np.asarray(inputs["x"][b], np.float32))
    feeds["mla_pos"] = np.ascontiguousarray(
        np.broadcast_to(np.asarray(inputs["positions"][b], np.int32)[None, :], (64, T)))
    return feeds, shared


def kernel(**inputs):
    inputs = {k: np.asarray(v) for k, v in inputs.items()}
    nb = inputs["x"].shape[0]
    nc, ins, out_name, _ = build(PHASES)
    in_maps = []
    shared = None
    for b in range(nb):
        feeds, shared = make_feeds(inputs, b, shared)
        in_maps.append({k: feeds[k] for k in ins})
    res = run_bass_kernel_spmd(nc, in_maps, core_ids=list(range(nb)))
    out = np.stack([np.asarray(res.results[b][out_name], np.float32) for b in range(nb)], axis=0)
    return out
```

```python
import numpy as np
import concourse.bass as bass
import concourse.mybir as mybir
from concourse.bass_utils import run_bass_kernel_spmd

F32 = mybir.dt.float32
BF16 = mybir.dt.bfloat16
I32 = mybir.dt.int32
AF = mybir.ActivationFunctionType
ALU = mybir.AluOpType
AX = mybir.AxisListType

T = 2048
D = 2048
FF = 5504
NCH = 16
NFC = 43
RMS_EPS = 1e-6
SB_BYTES = 207872


class Tok:
    __slots__ = ("w", "r", "excl")

    def __init__(self, excl=False):
        self.w = None
        self.r = {}
        self.excl = excl


class DSem:
    def __init__(self, h, idx):
        self.h = h
        self.idx = idx
        self.count = 0


class Prog:
    CE = ("pe", "act", "dve", "pool")

    def __init__(self, nc, esems, dsems):
        self.nc = nc
        self.code = {e: [] for e in ("pe", "act", "dve", "pool", "sp")}
        self.esem = esems
        self.ecnt = {e: 0 for e in self.CE}
        self.seen = {e: {} for e in self.code}
        self.free_dsems = [DSem(h, i) for i, h in enumerate(dsems)]
        self.all_dsems = list(self.free_dsems)
        self.ninstr = 0

    def dsem(self):
        return self.free_dsems.pop()

    def _need(self, eng, waits, ev):
        if ev is None:
            return
        kind, s, v = ev
        if kind == "d":
            v = s.count
            key = ("d", s.idx)
        else:
            if s == eng and eng == "pe":
                return
            key = ("e", s)
        if self.seen[eng].get(key, 0) >= v:
            return
        if waits.get(key, (None, 0))[1] < v:
            waits[key] = (s, v)

    def op(self, eng, fn, reads=(), writes=(), inc=True, dsem=None):
        if any(t.excl for t in reads):
            writes = tuple(writes) + tuple(t for t in reads if t.excl)
            reads = tuple(t for t in reads if not t.excl)
        waits = {}
        for t in reads:
            self._need(eng, waits, t.w)
        for t in writes:
            if t.w is not None and not (t.w[0] == "e" and t.w[1] == eng):
                self._need(eng, waits, t.w)
            for ev in t.r.values():
                if not (ev[0] == "e" and ev[1] == eng):
                    self._need(eng, waits, ev)
        wl = []
        for key, (s, v) in waits.items():
            self.seen[eng][key] = v
            wl.append((s.h if key[0] == "d" else self.esem[s], v))
        if dsem is not None:
            dsem.count += 16
            ev = ("d", dsem, dsem.count)
            incspec = (dsem.h, 16)
            rkey = ("d", dsem.idx)
        else:
            if inc:
                self.ecnt[eng] += 1
                ev = ("e", eng, self.ecnt[eng])
                incspec = (self.esem[eng], 1)
            else:
                ev = ("e", eng, self.ecnt[eng] + 1)
                incspec = None
            rkey = ("e", eng)
        for t in reads:
            t.r[rkey] = ev
        for t in writes:
            t.w = ev
            t.r = {}
        self.code[eng].append((wl, fn, incspec))
        self.ninstr += 1

    def barrier(self):
        evs = [("e", e, self.ecnt[e]) for e in self.CE if self.ecnt[e] > 0]
        evs += [("d", d, d.count) for d in self.all_dsems if d.count > 0]
        for eng in self.code:
            waits = {}
            for ev in evs:
                if ev[0] == "e" and ev[1] == eng and eng == "pe":
                    continue
                self._need(eng, waits, ev)
            wl = []
            for key, (s, v) in waits.items():
                self.seen[eng][key] = v
                wl.append((s.h if key[0] == "d" else self.esem[s], v))
            if wl:
                self.code[eng].append((wl, None, None))

    def emit(self, block):
        def mk(name):
            def body(e):
                for wl, fn, incspec in self.code[name]:
                    for h, v in wl:
                        e.wait_ge(h, v)
                    if fn is None:
                        continue
                    ins = fn(e)
                    if incspec is not None:
                        ins.then_inc(incspec[0], incspec[1])
            return body

        block.tensor(mk("pe"))
        block.scalar(mk("act"))
        block.vector(mk("dve"))
        block.gpsimd(mk("pool"))
        block.sync(mk("sp"))

    def mm(self, out, lhsT, rhs, start, stop, reads, writes, inc=None):
        self.op("pe", lambda e: e.matmul(out, lhsT, rhs, start=start, stop=stop),
                reads, writes, inc=(stop if inc is None else inc))

    def tr(self, out, in_, ident, reads, writes, inc=True):
        self.op("pe", lambda e: e.transpose(out, in_, ident), reads, writes, inc=inc)

    def dma(self, q, out, in_, dsem, reads, writes):
        self.op(q, lambda e: e.dma_start(out=out, in_=in_), reads, writes, dsem=dsem)

    def actf(self, out, in_, func, reads, writes, bias=None, scale=None, eng="act"):
        kw = {}
        if bias is not None:
            kw["bias"] = bias
        if scale is not None:
            kw["scale"] = scale
        self.op("act", lambda e: e.activation(out, in_, func, **kw), reads, writes)

    def copy(self, eng, out, in_, reads, writes):
        if eng == "act":
            self.op("act", lambda e: e.copy(out, in_), reads, writes)
        else:
            self.op(eng, lambda e: e.tensor_copy(out, in_), reads, writes)

    def tt(self, eng, out, in0, in1, op, reads, writes):
        self.op(eng, lambda e: e.tensor_tensor(out, in0, in1, op), reads, writes)

    def ts(self, eng, out, in0, s1, s2, op0, op1, reads, writes):
        if s2 is None:
            self.op(eng, lambda e: e.tensor_scalar(out, in0, s1, None, op0), reads, writes)
        else:
            self.op(eng, lambda e: e.tensor_scalar(out, in0, s1, s2, op0, op1), reads, writes)

    def stt(self, eng, out, in0, scalar, in1, op0, op1, reads, writes):
        self.op(eng, lambda e: e.scalar_tensor_tensor(out, in0, scalar, in1, op0, op1), reads, writes)

    def memset(self, eng, ap, val, writes):
        self.op(eng, lambda e: e.memset(ap, val), (), writes)


class Arena:
    def __init__(self, t32, nbytes):
        self.v = {F32: t32, BF16: t32.bitcast(BF16), I32: t32.bitcast(I32)}
        self.cap = nbytes
        self.top = 0
        self.marks = []

    def alloc(self, dtype, n, parts=128, p0=0):
        sz = 2 if dtype == BF16 else 4
        off = (self.top + 63) // 64 * 64
        self.top = off + n * sz
        assert self.top <= self.cap, f"SBUF arena overflow {self.top} > {self.cap}"
        return self.v[dtype][p0:p0 + parts, off // sz: off // sz + n]

    def mark(self):
        self.marks.append(self.top)

    def release(self):
        self.top = self.marks.pop()


class Ctx:
    pass


def phase_tin(cx, x_tm, xs_dst):
    P, A = cx.P, cx.A
    A.mark()
    xin = [A.alloc(F32, D) for _ in range(2)]
    xo = [A.alloc(F32, NCH * 128) for _ in range(2)]
    t_in = [Tok() for _ in range(2)]
    t_o = [Tok() for _ in range(2)]
    ds_in = [P.dsem() for _ in range(2)]
    ds_o = [P.dsem() for _ in range(2)]
    t_ps = [Tok(True) for _ in range(2)]
    dst_v = xs_dst.rearrange("c p t -> p c t")
    for tb in range(T // 128):
        s = tb % 2
        P.dma("sp", xin[s], x_tm[tb * 128:(tb + 1) * 128, :], ds_in[s], (), (t_in[s],))
        for q in range(4):
            b = (tb * 4 + q) % 2
            bank = cx.bank(b)
            for i in range(4):
                c = q * 4 + i
                P.tr(bank[:, i * 128:(i + 1) * 128], xin[s][:, c * 128:(c + 1) * 128], cx.ident,
                     (t_in[s],), (t_ps[b],), inc=(i == 3))
            eng = "dve" if q % 2 == 0 else "act"
            P.copy(eng, xo[s][:, q * 512:(q + 1) * 512], bank, (t_ps[b],), (t_o[s],))
        P.dma("sp", dst_v[:, :, tb * 128:(tb + 1) * 128],
              xo[s].rearrange("p (c t) -> p c t", c=NCH), ds_o[s], (t_o[s],), cx.xs_tok(None, tb // 4))
    P.barrier()
    for d in ds_in + ds_o:
        P.free_dsems.append(d)
    A.release()


def phase_tout(cx, xs_src, out_tm):
    P, A = cx.P, cx.A
    A.mark()
    xin = [A.alloc(F32, NCH * 128) for _ in range(2)]
    xo = [A.alloc(F32, D) for _ in range(2)]
    t_in = [Tok() for _ in range(2)]
    t_o = [Tok() for _ in range(2)]
    ds_in = [P.dsem() for _ in range(2)]
    ds_o = [P.dsem() for _ in range(2)]
    t_ps = [Tok(True) for _ in range(2)]
    src_v = xs_src.rearrange("c p t -> p c t")
    for tb in range(T // 128):
        s = tb % 2
        P.dma("sp", xin[s].rearrange("p (c t) -> p c t", c=NCH), src_v[:, :, tb * 128:(tb + 1) * 128],
              ds_in[s], cx.xs_tok(None, tb // 4), (t_in[s],))
        for q in range(4):
            b = (tb * 4 + q) % 2
            bank = cx.bank(b)
            for i in range(4):
                c = q * 4 + i
                P.tr(bank[:, i * 128:(i + 1) * 128], xin[s][:, c * 128:(c + 1) * 128], cx.ident,
                     (t_in[s],), (t_ps[b],), inc=(i == 3))
            eng = "dve" if q % 2 == 0 else "act"
            P.copy(eng, xo[s][:, q * 512:(q + 1) * 512], bank, (t_ps[b],), (t_o[s],))
        P.dma("sp", out_tm[tb * 128:(tb + 1) * 128, :], xo[s], ds_o[s], (t_o[s],), (cx.t_out,))
    P.barrier()
    for d in ds_in + ds_o:
        P.free_dsems.append(d)
    A.release()


def rmsnorm_tile(cx, xt, t_xt, gcol, hT_out, t_h, ntok, sqb, t_sq, rstd, t_rstd, bank, t_bank):
    P = cx.P
    xv = xt.rearrange("p (c t) -> p c t", c=NCH)
    for c in range(NCH):
        s = c % len(sqb)
        P.actf(sqb[s][:, :ntok], xv[:, c, :], AF.Square, (t_xt,), (t_sq[s],))
        P.mm(bank[:, :ntok], cx.ones_bf, sqb[s][:, :ntok], c == 0, c == NCH - 1,
             (t_sq[s], cx.t_const), (t_bank,), inc=True)
    P.ts("dve", rstd[:, :ntok], bank[:, :ntok], RMS_EPS, None, ALU.add, None, (t_bank,), (t_rstd,))
    P.actf(rstd[:, :ntok], rstd[:, :ntok], AF.Ln, (t_rstd,), (t_rstd,))
    P.actf(rstd[:, :ntok], rstd[:, :ntok], AF.Exp, (t_rstd,), (t_rstd,), scale=-0.5)
    for c in range(NCH):
        P.stt("dve", hT_out(c), xv[:, c, :], gcol[:, c:c + 1], rstd[:, :ntok], ALU.mult, ALU.mult,
              (t_xt, t_rstd, cx.t_const), (t_h,))


def phase_ffn(cx, xs_src, xs_dst, w13, w2, gcol):
    P, A = cx.P, cx.A
    A.mark()
    HALF = 1024
    NTT = HALF // 512
    hT = A.alloc(BF16, NCH * HALF)
    hTv = hT.rearrange("p (c t) -> p c t", c=NCH)
    actT = A.alloc(BF16, NFC * HALF)
    actTv = actT.rearrange("p (j t) -> p j t", j=NFC)
    t_h = Tok()
    t_act = [Tok() for _ in range(NFC)]
    sqb = [A.alloc(BF16, 512) for _ in range(3)]
    t_sq = [Tok() for _ in range(3)]
    rstd = A.alloc(F32, 512)
    t_rstd = Tok()
    sg = [A.alloc(F32, 512) for _ in range(2)]
    t_sg = [Tok() for _ in range(2)]
    xold = [A.alloc(F32, 512) for _ in range(2)]
    t_xold = [Tok() for _ in range(2)]
    ds_xold = [P.dsem() for _ in range(2)]
    xnew = [A.alloc(F32, 512) for _ in range(2)]
    t_xnew = [Tok() for _ in range(2)]
    ds_xnew = [P.dsem() for _ in range(2)]
    tops = []
    A.mark()
    xt = [A.alloc(F32, NCH * 512) for _ in range(2)]
    tops.append(A.top); A.release(); A.mark()
    w13s = [A.alloc(F32, 2 * NCH * 128) for _ in range(2)]
    w13b = [A.alloc(BF16, 2 * NCH * 128) for _ in range(2)]
    tops.append(A.top); A.release(); A.mark()
    w2s = [A.alloc(F32, NFC * 128) for _ in range(2)]
    w2b = [A.alloc(BF16, NFC * 128) for _ in range(2)]
    tops.append(A.top); A.release()
    A.top = max(tops)
    t_reg = [Tok() for _ in range(2)]
    t_regb = [Tok() for _ in range(2)]
    t_regb2 = [Tok() for _ in range(2)]
    ds_stage = [P.dsem() for _ in range(2)]
    src_v = xs_src.rearrange("c p t -> p c t")
    dst_v = xs_dst.rearrange("c p t -> p c t")
    w13v = w13.rearrange("(c p) f -> p c f", p=128)
    w2v = w2.rearrange("(j p) e -> p j e", p=128)
    t_bA = Tok(True)
    t_bB = [Tok(True) for _ in range(4)]
    t_bC = [Tok(True) for _ in range(2)]
    for th in range(T // HALF):
        tok0 = th * HALF
        for tt in range(NTT):
            s = tt % 2
            P.dma("sp", xt[s].rearrange("p (c t) -> p c t", c=NCH),
                  src_v[:, :, tok0 + tt * 512: tok0 + (tt + 1) * 512], ds_stage[s], cx.xs_tok(None, th * NTT + tt), (t_reg[s],))
            rmsnorm_tile(cx, xt[s], t_reg[s], gcol, lambda c, tt=tt: hTv[:, c, tt * 512:(tt + 1) * 512], t_h,
                         512, sqb, t_sq, rstd, t_rstd, cx.bank(6), t_bA)
        P.barrier()
        for j in range(NFC):
            s = j % 2
            stg = w13s[s].rearrange("p (g c f) -> p g c f", g=2, c=NCH)
            stb = w13b[s].rearrange("p (g c f) -> p g c f", g=2, c=NCH)
            P.dma("sp", stg[:, 0], w13v[:, :, j * 128:(j + 1) * 128], ds_stage[s], (), (t_reg[s],))
            P.dma("sp", stg[:, 1], w13v[:, :, FF + j * 128: FF + (j + 1) * 128], ds_stage[s], (), (t_reg[s],))
            P.copy("dve", stb[:, 0], stg[:, 0], (t_reg[s],), (t_regb[s],))
            P.copy("act", stb[:, 1], stg[:, 1], (t_reg[s],), (t_regb2[s],))
            for tt in range(NTT):
                bi = (j * NTT + tt) % 2
                bg, bu = cx.bank(2 * bi), cx.bank(2 * bi + 1)
                tg, tu = t_bB[2 * bi], t_bB[2 * bi + 1]
                rhs_t = slice(tt * 512, (tt + 1) * 512)
                for c in range(NCH):
                    P.mm(bg, stb[:, 0, c, :], hTv[:, c, rhs_t], c == 0, c == NCH - 1, (t_regb[s], t_h), (tg,))
                for c in range(NCH):
                    P.mm(bu, stb[:, 1, c, :], hTv[:, c, rhs_t], c == 0, c == NCH - 1, (t_regb2[s], t_h), (tu,))
                P.actf(sg[bi], bg, AF.Silu, (tg,), (t_sg[bi],))
                P.tt("dve", actTv[:, j, rhs_t], sg[bi], bu, ALU.mult, (t_sg[bi], tu), (t_act[j],))
        P.barrier()
        for e in range(NCH):
            s = e % 2
            stg = w2s[s].rearrange("p (j e) -> p j e", j=NFC)
            stb = w2b[s].rearrange("p (j e) -> p j e", j=NFC)
            P.dma("sp", stg[:, 0:22, :], w2v[:, 0:22, e * 128:(e + 1) * 128], ds_stage[s], (), (t_reg[s],))
            P.dma("sp", stg[:, 22:NFC, :], w2v[:, 22:NFC, e * 128:(e + 1) * 128], ds_stage[s], (), (t_reg[s],))
            P.copy("dve", stb[:, 0:22, :], stg[:, 0:22, :], (t_reg[s],), (t_regb[s],))
            P.copy("act", stb[:, 22:NFC, :], stg[:, 22:NFC, :], (t_reg[s],), (t_regb2[s],))
            for tt in range(NTT):
                bi = (e * NTT + tt) % 2
                bk, tb_ = cx.bank(4 + bi), t_bC[bi]
                rhs_t = slice(tt * 512, (tt + 1) * 512)
                tsl = slice(tok0 + tt * 512, tok0 + (tt + 1) * 512)
                P.dma("act", xold[bi], src_v[:, e, tsl], ds_xold[bi], cx.xs_tok(e, th * NTT + tt), (t_xold[bi],))
                for j in range(NFC):
                    P.mm(bk, stb[:, j, :], actTv[:, j, rhs_t], j == 0, j == NFC - 1,
                         (t_regb[s] if j < 22 else t_regb2[s], t_act[j]), (tb_,))
                P.stt("dve", xnew[bi], bk, 0.5, xold[bi], ALU.mult, ALU.add, (tb_, t_xold[bi]), (t_xnew[bi],))
                P.dma("pool", dst_v[:, e, tsl], xnew[bi], ds_xnew[bi], (t_xnew[bi],), cx.xs_tok(e, th * NTT + tt))
        P.barrier()
    for d in ds_xold + ds_xnew + ds_stage:
        P.free_dsems.append(d)
    A.release()


def norm_from_bank(cx, rstd, t_rstd, bank, t_bank, n, mean_scale, eps, parts=128):
    P = cx.P
    P.ts("dve", rstd[:parts, :n], bank[:parts, :n], mean_scale, eps, ALU.mult, ALU.add, (t_bank,), (t_rstd,))
    P.actf(rstd[:parts, :n], rstd[:parts, :n], AF.Ln, (t_rstd,), (t_rstd,))
    P.actf(rstd[:parts, :n], rstd[:parts, :n], AF.Exp, (t_rstd,), (t_rstd,), scale=-0.5)


MLA_H = 16
STOP = 0
DBG = False
SM_SCALE = 1.0 / float(np.sqrt(192.0))


def phase_mla(cx, xs_src, xs_dst, wd, wuq, wukv, wo, gcol, mcols_d, pos_d):
    P, A = cx.P, cx.A
    A.mark()
    src_v = xs_src.rearrange("c p t -> p c t")
    dst_v = xs_dst.rearrange("c p t -> p c t")
    mc = A.alloc(F32, 16)
    t_mc = Tok()
    dsm = P.dsem()
    P.dma("sp", mc, mcols_d, dsm, (), (t_mc,))
    cqn = A.alloc(BF16, 4 * T); cqnv = cqn.rearrange("p (c t) -> p c t", c=4)
    ckvn = A.alloc(BF16, 4 * T); ckvnv = ckvn.rearrange("p (c t) -> p c t", c=4)
    kpe = A.alloc(F32, T)
    kpesw = A.alloc(F32, T)
    t_cqn, t_ckvn, t_kpe = Tok(), Tok(), Tok()
    rstd = A.alloc(F32, 512); t_rstd = Tok()
    sqb = [A.alloc(BF16, 512) for _ in range(3)]; t_sq = [Tok() for _ in range(3)]
    A.mark()
    wdb = A.alloc(BF16, NCH * 1152); wdbv = wdb.rearrange("p (c f) -> p c f", c=NCH)
    t_wdb = Tok()
    stg = [A.alloc(F32, NCH * 128) for _ in range(2)]; t_stg = [Tok() for _ in range(2)]
    ds_stg = [P.dsem() for _ in range(2)]
    wdv = wd.rearrange("(c p) f -> p c f", p=128)
    for i in range(9):
        s = i % 2
        P.dma("sp", stg[s].rearrange("p (c f) -> p c f", c=NCH), wdv[:, :, i * 128:(i + 1) * 128], ds_stg[s], (), (t_stg[s],))
        P.copy("act" if i % 2 == 0 else "dve", wdbv[:, :, i * 128:(i + 1) * 128], stg[s].rearrange("p (c f) -> p c f", c=NCH), (t_stg[s],), (t_wdb,))
    if STOP == 11:
        P.barrier(); A.release(); A.release(); return
    xt = A.alloc(F32, NCH * 512); t_xt = Tok(); ds_xt = P.dsem()
    hT = A.alloc(BF16, NCH * 512); hTv = hT.rearrange("p (c t) -> p c t", c=NCH); t_h = Tok()
    cT = A.alloc(F32, 8 * 512); cTv = cT.rearrange("p (c t) -> p c t", c=8); t_cT = Tok()
    t_b = [Tok(True) for _ in range(8)]
    for tt in range(4):
        tsl = slice(tt * 512, (tt + 1) * 512)
        P.dma("sp", xt.rearrange("p (c t) -> p c t", c=NCH), src_v[:, :, tsl], ds_xt, cx.xs_tok(None, tt), (t_xt,))
        rmsnorm_tile(cx, xt, t_xt, gcol, lambda c: hTv[:, c, :], t_h, 512, sqb, t_sq, rstd, t_rstd, cx.bank(6), t_b[6])
        for oc in range(10):
            if STOP == 12 or (STOP == 13 and oc >= 8):
                break
            b = oc % 2
            bank = cx.bank(b)
            if oc < 8:
                for c in range(NCH):
                    P.mm(bank, wdbv[:, c, oc * 128:(oc + 1) * 128], hTv[:, c, :], c == 0, c == NCH - 1, (t_wdb, t_h), (t_b[b],))
                eng = "act" if oc % 2 == 0 else "dve"
                P.copy(eng, cTv[:, oc, :], bank, (t_b[b],), (t_cT,))
            else:
                c0 = 1024 + (oc - 8) * 64
                for c in range(NCH):
                    P.mm(bank[0:64, :], wdbv[:, c, c0:c0 + 64], hTv[:, c, :], c == 0, c == NCH - 1, (t_wdb, t_h), (t_b[b],))
                dstb = kpe if oc == 8 else kpesw
                P.copy("act", dstb[0:64, tsl], bank[0:64, :], (t_b[b],), (t_kpe,))
        for which in range(2):
            if STOP in (12, 13, 14):
                break
            for c in range(4):
                s = c % 3
                P.actf(sqb[s], cTv[:, which * 4 + c, :], AF.Square, (t_cT,), (t_sq[s],))
                P.mm(cx.bank(6), cx.ones1_bf, sqb[s], c == 0, c == 3, (t_sq[s], cx.t_const), (t_b[6],), inc=True)
            norm_from_bank(cx, rstd, t_rstd, cx.bank(6), t_b[6], 512, 1.0 / 512.0, RMS_EPS)
            dstv, tk = (cqnv, t_cqn) if which == 0 else (ckvnv, t_ckvn)
            for c in range(4):
                P.stt("dve", dstv[:, c, tsl], cTv[:, which * 4 + c, :], mc[:, which * 4 + c: which * 4 + c + 1], rstd,
                      ALU.mult, ALU.mult, (t_cT, t_rstd, t_mc), (tk,))
    P.barrier()
    A.release()
    for d in ds_stg + [ds_xt]:
        P.free_dsems.append(d)
    if STOP == 1:
        A.release(); return
    Cq = A.alloc(F32, T); Sq = A.alloc(F32, T)
    t_tab = Tok()
    kperot = A.alloc(F32, T); sqkpe = A.alloc(BF16, T); t_kr = Tok()
    t_OT = Tok()
    A.mark()
    posi = A.alloc(I32, T); posf = A.alloc(F32, T); ang = posf
    t_pos, t_ang = Tok(), Tok()
    t_ang = t_pos
    Ck = A.alloc(F32, T); Sk = A.alloc(F32, T)
    tmp = A.alloc(F32, T); t_tmp = Tok()
    P.dma("sp", posi[0:64, :], pos_d, dsm, (), (t_pos,))
    P.copy("dve", posf[0:64, :], posi[0:64, :], (t_pos,), (t_pos,))
    TWO_PI = float(2.0 * np.pi)
    PI = float(np.pi)
    P.ts("dve", ang[0:64, :], posf[0:64, :], mc[0:64, 14:15], None, ALU.mult, None, (t_pos, t_mc), (t_ang,))
    ki = posi
    C1 = 6.28125
    C2 = float(2.0 * np.pi - 6.28125)

    def sin_of(out, shift):
        P.ts("dve", tmp[0:64, :], ang[0:64, :], shift, 1.0 / TWO_PI, ALU.add, ALU.mult, (t_ang,), (t_tmp,))
        P.copy("dve", ki[0:64, :], tmp[0:64, :], (t_tmp,), (t_ki,))
        P.copy("dve", tmp[0:64, :], ki[0:64, :], (t_ki,), (t_tmp,))
        P.ts("dve", out, ang[0:64, :], shift, None, ALU.add, None, (t_ang,), (t_tab,))
        P.stt("dve", out, tmp[0:64, :], -C1, out, ALU.mult, ALU.add, (t_tmp, t_tab), (t_tab,))
        P.stt("dve", out, tmp[0:64, :], -C2, out, ALU.mult, ALU.add, (t_tmp, t_tab), (t_tab,))
        P.ts("dve", tmp[0:64, :], out, PI, TWO_PI, ALU.is_gt, ALU.mult, (t_tab,), (t_tmp,))
        P.tt("dve", out, out, tmp[0:64, :], ALU.subtract, (t_tab, t_tmp), (t_tab,))
        P.ts("dve", out, out, -PI, PI, ALU.max, ALU.min, (t_tab,), (t_tab,))
        P.actf(out, out, AF.Sin, (t_tab,), (t_tab,))

    t_ki = Tok()
    sin_of(Sq[0:64, :], 0.0)
    sin_of(Cq[0:64, :], 0.5 * PI)
    P.ts("dve", Sq[0:64, :], Sq[0:64, :], mc[0:64, 15:16], None, ALU.mult, None, (t_tab, t_mc), (t_tab,))
    P.ts("dve", Ck[0:64, :], Cq[0:64, :], mc[0:64, 12:13], None, ALU.mult, None, (t_tab, t_mc), (t_tab,))
    P.ts("dve", Sk[0:64, :], Sq[0:64, :], mc[0:64, 13:14], None, ALU.mult, None, (t_tab, t_mc), (t_tab,))
    P.ts("dve", Cq[0:64, :], Cq[0:64, :], mc[0:64, 10:11], None, ALU.mult, None, (t_tab, t_mc), (t_tab,))
    P.ts("dve", Sq[0:64, :], Sq[0:64, :], mc[0:64, 11:12], None, ALU.mult, None, (t_tab, t_mc), (t_tab,))
    P.tt("dve", kperot[0:64, :], kpe[0:64, :], Ck[0:64, :], ALU.mult, (t_kpe, t_tab), (t_kr,))
    P.tt("dve", tmp[0:64, :], kpesw[0:64, :], Sk[0:64, :], ALU.mult, (t_kpe, t_tab), (t_tmp,))
    P.tt("dve", kperot[0:64, :], kperot[0:64, :], tmp[0:64, :], ALU.add, (t_kr, t_tmp), (t_kr,))
    P.memset("pool", sqkpe, 0.0, (t_kr,))
    P.actf(sqkpe[0:64, :], kpe[0:64, :], AF.Square, (t_kpe,), (t_kr,))
    P.barrier()
    A.release()
    if STOP == 2:
        A.release(); return
    A.mark()
    wq_s = [A.alloc(F32, 4 * 256) for _ in range(2)]; wq_b = [A.alloc(BF16, 4 * 256) for _ in range(2)]
    wk_s = [A.alloc(F32, 4 * 256) for _ in range(2)]; wk_b = [A.alloc(BF16, 4 * 256) for _ in range(2)]
    t_wqs = [Tok() for _ in range(2)]; t_wqb = [Tok() for _ in range(2)]
    t_wks = [Tok() for _ in range(2)]; t_wkb = [Tok() for _ in range(2)]
    ds_w = [P.dsem() for _ in range(2)]
    wuqv = wuq.rearrange("(c p) h f -> p c h f", p=128)
    wukvv = wukv.rearrange("(c p) h f -> p c h f", p=128)
    qn = A.alloc(BF16, T); qr = A.alloc(BF16, T); kn = A.alloc(BF16, T); kr = A.alloc(BF16, T)
    t_q, t_k = Tok(), Tok()
    P.memset("pool", qr, 0.0, (t_q,))
    P.memset("pool", kr, 0.0, (t_k,))
    P.memset("pool", sqb[1], 0.0, (t_sq[1],))
    Vb = A.alloc(BF16, 16 * 128); Vv = Vb.rearrange("p (s d) -> p s d", s=16); t_V = Tok()
    pT = [A.alloc(BF16, 512) for _ in range(3)]; t_pT = [Tok() for _ in range(3)]
    rs = A.alloc(F32, 512); t_rs = Tok()
    t1 = [A.alloc(F32, 512) for _ in range(4)]; t2 = [A.alloc(F32, 512) for _ in range(4)]
    rst = [A.alloc(F32, 512) for _ in range(4)]
    sqn = [A.alloc(BF16, 512) for _ in range(4)]; sqp = [A.alloc(BF16, 512) for _ in range(4)]
    t_t1 = [Tok() for _ in range(4)]; t_t2 = [Tok() for _ in range(4)]; t_rst = [Tok() for _ in range(4)]
    t_sqn = [Tok() for _ in range(4)]; t_sqp = [Tok() for _ in range(4)]
    for tt in range(4):
        P.memset("pool", sqp[tt], 0.0, (t_sqp[tt],))
    Ob = [A.alloc(BF16, T) for _ in range(2)]; t_Ob = [Tok() for _ in range(2)]; ds_Ob = [P.dsem() for _ in range(2)]
    t_b = [Tok(True) for _ in range(8)]
    pj = 0
    sc = 0
    for h in range(MLA_H):
        s = h % 2
        P.dma("sp", wq_s[s].rearrange("p (c f) -> p c f", c=4), wuqv[:, :, h, :], ds_w[s], (), (t_wqs[s],))
        P.dma("sp", wk_s[s].rearrange("p (c f) -> p c f", c=4), wukvv[:, :, h, :], ds_w[s], (), (t_wks[s],))
        P.copy("act", wq_b[s], wq_s[s], (t_wqs[s],), (t_wqb[s],))
        P.copy("dve", wk_b[s], wk_s[s], (t_wks[s],), (t_wkb[s],))
        wqv = wq_b[s].rearrange("p (c f) -> p c f", c=4)
        wkv = wk_b[s].rearrange("p (c f) -> p c f", c=4)
        for g in range(4):
            b = 6 + (pj % 2); pj += 1
            for i in range(4):
                st = g * 4 + i
                for c in range(4):
                    P.mm(cx.bank(b)[:, i * 128:(i + 1) * 128], ckvnv[:, c, st * 128:(st + 1) * 128], wkv[:, c, 128:256],
                         c == 0, c == 3, (t_ckvn, t_wkb[s]), (t_b[b],), inc=(c == 3 and i == 3))
            P.copy("act", Vb[:, g * 512:(g + 1) * 512], cx.bank(b), (t_b[b],), (t_V,))
        if STOP == 31:
            continue
        TS = [slice(tt * 512, (tt + 1) * 512) for tt in range(4)]
        R4 = range(4)
        for tt in R4:
            for c in range(4):
                P.mm(cx.bank(tt)[0:64, :], wqv[:, c, 128:192], cqnv[:, c, TS[tt]], c == 0, c == 3, (t_wqb[s], t_cqn), (t_b[tt],))
        for tt in R4:
            P.actf(sqp[tt][0:64, :], cx.bank(tt)[0:64, :], AF.Square, (t_b[tt],), (t_sqp[tt],))
            P.tt("dve", t1[tt][0:64, :], cx.bank(tt)[0:64, :], Cq[0:64, TS[tt]], ALU.mult, (t_b[tt], t_tab), (t_t1[tt],))
        for tt in R4:
            for c in range(4):
                P.mm(cx.bank(4 + tt)[0:64, :], wqv[:, c, 192:256], cqnv[:, c, TS[tt]], c == 0, c == 3, (t_wqb[s], t_cqn), (t_b[4 + tt],))
        for tt in R4:
            P.tt("dve", t2[tt][0:64, :], cx.bank(4 + tt)[0:64, :], Sq[0:64, TS[tt]], ALU.mult, (t_b[4 + tt], t_tab), (t_t2[tt],))
            P.tt("dve", t1[tt][0:64, :], t1[tt][0:64, :], t2[tt][0:64, :], ALU.add, (t_t1[tt], t_t2[tt]), (t_t1[tt],))
        for tt in R4:
            for c in range(4):
                P.mm(cx.bank(tt), wqv[:, c, 0:128], cqnv[:, c, TS[tt]], c == 0, c == 3, (t_wqb[s], t_cqn), (t_b[tt],))
        for tt in R4:
            P.actf(sqn[tt], cx.bank(tt), AF.Square, (t_b[tt],), (t_sqn[tt],))
        for tt in R4:
            P.mm(cx.bank(4 + tt), cx.ones1_bf, sqn[tt], True, False, (t_sqn[tt], cx.t_const), (t_b[4 + tt],), inc=False)
            P.mm(cx.bank(4 + tt), cx.ones1_bf, sqp[tt], False, True, (t_sqp[tt], cx.t_const), (t_b[4 + tt],), inc=True)
        for tt in R4:
            P.ts("dve", rst[tt], cx.bank(4 + tt), 1.0 / 192.0, RMS_EPS, ALU.mult, ALU.add, (t_b[4 + tt],), (t_rst[tt],))
        for tt in R4:
            P.actf(rst[tt], rst[tt], AF.Ln, (t_rst[tt],), (t_rst[tt],))
            P.actf(rst[tt], rst[tt], AF.Exp, (t_rst[tt],), (t_rst[tt],), scale=-0.5)
        for tt in R4:
            P.stt("dve", qn[:, TS[tt]], cx.bank(tt), mc[:, 8:9], rst[tt], ALU.mult, ALU.mult, (t_b[tt], t_rst[tt], t_mc), (t_q,))
            P.tt("dve", qr[0:64, TS[tt]], t1[tt][0:64, :], rst[tt][0:64, :], ALU.mult, (t_t1[tt], t_rst[tt]), (t_q,))
        for tt in R4:
            for c in range(4):
                P.mm(cx.bank(tt), wkv[:, c, 0:128], ckvnv[:, c, TS[tt]], c == 0, c == 3, (t_wkb[s], t_ckvn), (t_b[tt],))
        for tt in R4:
            P.actf(sqn[tt], cx.bank(tt), AF.Square, (t_b[tt],), (t_sqn[tt],))
        for tt in R4:
            P.mm(cx.bank(4 + tt), cx.ones1_bf, sqn[tt], True, False, (t_sqn[tt], cx.t_const), (t_b[4 + tt],), inc=False)
            P.mm(cx.bank(4 + tt), cx.ones1_bf, sqkpe[:, TS[tt]], False, True, (t_kr, cx.t_const), (t_b[4 + tt],), inc=True)
        for tt in R4:
            P.ts("dve", rst[tt], cx.bank(4 + tt), 1.0 / 192.0, RMS_EPS, ALU.mult, ALU.add, (t_b[4 + tt],), (t_rst[tt],))
        for tt in R4:
            P.actf(rst[tt], rst[tt], AF.Ln, (t_rst[tt],), (t_rst[tt],))
            P.actf(rst[tt], rst[tt], AF.Exp, (t_rst[tt],), (t_rst[tt],), scale=-0.5)
        for tt in R4:
            P.stt("dve", kn[:, TS[tt]], cx.bank(tt), mc[:, 9:10], rst[tt], ALU.mult, ALU.mult, (t_b[tt], t_rst[tt], t_mc), (t_k,))
            P.tt("dve", kr[0:64, TS[tt]], kperot[0:64, TS[tt]], rst[tt][0:64, :], ALU.mult, (t_kr, t_rst[tt]), (t_k,))
        if STOP == 32:
            continue
        for qt in range(4):
            qb0 = qt * 4
            bO, bS = 3, 4
            nkt = qb0 + 4
            for kt in range(nkt):
                c0 = max(0, kt - qb0) * 128
                n = 512 - c0
                qsl = slice(qt * 512 + c0, (qt + 1) * 512)
                ksl = slice(kt * 128, (kt + 1) * 128)
                b = sc % 3; sc += 1
                ip = b
                P.mm(cx.bank(b)[:, 0:n], kn[:, ksl], qn[:, qsl], True, False, (t_k, t_q), (t_b[b],), inc=False)
                P.mm(cx.bank(b)[:, 0:n], kr[:, ksl], qr[:, qsl], False, True, (t_k, t_q), (t_b[b],))
                P.actf(pT[ip][:, 0:n], cx.bank(b)[:, 0:n], AF.Exp, (t_b[b],), (t_pT[ip],), scale=SM_SCALE)
                if kt >= qb0:
                    P.memset("pool", pT[ip][64:128, 0:64], 0.0, (t_pT[ip],))
                P.mm(cx.bank(bO)[:, c0:512], Vv[:, kt, :], pT[ip][:, 0:n], kt == 0, kt == nkt - 1, (t_V, t_pT[ip]), (t_b[bO],), inc=False)
                P.mm(cx.bank(bS)[:, c0:512], cx.ones1_bf, pT[ip][:, 0:n], kt == 0, kt == nkt - 1, (cx.t_const, t_pT[ip]), (t_b[bS],), inc=True)
            P.actf(rs, cx.bank(bS), AF.Ln, (t_b[bS],), (t_rs,))
            P.actf(rs, rs, AF.Exp, (t_rs,), (t_rs,), scale=-1.0)
            P.tt("dve", Ob[h % 2][:, qt * 512:(qt + 1) * 512], cx.bank(bO), rs, ALU.mult, (t_b[bO], t_rs), (t_Ob[h % 2],))
        P.dma("pool", cx.ots[h], Ob[h % 2], ds_Ob[h % 2], (t_Ob[h % 2],), (t_OT,))
    P.barrier()
    A.release()
    if STOP == 3:
        A.release(); return
    OT = A.alloc(BF16, MLA_H * T); OTv = OT.rearrange("p (h t) -> p h t", h=MLA_H)
    for c4 in range(4):
        P.dma("sp", OTv[:, c4 * 4:(c4 + 1) * 4, :], cx.ots.rearrange("h p t -> p h t")[:, c4 * 4:(c4 + 1) * 4, :], dsm, (t_OT,), (t_OT,))
    wo_s = [A.alloc(F32, MLA_H * 128) for _ in range(2)]; wo_b = [A.alloc(BF16, MLA_H * 128) for _ in range(2)]
    t_wos = [Tok() for _ in range(2)]; t_wob = [Tok() for _ in range(2)]
    xold = [A.alloc(F32, 512) for _ in range(2)]; t_xold = [Tok() for _ in range(2)]; ds_xold = [P.dsem() for _ in range(2)]
    xnew = [A.alloc(F32, 512) for _ in range(2)]; t_xnew = [Tok() for _ in range(2)]; ds_xnew = [P.dsem() for _ in range(2)]
    wov = wo.rearrange("h p e -> p h e")
    k = 0
    for e in range(NCH):
        s = e % 2
        P.dma("sp", wo_s[s].rearrange("p (h e) -> p h e", h=MLA_H), wov[:, :, e * 128:(e + 1) * 128], ds_w[s], (), (t_wos[s],))
        P.copy("act" if e % 2 == 0 else "dve", wo_b[s], wo_s[s], (t_wos[s],), (t_wob[s],))
        wb = wo_b[s].rearrange("p (h e) -> p h e", h=MLA_H)
        for tt in range(4):
            bi = k % 2; k += 1
            b = 6 + bi
            tsl = slice(tt * 512, (tt + 1) * 512)
            P.dma("act", xold[bi], src_v[:, e, tsl], ds_xold[bi], cx.xs_tok(e, tt), (t_xold[bi],))
            for h in range(MLA_H):
                P.mm(cx.bank(b), wb[:, h, :], OTv[:, h, tsl], h == 0, h == MLA_H - 1, (t_wob[s], t_OT), (t_b[b],))
            P.tt("dve", xnew[bi], cx.bank(b), xold[bi], ALU.add, (t_b[b], t_xold[bi]), (t_xnew[bi],))
            P.dma("pool", dst_v[:, e, tsl], xnew[bi], ds_xnew[bi], (t_xnew[bi],), cx.xs_tok(e, tt))
    P.barrier()
    for d in ds_w + ds_xold + ds_xnew + ds_Ob + [dsm]:
        P.free_dsems.append(d)
    A.release()


RW_L = 64
RW_NCK = T // RW_L
DEC_C = float(np.exp(-0.5))


def linear_fm(cx, actv, t_act, w_ap, n_in_chunks, out_cols, evac, bank0=0, M=128):
    P, A = cx.P, cx.A
    A.mark()
    stg = [A.alloc(F32, n_in_chunks * 128) for _ in range(2)]
    wb = [A.alloc(BF16, n_in_chunks * 128) for _ in range(2)]
    t_s = [Tok() for _ in range(2)]; t_w = [Tok() for _ in range(2)]; t_w2 = [Tok() for _ in range(2)]
    ds = [P.dsem() for _ in range(2)]
    t_bk = [Tok(True) for _ in range(2)]
    wv = w_ap.rearrange("(c p) f -> p c f", p=128)
    k = 0
    for oi, (c0, m) in enumerate(out_cols):
        s = oi % 2
        sv = stg[s].rearrange("p (c f) -> p c f", c=n_in_chunks)
        bv = wb[s].rearrange("p (c f) -> p c f", c=n_in_chunks)
        P.dma("sp", sv[:, :, 0:m], wv[:, :, c0:c0 + m], ds[s], (), (t_s[s],))
        hc = n_in_chunks // 2
        P.copy("dve", bv[:, 0:hc, 0:m], sv[:, 0:hc, 0:m], (t_s[s],), (t_w[s],))
        P.copy("act", bv[:, hc:, 0:m], sv[:, hc:, 0:m], (t_s[s],), (t_w2[s],))
        for tt in range(4):
            bi = k % 2; k += 1
            bank = cx.bank(bank0 + bi)
            for c in range(n_in_chunks):
                P.mm(bank[0:m, :], bv[:, c, 0:m], actv[:, c, tt * 512:(tt + 1) * 512], c == 0, c == n_in_chunks - 1,
                     (t_w[s] if c < hc else t_w2[s], t_act), (t_bk[bi],))
            evac(oi, tt, bank, t_bk[bi])
    P.barrier()
    for d in ds:
        P.free_dsems.append(d)
    A.release()


def phase_rwkv(cx, xs_src, xs_dst, W, gcol, rc_d):
    P, A = cx.P, cx.A
    A.mark()
    src_v = xs_src.rearrange("c p t -> p c t")
    dst_v = xs_dst.rearrange("c p t -> p c t")
    rc = A.alloc(F32, 13 * 16); t_rc = Tok(); dsm = P.dsem()
    P.dma("sp", rc, rc_d, dsm, (), (t_rc,))
    omka = A.alloc(F32, 16)
    P.ts("dve", omka, rc[:, 9 * 16:10 * 16], -1.0, 1.0, ALU.mult, ALU.add, (t_rc,), (t_rc,))
    col = lambda k, e: rc[:, k * 16 + e: k * 16 + e + 1]
    t_xm = cx.t_xmix
    A.mark()
    TT = 256
    xt = [A.alloc(F32, NCH * TT) for _ in range(2)]; t_xt = [Tok() for _ in range(2)]; ds_xt = [P.dsem() for _ in range(2)]
    hb = A.alloc(F32, NCH * (TT + 4)); hbv = hb.rearrange("p (c t) -> p c t", c=NCH); t_hb = Tok()
    xx = A.alloc(F32, NCH * TT); xxv = xx.rearrange("p (c t) -> p c t", c=NCH); t_xx = Tok()
    xm = [A.alloc(BF16, NCH * TT) for _ in range(6)]; t_xmb = [Tok() for _ in range(6)]; ds_xm = [P.dsem() for _ in range(6)]
    sqb = [A.alloc(BF16, 512) for _ in range(3)]; t_sq = [Tok() for _ in range(3)]
    rstd = A.alloc(F32, 512); t_rstd = Tok()
    t_bn = Tok(True)
    P.memset("dve", hbv[:, :, 3:4], 0.0, (t_hb,))
    for ti in range(T // TT):
        s = ti % 2
        tsl = slice(ti * TT, (ti + 1) * TT)
        P.dma("sp", xt[s].rearrange("p (c t) -> p c t", c=NCH), src_v[:, :, tsl], ds_xt[s], cx.xs_tok(None, ti // 2), (t_xt[s],))
        if ti > 0:
            P.copy("dve", hbv[:, :, 3:4], hbv[:, :, TT + 3:TT + 4], (t_hb,), (t_hb,))
        rmsnorm_tile(cx, xt[s], t_xt[s], gcol, lambda c: hbv[:, c, 4:TT + 4], t_hb, TT, sqb, t_sq, rstd, t_rstd,
                     cx.bank(6), t_bn)
        P.tt("dve", xxv, hbv[:, :, 3:TT + 3], hbv[:, :, 4:TT + 4], ALU.subtract, (t_hb,), (t_xx,))
        for j in range(6):
            xmv = xm[j].rearrange("p (c t) -> p c t", c=NCH)
            for c in range(NCH):
                P.stt("dve", xmv[:, c, :], xxv[:, c, :], col(j, c), hbv[:, c, 4:TT + 4], ALU.mult, ALU.add,
                      (t_xx, t_hb, t_rc), (t_xmb[j],))
            P.dma("sp", cx.xmix[j].rearrange("c p t -> p c t")[:, :, tsl], xmv, ds_xm[j], (t_xmb[j],), (t_xm,))
    P.barrier()
    for d in ds_xt + ds_xm:
        P.free_dsems.append(d)
    A.release()
    A.mark()
    lw = A.alloc(BF16, T); la = A.alloc(BF16, T); lg = A.alloc(BF16, 2 * T); lgv = lg.rearrange("p (c t) -> p c t", c=2)
    t_l = Tok()
    A.mark()
    acts = [A.alloc(BF16, NCH * T) for _ in range(2)]
    t_acts = [Tok() for _ in range(2)]; ds_act = [P.dsem() for _ in range(2)]
    ot = [A.alloc(F32, 512) for _ in range(4)]; t_ot = [Tok() for _ in range(4)]; ds_ot = [P.dsem() for _ in range(4)]
    cnt = [0]
    order = [(0, "r"), (2, "k"), (3, "v"), (1, "lw"), (4, "la"), (5, "lg")]

    def load_act(j):
        av = acts[j % 2].rearrange("p (c t) -> p c t", c=NCH)
        for c4 in range(4):
            P.dma("sp", av[:, c4 * 4:(c4 + 1) * 4, :], cx.xmix[order[j][0]].rearrange("c p t -> p c t")[:, c4 * 4:(c4 + 1) * 4, :],
                  ds_act[j % 2], (t_xm,), (t_acts[j % 2],))

    load_act(0)
    for j, (src_j, kind) in enumerate(order):
        if j + 1 < len(order):
            load_act(j + 1)
        actv = acts[j % 2].rearrange("p (c t) -> p c t", c=NCH)
        t_act = t_acts[j % 2]
        if kind in ("r", "k", "v"):
            ji = "rkv".index(kind)

            def ev(oi, tt, bank, tb, ji=ji):
                i = cnt[0] % 4; cnt[0] += 1
                P.copy("act" if i % 2 == 0 else "dve", ot[i], bank, (tb,), (t_ot[i],))
                P.dma("pool", cx.rkv[ji].rearrange("c p t -> p c t")[:, oi, tt * 512:(tt + 1) * 512], ot[i], ds_ot[i],
                      (t_ot[i],), (cx.t_rkv,))
            linear_fm(cx, actv, t_act, W["w_rkv"][ji], NCH, [(e * 128, 128) for e in range(NCH)], ev)
        elif kind == "lw":
            def ev(oi, tt, bank, tb):
                P.actf(lw[0:96, tt * 512:(tt + 1) * 512], bank[0:96, :], AF.Tanh, (tb,), (t_l,))
            linear_fm(cx, actv, t_act, W["w1"], NCH, [(0, 96)], ev)
        elif kind == "la":
            def ev(oi, tt, bank, tb):
                P.copy("act", la[0:96, tt * 512:(tt + 1) * 512], bank[0:96, :], (tb,), (t_l,))
            linear_fm(cx, actv, t_act, W["a1"], NCH, [(0, 96)], ev)
        else:
            def ev(oi, tt, bank, tb):
                P.actf(lgv[:, oi, tt * 512:(tt + 1) * 512], bank, AF.Sigmoid, (tb,), (t_l,))
            linear_fm(cx, actv, t_act, W["g1"], NCH, [(0, 128), (128, 128)], ev)
    P.free_dsems.extend(ds_act + ds_ot)
    A.release()
    w2b = A.alloc(BF16, D); a2b = A.alloc(BF16, D); g2b = A.alloc(BF16, 2 * D); g2bv = g2b.rearrange("p (c f) -> p c f", c=2)
    t_lw2 = Tok()
    st32 = A.alloc(F32, 2 * D)
    P.dma("sp", st32[0:96, 0:D], W["w2"], dsm, (), (t_lw2,))
    P.copy("dve", w2b[0:96, :], st32[0:96, 0:D], (t_lw2,), (t_lw2,))
    P.dma("sp", st32[0:96, 0:D], W["a2"], dsm, (t_lw2,), (t_lw2,))
    P.copy("dve", a2b[0:96, :], st32[0:96, 0:D], (t_lw2,), (t_lw2,))
    P.dma("sp", st32.rearrange("p (c f) -> p c f", c=2), W["g2"].rearrange("(c p) f -> p c f", p=128), dsm, (t_lw2,), (t_lw2,))
    P.copy("dve", g2b, st32, (t_lw2,), (t_lw2,))
    t_b2 = [Tok(True) for _ in range(6)]
    big = [[A.alloc(F32, T) for _ in range(3)] for _ in range(2)]
    t_big = [[Tok() for _ in range(3)] for _ in range(2)]
    ds_big = [P.dsem() for _ in range(2)]
    for e in range(NCH):
        esl = slice(e * 128, (e + 1) * 128)
        sb_ = e % 2
        for tt in range(4):
            tsl = slice(tt * 512, (tt + 1) * 512)
            b0 = (tt % 2) * 3
            P.mm(cx.bank(b0), w2b[0:96, esl], lw[0:96, tsl], True, True, (t_lw2, t_l), (t_b2[b0],))
            P.actf(big[sb_][0][:, tsl], cx.bank(b0), AF.Sigmoid, (t_b2[b0], t_rc), (t_big[sb_][0],), bias=col(6, e))
            P.mm(cx.bank(b0 + 1), a2b[0:96, esl], la[0:96, tsl], True, True, (t_lw2, t_l), (t_b2[b0 + 1],))
            P.actf(big[sb_][1][:, tsl], cx.bank(b0 + 1), AF.Sigmoid, (t_b2[b0 + 1], t_rc), (t_big[sb_][1],), bias=col(7, e))
            for c in range(2):
                P.mm(cx.bank(b0 + 2), g2bv[:, c, esl], lgv[:, c, tsl], c == 0, c == 1, (t_lw2, t_l), (t_b2[b0 + 2],))
            P.copy("dve", big[sb_][2][:, tsl], cx.bank(b0 + 2), (t_b2[b0 + 2],), (t_big[sb_][2],))
        for pl in range(3):
            P.dma("pool" if pl != 1 else "sp", cx.rkv[3 + pl][e], big[sb_][pl], ds_big[sb_], (t_big[sb_][pl],), (cx.t_rkv,))
    P.barrier()
    P.free_dsems.extend(ds_big)
    A.release()
    if STOP == 52:
        A.release(); return
    t_yg = Tok()
    A.mark()
    msk = A.alloc(F32, 3 * 512); t_k = Tok()
    P.dma("sp", msk, cx.rw_masks_d, dsm, (), (t_k,))
    ML_s, MU_s, MU_i = msk[:, 0:512], msk[:, 512:1024], msk[:, 1024:1536]
    id4 = A.alloc(BF16, 512)
    for q in range(4):
        P.copy("dve", id4[:, q * 128:(q + 1) * 128], cx.ident, (cx.t_const,), (t_k,))
    idb = id4[:, 0:128]
    bones = A.alloc(F32, 128)
    P.memset("pool", bones, 0.0, (t_k,))
    P.memset("pool", bones[0:64, 0:64], 1.0, (t_k,))
    P.memset("pool", bones[64:128, 64:128], 1.0, (t_k,))
    rmask = A.alloc(BF16, T)
    P.memset("pool", rmask, 1.0, (t_k,))
    P.memset("pool", rmask.rearrange("p (c t) -> p c t", t=RW_L)[:, :, 0:1], 0.0, (t_k,))
    xop = [A.alloc(BF16, RW_NCK * 128) for _ in range(7)]
    t_xop = Tok()
    for x_ in xop:
        P.memset("pool", x_, 0.0, (t_xop,))
    RTx, KTx, BTx, KHx, BHx, ATx, Vx = xop
    gam = A.alloc(F32, RW_NCK); t_gam = Tok()
    bon = A.alloc(F32, T); t_bon = Tok()
    psb = cx.ps_bf

    def xview(xo, h):
        return xo.rearrange("p (c i) -> p c i", i=128)[h * 64:(h + 1) * 64, :, h * 64:(h + 1) * 64]

    def hview(ap, h):
        return ap.rearrange("p (c t) -> p c t", t=RW_L)[h * 64:(h + 1) * 64, :, :]

    def ch(xo, c):
        return xo[:, c * 128:(c + 1) * 128]

    def q4(ap, q):
        return ap[:, q * 128:(q + 1) * 128]

    NG = RW_NCK // 4
    NSLOT = 3
    for e in range(NCH):
        A.mark()
        r_, k_, v_, a_, cum, sg_, tA, tB, tC, tD, tE, tF = [A.alloc(F32, T) for _ in range(12)]
        t_r, t_kk, t_v, t_a, t_cum, t_sg, t_tA, t_tB, t_tC, t_tD, t_tE, t_tF = [Tok() for _ in range(12)]
        t_bk = [Tok(True) for _ in range(8)]
        for ji, (dst, tk) in enumerate([(sg_, t_sg), (k_, t_kk), (a_, t_a), (r_, t_r), (v_, t_v)]):
            P.dma("sp", dst, cx.rkv[[3, 1, 4, 0, 2][ji]][e], dsm, (cx.t_rkv,), (tk,))
        cumv = cum.rearrange("p (c t) -> p c t", t=RW_L)
        TS = [slice(tt * 512, (tt + 1) * 512) for tt in range(4)]
        P.op("dve", lambda en, cum=cum, sg_=sg_: en.tensor_tensor_scan(cum, rmask, sg_, 0.0, ALU.mult, ALU.add),
             (t_sg, t_k), (t_cum,))
        P.ts("dve", tA, k_, col(8, e), None, ALU.mult, None, (t_kk, t_rc), (t_tA,))
        P.actf(tB, tA, AF.Square, (t_tA,), (t_tB,))
        P.actf(tC, cum, AF.Exp, (t_cum,), (t_tC,), scale=-DEC_C)
        for tt in range(4):
            P.mm(cx.bank(tt), bones, tB[:, TS[tt]], True, True, (t_k, t_tB), (t_bk[tt],))
        P.ts("dve", tE, a_, col(9, e), omka[:, e:e + 1], ALU.mult, ALU.add, (t_a, t_rc), (t_tE,))
        P.tt("dve", k_, k_, tE, ALU.mult, (t_kk, t_tE), (t_kk,))
        for tt in range(4):
            P.ts("dve", tD[:, TS[tt]], cx.bank(tt), 1e-18, None, ALU.max, None, (t_bk[tt],), (t_tD,))
        P.actf(tD, tD, AF.Ln, (t_tD,), (t_tD,))
        P.actf(tD, tD, AF.Exp, (t_tD,), (t_tD,), scale=-0.5)
        P.copy("dve", gam, cumv[:, :, RW_L - 1], (t_cum,), (t_gam,))
        P.actf(gam, gam, AF.Exp, (t_gam,), (t_gam,), scale=-DEC_C)
        for h in range(2):
            P.tt("dve", xview(RTx, h), hview(r_, h), hview(tC, h), ALU.mult, (t_r, t_tC), (t_xop,))
        P.tt("dve", tE, cum, sg_, ALU.subtract, (t_cum, t_sg, t_tE), (t_tE,))
        P.actf(tE, tE, AF.Exp, (t_tE,), (t_tE,), scale=-DEC_C)
        P.stt("dve", tF, r_, col(10, e), k_, ALU.mult, ALU.mult, (t_r, t_kk, t_rc), (t_tF,))
        for tt in range(4):
            P.mm(cx.bank(4 + tt), bones, tF[:, TS[tt]], True, True, (t_k, t_tF), (t_bk[4 + tt],))
        P.tt("dve", tA, tA, tD, ALU.mult, (t_tA, t_tD), (t_tA,))
        P.tt("dve", tB, tA, a_, ALU.mult, (t_tA, t_a, t_tB), (t_tB,))
        P.actf(tC, cum, AF.Exp, (t_cum,), (t_tC,), scale=DEC_C)
        for tt in range(4):
            P.tt("dve", bon[:, TS[tt]], cx.bank(4 + tt), v_[:, TS[tt]], ALU.mult, (t_bk[4 + tt], t_v), (t_bon,))
        for h in range(2):
            P.stt("dve", xview(ATx, h), hview(tA, h), -1.0, hview(tE, h), ALU.mult, ALU.mult, (t_tA, t_tE), (t_xop,))
        P.tt("dve", r_.rearrange("p (c t) -> p c t", t=RW_L), cumv[:, :, RW_L - 1:RW_L].to_broadcast([128, RW_NCK, RW_L]),
             cumv, ALU.subtract, (t_cum, t_r), (t_r,))
        P.actf(r_, r_, AF.Exp, (t_r,), (t_r,), scale=-DEC_C)
        for h in range(2):
            P.copy("act", xview(Vx, h), hview(v_, h), (t_v,), (t_xop,))
        for h in range(2):
            P.tt("dve", xview(KTx, h), hview(k_, h), hview(tC, h), ALU.mult, (t_kk, t_tC), (t_xop,))
            P.tt("dve", xview(BTx, h), hview(tB, h), hview(tC, h), ALU.mult, (t_tB, t_tC), (t_xop,))
        for h in range(2):
            P.tt("dve", xview(KHx, h), hview(k_, h), hview(r_, h), ALU.mult, (t_kk, t_r), (t_xop,))
            P.tt("dve", xview(BHx, h), hview(tB, h), hview(r_, h), ALU.mult, (t_tB, t_r), (t_xop,))
        P.barrier()
        A.release()
        A.mark()
        GT = A.alloc(BF16, RW_NCK * 128); Hh = A.alloc(F32, RW_NCK * 128); Rb = A.alloc(BF16, RW_NCK * 128)
        y0 = A.alloc(F32, T); Sall = A.alloc(BF16, RW_NCK * 128); yT = A.alloc(F32, T)
        t_GT = [Tok() for _ in range(NG)]; t_H = [Tok() for _ in range(NG)]; t_Rb = [Tok() for _ in range(NG)]
        t_y0 = [Tok() for _ in range(NG)]
        gT = A.alloc(F32, T); t_g = Tok()
        P.dma("sp", gT, cx.rkv[5][e], dsm, (cx.t_rkv,), (t_g,))
        t_bk = [Tok(True) for _ in range(8)]
        slots = []
        for sl in range(NSLOT):
            d_ = {}
            d_["TMb"] = [A.alloc(BF16, 512) for _ in range(3)]
            d_["WUin"] = A.alloc(BF16, 4 * 256); d_["WU"] = A.alloc(BF16, 4 * 256)
            d_["Nb"] = [A.alloc(BF16, 512) for _ in range(2)]; d_["Qb"] = [A.alloc(BF16, 512) for _ in range(2)]
            d_["Pb"] = [A.alloc(BF16, 512) for _ in range(2)]
            d_["M"] = [A.alloc(BF16, 512) for _ in range(3)]
            d_["tok"] = {k: Tok() for k in ("TM", "WUin", "WU", "N", "Q", "P", "M")}
            d_["banks"] = (2 * sl, 2 * sl + 1)
            slots.append(d_)
        ev_i = [0]

        def group_steps(g, sd):
            cs = [g * 4 + q for q in range(4)]
            TMb, WUin, WU, Nb, Qb, Pb = sd["TMb"], sd["WUin"], sd["WU"], sd["Nb"], sd["Qb"], sd["Pb"]
            Mak, Mrb, Mrk = sd["M"]
            tk = sd["tok"]
            WUinv = WUin.rearrange("p (q f) -> p q f", q=4)
            WUv = WU.rearrange("p (q f) -> p q f", q=4)
            bi = [0]

            def nb():
                bi[0] ^= 1
                return sd["banks"][bi[0]]

            def eng2():
                ev_i[0] += 1
                return "dve" if ev_i[0] % 2 == 0 else "act"
            for half, ops_ in enumerate([(BHx, KHx), (Vx, ATx)]):
                b = nb()
                for oi, xo in enumerate(ops_):
                    for q in range(4):
                        P.tr(psb[:, b * 1024 + (oi * 4 + q) * 128: b * 1024 + (oi * 4 + q + 1) * 128], ch(xo, cs[q]), idb,
                             (t_xop, t_k), (t_bk[b],), inc=(oi == 1 and q == 3))
                if half == 0:
                    P.copy("act", TMb[0], psb[:, b * 1024: b * 1024 + 512], (t_bk[b],), (tk["TM"],))
                    P.copy("dve", TMb[1], psb[:, b * 1024 + 512: b * 1024 + 1024], (t_bk[b],), (tk["TM"],))
                else:
                    P.copy("act", TMb[2], psb[:, b * 1024: b * 1024 + 512], (t_bk[b],), (tk["TM"],))
                    P.copy("dve", WUinv[:, :, 0:128], psb[:, b * 1024 + 512: b * 1024 + 1024].rearrange("p (q f) -> p q f", q=4),
                           (t_bk[b],), (tk["WUin"],))
                yield
            b = nb()
            for q in range(4):
                P.mm(q4(cx.bank(b), q), ch(ATx, cs[q]), ch(BTx, cs[q]), True, True, (t_xop,), (t_bk[b],), inc=(q == 3))
            P.tt("dve", Nb[0], cx.bank(b), ML_s, ALU.mult, (t_bk[b], t_k), (tk["N"],))
            yield
            b = nb()
            for q in range(4):
                P.mm(q4(cx.bank(b), q), ch(BTx, cs[q]), ch(ATx, cs[q]), True, True, (t_xop,), (t_bk[b],), inc=(q == 3))
            P.tt("dve", Qb[0], cx.bank(b), MU_s, ALU.mult, (t_bk[b], t_k), (tk["Q"],))
            P.tt("pool", Pb[0], Qb[0], id4, ALU.add, (tk["Q"], t_k), (tk["P"],))
            yield
            for (lx, rx, mk, dst) in ((KTx, ATx, MU_s, Mak), (BTx, RTx, MU_i, Mrb), (KTx, RTx, MU_i, Mrk)):
                b = nb()
                for q in range(4):
                    P.mm(q4(cx.bank(b), q), ch(lx, cs[q]), ch(rx, cs[q]), True, True, (t_xop,), (t_bk[b],), inc=(q == 3))
                P.tt("dve", dst, cx.bank(b), mk, ALU.mult, (t_bk[b], t_k), (tk["M"],))
                yield
            pi = 0
            for j in range(1, 6):
                i0, i1 = (j - 1) % 2, j % 2
                b = nb()
                for q in range(4):
                    P.mm(q4(cx.bank(b), q), q4(Qb[i0], q), q4(Nb[i0], q), True, True, (tk["N"], tk["Q"]), (t_bk[b],), inc=(q == 3))
                if j < 5:
                    b2 = nb()
                    for q in range(4):
                        P.mm(q4(cx.bank(b2), q), q4(Nb[i0], q), q4(Qb[i0], q), True, True, (tk["N"], tk["Q"]), (t_bk[b2],), inc=(q == 3))
                P.copy("act", Nb[i1], cx.bank(b), (t_bk[b],), (tk["N"],))
                if j < 5:
                    P.copy(eng2(), Qb[i1], cx.bank(b2), (t_bk[b2],), (tk["Q"],))
                yield
                b = nb()
                for q in range(4):
                    P.mm(q4(cx.bank(b), q), idb, q4(Pb[pi], q), True, False, (t_k, tk["P"]), (t_bk[b],), inc=False)
                    P.mm(q4(cx.bank(b), q), q4(Nb[i1], q), q4(Pb[pi], q), False, True, (tk["N"], tk["P"]), (t_bk[b],), inc=(q == 3))
                P.copy(eng2(), Pb[1 - pi], cx.bank(b), (t_bk[b],), (tk["P"],))
                pi = 1 - pi
                yield
            TiT = Pb[pi]
            b = nb()
            for q in range(4):
                P.mm(q4(cx.bank(b), q), q4(Mak, q), q4(TMb[2], q), True, True, (tk["M"], tk["TM"]), (t_bk[b],), inc=(q == 3))
            P.copy("act", WUinv[:, :, 128:256], cx.bank(b).rearrange("p (q f) -> p q f", q=4), (t_bk[b],), (tk["WUin"],))
            yield
            for hf in range(2):
                b = nb()
                for qq in range(2):
                    q = hf * 2 + qq
                    P.mm(cx.bank(b)[:, qq * 256:(qq + 1) * 256], q4(TiT, q), WUinv[:, q, :], True, True, (tk["P"], tk["WUin"]),
                         (t_bk[b],), inc=(qq == 1))
                P.copy("act" if hf == 0 else "dve", WU[:, hf * 512:(hf + 1) * 512], cx.bank(b), (t_bk[b],), (tk["WU"],))
            yield
            b = nb()
            for q in range(4):
                P.mm(q4(cx.bank(b), q), WUv[:, q, 0:128], q4(TMb[0], q), True, True, (tk["WU"], tk["TM"]), (t_bk[b],), inc=(q == 3))
            P.copy("act", GT[:, g * 512:(g + 1) * 512], cx.bank(b), (t_bk[b],), (t_GT[g],))
            b = nb()
            for q in range(4):
                P.mm(q4(cx.bank(b), q), q4(TMb[0], q), WUv[:, q, 128:256], True, False, (tk["WU"], tk["TM"]), (t_bk[b],), inc=False)
                P.mm(q4(cx.bank(b), q), q4(TMb[1], q), q4(TMb[2], q), False, True, (tk["TM"],), (t_bk[b],), inc=(q == 3))
            P.copy("dve", Hh[:, g * 512:(g + 1) * 512], cx.bank(b), (t_bk[b],), (t_H[g],))
            yield
            b = nb()
            for q in range(4):
                P.mm(q4(cx.bank(b), q), idb, ch(RTx, cs[q]), True, False, (t_k, t_xop), (t_bk[b],), inc=False)
                P.mm(q4(cx.bank(b), q), WUv[:, q, 0:128], q4(Mrb, q), False, True, (tk["WU"], tk["M"]), (t_bk[b],), inc=(q == 3))
            P.copy("act", Rb[:, g * 512:(g + 1) * 512], cx.bank(b), (t_bk[b],), (t_Rb[g],))
            b = nb()
            for q in range(4):
                P.mm(q4(cx.bank(b), q), WUv[:, q, 128:256], q4(Mrb, q), True, False, (tk["WU"], tk["M"]), (t_bk[b],), inc=False)
                P.mm(q4(cx.bank(b), q), q4(TMb[2], q), q4(Mrk, q), False, True, (tk["TM"], tk["M"]), (t_bk[b],), inc=(q == 3))
            for h in range(2):
                bv = cx.bank(b).rearrange("p (q i) -> p q i", q=4)[h * 64:(h + 1) * 64, :, h * 64:(h + 1) * 64]
                P.copy("act", hview(y0, h)[:, g * 4:(g + 1) * 4, :], bv, (t_bk[b],), (t_y0[g],))
            yield

        pending = list(range(NG))
        active = []
        free_slots = list(range(NSLOT))
        while pending or active:
            while pending and free_slots:
                sl = free_slots.pop(0)
                active.append((group_steps(pending.pop(0), slots[sl]), sl))
            for item in list(active):
                gen, sl = item
                try:
                    next(gen)
                except StopIteration:
                    active.remove(item)
                    free_slots.append(sl)
        Sf = [A.alloc(F32, 128) for _ in range(2)]; tAq = [A.alloc(F32, 128) for _ in range(2)]
        t_Sf, t_tAq = [Tok() for _ in range(2)], [Tok() for _ in range(2)]
        t_Sg = [Tok() for _ in range(NG)]
        t_yq = [Tok() for _ in range(4)]

        def emit_y(g):
            b = 4 + (g % 2)
            for q in range(4):
                c = g * 4 + q
                P.mm(q4(cx.bank(b), q), ch(Sall, c), ch(Rb, c), True, True, (t_Sg[g], t_Rb[g]), (t_bk[b],), inc=(q == 3))
            for h in range(2):
                bv = cx.bank(b).rearrange("p (q i) -> p q i", q=4)[h * 64:(h + 1) * 64, :, h * 64:(h + 1) * 64]
                P.tt("dve", hview(yT, h)[:, g * 4:(g + 1) * 4, :], bv, hview(y0, h)[:, g * 4:(g + 1) * 4, :], ALU.add,
                     (t_bk[b], t_y0[g]), (t_yq[g // 2],))

        P.memset("pool", Sall[:, 0:128], 0.0, (t_Sg[0],))
        P.copy("dve", tAq[0], ch(Hh, 0), (t_H[0],), (t_tAq[0],))
        for c in range(RW_NCK - 1):
            si = c % 2
            b = 6 + (c % 2)
            P.mm(cx.bank(b)[:, 0:128], ch(GT, c), ch(Sall, c), True, True, (t_GT[c // 4], t_Sg[c // 4]), (t_bk[b],))
            P.tt("dve", ch(Sall, c + 1), cx.bank(b)[:, 0:128], tAq[si], ALU.add, (t_bk[b], t_tAq[si]), (t_Sg[(c + 1) // 4],))
            if c + 1 < RW_NCK - 1:
                P.tt("dve", Sf[si], cx.bank(b)[:, 0:128], tAq[si], ALU.add, (t_bk[b], t_tAq[si]), (t_Sf[si],))
                P.stt("dve", tAq[1 - si], Sf[si], gam[:, c + 1:c + 2], ch(Hh, c + 1), ALU.mult, ALU.add,
                      (t_Sf[si], t_gam, t_H[(c + 1) // 4]), (t_tAq[1 - si],))
            if (c + 1) % 4 == 3:
                emit_y((c + 1) // 4)
        if cx.dbg_y is not None:
            P.dma("sp", cx.dbg_y[e], yT, dsm, tuple(t_yq), (cx.t_out,))
        mean = [Hh[:, tt * 512:(tt + 1) * 512] for tt in range(4)]; t_mean = [Tok() for _ in range(4)]
        ygb = A.alloc(BF16, T); t_ygb = Tok()
        t_y0p = [Tok() for _ in range(4)]
        TS = [slice(tt * 512, (tt + 1) * 512) for tt in range(4)]
        for tt in range(4):
            P.mm(cx.bank(tt), bones, yT[:, TS[tt]], True, True, (t_k, t_yq[tt]), (t_bk[tt],))
        for tt in range(4):
            P.stt("dve", yT[:, TS[tt]], cx.bank(tt), -1.0 / 64.0, yT[:, TS[tt]], ALU.mult, ALU.add, (t_bk[tt], t_yq[tt]), (t_yq[tt],))
        for tt in range(4):
            P.actf(y0[:, TS[tt]], yT[:, TS[tt]], AF.Square, (t_yq[tt], t_y0[2 * tt], t_y0[2 * tt + 1]), (t_y0p[tt],))
        for tt in range(4):
            P.mm(cx.bank(4 + tt), bones, y0[:, TS[tt]], True, True, (t_k, t_y0p[tt]), (t_bk[4 + tt],))
        for tt in range(4):
            P.ts("dve", mean[tt], cx.bank(4 + tt), 1.0 / 64.0, 64e-5, ALU.mult, ALU.add, (t_bk[4 + tt],), (t_mean[tt],) + tuple(t_H))
        for tt in range(4):
            P.actf(mean[tt], mean[tt], AF.Ln, (t_mean[tt],), (t_mean[tt],))
            P.actf(mean[tt], mean[tt], AF.Exp, (t_mean[tt],), (t_mean[tt],), scale=-0.5)
        for tt in range(4):
            P.tt("dve", yT[:, TS[tt]], yT[:, TS[tt]], mean[tt], ALU.mult, (t_yq[tt], t_mean[tt]), (t_yq[tt],))
            P.ts("dve", yT[:, TS[tt]], yT[:, TS[tt]], col(11, e), col(12, e), ALU.mult, ALU.add, (t_yq[tt], t_rc), (t_yq[tt],))
            P.tt("dve", yT[:, TS[tt]], yT[:, TS[tt]], bon[:, TS[tt]], ALU.add, (t_yq[tt], t_bon), (t_yq[tt],))
            P.tt("dve", ygb[:, TS[tt]], yT[:, TS[tt]], gT[:, TS[tt]], ALU.mult, (t_yq[tt], t_g), (t_ygb,))
        P.dma("sp", cx.ygs[e], ygb, dsm, (t_ygb,), (t_yg,))
        P.barrier()
        A.release()
    A.release()
    xold = [A.alloc(F32, 512) for _ in range(2)]; t_xold = [Tok() for _ in range(2)]; ds_xold = [P.dsem() for _ in range(2)]
    xnew = [A.alloc(F32, 512) for _ in range(2)]; t_xnew = [Tok() for _ in range(2)]; ds_xnew = [P.dsem() for _ in range(2)]
    kk_ = [0]

    def ev_o(oi, tt, bank, tb):
        bi = kk_[0] % 2; kk_[0] += 1
        tsl = slice(tt * 512, (tt + 1) * 512)
        P.dma("act", xold[bi], src_v[:, oi, tsl], ds_xold[bi], cx.xs_tok(oi, tt), (t_xold[bi],))
        P.tt("dve", xnew[bi], bank, xold[bi], ALU.add, (tb, t_xold[bi]), (t_xnew[bi],))
        P.dma("pool", dst_v[:, oi, tsl], xnew[bi], ds_xnew[bi], (t_xnew[bi],), cx.xs_tok(oi, tt))
    ygT = A.alloc(BF16, NCH * T); ygv = ygT.rearrange("p (c t) -> p c t", c=NCH)
    for c4 in range(4):
        P.dma("sp", ygv[:, c4 * 4:(c4 + 1) * 4, :], cx.ygs.rearrange("c p t -> p c t")[:, c4 * 4:(c4 + 1) * 4, :], dsm, (t_yg,), (t_yg,))
    linear_fm(cx, ygv, t_yg, W["w_o"], NCH, [(e * 128, 128) for e in range(NCH)], ev_o, bank0=2)
    P.barrier()
    P.free_dsems.extend(ds_xold + ds_xnew + [dsm])
    A.release()


def pack_cols(vecs):
    return np.ascontiguousarray(
        np.concatenate([np.asarray(v, np.float32).reshape(NCH, 128).T for v in vecs], axis=1))


def build(phases):
    nc = bass.Bass("TRN2", target_bir_lowering=False)
    names = [p[0] for p in phases]
    dram = {}

    def din(name, shape, dt=F32):
        dram[name] = nc.dram_tensor(name, list(shape), dt, kind="ExternalInput").ap()
        return dram[name]

    ins = []
    if "tin" in names:
        x_tm = din("x", [T, D]); ins.append("x")
    else:
        xs_in = din("xs_in", [NCH, 128, T]); ins.append("xs_in")
    if "tout" in names:
        out_ap = nc.dram_tensor("out", [T, D], F32, kind="ExternalOutput").ap()
        out_name = "out"
    else:
        out_ap = nc.dram_tensor("xs_out", [NCH, 128, T], F32, kind="ExternalOutput").ap()
        out_name = "xs_out"
    ident_d = din("ident", [128, 128]); ins.append("ident")
    ncols = 16 * 8
    cols_d = din("cols", [128, ncols]); ins.append("cols")
    for p in phases:
        if p[0] == "ffn":
            l, s = p[1], p[2]
            din(f"w13_{l}{s}", [D, 2 * FF]); ins.append(f"w13_{l}{s}")
            din(f"w2_{l}{s}", [FF, D]); ins.append(f"w2_{l}{s}")
        if p[0] == "rwkv":
            din("rw_w_rkv", [3, D, D]); din("rw_w1", [D, 96]); din("rw_w2", [96, D]); din("rw_a1", [D, 96])
            din("rw_a2", [96, D]); din("rw_g1", [D, 256]); din("rw_g2", [256, D]); din("rw_w_o", [D, D])
            din("rw_cols", [128, 13 * 16]); din("rw_masks", [128, 1536])
            ins.extend(["rw_w_rkv", "rw_w1", "rw_w2", "rw_a1", "rw_a2", "rw_g1", "rw_g2", "rw_w_o", "rw_cols", "rw_masks"])
        if p[0] == "mla":
            din("mla_wd", [D, 1152]); din("mla_wuq", [512, 16, 256]); din("mla_wukv", [512, 16, 256])
            din("mla_wo", [16, 128, D]); din("mla_cols", [128, 16]); din("mla_pos", [64, T], I32)
            ins.extend(["mla_wd", "mla_wuq", "mla_wukv", "mla_wo", "mla_cols", "mla_pos"])
    xs_a = nc.dram_tensor("xs_a", [NCH, 128, T], F32, kind="Internal").ap()
    has_rw = "rwkv" in names
    ots_d = nc.dram_tensor("ots", [MLA_H, 128, T], BF16, kind="Internal").ap() if "mla" in names else None
    if has_rw:
        xmix_d = nc.dram_tensor("xmix", [6, NCH, 128, T], BF16, kind="Internal").ap()
        rkv_d = nc.dram_tensor("rkv", [6, NCH, 128, T], F32, kind="Internal").ap()
        ygs_d = nc.dram_tensor("ygs", [NCH, 128, T], BF16, kind="Internal").ap()
        dbg_d = nc.dram_tensor("dbg_y", [NCH, 128, T], F32, kind="ExternalOutput").ap() if DBG else None

    from contextlib import ExitStack
    with ExitStack() as es:
        sb = es.enter_context(nc.sbuf_tensor("sb", [128, SB_BYTES // 4], F32))
        ps = es.enter_context(nc.psum_tensor("ps", [128, 4096], F32))
        esems = {e: es.enter_context(nc.semaphore("s_" + e)) for e in Prog.CE}
        dsems = [es.enter_context(nc.semaphore(f"d{i}")) for i in range(40)]
        block = es.enter_context(nc.Block())
        P = Prog(nc, esems, dsems)
        A = Arena(sb, SB_BYTES)
        cx = Ctx()
        cx.P, cx.A, cx.nc = P, A, nc
        cx.bank = lambda b: ps[:, b * 512:(b + 1) * 512]
        cx.t_out, cx.t_const = Tok(), Tok()
        xs_toks = [[Tok() for _ in range(T // 512)] for _ in range(NCH)]

        def xs_tok(c=None, tt=None):
            cs = range(NCH) if c is None else [c]
            ts_ = range(T // 512) if tt is None else [tt]
            return tuple(xs_toks[ci][ti] for ci in cs for ti in ts_)
        cx.xs_tok = xs_tok
        cx.ps_bf = ps.bitcast(BF16)
        cx.ots = ots_d
        if has_rw:
            cx.xmix, cx.rkv, cx.ygs, cx.dbg_y = xmix_d, rkv_d, ygs_d, dbg_d
            cx.t_xmix, cx.t_rkv = Tok(), Tok()
            cx.rw_masks_d = dram["rw_masks"]
        cx.ident = A.alloc(F32, 128)
        cx.cols = A.alloc(F32, ncols)
        cx.ones_bf = A.alloc(BF16, 128)
        dsc = P.dsem()
        P.dma("sp", cx.ident, ident_d, dsc, (), (cx.t_const,))
        P.dma("sp", cx.cols, cols_d, dsc, (), (cx.t_const,))
        P.memset("pool", cx.ones_bf, 1.0 / D, (cx.t_const,))
        cx.ones1_bf = A.alloc(BF16, 128)
        P.memset("pool", cx.ones1_bf, 1.0, (cx.t_const,))
        P.barrier()
        cur = None if "tin" in names else xs_in
        n_ph = len(phases)
        for i, p in enumerate(phases):
            last = (i == n_ph - 1)
            if p[0] == "tin":
                dst = out_ap if last else xs_a
                phase_tin(cx, x_tm, dst)
                cur = dst
            elif p[0] == "tout":
                phase_tout(cx, cur, out_ap)
            elif p[0] == "ffn":
                l, s = p[1], p[2]
                nxt_is_out = last
                dst = out_ap if nxt_is_out else xs_a
                if cur is not xs_a and dst is xs_a:
                    pass
                k = (l * 2 + s)
                phase_ffn(cx, cur, dst, dram[f"w13_{l}{s}"], dram[f"w2_{l}{s}"], cx.cols[:, k * 16:(k + 1) * 16])
                cur = dst
            elif p[0] == "rwkv":
                dst = out_ap if last else xs_a
                Wd = {k: dram["rw_" + k] for k in ("w_rkv", "w1", "w2", "a1", "a2", "g1", "g2", "w_o")}
                phase_rwkv(cx, cur, dst, Wd, cx.cols[:, 4 * 16:5 * 16], dram["rw_cols"])
                cur = dst
            elif p[0] == "mla":
                dst = out_ap if last else xs_a
                phase_mla(cx, cur, dst, dram["mla_wd"], dram["mla_wuq"], dram["mla_wukv"], dram["mla_wo"],
                          cx.cols[:, 5 * 16:6 * 16], dram["mla_cols"], dram["mla_pos"])
                cur = dst
            else:
                raise ValueError(p)
        P.barrier()
        P.emit(block)
    return nc, ins, out_name, P


def host_consts(inputs):
    ident = np.eye(128, dtype=np.float32)
    fn = inputs["ffn_norm"]
    cols = pack_cols([fn[0, 0], fn[0, 1], fn[1, 0], fn[1, 1],
                      inputs["mix_norm"][0], inputs["mix_norm"][1], np.zeros(D), np.zeros(D)])
    return ident, cols


ROPE_PERM = np.concatenate([np.arange(32, 64), np.arange(0, 32)])


def mla_host(inputs, b):
    wd = inputs["mla_w_down"][0]
    wd_ext = np.ascontiguousarray(np.concatenate([wd, wd[:, 1024 + ROPE_PERM]], axis=1))
    wuq = inputs["mla_w_uq"][0]
    wuq_ext = np.ascontiguousarray(np.concatenate([wuq, wuq[:, :, 128 + ROPE_PERM]], axis=2))
    qn, kn = inputs["mla_q_norm"][0], inputs["mla_k_norm"][0]
    mc = np.zeros((128, 16), np.float32)
    mc[:, 0:4] = inputs["mla_q_a_norm"][0].reshape(4, 128).T
    mc[:, 4:8] = inputs["mla_kv_a_norm"][0].reshape(4, 128).T
    mc[:, 8] = qn[0:128]
    mc[:, 9] = kn[0:128]
    mc[0:64, 10] = qn[128:192]
    mc[0:64, 11] = qn[128 + ROPE_PERM]
    mc[0:64, 12] = kn[128:192]
    mc[0:64, 13] = kn[128 + ROPE_PERM]
    inv_freq = (np.float32(10000.0) ** (-np.arange(0, 64, 2, dtype=np.float32) / np.float32(64))).astype(np.float32)
    mc[0:64, 14] = np.concatenate([inv_freq, inv_freq])
    mc[0:32, 15] = -1.0
    mc[32:64, 15] = 1.0
    pos = np.ascontiguousarray(np.broadcast_to(inputs["positions"][b][None, :], (64, T))).astype(np.int32)
    return {"mla_wd": wd_ext, "mla_wuq": wuq_ext, "mla_wukv": np.ascontiguousarray(inputs["mla_w_ukv"][0]),
            "mla_wo": np.ascontiguousarray(inputs["mla_w_o"][0]), "mla_cols": mc, "mla_pos": pos}


def rwkv_host(inputs):
    g = lambda k: inputs["rwkv_" + k][0]
    vecs = [g("mu")[j] for j in range(6)] + [g("w0"), g("a0"), g("k_k"), g("k_a"), g("r_k").reshape(-1), g("ln_w"), g("ln_b")]
    idx = np.arange(128)
    same = (idx[:, None] // 64) == (idx[None, :] // 64)
    ti, tj = idx[:, None] % 64, idx[None, :] % 64
    ML_s = (same & (ti > tj)).astype(np.float32)
    MU_s = (same & (ti < tj)).astype(np.float32)
    MU_i = (same & (ti <= tj)).astype(np.float32)
    masks = np.ascontiguousarray(np.concatenate([np.tile(m, (1, 4)) for m in (ML_s, MU_s, MU_i)], axis=1))
    return {"rw_w_rkv": np.ascontiguousarray(g("w_rkv")), "rw_w1": g("w1"), "rw_w2": g("w2"), "rw_a1": g("a1"), "rw_a2": g("a2"),
            "rw_g1": g("g1"), "rw_g2": g("g2"), "rw_w_o": g("w_o"), "rw_cols": pack_cols(vecs), "rw_masks": masks}


PHASES = [("tin",), ("ffn", 0, 0), ("rwkv",), ("ffn", 0, 1), ("ffn", 1, 0), ("mla",), ("ffn", 1, 1), ("tout",)]


def make_feeds(inputs, b, shared=None):
    if shared is None:
        shared = {}
        ident, cols = host_consts(inputs)
        shared["ident"] = ident
        shared["cols"] = cols
        for l in range(2):
            for s_ in range(2):
                shared[f"w13_{l}{s_}"] = np.ascontiguousarray(np.asarray(inputs["ffn_w13"][l, s_], np.float32))
                shared[f"w2_{l}{s_}"] = np.ascontiguousarray(np.asarray(inputs["ffn_w2"][l, s_], np.float32))
        shared.update(rwkv_host(inputs))
        m = mla_host(inputs, 0)
        m.pop("mla_pos")
        shared.update(m)
    feeds = dict(shared)
    feeds["x"] = np.ascontiguousarray(np.asarray(inputs["x"][b], np.float32))
    feeds["mla_pos"] = np.ascontiguousarray(
        np.broadcast_to(np.asarray(inputs["positions"][b], np.int32)[None, :], (64, T)))
    return feeds, shared


def kernel(**inputs):
    inputs = {k: np.asarray(v) for k, v in inputs.items()}
    nb = inputs["x"].shape[0]
    nc, ins, out_name, _ = build(PHASES)
    in_maps = []
    shared = None
    for b in range(nb):
        feeds, shared = make_feeds(inputs, b, shared)
        in_maps.append({k: feeds[k] for k in ins})
    res = run_bass_kernel_spmd(nc, in_maps, core_ids=list(range(nb)))
    out = np.stack([np.asarray(res.results[b][out_name], np.float32) for b in range(nb)], axis=0)
    return out
```

```python
import numpy as np
import concourse.bass as bass
import concourse.mybir as mybir
from concourse.bass_utils import run_bass_kernel_spmd

F32 = mybir.dt.float32
BF16 = mybir.dt.bfloat16
I32 = mybir.dt.int32
AF = mybir.ActivationFunctionType
ALU = mybir.AluOpType
AX = mybir.AxisListType

T = 2048
D = 2048
FF = 5504
NCH = 16
NFC = 43
RMS_EPS = 1e-6
SB_BYTES = 207872


class Tok:
    __slots__ = ("w", "r", "excl")

    def __init__(self, excl=False):
        self.w = None
        self.r = {}
        self.excl = excl


class DSem:
    def __init__(self, h, idx):
        self.h = h
        self.idx = idx
        self.count = 0


class Prog:
    CE = ("pe", "act", "dve", "pool")

    def __init__(self, nc, esems, dsems):
        self.nc = nc
        self.code = {e: [] for e in ("pe", "act", "dve", "pool", "sp")}
        self.esem = esems
        self.ecnt = {e: 0 for e in self.CE}
        self.seen = {e: {} for e in self.code}
        self.free_dsems = [DSem(h, i) for i, h in enumerate(dsems)]
        self.all_dsems = list(self.free_dsems)
        self.ninstr = 0

    def dsem(self):
        return self.free_dsems.pop()

    def _need(self, eng, waits, ev):
        if ev is None:
            return
        kind, s, v = ev
        if kind == "d":
            v = s.count
            key = ("d", s.idx)
        else:
            if s == eng and eng == "pe":
                return
            key = ("e", s)
        if self.seen[eng].get(key, 0) >= v:
            return
        if waits.get(key, (None, 0))[1] < v:
            waits[key] = (s, v)

    def op(self, eng, fn, reads=(), writes=(), inc=True, dsem=None):
        if any(t.excl for t in reads):
            writes = tuple(writes) + tuple(t for t in reads if t.excl)
            reads = tuple(t for t in reads if not t.excl)
        waits = {}
        for t in reads:
            self._need(eng, waits, t.w)
        for t in writes:
            if t.w is not None and not (t.w[0] == "e" and t.w[1] == eng):
                self._need(eng, waits, t.w)
            for ev in t.r.values():
                if not (ev[0] == "e" and ev[1] == eng):
                    self._need(eng, waits, ev)
        wl = []
        for key, (s, v) in waits.items():
            self.seen[eng][key] = v
            wl.append((s.h if key[0] == "d" else self.esem[s], v))
        if dsem is not None:
            dsem.count += 16
            ev = ("d", dsem, dsem.count)
            incspec = (dsem.h, 16)
            rkey = ("d", dsem.idx)
        else:
            if inc:
                self.ecnt[eng] += 1
                ev = ("e", eng, self.ecnt[eng])
                incspec = (self.esem[eng], 1)
            else:
                ev = ("e", eng, self.ecnt[eng] + 1)
                incspec = None
            rkey = ("e", eng)
        for t in reads:
            t.r[rkey] = ev
        for t in writes:
            t.w = ev
            t.r = {}
        self.code[eng].append((wl, fn, incspec))
        self.ninstr += 1

    def barrier(self):
        evs = [("e", e, self.ecnt[e]) for e in self.CE if self.ecnt[e] > 0]
        evs += [("d", d, d.count) for d in self.all_dsems if d.count > 0]
        for eng in self.code:
            waits = {}
            for ev in evs:
                if ev[0] == "e" and ev[1] == eng and eng == "pe":
                    continue
                self._need(eng, waits, ev)
            wl = []
            for key, (s, v) in waits.items():
                self.seen[eng][key] = v
                wl.append((s.h if key[0] == "d" else self.esem[s], v))
            if wl:
                self.code[eng].append((wl, None, None))

    def emit(self, block):
        def mk(name):
            def body(e):
                for wl, fn, incspec in self.code[name]:
                    for h, v in wl:
                        e.wait_ge(h, v)
                    if fn is None:
                        continue
                    ins = fn(e)
                    if incspec is not None:
                        ins.then_inc(incspec[0], incspec[1])
            return body

        block.tensor(mk("pe"))
        block.scalar(mk("act"))
        block.vector(mk("dve"))
        block.gpsimd(mk("pool"))
        block.sync(mk("sp"))

    def mm(self, out, lhsT, rhs, start, stop, reads, writes, inc=None):
        self.op("pe", lambda e: e.matmul(out, lhsT, rhs, start=start, stop=stop),
                reads, writes, inc=(stop if inc is None else inc))

    def tr(self, out, in_, ident, reads, writes, inc=True):
        self.op("pe", lambda e: e.transpose(out, in_, ident), reads, writes, inc=inc)

    def dma(self, q, out, in_, dsem, reads, writes):
        self.op(q, lambda e: e.dma_start(out=out, in_=in_), reads, writes, dsem=dsem)

    def actf(self, out, in_, func, reads, writes, bias=None, scale=None, eng="act"):
        kw = {}
        if bias is not None:
            kw["bias"] = bias
        if scale is not None:
            kw["scale"] = scale
        self.op("act", lambda e: e.activation(out, in_, func, **kw), reads, writes)

    def copy(self, eng, out, in_, reads, writes):
        if eng == "act":
            self.op("act", lambda e: e.copy(out, in_), reads, writes)
        else:
            self.op(eng, lambda e: e.tensor_copy(out, in_), reads, writes)

    def tt(self, eng, out, in0, in1, op, reads, writes):
        self.op(eng, lambda e: e.tensor_tensor(out, in0, in1, op), reads, writes)

    def ts(self, eng, out, in0, s1, s2, op0, op1, reads, writes):
        if s2 is None:
            self.op(eng, lambda e: e.tensor_scalar(out, in0, s1, None, op0), reads, writes)
        else:
            self.op(eng, lambda e: e.tensor_scalar(out, in0, s1, s2, op0, op1), reads, writes)

    def stt(self, eng, out, in0, scalar, in1, op0, op1, reads, writes):
        self.op(eng, lambda e: e.scalar_tensor_tensor(out, in0, scalar, in1, op0, op1), reads, writes)

    def memset(self, eng, ap, val, writes):
        self.op(eng, lambda e: e.memset(ap, val), (), writes)


class Arena:
    def __init__(self, t32, nbytes):
        self.v = {F32: t32, BF16: t32.bitcast(BF16), I32: t32.bitcast(I32)}
        self.cap = nbytes
        self.top = 0
        self.marks = []

    def alloc(self, dtype, n, parts=128, p0=0):
        sz = 2 if dtype == BF16 else 4
        off = (self.top + 63) // 64 * 64
        self.top = off + n * sz
        assert self.top <= self.cap, f"SBUF arena overflow {self.top} > {self.cap}"
        return self.v[dtype][p0:p0 + parts, off // sz: off // sz + n]

    def mark(self):
        self.marks.append(self.top)

    def release(self):
        self.top = self.marks.pop()


class Ctx:
    pass


def phase_tin(cx, x_tm, xs_dst):
    P, A = cx.P, cx.A
    A.mark()
    xin = [A.alloc(F32, D) for _ in range(2)]
    xo = [A.alloc(F32, NCH * 128) for _ in range(2)]
    t_in = [Tok() for _ in range(2)]
    t_o = [Tok() for _ in range(2)]
    ds_in = [P.dsem() for _ in range(2)]
    ds_o = [P.dsem() for _ in range(2)]
    t_ps = [Tok(True) for _ in range(2)]
    dst_v = xs_dst.rearrange("c p t -> p c t")
    for tb in range(T // 128):
        s = tb % 2
        P.dma("sp", xin[s], x_tm[tb * 128:(tb + 1) * 128, :], ds_in[s], (), (t_in[s],))
        for q in range(4):
            b = (tb * 4 + q) % 2
            bank = cx.bank(b)
            for i in range(4):
                c = q * 4 + i
                P.tr(bank[:, i * 128:(i + 1) * 128], xin[s][:, c * 128:(c + 1) * 128], cx.ident,
                     (t_in[s],), (t_ps[b],), inc=(i == 3))
            eng = "dve" if q % 2 == 0 else "act"
            P.copy(eng, xo[s][:, q * 512:(q + 1) * 512], bank, (t_ps[b],), (t_o[s],))
        P.dma("sp", dst_v[:, :, tb * 128:(tb + 1) * 128],
              xo[s].rearrange("p (c t) -> p c t", c=NCH), ds_o[s], (t_o[s],), cx.xs_tok(None, tb // 4))
    P.barrier()
    for d in ds_in + ds_o:
        P.free_dsems.append(d)
    A.release()


def phase_tout(cx, xs_src, out_tm):
    P, A = cx.P, cx.A
    A.mark()
    xin = [A.alloc(F32, NCH * 128) for _ in range(2)]
    xo = [A.alloc(F32, D) for _ in range(2)]
    t_in = [Tok() for _ in range(2)]
    t_o = [Tok() for _ in range(2)]
    ds_in = [P.dsem() for _ in range(2)]
    ds_o = [P.dsem() for _ in range(2)]
    t_ps = [Tok(True) for _ in range(2)]
    src_v = xs_src.rearrange("c p t -> p c t")
    for tb in range(T // 128):
        s = tb % 2
        P.dma("sp", xin[s].rearrange("p (c t) -> p c t", c=NCH), src_v[:, :, tb * 128:(tb + 1) * 128],
              ds_in[s], cx.xs_tok(None, tb // 4), (t_in[s],))
        for q in range(4):
            b = (tb * 4 + q) % 2
            bank = cx.bank(b)
            for i in range(4):
                c = q * 4 + i
                P.tr(bank[:, i * 128:(i + 1) * 128], xin[s][:, c * 128:(c + 1) * 128], cx.ident,
                     (t_in[s],), (t_ps[b],), inc=(i == 3))
            eng = "dve" if q % 2 == 0 else "act"
            P.copy(eng, xo[s][:, q * 512:(q + 1) * 512], bank, (t_ps[b],), (t_o[s],))
        P.dma("sp", out_tm[tb * 128:(tb + 1) * 128, :], xo[s], ds_o[s], (t_o[s],), (cx.t_out,))
    P.barrier()
    for d in ds_in + ds_o:
        P.free_dsems.append(d)
    A.release()


def rmsnorm_tile(cx, xt, t_xt, gcol, hT_out, t_h, ntok, sqb, t_sq, rstd, t_rstd, bank, t_bank):
    P = cx.P
    xv = xt.rearrange("p (c t) -> p c t", c=NCH)
    for c in range(NCH):
        s = c % len(sqb)
        P.actf(sqb[s][:, :ntok], xv[:, c, :], AF.Square, (t_xt,), (t_sq[s],))
        P.mm(bank[:, :ntok], cx.ones_bf, sqb[s][:, :ntok], c == 0, c == NCH - 1,
             (t_sq[s], cx.t_const), (t_bank,), inc=True)
    P.ts("dve", rstd[:, :ntok], bank[:, :ntok], RMS_EPS, None, ALU.add, None, (t_bank,), (t_rstd,))
    P.actf(rstd[:, :ntok], rstd[:, :ntok], AF.Ln, (t_rstd,), (t_rstd,))
    P.actf(rstd[:, :ntok], rstd[:, :ntok], AF.Exp, (t_rstd,), (t_rstd,), scale=-0.5)
    for c in range(NCH):
        P.stt("dve", hT_out(c), xv[:, c, :], gcol[:, c:c + 1], rstd[:, :ntok], ALU.mult, ALU.mult,
              (t_xt, t_rstd, cx.t_const), (t_h,))


def phase_ffn(cx, xs_src, xs_dst, w13, w2, gcol):
    P, A = cx.P, cx.A
    A.mark()
    HALF = 1024
    NTT = HALF // 512
    hT = A.alloc(BF16, NCH * HALF)
    hTv = hT.rearrange("p (c t) -> p c t", c=NCH)
    actT = A.alloc(BF16, NFC * HALF)
    actTv = actT.rearrange("p (j t) -> p j t", j=NFC)
    t_h = Tok()
    t_act = [Tok() for _ in range(NFC)]
    sqb = [A.alloc(BF16, 512) for _ in range(3)]
    t_sq = [Tok() for _ in range(3)]
    rstd = A.alloc(F32, 512)
    t_rstd = Tok()
    sg = [A.alloc(F32, 512) for _ in range(2)]
    t_sg = [Tok() for _ in range(2)]
    xold = [A.alloc(F32, 512) for _ in range(2)]
    t_xold = [Tok() for _ in range(2)]
    ds_xold = [P.dsem() for _ in range(2)]
    xnew = [A.alloc(F32, 512) for _ in range(2)]
    t_xnew = [Tok() for _ in range(2)]
    ds_xnew = [P.dsem() for _ in range(2)]
    tops = []
    A.mark()
    xt = [A.alloc(F32, NCH * 512) for _ in range(2)]
    tops.append(A.top); A.release(); A.mark()
    w13s = [A.alloc(F32, 2 * NCH * 128) for _ in range(2)]
    w13b = [A.alloc(BF16, 2 * NCH * 128) for _ in range(2)]
    tops.append(A.top); A.release(); A.mark()
    w2s = [A.alloc(F32, NFC * 128) for _ in range(2)]
    w2b = [A.alloc(BF16, NFC * 128) for _ in range(2)]
    tops.append(A.top); A.release()
    A.top = max(tops)
    t_reg = [Tok() for _ in range(2)]
    t_regb = [Tok() for _ in range(2)]
    t_regb2 = [Tok() for _ in range(2)]
    ds_stage = [P.dsem() for _ in range(2)]
    src_v = xs_src.rearrange("c p t -> p c t")
    dst_v = xs_dst.rearrange("c p t -> p c t")
    w13v = w13.rearrange("(c p) f -> p c f", p=128)
    w2v = w2.rearrange("(j p) e -> p j e", p=128)
    t_bA = Tok(True)
    t_bB = [Tok(True) for _ in range(4)]
    t_bC = [Tok(True) for _ in range(2)]
    for th in range(T // HALF):
        tok0 = th * HALF
        for tt in range(NTT):
            s = tt % 2
            P.dma("sp", xt[s].rearrange("p (c t) -> p c t", c=NCH),
                  src_v[:, :, tok0 + tt * 512: tok0 + (tt + 1) * 512], ds_stage[s], cx.xs_tok(None, th * NTT + tt), (t_reg[s],))
            rmsnorm_tile(cx, xt[s], t_reg[s], gcol, lambda c, tt=tt: hTv[:, c, tt * 512:(tt + 1) * 512], t_h,
                         512, sqb, t_sq, rstd, t_rstd, cx.bank(6), t_bA)
        P.barrier()
        def b_load(j):
            s = j % 2
            stg = w13s[s].rearrange("p (g c f) -> p g c f", g=2, c=NCH)
            P.dma("sp", stg[:, 0], w13v[:, :, j * 128:(j + 1) * 128], ds_stage[s], (), (t_reg[s],))
            P.dma("sp", stg[:, 1], w13v[:, :, FF + j * 128: FF + (j + 1) * 128], ds_stage[s], (), (t_reg[s],))

        def b_cast(j):
            s = j % 2
            stg = w13s[s].rearrange("p (g c f) -> p g c f", g=2, c=NCH)
            stb = w13b[s].rearrange("p (g c f) -> p g c f", g=2, c=NCH)
            P.copy("dve", stb[:, 0], stg[:, 0], (t_reg[s],), (t_regb[s],))
            P.copy("act", stb[:, 1], stg[:, 1], (t_reg[s],), (t_regb2[s],))

        b_load(0)
        b_load(1)
        b_cast(0)
        for j in range(NFC):
            s = j % 2
            stb = w13b[s].rearrange("p (g c f) -> p g c f", g=2, c=NCH)
            if j + 1 < NFC:
                b_cast(j + 1)
            if j + 2 < NFC:
                b_load(j + 2)
            for tt in range(NTT):
                bi = (j * NTT + tt) % 2
                bg, bu = cx.bank(2 * bi), cx.bank(2 * bi + 1)
                tg, tu = t_bB[2 * bi], t_bB[2 * bi + 1]
                rhs_t = slice(tt * 512, (tt + 1) * 512)
                for c in range(NCH):
                    P.mm(bg, stb[:, 0, c, :], hTv[:, c, rhs_t], c == 0, c == NCH - 1, (t_regb[s], t_h), (tg,))
                for c in range(NCH):
                    P.mm(bu, stb[:, 1, c, :], hTv[:, c, rhs_t], c == 0, c == NCH - 1, (t_regb2[s], t_h), (tu,))
                P.actf(sg[bi], bg, AF.Silu, (tg,), (t_sg[bi],))
                P.tt("dve", actTv[:, j, rhs_t], sg[bi], bu, ALU.mult, (t_sg[bi], tu), (t_act[j],))
        P.barrier()
        def c_load(e):
            s = e % 2
            stg = w2s[s].rearrange("p (j e) -> p j e", j=NFC)
            P.dma("sp", stg[:, 0:22, :], w2v[:, 0:22, e * 128:(e + 1) * 128], ds_stage[s], (), (t_reg[s],))
            P.dma("sp", stg[:, 22:NFC, :], w2v[:, 22:NFC, e * 128:(e + 1) * 128], ds_stage[s], (), (t_reg[s],))

        def c_cast(e):
            s = e % 2
            stg = w2s[s].rearrange("p (j e) -> p j e", j=NFC)
            stb = w2b[s].rearrange("p (j e) -> p j e", j=NFC)
            P.copy("dve", stb[:, 0:22, :], stg[:, 0:22, :], (t_reg[s],), (t_regb[s],))
            P.copy("act", stb[:, 22:NFC, :], stg[:, 22:NFC, :], (t_reg[s],), (t_regb2[s],))

        c_load(0)
        c_load(1)
        c_cast(0)
        for e in range(NCH):
            s = e % 2
            stb = w2b[s].rearrange("p (j e) -> p j e", j=NFC)
            if e + 1 < NCH:
                c_cast(e + 1)
            if e + 2 < NCH:
                c_load(e + 2)
            for tt in range(NTT):
                bi = (e * NTT + tt) % 2
                bk, tb_ = cx.bank(4 + bi), t_bC[bi]
                rhs_t = slice(tt * 512, (tt + 1) * 512)
                tsl = slice(tok0 + tt * 512, tok0 + (tt + 1) * 512)
                P.dma("act", xold[bi], src_v[:, e, tsl], ds_xold[bi], cx.xs_tok(e, th * NTT + tt), (t_xold[bi],))
                for j in range(NFC):
                    P.mm(bk, stb[:, j, :], actTv[:, j, rhs_t], j == 0, j == NFC - 1,
                         (t_regb[s] if j < 22 else t_regb2[s], t_act[j]), (tb_,))
                P.stt("dve", xnew[bi], bk, 0.5, xold[bi], ALU.mult, ALU.add, (tb_, t_xold[bi]), (t_xnew[bi],))
                P.dma("pool", dst_v[:, e, tsl], xnew[bi], ds_xnew[bi], (t_xnew[bi],), cx.xs_tok(e, th * NTT + tt))
        P.barrier()
    for d in ds_xold + ds_xnew + ds_stage:
        P.free_dsems.append(d)
    A.release()


def norm_from_bank(cx, rstd, t_rstd, bank, t_bank, n, mean_scale, eps, parts=128):
    P = cx.P
    P.ts("dve", rstd[:parts, :n], bank[:parts, :n], mean_scale, eps, ALU.mult, ALU.add, (t_bank,), (t_rstd,))
    P.actf(rstd[:parts, :n], rstd[:parts, :n], AF.Ln, (t_rstd,), (t_rstd,))
    P.actf(rstd[:parts, :n], rstd[:parts, :n], AF.Exp, (t_rstd,), (t_rstd,), scale=-0.5)


MLA_H = 16
STOP = 0
DBG = False
SM_SCALE = 1.0 / float(np.sqrt(192.0))


def phase_mla(cx, xs_src, xs_dst, wd, wuq, wukv, wo, gcol, mcols_d, pos_d):
    P, A = cx.P, cx.A
    A.mark()
    src_v = xs_src.rearrange("c p t -> p c t")
    dst_v = xs_dst.rearrange("c p t -> p c t")
    mc = A.alloc(F32, 16)
    t_mc = Tok()
    dsm = P.dsem()
    P.dma("sp", mc, mcols_d, dsm, (), (t_mc,))
    cqn = A.alloc(BF16, 4 * T); cqnv = cqn.rearrange("p (c t) -> p c t", c=4)
    ckvn = A.alloc(BF16, 4 * T); ckvnv = ckvn.rearrange("p (c t) -> p c t", c=4)
    kpe = A.alloc(F32, T)
    kpesw = A.alloc(F32, T)
    t_cqn, t_ckvn, t_kpe = Tok(), Tok(), Tok()
    rstd = A.alloc(F32, 512); t_rstd = Tok()
    sqb = [A.alloc(BF16, 512) for _ in range(3)]; t_sq = [Tok() for _ in range(3)]
    A.mark()
    wdb = A.alloc(BF16, NCH * 1152); wdbv = wdb.rearrange("p (c f) -> p c f", c=NCH)
    t_wdb = Tok()
    stg = [A.alloc(F32, NCH * 128) for _ in range(2)]; t_stg = [Tok() for _ in range(2)]
    ds_stg = [P.dsem() for _ in range(2)]
    wdv = wd.rearrange("(c p) f -> p c f", p=128)
    for i in range(9):
        s = i % 2
        P.dma("sp", stg[s].rearrange("p (c f) -> p c f", c=NCH), wdv[:, :, i * 128:(i + 1) * 128], ds_stg[s], (), (t_stg[s],))
        P.copy("act" if i % 2 == 0 else "dve", wdbv[:, :, i * 128:(i + 1) * 128], stg[s].rearrange("p (c f) -> p c f", c=NCH), (t_stg[s],), (t_wdb,))
    if STOP == 11:
        P.barrier(); A.release(); A.release(); return
    xt = A.alloc(F32, NCH * 512); t_xt = Tok(); ds_xt = P.dsem()
    hT = A.alloc(BF16, NCH * 512); hTv = hT.rearrange("p (c t) -> p c t", c=NCH); t_h = Tok()
    cT = A.alloc(F32, 8 * 512); cTv = cT.rearrange("p (c t) -> p c t", c=8); t_cT = Tok()
    t_b = [Tok(True) for _ in range(8)]
    for tt in range(4):
        tsl = slice(tt * 512, (tt + 1) * 512)
        P.dma("sp", xt.rearrange("p (c t) -> p c t", c=NCH), src_v[:, :, tsl], ds_xt, cx.xs_tok(None, tt), (t_xt,))
        rmsnorm_tile(cx, xt, t_xt, gcol, lambda c: hTv[:, c, :], t_h, 512, sqb, t_sq, rstd, t_rstd, cx.bank(6), t_b[6])
        for oc in range(10):
            if STOP == 12 or (STOP == 13 and oc >= 8):
                break
            b = oc % 2
            bank = cx.bank(b)
            if oc < 8:
                for c in range(NCH):
                    P.mm(bank, wdbv[:, c, oc * 128:(oc + 1) * 128], hTv[:, c, :], c == 0, c == NCH - 1, (t_wdb, t_h), (t_b[b],))
                eng = "act" if oc % 2 == 0 else "dve"
                P.copy(eng, cTv[:, oc, :], bank, (t_b[b],), (t_cT,))
            else:
                c0 = 1024 + (oc - 8) * 64
                for c in range(NCH):
                    P.mm(bank[0:64, :], wdbv[:, c, c0:c0 + 64], hTv[:, c, :], c == 0, c == NCH - 1, (t_wdb, t_h), (t_b[b],))
                dstb = kpe if oc == 8 else kpesw
                P.copy("act", dstb[0:64, tsl], bank[0:64, :], (t_b[b],), (t_kpe,))
        for which in range(2):
            if STOP in (12, 13, 14):
                break
            for c in range(4):
                s = c % 3
                P.actf(sqb[s], cTv[:, which * 4 + c, :], AF.Square, (t_cT,), (t_sq[s],))
                P.mm(cx.bank(6), cx.ones1_bf, sqb[s], c == 0, c == 3, (t_sq[s], cx.t_const), (t_b[6],), inc=True)
            norm_from_bank(cx, rstd, t_rstd, cx.bank(6), t_b[6], 512, 1.0 / 512.0, RMS_EPS)
            dstv, tk = (cqnv, t_cqn) if which == 0 else (ckvnv, t_ckvn)
            for c in range(4):
                P.stt("dve", dstv[:, c, tsl], cTv[:, which * 4 + c, :], mc[:, which * 4 + c: which * 4 + c + 1], rstd,
                      ALU.mult, ALU.mult, (t_cT, t_rstd, t_mc), (tk,))
    P.barrier()
    A.release()
    for d in ds_stg + [ds_xt]:
        P.free_dsems.append(d)
    if STOP == 1:
        A.release(); return
    Cq = A.alloc(F32, T); Sq = A.alloc(F32, T)
    t_tab = Tok()
    kperot = A.alloc(F32, T); sqkpe = A.alloc(BF16, T); t_kr = Tok()
    t_OT = Tok()
    A.mark()
    posi = A.alloc(I32, T); posf = A.alloc(F32, T); ang = posf
    t_pos, t_ang = Tok(), Tok()
    t_ang = t_pos
    Ck = A.alloc(F32, T); Sk = A.alloc(F32, T)
    tmp = A.alloc(F32, T); t_tmp = Tok()
    P.dma("sp", posi[0:64, :], pos_d, dsm, (), (t_pos,))
    P.copy("dve", posf[0:64, :], posi[0:64, :], (t_pos,), (t_pos,))
    TWO_PI = float(2.0 * np.pi)
    PI = float(np.pi)
    P.ts("dve", ang[0:64, :], posf[0:64, :], mc[0:64, 14:15], None, ALU.mult, None, (t_pos, t_mc), (t_ang,))
    ki = posi
    C1 = 6.28125
    C2 = float(2.0 * np.pi - 6.28125)

    def sin_of(out, shift):
        P.ts("dve", tmp[0:64, :], ang[0:64, :], shift, 1.0 / TWO_PI, ALU.add, ALU.mult, (t_ang,), (t_tmp,))
        P.copy("dve", ki[0:64, :], tmp[0:64, :], (t_tmp,), (t_ki,))
        P.copy("dve", tmp[0:64, :], ki[0:64, :], (t_ki,), (t_tmp,))
        P.ts("dve", out, ang[0:64, :], shift, None, ALU.add, None, (t_ang,), (t_tab,))
        P.stt("dve", out, tmp[0:64, :], -C1, out, ALU.mult, ALU.add, (t_tmp, t_tab), (t_tab,))
        P.stt("dve", out, tmp[0:64, :], -C2, out, ALU.mult, ALU.add, (t_tmp, t_tab), (t_tab,))
        P.ts("dve", tmp[0:64, :], out, PI, TWO_PI, ALU.is_gt, ALU.mult, (t_tab,), (t_tmp,))
        P.tt("dve", out, out, tmp[0:64, :], ALU.subtract, (t_tab, t_tmp), (t_tab,))
        P.ts("dve", out, out, -PI, PI, ALU.max, ALU.min, (t_tab,), (t_tab,))
        P.actf(out, out, AF.Sin, (t_tab,), (t_tab,))

    t_ki = Tok()
    sin_of(Sq[0:64, :], 0.0)
    sin_of(Cq[0:64, :], 0.5 * PI)
    P.ts("dve", Sq[0:64, :], Sq[0:64, :], mc[0:64, 15:16], None, ALU.mult, None, (t_tab, t_mc), (t_tab,))
    P.ts("dve", Ck[0:64, :], Cq[0:64, :], mc[0:64, 12:13], None, ALU.mult, None, (t_tab, t_mc), (t_tab,))
    P.ts("dve", Sk[0:64, :], Sq[0:64, :], mc[0:64, 13:14], None, ALU.mult, None, (t_tab, t_mc), (t_tab,))
    P.ts("dve", Cq[0:64, :], Cq[0:64, :], mc[0:64, 10:11], None, ALU.mult, None, (t_tab, t_mc), (t_tab,))
    P.ts("dve", Sq[0:64, :], Sq[0:64, :], mc[0:64, 11:12], None, ALU.mult, None, (t_tab, t_mc), (t_tab,))
    P.tt("dve", kperot[0:64, :], kpe[0:64, :], Ck[0:64, :], ALU.mult, (t_kpe, t_tab), (t_kr,))
    P.tt("dve", tmp[0:64, :], kpesw[0:64, :], Sk[0:64, :], ALU.mult, (t_kpe, t_tab), (t_tmp,))
    P.tt("dve", kperot[0:64, :], kperot[0:64, :], tmp[0:64, :], ALU.add, (t_kr, t_tmp), (t_kr,))
    P.memset("pool", sqkpe, 0.0, (t_kr,))
    P.actf(sqkpe[0:64, :], kpe[0:64, :], AF.Square, (t_kpe,), (t_kr,))
    P.barrier()
    A.release()
    if STOP == 2:
        A.release(); return
    A.mark()
    wq_s = [A.alloc(F32, 4 * 256) for _ in range(2)]; wq_b = [A.alloc(BF16, 4 * 256) for _ in range(2)]
    wk_s = [A.alloc(F32, 4 * 256) for _ in range(2)]; wk_b = [A.alloc(BF16, 4 * 256) for _ in range(2)]
    t_wqs = [Tok() for _ in range(2)]; t_wqb = [Tok() for _ in range(2)]
    t_wks = [Tok() for _ in range(2)]; t_wkb = [Tok() for _ in range(2)]
    ds_w = [P.dsem() for _ in range(2)]
    wuqv = wuq.rearrange("(c p) h f -> p c h f", p=128)
    wukvv = wukv.rearrange("(c p) h f -> p c h f", p=128)
    qn = A.alloc(BF16, T); qr = A.alloc(BF16, T); kn = A.alloc(BF16, T); kr = A.alloc(BF16, T)
    t_q, t_k = Tok(), Tok()
    P.memset("pool", qr, 0.0, (t_q,))
    P.memset("pool", kr, 0.0, (t_k,))
    P.memset("pool", sqb[1], 0.0, (t_sq[1],))
    Vb = A.alloc(BF16, 16 * 128); Vv = Vb.rearrange("p (s d) -> p s d", s=16); t_V = Tok()
    pT = [A.alloc(BF16, 512) for _ in range(3)]; t_pT = [Tok() for _ in range(3)]
    rs = A.alloc(F32, 512); t_rs = Tok()
    t1 = [A.alloc(F32, 512) for _ in range(4)]; t2 = [A.alloc(F32, 512) for _ in range(4)]
    rst = [A.alloc(F32, 512) for _ in range(4)]
    sqn = [A.alloc(BF16, 512) for _ in range(4)]; sqp = [A.alloc(BF16, 512) for _ in range(4)]
    t_t1 = [Tok() for _ in range(4)]; t_t2 = [Tok() for _ in range(4)]; t_rst = [Tok() for _ in range(4)]
    t_sqn = [Tok() for _ in range(4)]; t_sqp = [Tok() for _ in range(4)]
    for tt in range(4):
        P.memset("pool", sqp[tt], 0.0, (t_sqp[tt],))
    Ob = [A.alloc(BF16, T) for _ in range(2)]; t_Ob = [Tok() for _ in range(2)]; ds_Ob = [P.dsem() for _ in range(2)]
    t_b = [Tok(True) for _ in range(8)]
    pj = 0
    sc = 0
    for h in range(MLA_H):
        s = h % 2
        P.dma("sp", wq_s[s].rearrange("p (c f) -> p c f", c=4), wuqv[:, :, h, :], ds_w[s], (), (t_wqs[s],))
        P.dma("sp", wk_s[s].rearrange("p (c f) -> p c f", c=4), wukvv[:, :, h, :], ds_w[s], (), (t_wks[s],))
        P.copy("act", wq_b[s], wq_s[s], (t_wqs[s],), (t_wqb[s],))
        P.copy("dve", wk_b[s], wk_s[s], (t_wks[s],), (t_wkb[s],))
        wqv = wq_b[s].rearrange("p (c f) -> p c f", c=4)
        wkv = wk_b[s].rearrange("p (c f) -> p c f", c=4)
        for g in range(4):
            b = 6 + (pj % 2); pj += 1
            for i in range(4):
                st = g * 4 + i
                for c in range(4):
                    P.mm(cx.bank(b)[:, i * 128:(i + 1) * 128], ckvnv[:, c, st * 128:(st + 1) * 128], wkv[:, c, 128:256],
                         c == 0, c == 3, (t_ckvn, t_wkb[s]), (t_b[b],), inc=(c == 3 and i == 3))
            P.copy("act", Vb[:, g * 512:(g + 1) * 512], cx.bank(b), (t_b[b],), (t_V,))
        if STOP == 31:
            continue
        TS = [slice(tt * 512, (tt + 1) * 512) for tt in range(4)]
        R4 = range(4)
        for tt in R4:
            for c in range(4):
                P.mm(cx.bank(tt)[0:64, :], wqv[:, c, 128:192], cqnv[:, c, TS[tt]], c == 0, c == 3, (t_wqb[s], t_cqn), (t_b[tt],))
        for tt in R4:
            P.actf(sqp[tt][0:64, :], cx.bank(tt)[0:64, :], AF.Square, (t_b[tt],), (t_sqp[tt],))
            P.tt("dve", t1[tt][0:64, :], cx.bank(tt)[0:64, :], Cq[0:64, TS[tt]], ALU.mult, (t_b[tt], t_tab), (t_t1[tt],))
        for tt in R4:
            for c in range(4):
                P.mm(cx.bank(4 + tt)[0:64, :], wqv[:, c, 192:256], cqnv[:, c, TS[tt]], c == 0, c == 3, (t_wqb[s], t_cqn), (t_b[4 + tt],))
        for tt in R4:
            P.tt("dve", t2[tt][0:64, :], cx.bank(4 + tt)[0:64, :], Sq[0:64, TS[tt]], ALU.mult, (t_b[4 + tt], t_tab), (t_t2[tt],))
            P.tt("dve", t1[tt][0:64, :], t1[tt][0:64, :], t2[tt][0:64, :], ALU.add, (t_t1[tt], t_t2[tt]), (t_t1[tt],))
        for tt in R4:
            for c in range(4):
                P.mm(cx.bank(tt), wqv[:, c, 0:128], cqnv[:, c, TS[tt]], c == 0, c == 3, (t_wqb[s], t_cqn), (t_b[tt],))
        for tt in R4:
            P.actf(sqn[tt], cx.bank(tt), AF.Square, (t_b[tt],), (t_sqn[tt],))
        for tt in R4:
            P.mm(cx.bank(4 + tt), cx.ones1_bf, sqn[tt], True, False, (t_sqn[tt], cx.t_const), (t_b[4 + tt],), inc=False)
            P.mm(cx.bank(4 + tt), cx.ones1_bf, sqp[tt], False, True, (t_sqp[tt], cx.t_const), (t_b[4 + tt],), inc=True)
        for tt in R4:
            P.ts("dve", rst[tt], cx.bank(4 + tt), 1.0 / 192.0, RMS_EPS, ALU.mult, ALU.add, (t_b[4 + tt],), (t_rst[tt],))
        for tt in R4:
            P.actf(rst[tt], rst[tt], AF.Ln, (t_rst[tt],), (t_rst[tt],))
            P.actf(rst[tt], rst[tt], AF.Exp, (t_rst[tt],), (t_rst[tt],), scale=-0.5)
        for tt in R4:
            P.stt("dve", qn[:, TS[tt]], cx.bank(tt), mc[:, 8:9], rst[tt], ALU.mult, ALU.mult, (t_b[tt], t_rst[tt], t_mc), (t_q,))
            P.tt("dve", qr[0:64, TS[tt]], t1[tt][0:64, :], rst[tt][0:64, :], ALU.mult, (t_t1[tt], t_rst[tt]), (t_q,))
        for tt in R4:
            for c in range(4):
                P.mm(cx.bank(tt), wkv[:, c, 0:128], ckvnv[:, c, TS[tt]], c == 0, c == 3, (t_wkb[s], t_ckvn), (t_b[tt],))
        for tt in R4:
            P.actf(sqn[tt], cx.bank(tt), AF.Square, (t_b[tt],), (t_sqn[tt],))
        for tt in R4:
            P.mm(cx.bank(4 + tt), cx.ones1_bf, sqn[tt], True, False, (t_sqn[tt], cx.t_const), (t_b[4 + tt],), inc=False)
            P.mm(cx.bank(4 + tt), cx.ones1_bf, sqkpe[:, TS[tt]], False, True, (t_kr, cx.t_const), (t_b[4 + tt],), inc=True)
        for tt in R4:
            P.ts("dve", rst[tt], cx.bank(4 + tt), 1.0 / 192.0, RMS_EPS, ALU.mult, ALU.add, (t_b[4 + tt],), (t_rst[tt],))
        for tt in R4:
            P.actf(rst[tt], rst[tt], AF.Ln, (t_rst[tt],), (t_rst[tt],))
            P.actf(rst[tt], rst[tt], AF.Exp, (t_rst[tt],), (t_rst[tt],), scale=-0.5)
        for tt in R4:
            P.stt("dve", kn[:, TS[tt]], cx.bank(tt), mc[:, 9:10], rst[tt], ALU.mult, ALU.mult, (t_b[tt], t_rst[tt], t_mc), (t_k,))
            P.tt("dve", kr[0:64, TS[tt]], kperot[0:64, TS[tt]], rst[tt][0:64, :], ALU.mult, (t_kr, t_rst[tt]), (t_k,))
        if STOP == 32:
            continue
        for qt in range(4):
            qb0 = qt * 4
            bO, bS = 3, 4
            nkt = qb0 + 4
            for kt in range(nkt):
                c0 = max(0, kt - qb0) * 128
                n = 512 - c0
                qsl = slice(qt * 512 + c0, (qt + 1) * 512)
                ksl = slice(kt * 128, (kt + 1) * 128)
                b = sc % 3; sc += 1
                ip = b
                P.mm(cx.bank(b)[:, 0:n], kn[:, ksl], qn[:, qsl], True, False, (t_k, t_q), (t_b[b],), inc=False)
                P.mm(cx.bank(b)[:, 0:n], kr[:, ksl], qr[:, qsl], False, True, (t_k, t_q), (t_b[b],))
                P.actf(pT[ip][:, 0:n], cx.bank(b)[:, 0:n], AF.Exp, (t_b[b],), (t_pT[ip],), scale=SM_SCALE)
                if kt >= qb0:
                    P.memset("pool", pT[ip][64:128, 0:64], 0.0, (t_pT[ip],))
                P.mm(cx.bank(bO)[:, c0:512], Vv[:, kt, :], pT[ip][:, 0:n], kt == 0, kt == nkt - 1, (t_V, t_pT[ip]), (t_b[bO],), inc=False)
                P.mm(cx.bank(bS)[:, c0:512], cx.ones1_bf, pT[ip][:, 0:n], kt == 0, kt == nkt - 1, (cx.t_const, t_pT[ip]), (t_b[bS],), inc=True)
            P.actf(rs, cx.bank(bS), AF.Ln, (t_b[bS],), (t_rs,))
            P.actf(rs, rs, AF.Exp, (t_rs,), (t_rs,), scale=-1.0)
            P.tt("dve", Ob[h % 2][:, qt * 512:(qt + 1) * 512], cx.bank(bO), rs, ALU.mult, (t_b[bO], t_rs), (t_Ob[h % 2],))
        P.dma("pool", cx.ots[h], Ob[h % 2], ds_Ob[h % 2], (t_Ob[h % 2],), (t_OT,))
    P.barrier()
    A.release()
    if STOP == 3:
        A.release(); return
    OT = A.alloc(BF16, MLA_H * T); OTv = OT.rearrange("p (h t) -> p h t", h=MLA_H)
    for c4 in range(4):
        P.dma("sp", OTv[:, c4 * 4:(c4 + 1) * 4, :], cx.ots.rearrange("h p t -> p h t")[:, c4 * 4:(c4 + 1) * 4, :], dsm, (t_OT,), (t_OT,))
    wo_s = [A.alloc(F32, MLA_H * 128) for _ in range(2)]; wo_b = [A.alloc(BF16, MLA_H * 128) for _ in range(2)]
    t_wos = [Tok() for _ in range(2)]; t_wob = [Tok() for _ in range(2)]
    xold = [A.alloc(F32, 512) for _ in range(2)]; t_xold = [Tok() for _ in range(2)]; ds_xold = [P.dsem() for _ in range(2)]
    xnew = [A.alloc(F32, 512) for _ in range(2)]; t_xnew = [Tok() for _ in range(2)]; ds_xnew = [P.dsem() for _ in range(2)]
    wov = wo.rearrange("h p e -> p h e")
    def o_load(e):
        s = e % 2
        P.dma("sp", wo_s[s].rearrange("p (h e) -> p h e", h=MLA_H), wov[:, :, e * 128:(e + 1) * 128], ds_w[s], (), (t_wos[s],))

    def o_cast(e):
        s = e % 2
        P.copy("act" if e % 2 == 0 else "dve", wo_b[s], wo_s[s], (t_wos[s],), (t_wob[s],))

    o_load(0)
    o_load(1)
    o_cast(0)
    k = 0
    for e in range(NCH):
        s = e % 2
        if e + 1 < NCH:
            o_cast(e + 1)
        if e + 2 < NCH:
            o_load(e + 2)
        wb = wo_b[s].rearrange("p (h e) -> p h e", h=MLA_H)
        for tt in range(4):
            bi = k % 2; k += 1
            b = 6 + bi
            tsl = slice(tt * 512, (tt + 1) * 512)
            P.dma("act", xold[bi], src_v[:, e, tsl], ds_xold[bi], cx.xs_tok(e, tt), (t_xold[bi],))
            for h in range(MLA_H):
                P.mm(cx.bank(b), wb[:, h, :], OTv[:, h, tsl], h == 0, h == MLA_H - 1, (t_wob[s], t_OT), (t_b[b],))
            P.tt("dve", xnew[bi], cx.bank(b), xold[bi], ALU.add, (t_b[b], t_xold[bi]), (t_xnew[bi],))
            P.dma("pool", dst_v[:, e, tsl], xnew[bi], ds_xnew[bi], (t_xnew[bi],), cx.xs_tok(e, tt))
    P.barrier()
    for d in ds_w + ds_xold + ds_xnew + ds_Ob + [dsm]:
        P.free_dsems.append(d)
    A.release()


RW_L = 64
RW_NCK = T // RW_L
DEC_C = float(np.exp(-0.5))


def linear_fm(cx, actv, t_act, w_ap, n_in_chunks, out_cols, evac, bank0=0, M=128):
    P, A = cx.P, cx.A
    A.mark()
    stg = [A.alloc(F32, n_in_chunks * 128) for _ in range(2)]
    wb = [A.alloc(BF16, n_in_chunks * 128) for _ in range(2)]
    t_s = [Tok() for _ in range(2)]; t_w = [Tok() for _ in range(2)]; t_w2 = [Tok() for _ in range(2)]
    ds = [P.dsem() for _ in range(2)]
    t_bk = [Tok(True) for _ in range(2)]
    wv = w_ap.rearrange("(c p) f -> p c f", p=128)
    hc = n_in_chunks // 2

    def l_load(oi):
        c0, m = out_cols[oi]
        s = oi % 2
        sv = stg[s].rearrange("p (c f) -> p c f", c=n_in_chunks)
        P.dma("sp", sv[:, :, 0:m], wv[:, :, c0:c0 + m], ds[s], (), (t_s[s],))

    def l_cast(oi):
        c0, m = out_cols[oi]
        s = oi % 2
        sv = stg[s].rearrange("p (c f) -> p c f", c=n_in_chunks)
        bv = wb[s].rearrange("p (c f) -> p c f", c=n_in_chunks)
        P.copy("dve", bv[:, 0:hc, 0:m], sv[:, 0:hc, 0:m], (t_s[s],), (t_w[s],))
        P.copy("act", bv[:, hc:, 0:m], sv[:, hc:, 0:m], (t_s[s],), (t_w2[s],))

    n_out = len(out_cols)
    l_load(0)
    if n_out > 1:
        l_load(1)
    l_cast(0)
    k = 0
    for oi, (c0, m) in enumerate(out_cols):
        s = oi % 2
        bv = wb[s].rearrange("p (c f) -> p c f", c=n_in_chunks)
        if oi + 1 < n_out:
            l_cast(oi + 1)
        if oi + 2 < n_out:
            l_load(oi + 2)
        for tt in range(4):
            bi = k % 2; k += 1
            bank = cx.bank(bank0 + bi)
            for c in range(n_in_chunks):
                P.mm(bank[0:m, :], bv[:, c, 0:m], actv[:, c, tt * 512:(tt + 1) * 512], c == 0, c == n_in_chunks - 1,
                     (t_w[s] if c < hc else t_w2[s], t_act), (t_bk[bi],))
            evac(oi, tt, bank, t_bk[bi])
    P.barrier()
    for d in ds:
        P.free_dsems.append(d)
    A.release()


def phase_rwkv(cx, xs_src, xs_dst, W, gcol, rc_d):
    P, A = cx.P, cx.A
    A.mark()
    src_v = xs_src.rearrange("c p t -> p c t")
    dst_v = xs_dst.rearrange("c p t -> p c t")
    rc = A.alloc(F32, 13 * 16); t_rc = Tok(); dsm = P.dsem()
    P.dma("sp", rc, rc_d, dsm, (), (t_rc,))
    omka = A.alloc(F32, 16)
    P.ts("dve", omka, rc[:, 9 * 16:10 * 16], -1.0, 1.0, ALU.mult, ALU.add, (t_rc,), (t_rc,))
    col = lambda k, e: rc[:, k * 16 + e: k * 16 + e + 1]
    t_xm = cx.t_xmix
    A.mark()
    TT = 256
    xt = [A.alloc(F32, NCH * TT) for _ in range(2)]; t_xt = [Tok() for _ in range(2)]; ds_xt = [P.dsem() for _ in range(2)]
    hb = A.alloc(F32, NCH * (TT + 4)); hbv = hb.rearrange("p (c t) -> p c t", c=NCH); t_hb = Tok()
    xx = A.alloc(F32, NCH * TT); xxv = xx.rearrange("p (c t) -> p c t", c=NCH); t_xx = Tok()
    xm = [A.alloc(BF16, NCH * TT) for _ in range(6)]; t_xmb = [Tok() for _ in range(6)]; ds_xm = [P.dsem() for _ in range(6)]
    sqb = [A.alloc(BF16, 512) for _ in range(3)]; t_sq = [Tok() for _ in range(3)]
    rstd = A.alloc(F32, 512); t_rstd = Tok()
    t_bn = Tok(True)
    utmp = [A.alloc(F32, TT) for _ in range(4)]; t_utmp = [Tok() for _ in range(4)]
    P.memset("dve", hbv[:, :, 3:4], 0.0, (t_hb,))
    for ti in range(T // TT):
        s = ti % 2
        tsl = slice(ti * TT, (ti + 1) * TT)
        P.dma("sp", xt[s].rearrange("p (c t) -> p c t", c=NCH), src_v[:, :, tsl], ds_xt[s], cx.xs_tok(None, ti // 2), (t_xt[s],))
        if ti > 0:
            P.copy("dve", hbv[:, :, 3:4], hbv[:, :, TT + 3:TT + 4], (t_hb,), (t_hb,))
        rmsnorm_tile(cx, xt[s], t_xt[s], gcol, lambda c: hbv[:, c, 4:TT + 4], t_hb, TT, sqb, t_sq, rstd, t_rstd,
                     cx.bank(6), t_bn)
        P.tt("dve", xxv, hbv[:, :, 3:TT + 3], hbv[:, :, 4:TT + 4], ALU.subtract, (t_hb,), (t_xx,))
        for j in range(6):
            xmv = xm[j].rearrange("p (c t) -> p c t", c=NCH)
            for c in range(NCH):
                if (j * NCH + c) % 3 == 2:
                    P.stt("dve", xmv[:, c, :], xxv[:, c, :], col(j, c), hbv[:, c, 4:TT + 4], ALU.mult, ALU.add,
                          (t_xx, t_hb, t_rc), (t_xmb[j],))
                else:
                    u = (j * NCH + c) % 4
                    P.actf(utmp[u], xxv[:, c, :], AF.Copy, (t_xx, t_rc), (t_utmp[u],), scale=col(j, c))
                    P.tt("dve", xmv[:, c, :], utmp[u], hbv[:, c, 4:TT + 4], ALU.add, (t_utmp[u], t_hb), (t_xmb[j],))
            P.dma("sp", cx.xmix[j].rearrange("c p t -> p c t")[:, :, tsl], xmv, ds_xm[j], (t_xmb[j],), (t_xm,))
    P.barrier()
    for d in ds_xt + ds_xm:
        P.free_dsems.append(d)
    A.release()
    A.mark()
    lw = A.alloc(BF16, T); la = A.alloc(BF16, T); lg = A.alloc(BF16, 2 * T); lgv = lg.rearrange("p (c t) -> p c t", c=2)
    t_l = Tok()
    A.mark()
    acts = [A.alloc(BF16, NCH * T) for _ in range(2)]
    t_acts = [Tok() for _ in range(2)]; ds_act = [P.dsem() for _ in range(2)]
    ot = [A.alloc(F32, 512) for _ in range(4)]; t_ot = [Tok() for _ in range(4)]; ds_ot = [P.dsem() for _ in range(4)]
    cnt = [0]
    order = [(0, "r"), (2, "k"), (3, "v"), (1, "lw"), (4, "la"), (5, "lg")]

    def load_act(j):
        av = acts[j % 2].rearrange("p (c t) -> p c t", c=NCH)
        for c4 in range(4):
            P.dma("sp", av[:, c4 * 4:(c4 + 1) * 4, :], cx.xmix[order[j][0]].rearrange("c p t -> p c t")[:, c4 * 4:(c4 + 1) * 4, :],
                  ds_act[j % 2], (t_xm,), (t_acts[j % 2],))

    load_act(0)
    for j, (src_j, kind) in enumerate(order):
        if j + 1 < len(order):
            load_act(j + 1)
        actv = acts[j % 2].rearrange("p (c t) -> p c t", c=NCH)
        t_act = t_acts[j % 2]
        if kind in ("r", "k", "v"):
            ji = "rkv".index(kind)

            def ev(oi, tt, bank, tb, ji=ji):
                i = cnt[0] % 4; cnt[0] += 1
                P.copy("act" if i % 2 == 0 else "dve", ot[i], bank, (tb,), (t_ot[i],))
                P.dma("pool", cx.rkv[ji].rearrange("c p t -> p c t")[:, oi, tt * 512:(tt + 1) * 512], ot[i], ds_ot[i],
                      (t_ot[i],), (cx.t_rkv,))
            linear_fm(cx, actv, t_act, W["w_rkv"][ji], NCH, [(e * 128, 128) for e in range(NCH)], ev)
        elif kind == "lw":
            def ev(oi, tt, bank, tb):
                P.actf(lw[0:96, tt * 512:(tt + 1) * 512], bank[0:96, :], AF.Tanh, (tb,), (t_l,))
            linear_fm(cx, actv, t_act, W["w1"], NCH, [(0, 96)], ev)
        elif kind == "la":
            def ev(oi, tt, bank, tb):
                P.copy("act", la[0:96, tt * 512:(tt + 1) * 512], bank[0:96, :], (tb,), (t_l,))
            linear_fm(cx, actv, t_act, W["a1"], NCH, [(0, 96)], ev)
        else:
            def ev(oi, tt, bank, tb):
                P.actf(lgv[:, oi, tt * 512:(tt + 1) * 512], bank, AF.Sigmoid, (tb,), (t_l,))
            linear_fm(cx, actv, t_act, W["g1"], NCH, [(0, 128), (128, 128)], ev)
    P.free_dsems.extend(ds_act + ds_ot)
    A.release()
    w2b = A.alloc(BF16, D); a2b = A.alloc(BF16, D); g2b = A.alloc(BF16, 2 * D); g2bv = g2b.rearrange("p (c f) -> p c f", c=2)
    t_lw2 = Tok()
    st32 = A.alloc(F32, 2 * D)
    P.dma("sp", st32[0:96, 0:D], W["w2"], dsm, (), (t_lw2,))
    P.copy("dve", w2b[0:96, :], st32[0:96, 0:D], (t_lw2,), (t_lw2,))
    P.dma("sp", st32[0:96, 0:D], W["a2"], dsm, (t_lw2,), (t_lw2,))
    P.copy("dve", a2b[0:96, :], st32[0:96, 0:D], (t_lw2,), (t_lw2,))
    P.dma("sp", st32.rearrange("p (c f) -> p c f", c=2), W["g2"].rearrange("(c p) f -> p c f", p=128), dsm, (t_lw2,), (t_lw2,))
    P.copy("dve", g2b, st32, (t_lw2,), (t_lw2,))
    t_b2 = [Tok(True) for _ in range(6)]
    big = [[A.alloc(F32, T) for _ in range(3)] for _ in range(2)]
    t_big = [[Tok() for _ in range(3)] for _ in range(2)]
    ds_big = [P.dsem() for _ in range(2)]
    for e in range(NCH):
        esl = slice(e * 128, (e + 1) * 128)
        sb_ = e % 2
        for tt in range(4):
            tsl = slice(tt * 512, (tt + 1) * 512)
            b0 = (tt % 2) * 3
            P.mm(cx.bank(b0), w2b[0:96, esl], lw[0:96, tsl], True, True, (t_lw2, t_l), (t_b2[b0],))
            P.actf(big[sb_][0][:, tsl], cx.bank(b0), AF.Sigmoid, (t_b2[b0], t_rc), (t_big[sb_][0],), bias=col(6, e))
            P.mm(cx.bank(b0 + 1), a2b[0:96, esl], la[0:96, tsl], True, True, (t_lw2, t_l), (t_b2[b0 + 1],))
            P.actf(big[sb_][1][:, tsl], cx.bank(b0 + 1), AF.Sigmoid, (t_b2[b0 + 1], t_rc), (t_big[sb_][1],), bias=col(7, e))
            for c in range(2):
                P.mm(cx.bank(b0 + 2), g2bv[:, c, esl], lgv[:, c, tsl], c == 0, c == 1, (t_lw2, t_l), (t_b2[b0 + 2],))
            P.copy("dve", big[sb_][2][:, tsl], cx.bank(b0 + 2), (t_b2[b0 + 2],), (t_big[sb_][2],))
        for pl in range(3):
            P.dma("pool" if pl != 1 else "sp", cx.rkv[3 + pl][e], big[sb_][pl], ds_big[sb_], (t_big[sb_][pl],), (cx.t_rkv,))
    P.barrier()
    P.free_dsems.extend(ds_big)
    A.release()
    if STOP == 52:
        A.release(); return
    t_yg = Tok()
    A.mark()
    msk = A.alloc(F32, 3 * 512); t_k = Tok()
    P.dma("sp", msk, cx.rw_masks_d, dsm, (), (t_k,))
    ML_s, MU_s, MU_i = msk[:, 0:512], msk[:, 512:1024], msk[:, 1024:1536]
    id4 = A.alloc(BF16, 512)
    for q in range(4):
        P.copy("dve", id4[:, q * 128:(q + 1) * 128], cx.ident, (cx.t_const,), (t_k,))
    idb = id4[:, 0:128]
    bones = A.alloc(F32, 128)
    P.memset("pool", bones, 0.0, (t_k,))
    P.memset("pool", bones[0:64, 0:64], 1.0, (t_k,))
    P.memset("pool", bones[64:128, 64:128], 1.0, (t_k,))
    rmask = A.alloc(BF16, T)
    P.memset("pool", rmask, 1.0, (t_k,))
    P.memset("pool", rmask.rearrange("p (c t) -> p c t", t=RW_L)[:, :, 0:1], 0.0, (t_k,))
    xop = [A.alloc(BF16, RW_NCK * 128) for _ in range(7)]
    t_xop = Tok()
    for x_ in xop:
        P.memset("pool", x_, 0.0, (t_xop,))
    RTx, KTx, BTx, KHx, BHx, ATx, Vx = xop
    gam = A.alloc(F32, RW_NCK); t_gam = Tok()
    bon = A.alloc(F32, T); t_bon = Tok()
    psb = cx.ps_bf

    def xview(xo, h):
        return xo.rearrange("p (c i) -> p c i", i=128)[h * 64:(h + 1) * 64, :, h * 64:(h + 1) * 64]

    def hview(ap, h):
        return ap.rearrange("p (c t) -> p c t", t=RW_L)[h * 64:(h + 1) * 64, :, :]

    def ch(xo, c):
        return xo[:, c * 128:(c + 1) * 128]

    def q4(ap, q):
        return ap[:, q * 128:(q + 1) * 128]

    NG = RW_NCK // 4
    NSLOT = 3
    for e in range(NCH):
        A.mark()
        r_, k_, v_, a_, cum, sg_, tA, tB, tC, tD, tE, tF = [A.alloc(F32, T) for _ in range(12)]
        t_r, t_kk, t_v, t_a, t_cum, t_sg, t_tA, t_tB, t_tC, t_tD, t_tE, t_tF = [Tok() for _ in range(12)]
        t_bk = [Tok(True) for _ in range(8)]
        t_xop2 = [Tok(), Tok()]
        for ji, (dst, tk) in enumerate([(sg_, t_sg), (k_, t_kk), (a_, t_a), (r_, t_r), (v_, t_v)]):
            P.dma("sp", dst, cx.rkv[[3, 1, 4, 0, 2][ji]][e], dsm, (cx.t_rkv,), (tk,))
        cumv = cum.rearrange("p (c t) -> p c t", t=RW_L)
        TS = [slice(tt * 512, (tt + 1) * 512) for tt in range(4)]
        P.op("dve", lambda en, cum=cum, sg_=sg_: en.tensor_tensor_scan(cum, rmask, sg_, 0.0, ALU.mult, ALU.add),
             (t_sg, t_k), (t_cum,))
        P.actf(tA, k_, AF.Copy, (t_kk, t_rc), (t_tA,), scale=col(8, e))
        P.actf(tB, tA, AF.Square, (t_tA,), (t_tB,))
        P.actf(tC, cum, AF.Exp, (t_cum,), (t_tC,), scale=-DEC_C)
        for tt in range(4):
            P.mm(cx.bank(tt), bones, tB[:, TS[tt]], True, True, (t_k, t_tB), (t_bk[tt],))
        P.actf(tE, a_, AF.Identity, (t_a, t_rc), (t_tE,), scale=col(9, e), bias=omka[:, e:e + 1])
        P.tt("dve", k_, k_, tE, ALU.mult, (t_kk, t_tE), (t_kk,))
        for tt in range(4):
            P.ts("dve", tD[:, TS[tt]], cx.bank(tt), 1e-18, None, ALU.max, None, (t_bk[tt],), (t_tD,))
        P.actf(tD, tD, AF.Ln, (t_tD,), (t_tD,))
        P.actf(tD, tD, AF.Exp, (t_tD,), (t_tD,), scale=-0.5)
        P.copy("dve", gam, cumv[:, :, RW_L - 1], (t_cum,), (t_gam,))
        P.actf(gam, gam, AF.Exp, (t_gam,), (t_gam,), scale=-DEC_C)
        for h in range(2):
            P.tt("dve" if h == 0 else "pool", xview(RTx, h), hview(r_, h), hview(tC, h), ALU.mult, (t_r, t_tC), (t_xop2[h],))
        P.tt("dve", tE, cum, sg_, ALU.subtract, (t_cum, t_sg, t_tE), (t_tE,))
        P.actf(tE, tE, AF.Exp, (t_tE,), (t_tE,), scale=-DEC_C)
        P.stt("dve", tF, r_, col(10, e), k_, ALU.mult, ALU.mult, (t_r, t_kk, t_rc), (t_tF,))
        for tt in range(4):
            P.mm(cx.bank(4 + tt), bones, tF[:, TS[tt]], True, True, (t_k, t_tF), (t_bk[4 + tt],))
        P.tt("dve", tA, tA, tD, ALU.mult, (t_tA, t_tD), (t_tA,))
        P.tt("dve", tB, tA, a_, ALU.mult, (t_tA, t_a, t_tB), (t_tB,))
        P.actf(tC, cum, AF.Exp, (t_cum,), (t_tC,), scale=DEC_C)
        for tt in range(4):
            P.tt("dve", bon[:, TS[tt]], cx.bank(4 + tt), v_[:, TS[tt]], ALU.mult, (t_bk[4 + tt], t_v), (t_bon,))
        for h in range(2):
            P.stt("dve", xview(ATx, h), hview(tA, h), -1.0, hview(tE, h), ALU.mult, ALU.mult, (t_tA, t_tE), (t_xop,))
        P.tt("dve", r_.rearrange("p (c t) -> p c t", t=RW_L), cumv[:, :, RW_L - 1:RW_L].to_broadcast([128, RW_NCK, RW_L]),
             cumv, ALU.subtract, (t_cum, t_r), (t_r,))
        P.actf(r_, r_, AF.Exp, (t_r,), (t_r,), scale=-DEC_C)
        for h in range(2):
            P.copy("act", xview(Vx, h), hview(v_, h), (t_v,), (t_xop,))
        for h in range(2):
            eng = "dve" if h == 0 else "pool"
            P.tt(eng, xview(KTx, h), hview(k_, h), hview(tC, h), ALU.mult, (t_kk, t_tC), (t_xop2[h],))
            P.tt(eng, xview(BTx, h), hview(tB, h), hview(tC, h), ALU.mult, (t_tB, t_tC), (t_xop2[h],))
        for h in range(2):
            P.tt("dve", xview(KHx, h), hview(k_, h), hview(r_, h), ALU.mult, (t_kk, t_r), (t_xop,))
            P.tt("dve", xview(BHx, h), hview(tB, h), hview(r_, h), ALU.mult, (t_tB, t_r), (t_xop,))
        P.barrier()
        A.release()
        A.mark()
        GT = A.alloc(BF16, RW_NCK * 128); Hh = A.alloc(F32, RW_NCK * 128); Rb = A.alloc(BF16, RW_NCK * 128)
        y0 = A.alloc(F32, T); Sall = A.alloc(BF16, RW_NCK * 128); yT = A.alloc(F32, T)
        t_GT = [Tok() for _ in range(NG)]; t_H = [Tok() for _ in range(NG)]; t_Rb = [Tok() for _ in range(NG)]
        t_y0 = [Tok() for _ in range(NG)]
        gT = A.alloc(F32, T); t_g = Tok()
        P.dma("sp", gT, cx.rkv[5][e], dsm, (cx.t_rkv,), (t_g,))
        t_bk = [Tok(True) for _ in range(8)]
        slots = []
        for sl in range(NSLOT):
            d_ = {}
            d_["TMb"] = [A.alloc(BF16, 512) for _ in range(3)]
            d_["WUin"] = A.alloc(BF16, 4 * 256); d_["WU"] = A.alloc(BF16, 4 * 256)
            d_["Nb"] = [A.alloc(BF16, 512) for _ in range(2)]; d_["Qb"] = [A.alloc(BF16, 512) for _ in range(2)]
            d_["Pb"] = [A.alloc(BF16, 512) for _ in range(2)]
            d_["M"] = [A.alloc(BF16, 512) for _ in range(3)]
            d_["tok"] = {k: Tok() for k in ("TM", "WUin", "WU", "N", "Q", "P", "M")}
            d_["banks"] = (2 * sl, 2 * sl + 1)
            slots.append(d_)
        ev_i = [0]

        def group_steps(g, sd):
            cs = [g * 4 + q for q in range(4)]
            TMb, WUin, WU, Nb, Qb, Pb = sd["TMb"], sd["WUin"], sd["WU"], sd["Nb"], sd["Qb"], sd["Pb"]
            Mak, Mrb, Mrk = sd["M"]
            tk = sd["tok"]
            WUinv = WUin.rearrange("p (q f) -> p q f", q=4)
            WUv = WU.rearrange("p (q f) -> p q f", q=4)
            bi = [0]

            def nb():
                bi[0] ^= 1
                return sd["banks"][bi[0]]

            def eng2():
                ev_i[0] += 1
                return "dve" if ev_i[0] % 2 == 0 else "act"
            for half, ops_ in enumerate([(BHx, KHx), (Vx, ATx)]):
                b = nb()
                for oi, xo in enumerate(ops_):
                    for q in range(4):
                        P.tr(psb[:, b * 1024 + (oi * 4 + q) * 128: b * 1024 + (oi * 4 + q + 1) * 128], ch(xo, cs[q]), idb,
                             (t_xop, t_k), (t_bk[b],), inc=(oi == 1 and q == 3))
                if half == 0:
                    P.copy("act", TMb[0], psb[:, b * 1024: b * 1024 + 512], (t_bk[b],), (tk["TM"],))
                    P.copy("dve", TMb[1], psb[:, b * 1024 + 512: b * 1024 + 1024], (t_bk[b],), (tk["TM"],))
                else:
                    P.copy("act", TMb[2], psb[:, b * 1024: b * 1024 + 512], (t_bk[b],), (tk["TM"],))
                    P.copy("dve", WUinv[:, :, 0:128], psb[:, b * 1024 + 512: b * 1024 + 1024].rearrange("p (q f) -> p q f", q=4),
                           (t_bk[b],), (tk["WUin"],))
                yield
            b = nb()
            for q in range(4):
                P.mm(q4(cx.bank(b), q), ch(ATx, cs[q]), ch(BTx, cs[q]), True, True, (t_xop,), (t_bk[b],), inc=(q == 3))
            P.tt("dve", Nb[0], cx.bank(b), ML_s, ALU.mult, (t_bk[b], t_k), (tk["N"],))
            yield
            b = nb()
            for q in range(4):
                P.mm(q4(cx.bank(b), q), ch(BTx, cs[q]), ch(ATx, cs[q]), True, True, (t_xop,), (t_bk[b],), inc=(q == 3))
            P.tt("dve", Qb[0], cx.bank(b), MU_s, ALU.mult, (t_bk[b], t_k), (tk["Q"],))
            P.tt("pool", Pb[0], Qb[0], id4, ALU.add, (tk["Q"], t_k), (tk["P"],))
            yield
            for (lx, rx, mk, dst) in ((KTx, ATx, MU_s, Mak), (BTx, RTx, MU_i, Mrb), (KTx, RTx, MU_i, Mrk)):
                b = nb()
                for q in range(4):
                    P.mm(q4(cx.bank(b), q), ch(lx, cs[q]), ch(rx, cs[q]), True, True, (t_xop,), (t_bk[b],), inc=(q == 3))
                P.tt("dve", dst, cx.bank(b), mk, ALU.mult, (t_bk[b], t_k), (tk["M"],))
                yield
            pi = 0
            for j in range(1, 6):
                i0, i1 = (j - 1) % 2, j % 2
                b = nb()
                for q in range(4):
                    P.mm(q4(cx.bank(b), q), q4(Qb[i0], q), q4(Nb[i0], q), True, True, (tk["N"], tk["Q"]), (t_bk[b],), inc=(q == 3))
                if j < 5:
                    b2 = nb()
                    for q in range(4):
                        P.mm(q4(cx.bank(b2), q), q4(Nb[i0], q), q4(Qb[i0], q), True, True, (tk["N"], tk["Q"]), (t_bk[b2],), inc=(q == 3))
                P.copy("act", Nb[i1], cx.bank(b), (t_bk[b],), (tk["N"],))
                if j < 5:
                    P.copy(eng2(), Qb[i1], cx.bank(b2), (t_bk[b2],), (tk["Q"],))
                yield
                b = nb()
                for q in range(4):
                    P.mm(q4(cx.bank(b), q), idb, q4(Pb[pi], q), True, False, (t_k, tk["P"]), (t_bk[b],), inc=False)
                    P.mm(q4(cx.bank(b), q), q4(Nb[i1], q), q4(Pb[pi], q), False, True, (tk["N"], tk["P"]), (t_bk[b],), inc=(q == 3))
                P.copy(eng2(), Pb[1 - pi], cx.bank(b), (t_bk[b],), (tk["P"],))
                pi = 1 - pi
                yield
            TiT = Pb[pi]
            b = nb()
            for q in range(4):
                P.mm(q4(cx.bank(b), q), q4(Mak, q), q4(TMb[2], q), True, True, (tk["M"], tk["TM"]), (t_bk[b],), inc=(q == 3))
            P.copy("act", WUinv[:, :, 128:256], cx.bank(b).rearrange("p (q f) -> p q f", q=4), (t_bk[b],), (tk["WUin"],))
            yield
            for hf in range(2):
                b = nb()
                for qq in range(2):
                    q = hf * 2 + qq
                    P.mm(cx.bank(b)[:, qq * 256:(qq + 1) * 256], q4(TiT, q), WUinv[:, q, :], True, True, (tk["P"], tk["WUin"]),
                         (t_bk[b],), inc=(qq == 1))
                P.copy("act" if hf == 0 else "dve", WU[:, hf * 512:(hf + 1) * 512], cx.bank(b), (t_bk[b],), (tk["WU"],))
            yield
            b = nb()
            for q in range(4):
                P.mm(q4(cx.bank(b), q), WUv[:, q, 0:128], q4(TMb[0], q), True, True, (tk["WU"], tk["TM"]), (t_bk[b],), inc=(q == 3))
            P.copy("act", GT[:, g * 512:(g + 1) * 512], cx.bank(b), (t_bk[b],), (t_GT[g],))
            b = nb()
            for q in range(4):
                P.mm(q4(cx.bank(b), q), q4(TMb[0], q), WUv[:, q, 128:256], True, False, (tk["WU"], tk["TM"]), (t_bk[b],), inc=False)
                P.mm(q4(cx.bank(b), q), q4(TMb[1], q), q4(TMb[2], q), False, True, (tk["TM"],), (t_bk[b],), inc=(q == 3))
            P.copy("dve", Hh[:, g * 512:(g + 1) * 512], cx.bank(b), (t_bk[b],), (t_H[g],))
            yield
            b = nb()
            for q in range(4):
                P.mm(q4(cx.bank(b), q), idb, ch(RTx, cs[q]), True, False, (t_k, t_xop), (t_bk[b],), inc=False)
                P.mm(q4(cx.bank(b), q), WUv[:, q, 0:128], q4(Mrb, q), False, True, (tk["WU"], tk["M"]), (t_bk[b],), inc=(q == 3))
            P.copy("act", Rb[:, g * 512:(g + 1) * 512], cx.bank(b), (t_bk[b],), (t_Rb[g],))
            b = nb()
            for q in range(4):
                P.mm(q4(cx.bank(b), q), WUv[:, q, 128:256], q4(Mrb, q), True, False, (tk["WU"], tk["M"]), (t_bk[b],), inc=False)
                P.mm(q4(cx.bank(b), q), q4(TMb[2], q), q4(Mrk, q), False, True, (tk["TM"], tk["M"]), (t_bk[b],), inc=(q == 3))
            for h in range(2):
                bv = cx.bank(b).rearrange("p (q i) -> p q i", q=4)[h * 64:(h + 1) * 64, :, h * 64:(h + 1) * 64]
                P.copy("act", hview(y0, h)[:, g * 4:(g + 1) * 4, :], bv, (t_bk[b],), (t_y0[g],))
            yield

        pending = list(range(NG))
        active = []
        free_slots = list(range(NSLOT))
        while pending or active:
            while pending and free_slots:
                sl = free_slots.pop(0)
                active.append((group_steps(pending.pop(0), slots[sl]), sl))
            for item in list(active):
                gen, sl = item
                try:
                    next(gen)
                except StopIteration:
                    active.remove(item)
                    free_slots.append(sl)
        Sf = [A.alloc(F32, 128) for _ in range(2)]; tAq = [A.alloc(F32, 128) for _ in range(2)]
        t_Sf, t_tAq = [Tok() for _ in range(2)], [Tok() for _ in range(2)]
        t_Sg = [Tok() for _ in range(NG)]
        t_yq = [Tok() for _ in range(4)]

        def emit_y(g):
            b = 4 + (g % 2)
            for q in range(4):
                c = g * 4 + q
                P.mm(q4(cx.bank(b), q), ch(Sall, c), ch(Rb, c), True, True, (t_Sg[g], t_Rb[g]), (t_bk[b],), inc=(q == 3))
            for h in range(2):
                bv = cx.bank(b).rearrange("p (q i) -> p q i", q=4)[h * 64:(h + 1) * 64, :, h * 64:(h + 1) * 64]
                P.tt("dve", hview(yT, h)[:, g * 4:(g + 1) * 4, :], bv, hview(y0, h)[:, g * 4:(g + 1) * 4, :], ALU.add,
                     (t_bk[b], t_y0[g]), (t_yq[g // 2],))

        P.memset("pool", Sall[:, 0:128], 0.0, (t_Sg[0],))
        P.copy("dve", tAq[0], ch(Hh, 0), (t_H[0],), (t_tAq[0],))
        for c in range(RW_NCK - 1):
            si = c % 2
            b = 6 + (c % 2)
            P.mm(cx.bank(b)[:, 0:128], ch(GT, c), ch(Sall, c), True, True, (t_GT[c // 4], t_Sg[c // 4]), (t_bk[b],))
            P.tt("dve", ch(Sall, c + 1), cx.bank(b)[:, 0:128], tAq[si], ALU.add, (t_bk[b], t_tAq[si]), (t_Sg[(c + 1) // 4],))
            if c + 1 < RW_NCK - 1:
                P.tt("dve", Sf[si], cx.bank(b)[:, 0:128], tAq[si], ALU.add, (t_bk[b], t_tAq[si]), (t_Sf[si],))
                P.stt("dve", tAq[1 - si], Sf[si], gam[:, c + 1:c + 2], ch(Hh, c + 1), ALU.mult, ALU.add,
                      (t_Sf[si], t_gam, t_H[(c + 1) // 4]), (t_tAq[1 - si],))
            if (c + 1) % 4 == 3:
                emit_y((c + 1) // 4)
        if cx.dbg_y is not None:
            P.dma("sp", cx.dbg_y[e], yT, dsm, tuple(t_yq), (cx.t_out,))
        mean = [Hh[:, tt * 512:(tt + 1) * 512] for tt in range(4)]; t_mean = [Tok() for _ in range(4)]
        ygb = A.alloc(BF16, T); t_ygb = Tok()
        t_y0p = [Tok() for _ in range(4)]
        TS = [slice(tt * 512, (tt + 1) * 512) for tt in range(4)]
        for tt in range(4):
            P.mm(cx.bank(tt), bones, yT[:, TS[tt]], True, True, (t_k, t_yq[tt]), (t_bk[tt],))
        for tt in range(4):
            P.stt("dve", yT[:, TS[tt]], cx.bank(tt), -1.0 / 64.0, yT[:, TS[tt]], ALU.mult, ALU.add, (t_bk[tt], t_yq[tt]), (t_yq[tt],))
        for tt in range(4):
            P.actf(y0[:, TS[tt]], yT[:, TS[tt]], AF.Square, (t_yq[tt], t_y0[2 * tt], t_y0[2 * tt + 1]), (t_y0p[tt],))
        for tt in range(4):
            P.mm(cx.bank(4 + tt), bones, y0[:, TS[tt]], True, True, (t_k, t_y0p[tt]), (t_bk[4 + tt],))
        for tt in range(4):
            P.ts("dve", mean[tt], cx.bank(4 + tt), 1.0 / 64.0, 64e-5, ALU.mult, ALU.add, (t_bk[4 + tt],), (t_mean[tt],) + tuple(t_H))
        for tt in range(4):
            P.actf(mean[tt], mean[tt], AF.Ln, (t_mean[tt],), (t_mean[tt],))
            P.actf(mean[tt], mean[tt], AF.Exp, (t_mean[tt],), (t_mean[tt],), scale=-0.5)
        for tt in range(4):
            P.tt("dve", yT[:, TS[tt]], yT[:, TS[tt]], mean[tt], ALU.mult, (t_yq[tt], t_mean[tt]), (t_yq[tt],))
            P.ts("dve", yT[:, TS[tt]], yT[:, TS[tt]], col(11, e), col(12, e), ALU.mult, ALU.add, (t_yq[tt], t_rc), (t_yq[tt],))
            P.tt("dve", yT[:, TS[tt]], yT[:, TS[tt]], bon[:, TS[tt]], ALU.add, (t_yq[tt], t_bon), (t_yq[tt],))
            P.tt("dve", ygb[:, TS[tt]], yT[:, TS[tt]], gT[:, TS[tt]], ALU.mult, (t_yq[tt], t_g), (t_ygb,))
        P.dma("sp", cx.ygs[e], ygb, dsm, (t_ygb,), (t_yg,))
        P.barrier()
        A.release()
    A.release()
    xold = [A.alloc(F32, 512) for _ in range(2)]; t_xold = [Tok() for _ in range(2)]; ds_xold = [P.dsem() for _ in range(2)]
    xnew = [A.alloc(F32, 512) for _ in range(2)]; t_xnew = [Tok() for _ in range(2)]; ds_xnew = [P.dsem() for _ in range(2)]
    kk_ = [0]

    def ev_o(oi, tt, bank, tb):
        bi = kk_[0] % 2; kk_[0] += 1
        tsl = slice(tt * 512, (tt + 1) * 512)
        P.dma("act", xold[bi], src_v[:, oi, tsl], ds_xold[bi], cx.xs_tok(oi, tt), (t_xold[bi],))
        P.tt("dve", xnew[bi], bank, xold[bi], ALU.add, (tb, t_xold[bi]), (t_xnew[bi],))
        P.dma("pool", dst_v[:, oi, tsl], xnew[bi], ds_xnew[bi], (t_xnew[bi],), cx.xs_tok(oi, tt))
    ygT = A.alloc(BF16, NCH * T); ygv = ygT.rearrange("p (c t) -> p c t", c=NCH)
    for c4 in range(4):
        P.dma("sp", ygv[:, c4 * 4:(c4 + 1) * 4, :], cx.ygs.rearrange("c p t -> p c t")[:, c4 * 4:(c4 + 1) * 4, :], dsm, (t_yg,), (t_yg,))
    linear_fm(cx, ygv, t_yg, W["w_o"], NCH, [(e * 128, 128) for e in range(NCH)], ev_o, bank0=2)
    P.barrier()
    P.free_dsems.extend(ds_xold + ds_xnew + [dsm])
    A.release()


def pack_cols(vecs):
    return np.ascontiguousarray(
        np.concatenate([np.asarray(v, np.float32).reshape(NCH, 128).T for v in vecs], axis=1))


def build(phases):
    nc = bass.Bass("TRN2", target_bir_lowering=False)
    names = [p[0] for p in phases]
    dram = {}

    def din(name, shape, dt=F32):
        dram[name] = nc.dram_tensor(name, list(shape), dt, kind="ExternalInput").ap()
        return dram[name]

    ins = []
    if "tin" in names:
        x_tm = din("x", [T, D]); ins.append("x")
    else:
        xs_in = din("xs_in", [NCH, 128, T]); ins.append("xs_in")
    if "tout" in names:
        out_ap = nc.dram_tensor("out", [T, D], F32, kind="ExternalOutput").ap()
        out_name = "out"
    else:
        out_ap = nc.dram_tensor("xs_out", [NCH, 128, T], F32, kind="ExternalOutput").ap()
        out_name = "xs_out"
    ident_d = din("ident", [128, 128]); ins.append("ident")
    ncols = 16 * 8
    cols_d = din("cols", [128, ncols]); ins.append("cols")
    for p in phases:
        if p[0] == "ffn":
            l, s = p[1], p[2]
            din(f"w13_{l}{s}", [D, 2 * FF]); ins.append(f"w13_{l}{s}")
            din(f"w2_{l}{s}", [FF, D]); ins.append(f"w2_{l}{s}")
        if p[0] == "rwkv":
            din("rw_w_rkv", [3, D, D]); din("rw_w1", [D, 96]); din("rw_w2", [96, D]); din("rw_a1", [D, 96])
            din("rw_a2", [96, D]); din("rw_g1", [D, 256]); din("rw_g2", [256, D]); din("rw_w_o", [D, D])
            din("rw_cols", [128, 13 * 16]); din("rw_masks", [128, 1536])
            ins.extend(["rw_w_rkv", "rw_w1", "rw_w2", "rw_a1", "rw_a2", "rw_g1", "rw_g2", "rw_w_o", "rw_cols", "rw_masks"])
        if p[0] == "mla":
            din("mla_wd", [D, 1152]); din("mla_wuq", [512, 16, 256]); din("mla_wukv", [512, 16, 256])
            din("mla_wo", [16, 128, D]); din("mla_cols", [128, 16]); din("mla_pos", [64, T], I32)
            ins.extend(["mla_wd", "mla_wuq", "mla_wukv", "mla_wo", "mla_cols", "mla_pos"])
    xs_a = nc.dram_tensor("xs_a", [NCH, 128, T], F32, kind="Internal").ap()
    has_rw = "rwkv" in names
    ots_d = nc.dram_tensor("ots", [MLA_H, 128, T], BF16, kind="Internal").ap() if "mla" in names else None
    if has_rw:
        xmix_d = nc.dram_tensor("xmix", [6, NCH, 128, T], BF16, kind="Internal").ap()
        rkv_d = nc.dram_tensor("rkv", [6, NCH, 128, T], F32, kind="Internal").ap()
        ygs_d = nc.dram_tensor("ygs", [NCH, 128, T], BF16, kind="Internal").ap()
        dbg_d = nc.dram_tensor("dbg_y", [NCH, 128, T], F32, kind="ExternalOutput").ap() if DBG else None

    from contextlib import ExitStack
    with ExitStack() as es:
        sb = es.enter_context(nc.sbuf_tensor("sb", [128, SB_BYTES // 4], F32))
        ps = es.enter_context(nc.psum_tensor("ps", [128, 4096], F32))
        esems = {e: es.enter_context(nc.semaphore("s_" + e)) for e in Prog.CE}
        dsems = [es.enter_context(nc.semaphore(f"d{i}")) for i in range(40)]
        block = es.enter_context(nc.Block())
        P = Prog(nc, esems, dsems)
        A = Arena(sb, SB_BYTES)
        cx = Ctx()
        cx.P, cx.A, cx.nc = P, A, nc
        cx.bank = lambda b: ps[:, b * 512:(b + 1) * 512]
        cx.t_out, cx.t_const = Tok(), Tok()
        xs_toks = [[Tok() for _ in range(T // 512)] for _ in range(NCH)]

        def xs_tok(c=None, tt=None):
            cs = range(NCH) if c is None else [c]
            ts_ = range(T // 512) if tt is None else [tt]
            return tuple(xs_toks[ci][ti] for ci in cs for ti in ts_)
        cx.xs_tok = xs_tok
        cx.ps_bf = ps.bitcast(BF16)
        cx.ots = ots_d
        if has_rw:
            cx.xmix, cx.rkv, cx.ygs, cx.dbg_y = xmix_d, rkv_d, ygs_d, dbg_d
            cx.t_xmix, cx.t_rkv = Tok(), Tok()
            cx.rw_masks_d = dram["rw_masks"]
        cx.ident = A.alloc(F32, 128)
        cx.cols = A.alloc(F32, ncols)
        cx.ones_bf = A.alloc(BF16, 128)
        dsc = P.dsem()
        P.dma("sp", cx.ident, ident_d, dsc, (), (cx.t_const,))
        P.dma("sp", cx.cols, cols_d, dsc, (), (cx.t_const,))
        P.memset("pool", cx.ones_bf, 1.0 / D, (cx.t_const,))
        cx.ones1_bf = A.alloc(BF16, 128)
        P.memset("pool", cx.ones1_bf, 1.0, (cx.t_const,))
        P.barrier()
        cur = None if "tin" in names else xs_in
        n_ph = len(phases)
        for i, p in enumerate(phases):
            last = (i == n_ph - 1)
            if p[0] == "tin":
                dst = out_ap if last else xs_a
                phase_tin(cx, x_tm, dst)
                cur = dst
            elif p[0] == "tout":
                phase_tout(cx, cur, out_ap)
            elif p[0] == "ffn":
                l, s = p[1], p[2]
                nxt_is_out = last
                dst = out_ap if nxt_is_out else xs_a
                if cur is not xs_a and dst is xs_a:
                    pass
                k = (l * 2 + s)
                phase_ffn(cx, cur, dst, dram[f"w13_{l}{s}"], dram[f"w2_{l}{s}"], cx.cols[:, k * 16:(k + 1) * 16])
                cur = dst
            elif p[0] == "rwkv":
                dst = out_ap if last else xs_a
                Wd = {k: dram["rw_" + k] for k in ("w_rkv", "w1", "w2", "a1", "a2", "g1", "g2", "w_o")}
                phase_rwkv(cx, cur, dst, Wd, cx.cols[:, 4 * 16:5 * 16], dram["rw_cols"])
                cur = dst
            elif p[0] == "mla":
                dst = out_ap if last else xs_a
                phase_mla(cx, cur, dst, dram["mla_wd"], dram["mla_wuq"], dram["mla_wukv"], dram["mla_wo"],
                          cx.cols[:, 5 * 16:6 * 16], dram["mla_cols"], dram["mla_pos"])
                cur = dst
            else:
                raise ValueError(p)
        P.barrier()
        P.emit(block)
    return nc, ins, out_name, P


def host_consts(inputs):
    ident = np.eye(128, dtype=np.float32)
    fn = inputs["ffn_norm"]
    cols = pack_cols([fn[0, 0], fn[0, 1], fn[1, 0], fn[1, 1],
                      inputs["mix_norm"][0], inputs["mix_norm"][1], np.zeros(D), np.zeros(D)])
    return ident, cols


ROPE_PERM = np.concatenate([np.arange(32, 64), np.arange(0, 32)])


def mla_host(inputs, b):
    wd = inputs["mla_w_down"][0]
    wd_ext = np.ascontiguousarray(np.concatenate([wd, wd[:, 1024 + ROPE_PERM]], axis=1))
    wuq = inputs["mla_w_uq"][0]
    wuq_ext = np.ascontiguousarray(np.concatenate([wuq, wuq[:, :, 128 + ROPE_PERM]], axis=2))
    qn, kn = inputs["mla_q_norm"][0], inputs["mla_k_norm"][0]
    mc = np.zeros((128, 16), np.float32)
    mc[:, 0:4] = inputs["mla_q_a_norm"][0].reshape(4, 128).T
    mc[:, 4:8] = inputs["mla_kv_a_norm"][0].reshape(4, 128).T
    mc[:, 8] = qn[0:128]
    mc[:, 9] = kn[0:128]
    mc[0:64, 10] = qn[128:192]
    mc[0:64, 11] = qn[128 + ROPE_PERM]
    mc[0:64, 12] = kn[128:192]
    mc[0:64, 13] = kn[128 + ROPE_PERM]
    inv_freq = (np.float32(10000.0) ** (-np.arange(0, 64, 2, dtype=np.float32) / np.float32(64))).astype(np.float32)
    mc[0:64, 14] = np.concatenate([inv_freq, inv_freq])
    mc[0:32, 15] = -1.0
    mc[32:64, 15] = 1.0
    pos = np.ascontiguousarray(np.broadcast_to(inputs["positions"][b][None, :], (64, T))).astype(np.int32)
    return {"mla_wd": wd_ext, "mla_wuq": wuq_ext, "mla_wukv": np.ascontiguousarray(inputs["mla_w_ukv"][0]),
            "mla_wo": np.ascontiguousarray(inputs["mla_w_o"][0]), "mla_cols": mc, "mla_pos": pos}


def rwkv_host(inputs):
    g = lambda k: inputs["rwkv_" + k][0]
    vecs = [g("mu")[j] for j in range(6)] + [g("w0"), g("a0"), g("k_k"), g("k_a"), g("r_k").reshape(-1), g("ln_w"), g("ln_b")]
    idx = np.arange(128)
    same = (idx[:, None] // 64) == (idx[None, :] // 64)
    ti, tj = idx[:, None] % 64, idx[None, :] % 64
    ML_s = (same & (ti > tj)).astype(np.float32)
    MU_s = (same & (ti < tj)).astype(np.float32)
    MU_i = (same & (ti <= tj)).astype(np.float32)
    masks = np.ascontiguousarray(np.concatenate([np.tile(m, (1, 4)) for m in (ML_s, MU_s, MU_i)], axis=1))
    return {"rw_w_rkv": np.ascontiguousarray(g("w_rkv")), "rw_w1": g("w1"), "rw_w2": g("w2"), "rw_a1": g("a1"), "rw_a2": g("a2"),
            "rw_g1": g("g1"), "rw_g2": g("g2"), "rw_w_o": g("w_o"), "rw_cols": pack_cols(vecs), "rw_masks": masks}


PHASES = [("tin",), ("ffn", 0, 0), ("rwkv",), ("ffn", 0, 1), ("ffn", 1, 0), ("mla",), ("ffn", 1, 1), ("tout",)]


def make_feeds(inputs, b, shared=None):
    if shared is None:
        shared = {}
        ident, cols = host_consts(inputs)
        shared["ident"] = ident
        shared["cols"] = cols
        for l in range(2):
            for s_ in range(2):
                shared[f"w13_{l}{s_}"] = np.ascontiguousarray(np.asarray(inputs["ffn_w13"][l, s_], np.float32))
                shared[f"w2_{l}{s_}"] = np.ascontiguousarray(np.asarray(inputs["ffn_w2"][l, s_], np.float32))
        shared.update(rwkv_host(inputs))
        m = mla_host(inputs, 0)
        m.pop("mla_pos")
        shared.update(m)
    feeds = dict(shared)
    feeds["x"] = np.ascontiguousarray(np.asarray(inputs["x"][b], np.float32))
    feeds["mla_pos"] = np.ascontiguousarray(
        np.broadcast_to(np.asarray(inputs["positions"][b], np.int32)[None, :], (64, T)))
    return feeds, shared


def kernel(**inputs):
    inputs = {k: np.asarray(v) for k, v in inputs.items()}
    nb = inputs["x"].shape[0]
    nc, ins, out_name, _ = build(PHASES)
    in_maps = []
    shared = None
    for b in range(nb):
        feeds, shared = make_feeds(inputs, b, shared)
        in_maps.append({k: feeds[k] for k in ins})
    res = run_bass_kernel_spmd(nc, in_maps, core_ids=list(range(nb)))
    out = np.stack([np.asarray(res.results[b][out_name], np.float32) for b in range(nb)], axis=0)
    return out
```

```python
import numpy as np
import concourse.bass as bass
import concourse.mybir as mybir
from concourse.bass_utils import run_bass_kernel_spmd

F32 = mybir.dt.float32
BF16 = mybir.dt.bfloat16
I32 = mybir.dt.int32
AF = mybir.ActivationFunctionType
ALU = mybir.AluOpType
AX = mybir.AxisListType

T = 2048
D = 2048
FF = 5504
NCH = 16
NFC = 43
RMS_EPS = 1e-6
SB_BYTES = 207872


class Tok:
    __slots__ = ("w", "r", "excl")

    def __init__(self, excl=False):
        self.w = None
        self.r = {}
        self.excl = excl


class DSem:
    def __init__(self, h, idx):
        self.h = h
        self.idx = idx
        self.count = 0


class Prog:
    CE = ("pe", "act", "dve", "pool")

    def __init__(self, nc, esems, dsems):
        self.nc = nc
        self.code = {e: [] for e in ("pe", "act", "dve", "pool", "sp")}
        self.esem = esems
        self.ecnt = {e: 0 for e in self.CE}
        self.seen = {e: {} for e in self.code}
        self.free_dsems = [DSem(h, i) for i, h in enumerate(dsems)]
        self.all_dsems = list(self.free_dsems)
        self.ninstr = 0

    def dsem(self):
        return self.free_dsems.pop()

    def _need(self, eng, waits, ev):
        if ev is None:
            return
        kind, s, v = ev
        if kind == "d":
            v = s.count
            key = ("d", s.idx)
        else:
            if s == eng and eng == "pe":
                return
            key = ("e", s)
        if self.seen[eng].get(key, 0) >= v:
            return
        if waits.get(key, (None, 0))[1] < v:
            waits[key] = (s, v)

    def op(self, eng, fn, reads=(), writes=(), inc=True, dsem=None):
        if any(t.excl for t in reads):
            writes = tuple(writes) + tuple(t for t in reads if t.excl)
            reads = tuple(t for t in reads if not t.excl)
        waits = {}
        for t in reads:
            self._need(eng, waits, t.w)
        for t in writes:
            if t.w is not None and not (t.w[0] == "e" and t.w[1] == eng):
                self._need(eng, waits, t.w)
            for ev in t.r.values():
                if not (ev[0] == "e" and ev[1] == eng):
                    self._need(eng, waits, ev)
        wl = []
        for key, (s, v) in waits.items():
            self.seen[eng][key] = v
            wl.append((s.h if key[0] == "d" else self.esem[s], v))
        if dsem is not None:
            dsem.count += 16
            ev = ("d", dsem, dsem.count)
            incspec = (dsem.h, 16)
            rkey = ("d", dsem.idx)
        else:
            if inc:
                self.ecnt[eng] += 1
                ev = ("e", eng, self.ecnt[eng])
                incspec = (self.esem[eng], 1)
            else:
                ev = ("e", eng, self.ecnt[eng] + 1)
                incspec = None
            rkey = ("e", eng)
        for t in reads:
            t.r[rkey] = ev
        for t in writes:
            t.w = ev
            t.r = {}
        self.code[eng].append((wl, fn, incspec))
        self.ninstr += 1

    def barrier(self):
        evs = [("e", e, self.ecnt[e]) for e in self.CE if self.ecnt[e] > 0]
        evs += [("d", d, d.count) for d in self.all_dsems if d.count > 0]
        for eng in self.code:
            waits = {}
            for ev in evs:
                if ev[0] == "e" and ev[1] == eng and eng == "pe":
                    continue
                self._need(eng, waits, ev)
            wl = []
            for key, (s, v) in waits.items():
                self.seen[eng][key] = v
                wl.append((s.h if key[0] == "d" else self.esem[s], v))
            if wl:
                self.code[eng].append((wl, None, None))

    def emit(self, block):
        def mk(name):
            def body(e):
                for wl, fn, incspec in self.code[name]:
                    for h, v in wl:
                        e.wait_ge(h, v)
                    if fn is None:
                        continue
                    ins = fn(e)
                    if incspec is not None:
                        ins.then_inc(incspec[0], incspec[1])
            return body

        block.tensor(mk("pe"))
        block.scalar(mk("act"))
        block.vector(mk("dve"))
        block.gpsimd(mk("pool"))
        block.sync(mk("sp"))

    def mm(self, out, lhsT, rhs, start, stop, reads, writes, inc=None):
        self.op("pe", lambda e: e.matmul(out, lhsT, rhs, start=start, stop=stop),
                reads, writes, inc=(stop if inc is None else inc))

    def tr(self, out, in_, ident, reads, writes, inc=True):
        self.op("pe", lambda e: e.transpose(out, in_, ident), reads, writes, inc=inc)

    def dma(self, q, out, in_, dsem, reads, writes):
        self.op(q, lambda e: e.dma_start(out=out, in_=in_), reads, writes, dsem=dsem)

    def actf(self, out, in_, func, reads, writes, bias=None, scale=None, eng="act"):
        kw = {}
        if bias is not None:
            kw["bias"] = bias
        if scale is not None:
            kw["scale"] = scale
        self.op("act", lambda e: e.activation(out, in_, func, **kw), reads, writes)

    def copy(self, eng, out, in_, reads, writes):
        if eng == "act":
            self.op("act", lambda e: e.copy(out, in_), reads, writes)
        else:
            self.op(eng, lambda e: e.tensor_copy(out, in_), reads, writes)

    def tt(self, eng, out, in0, in1, op, reads, writes):
        self.op(eng, lambda e: e.tensor_tensor(out, in0, in1, op), reads, writes)

    def ts(self, eng, out, in0, s1, s2, op0, op1, reads, writes):
        if s2 is None:
            self.op(eng, lambda e: e.tensor_scalar(out, in0, s1, None, op0), reads, writes)
        else:
            self.op(eng, lambda e: e.tensor_scalar(out, in0, s1, s2, op0, op1), reads, writes)

    def stt(self, eng, out, in0, scalar, in1, op0, op1, reads, writes):
        self.op(eng, lambda e: e.scalar_tensor_tensor(out, in0, scalar, in1, op0, op1), reads, writes)

    def memset(self, eng, ap, val, writes):
        self.op(eng, lambda e: e.memset(ap, val), (), writes)


class Arena:
    def __init__(self, t32, nbytes):
        self.v = {F32: t32, BF16: t32.bitcast(BF16), I32: t32.bitcast(I32)}
        self.cap = nbytes
        self.top = 0
        self.marks = []

    def alloc(self, dtype, n, parts=128, p0=0):
        sz = 2 if dtype == BF16 else 4
        off = (self.top + 63) // 64 * 64
        self.top = off + n * sz
        assert self.top <= self.cap, f"SBUF arena overflow {self.top} > {self.cap}"
        return self.v[dtype][p0:p0 + parts, off // sz: off // sz + n]

    def mark(self):
        self.marks.append(self.top)

    def release(self):
        self.top = self.marks.pop()


class Ctx:
    pass


def phase_tin(cx, x_tm, xs_dst):
    P, A = cx.P, cx.A
    A.mark()
    xin = [A.alloc(F32, D) for _ in range(2)]
    xo = [A.alloc(F32, NCH * 128) for _ in range(2)]
    t_in = [Tok() for _ in range(2)]
    t_o = [Tok() for _ in range(2)]
    ds_in = [P.dsem() for _ in range(2)]
    ds_o = [P.dsem() for _ in range(2)]
    t_ps = [Tok(True) for _ in range(2)]
    dst_v = xs_dst.rearrange("c p t -> p c t")
    for tb in range(T // 128):
        s = tb % 2
        P.dma("sp", xin[s], x_tm[tb * 128:(tb + 1) * 128, :], ds_in[s], (), (t_in[s],))
        for q in range(4):
            b = (tb * 4 + q) % 2
            bank = cx.bank(b)
            for i in range(4):
                c = q * 4 + i
                P.tr(bank[:, i * 128:(i + 1) * 128], xin[s][:, c * 128:(c + 1) * 128], cx.ident,
                     (t_in[s],), (t_ps[b],), inc=(i == 3))
            eng = "dve" if q % 2 == 0 else "act"
            P.copy(eng, xo[s][:, q * 512:(q + 1) * 512], bank, (t_ps[b],), (t_o[s],))
        P.dma("sp", dst_v[:, :, tb * 128:(tb + 1) * 128],
              xo[s].rearrange("p (c t) -> p c t", c=NCH), ds_o[s], (t_o[s],), cx.xs_tok(None, tb // 4))
    P.barrier()
    for d in ds_in + ds_o:
        P.free_dsems.append(d)
    A.release()


def phase_tout(cx, xs_src, out_tm):
    P, A = cx.P, cx.A
    A.mark()
    xin = [A.alloc(F32, NCH * 128) for _ in range(2)]
    xo = [A.alloc(F32, D) for _ in range(2)]
    t_in = [Tok() for _ in range(2)]
    t_o = [Tok() for _ in range(2)]
    ds_in = [P.dsem() for _ in range(2)]
    ds_o = [P.dsem() for _ in range(2)]
    t_ps = [Tok(True) for _ in range(2)]
    src_v = xs_src.rearrange("c p t -> p c t")
    for tb in range(T // 128):
        s = tb % 2
        P.dma("sp", xin[s].rearrange("p (c t) -> p c t", c=NCH), src_v[:, :, tb * 128:(tb + 1) * 128],
              ds_in[s], cx.xs_tok(None, tb // 4), (t_in[s],))
        for q in range(4):
            b = (tb * 4 + q) % 2
            bank = cx.bank(b)
            for i in range(4):
                c = q * 4 + i
                P.tr(bank[:, i * 128:(i + 1) * 128], xin[s][:, c * 128:(c + 1) * 128], cx.ident,
                     (t_in[s],), (t_ps[b],), inc=(i == 3))
            eng = "dve" if q % 2 == 0 else "act"
            P.copy(eng, xo[s][:, q * 512:(q + 1) * 512], bank, (t_ps[b],), (t_o[s],))
        P.dma("sp", out_tm[tb * 128:(tb + 1) * 128, :], xo[s], ds_o[s], (t_o[s],), (cx.t_out,))
    P.barrier()
    for d in ds_in + ds_o:
        P.free_dsems.append(d)
    A.release()


def rmsnorm_tile(cx, xt, t_xt, gcol, hT_out, t_h, ntok, sqb, t_sq, rstd, t_rstd, bank, t_bank):
    P = cx.P
    xv = xt.rearrange("p (c t) -> p c t", c=NCH)
    for c in range(NCH):
        s = c % len(sqb)
        P.actf(sqb[s][:, :ntok], xv[:, c, :], AF.Square, (t_xt,), (t_sq[s],))
        P.mm(bank[:, :ntok], cx.ones_bf, sqb[s][:, :ntok], c == 0, c == NCH - 1,
             (t_sq[s], cx.t_const), (t_bank,), inc=True)
    P.ts("dve", rstd[:, :ntok], bank[:, :ntok], RMS_EPS, None, ALU.add, None, (t_bank,), (t_rstd,))
    P.actf(rstd[:, :ntok], rstd[:, :ntok], AF.Ln, (t_rstd,), (t_rstd,))
    P.actf(rstd[:, :ntok], rstd[:, :ntok], AF.Exp, (t_rstd,), (t_rstd,), scale=-0.5)
    for c in range(NCH):
        P.stt("dve", hT_out(c), xv[:, c, :], gcol[:, c:c + 1], rstd[:, :ntok], ALU.mult, ALU.mult,
              (t_xt, t_rstd, cx.t_const), (t_h,))


def phase_ffn(cx, xs_src, xs_dst, w13, w2, gcol):
    P, A = cx.P, cx.A
    A.mark()
    HALF = 1024
    NTT = HALF // 512
    hT = A.alloc(BF16, NCH * HALF)
    hTv = hT.rearrange("p (c t) -> p c t", c=NCH)
    actT = A.alloc(BF16, NFC * HALF)
    actTv = actT.rearrange("p (j t) -> p j t", j=NFC)
    t_h = Tok()
    t_act = [Tok() for _ in range(NFC)]
    sqb = [A.alloc(BF16, 512) for _ in range(3)]
    t_sq = [Tok() for _ in range(3)]
    rstd = A.alloc(F32, 512)
    t_rstd = Tok()
    sg = [A.alloc(F32, 512) for _ in range(2)]
    t_sg = [Tok() for _ in range(2)]
    xold = [A.alloc(F32, 512) for _ in range(2)]
    t_xold = [Tok() for _ in range(2)]
    ds_xold = [P.dsem() for _ in range(2)]
    xnew = [A.alloc(F32, 512) for _ in range(2)]
    t_xnew = [Tok() for _ in range(2)]
    ds_xnew = [P.dsem() for _ in range(2)]
    tops = []
    A.mark()
    xt = [A.alloc(F32, NCH * 512) for _ in range(2)]
    tops.append(A.top); A.release(); A.mark()
    w13s = [A.alloc(F32, 2 * NCH * 128) for _ in range(2)]
    w13b = [A.alloc(BF16, 2 * NCH * 128) for _ in range(2)]
    tops.append(A.top); A.release(); A.mark()
    w2s = [A.alloc(F32, NFC * 128) for _ in range(2)]
    w2b = [A.alloc(BF16, NFC * 128) for _ in range(2)]
    tops.append(A.top); A.release()
    A.top = max(tops)
    t_reg = [Tok() for _ in range(2)]
    t_regb = [Tok() for _ in range(2)]
    t_regb2 = [Tok() for _ in range(2)]
    ds_stage = [P.dsem() for _ in range(2)]
    src_v = xs_src.rearrange("c p t -> p c t")
    dst_v = xs_dst.rearrange("c p t -> p c t")
    w13v = w13.rearrange("(c p) f -> p c f", p=128)
    w2v = w2.rearrange("(j p) e -> p j e", p=128)
    t_bA = Tok(True)
    t_bB = [Tok(True) for _ in range(4)]
    t_bC = [Tok(True) for _ in range(2)]
    for th in range(T // HALF):
        tok0 = th * HALF
        for tt in range(NTT):
            s = tt % 2
            P.dma("sp", xt[s].rearrange("p (c t) -> p c t", c=NCH),
                  src_v[:, :, tok0 + tt * 512: tok0 + (tt + 1) * 512], ds_stage[s], cx.xs_tok(None, th * NTT + tt), (t_reg[s],))
            rmsnorm_tile(cx, xt[s], t_reg[s], gcol, lambda c, tt=tt: hTv[:, c, tt * 512:(tt + 1) * 512], t_h,
                         512, sqb, t_sq, rstd, t_rstd, cx.bank(6), t_bA)
        P.barrier()
        def b_load(j):
            if not DMACAST:
                s = j % 2
                stg = w13s[s].rearrange("p (g c f) -> p g c f", g=2, c=NCH)
                P.dma("sp", stg[:, 0], w13v[:, :, j * 128:(j + 1) * 128], ds_stage[s], (), (t_reg[s],))
                P.dma("sp", stg[:, 1], w13v[:, :, FF + j * 128: FF + (j + 1) * 128], ds_stage[s], (), (t_reg[s],))

        def b_cast(j):
            s = j % 2
            stg = w13s[s].rearrange("p (g c f) -> p g c f", g=2, c=NCH)
            stb = w13b[s].rearrange("p (g c f) -> p g c f", g=2, c=NCH)
            if DMACAST:
                P.dma("pool", stb[:, 0], w13v[:, :, j * 128:(j + 1) * 128], ds_stage[s], (), (t_regb[s],))
                P.dma("pool", stb[:, 1], w13v[:, :, FF + j * 128: FF + (j + 1) * 128], ds_stage[s], (), (t_regb2[s],))
            else:
                P.copy("dve", stb[:, 0], stg[:, 0], (t_reg[s],), (t_regb[s],))
                P.copy("act", stb[:, 1], stg[:, 1], (t_reg[s],), (t_regb2[s],))

        b_load(0)
        b_load(1)
        b_cast(0)
        for j in range(NFC):
            s = j % 2
            stb = w13b[s].rearrange("p (g c f) -> p g c f", g=2, c=NCH)
            if j + 1 < NFC:
                b_cast(j + 1)
            if j + 2 < NFC:
                b_load(j + 2)
            for tt in range(NTT):
                bi = (j * NTT + tt) % 2
                bg, bu = cx.bank(2 * bi), cx.bank(2 * bi + 1)
                tg, tu = t_bB[2 * bi], t_bB[2 * bi + 1]
                rhs_t = slice(tt * 512, (tt + 1) * 512)
                for c in range(NCH):
                    P.mm(bg, stb[:, 0, c, :], hTv[:, c, rhs_t], c == 0, c == NCH - 1, (t_regb[s], t_h), (tg,))
                for c in range(NCH):
                    P.mm(bu, stb[:, 1, c, :], hTv[:, c, rhs_t], c == 0, c == NCH - 1, (t_regb2[s], t_h), (tu,))
                P.actf(sg[bi], bg, AF.Silu, (tg,), (t_sg[bi],))
                P.tt("dve", actTv[:, j, rhs_t], sg[bi], bu, ALU.mult, (t_sg[bi], tu), (t_act[j],))
        P.barrier()
        def c_load(e):
            s = e % 2
            stg = w2s[s].rearrange("p (j e) -> p j e", j=NFC)
            P.dma("sp", stg[:, 0:22, :], w2v[:, 0:22, e * 128:(e + 1) * 128], ds_stage[s], (), (t_reg[s],))
            P.dma("sp", stg[:, 22:NFC, :], w2v[:, 22:NFC, e * 128:(e + 1) * 128], ds_stage[s], (), (t_reg[s],))

        def c_cast(e):
            s = e % 2
            stg = w2s[s].rearrange("p (j e) -> p j e", j=NFC)
            stb = w2b[s].rearrange("p (j e) -> p j e", j=NFC)
            P.copy("dve", stb[:, 0:22, :], stg[:, 0:22, :], (t_reg[s],), (t_regb[s],))
            P.copy("act", stb[:, 22:NFC, :], stg[:, 22:NFC, :], (t_reg[s],), (t_regb2[s],))

        c_load(0)
        c_load(1)
        c_cast(0)
        for e in range(NCH):
            s = e % 2
            stb = w2b[s].rearrange("p (j e) -> p j e", j=NFC)
            if e + 1 < NCH:
                c_cast(e + 1)
            if e + 2 < NCH:
                c_load(e + 2)
            for tt in range(NTT):
                bi = (e * NTT + tt) % 2
                bk, tb_ = cx.bank(4 + bi), t_bC[bi]
                rhs_t = slice(tt * 512, (tt + 1) * 512)
                tsl = slice(tok0 + tt * 512, tok0 + (tt + 1) * 512)
                P.dma("act", xold[bi], src_v[:, e, tsl], ds_xold[bi], cx.xs_tok(e, th * NTT + tt), (t_xold[bi],))
                for j in range(NFC):
                    P.mm(bk, stb[:, j, :], actTv[:, j, rhs_t], j == 0, j == NFC - 1,
                         (t_regb[s] if j < 22 else t_regb2[s], t_act[j]), (tb_,))
                P.stt("dve", xnew[bi], bk, 0.5, xold[bi], ALU.mult, ALU.add, (tb_, t_xold[bi]), (t_xnew[bi],))
                P.dma("pool", dst_v[:, e, tsl], xnew[bi], ds_xnew[bi], (t_xnew[bi],), cx.xs_tok(e, th * NTT + tt))
        P.barrier()
    for d in ds_xold + ds_xnew + ds_stage:
        P.free_dsems.append(d)
    A.release()


def norm_from_bank(cx, rstd, t_rstd, bank, t_bank, n, mean_scale, eps, parts=128):
    P = cx.P
    P.ts("dve", rstd[:parts, :n], bank[:parts, :n], mean_scale, eps, ALU.mult, ALU.add, (t_bank,), (t_rstd,))
    P.actf(rstd[:parts, :n], rstd[:parts, :n], AF.Ln, (t_rstd,), (t_rstd,))
    P.actf(rstd[:parts, :n], rstd[:parts, :n], AF.Exp, (t_rstd,), (t_rstd,), scale=-0.5)


MLA_H = 16
STOP = 0
DBG = False
DMACAST = False
SM_SCALE = 1.0 / float(np.sqrt(192.0))


def phase_mla(cx, xs_src, xs_dst, wd, wuq, wukv, wo, gcol, mcols_d, pos_d):
    P, A = cx.P, cx.A
    A.mark()
    src_v = xs_src.rearrange("c p t -> p c t")
    dst_v = xs_dst.rearrange("c p t -> p c t")
    mc = A.alloc(F32, 16)
    t_mc = Tok()
    dsm = P.dsem()
    P.dma("sp", mc, mcols_d, dsm, (), (t_mc,))
    cqn = A.alloc(BF16, 4 * T); cqnv = cqn.rearrange("p (c t) -> p c t", c=4)
    ckvn = A.alloc(BF16, 4 * T); ckvnv = ckvn.rearrange("p (c t) -> p c t", c=4)
    kpe = A.alloc(F32, T)
    kpesw = A.alloc(F32, T)
    t_cqn, t_ckvn, t_kpe = Tok(), Tok(), Tok()
    rstd = A.alloc(F32, 512); t_rstd = Tok()
    sqb = [A.alloc(BF16, 512) for _ in range(3)]; t_sq = [Tok() for _ in range(3)]
    A.mark()
    wdb = A.alloc(BF16, NCH * 1152); wdbv = wdb.rearrange("p (c f) -> p c f", c=NCH)
    t_wdb = Tok()
    stg = [A.alloc(F32, NCH * 128) for _ in range(2)]; t_stg = [Tok() for _ in range(2)]
    ds_stg = [P.dsem() for _ in range(2)]
    wdv = wd.rearrange("(c p) f -> p c f", p=128)
    for i in range(9):
        s = i % 2
        P.dma("sp", stg[s].rearrange("p (c f) -> p c f", c=NCH), wdv[:, :, i * 128:(i + 1) * 128], ds_stg[s], (), (t_stg[s],))
        P.copy("act" if i % 2 == 0 else "dve", wdbv[:, :, i * 128:(i + 1) * 128], stg[s].rearrange("p (c f) -> p c f", c=NCH), (t_stg[s],), (t_wdb,))
    if STOP == 11:
        P.barrier(); A.release(); A.release(); return
    xt = A.alloc(F32, NCH * 512); t_xt = Tok(); ds_xt = P.dsem()
    hT = A.alloc(BF16, NCH * 512); hTv = hT.rearrange("p (c t) -> p c t", c=NCH); t_h = Tok()
    cT = A.alloc(F32, 8 * 512); cTv = cT.rearrange("p (c t) -> p c t", c=8); t_cT = Tok()
    t_b = [Tok(True) for _ in range(8)]
    for tt in range(4):
        tsl = slice(tt * 512, (tt + 1) * 512)
        P.dma("sp", xt.rearrange("p (c t) -> p c t", c=NCH), src_v[:, :, tsl], ds_xt, cx.xs_tok(None, tt), (t_xt,))
        rmsnorm_tile(cx, xt, t_xt, gcol, lambda c: hTv[:, c, :], t_h, 512, sqb, t_sq, rstd, t_rstd, cx.bank(6), t_b[6])
        for oc in range(10):
            if STOP == 12 or (STOP == 13 and oc >= 8):
                break
            b = oc % 2
            bank = cx.bank(b)
            if oc < 8:
                for c in range(NCH):
                    P.mm(bank, wdbv[:, c, oc * 128:(oc + 1) * 128], hTv[:, c, :], c == 0, c == NCH - 1, (t_wdb, t_h), (t_b[b],))
                eng = "act" if oc % 2 == 0 else "dve"
                P.copy(eng, cTv[:, oc, :], bank, (t_b[b],), (t_cT,))
            else:
                c0 = 1024 + (oc - 8) * 64
                for c in range(NCH):
                    P.mm(bank[0:64, :], wdbv[:, c, c0:c0 + 64], hTv[:, c, :], c == 0, c == NCH - 1, (t_wdb, t_h), (t_b[b],))
                dstb = kpe if oc == 8 else kpesw
                P.copy("act", dstb[0:64, tsl], bank[0:64, :], (t_b[b],), (t_kpe,))
        for which in range(2):
            if STOP in (12, 13, 14):
                break
            for c in range(4):
                s = c % 3
                P.actf(sqb[s], cTv[:, which * 4 + c, :], AF.Square, (t_cT,), (t_sq[s],))
                P.mm(cx.bank(6), cx.ones1_bf, sqb[s], c == 0, c == 3, (t_sq[s], cx.t_const), (t_b[6],), inc=True)
            norm_from_bank(cx, rstd, t_rstd, cx.bank(6), t_b[6], 512, 1.0 / 512.0, RMS_EPS)
            dstv, tk = (cqnv, t_cqn) if which == 0 else (ckvnv, t_ckvn)
            for c in range(4):
                P.stt("dve", dstv[:, c, tsl], cTv[:, which * 4 + c, :], mc[:, which * 4 + c: which * 4 + c + 1], rstd,
                      ALU.mult, ALU.mult, (t_cT, t_rstd, t_mc), (tk,))
    P.barrier()
    A.release()
    for d in ds_stg + [ds_xt]:
        P.free_dsems.append(d)
    if STOP == 1:
        A.release(); return
    Cq = A.alloc(F32, T); Sq = A.alloc(F32, T)
    t_tab = Tok()
    kperot = A.alloc(F32, T); sqkpe = A.alloc(BF16, T); t_kr = Tok()
    t_OT = Tok()
    A.mark()
    posi = A.alloc(I32, T); posf = A.alloc(F32, T); ang = posf
    t_pos, t_ang = Tok(), Tok()
    t_ang = t_pos
    Ck = A.alloc(F32, T); Sk = A.alloc(F32, T)
    tmp = A.alloc(F32, T); t_tmp = Tok()
    P.dma("sp", posi[0:64, :], pos_d, dsm, (), (t_pos,))
    P.copy("dve", posf[0:64, :], posi[0:64, :], (t_pos,), (t_pos,))
    TWO_PI = float(2.0 * np.pi)
    PI = float(np.pi)
    P.ts("dve", ang[0:64, :], posf[0:64, :], mc[0:64, 14:15], None, ALU.mult, None, (t_pos, t_mc), (t_ang,))
    ki = posi
    C1 = 6.28125
    C2 = float(2.0 * np.pi - 6.28125)

    def sin_of(out, shift):
        P.ts("dve", tmp[0:64, :], ang[0:64, :], shift, 1.0 / TWO_PI, ALU.add, ALU.mult, (t_ang,), (t_tmp,))
        P.copy("dve", ki[0:64, :], tmp[0:64, :], (t_tmp,), (t_ki,))
        P.copy("dve", tmp[0:64, :], ki[0:64, :], (t_ki,), (t_tmp,))
        P.ts("dve", out, ang[0:64, :], shift, None, ALU.add, None, (t_ang,), (t_tab,))
        P.stt("dve", out, tmp[0:64, :], -C1, out, ALU.mult, ALU.add, (t_tmp, t_tab), (t_tab,))
        P.stt("dve", out, tmp[0:64, :], -C2, out, ALU.mult, ALU.add, (t_tmp, t_tab), (t_tab,))
        P.ts("dve", tmp[0:64, :], out, PI, TWO_PI, ALU.is_gt, ALU.mult, (t_tab,), (t_tmp,))
        P.tt("dve", out, out, tmp[0:64, :], ALU.subtract, (t_tab, t_tmp), (t_tab,))
        P.ts("dve", out, out, -PI, PI, ALU.max, ALU.min, (t_tab,), (t_tab,))
        P.actf(out, out, AF.Sin, (t_tab,), (t_tab,))

    t_ki = Tok()
    sin_of(Sq[0:64, :], 0.0)
    sin_of(Cq[0:64, :], 0.5 * PI)
    P.ts("dve", Sq[0:64, :], Sq[0:64, :], mc[0:64, 15:16], None, ALU.mult, None, (t_tab, t_mc), (t_tab,))
    P.ts("dve", Ck[0:64, :], Cq[0:64, :], mc[0:64, 12:13], None, ALU.mult, None, (t_tab, t_mc), (t_tab,))
    P.ts("dve", Sk[0:64, :], Sq[0:64, :], mc[0:64, 13:14], None, ALU.mult, None, (t_tab, t_mc), (t_tab,))
    P.ts("dve", Cq[0:64, :], Cq[0:64, :], mc[0:64, 10:11], None, ALU.mult, None, (t_tab, t_mc), (t_tab,))
    P.ts("dve", Sq[0:64, :], Sq[0:64, :], mc[0:64, 11:12], None, ALU.mult, None, (t_tab, t_mc), (t_tab,))
    P.tt("dve", kperot[0:64, :], kpe[0:64, :], Ck[0:64, :], ALU.mult, (t_kpe, t_tab), (t_kr,))
    P.tt("dve", tmp[0:64, :], kpesw[0:64, :], Sk[0:64, :], ALU.mult, (t_kpe, t_tab), (t_tmp,))
    P.tt("dve", kperot[0:64, :], kperot[0:64, :], tmp[0:64, :], ALU.add, (t_kr, t_tmp), (t_kr,))
    P.memset("pool", sqkpe, 0.0, (t_kr,))
    P.actf(sqkpe[0:64, :], kpe[0:64, :], AF.Square, (t_kpe,), (t_kr,))
    P.barrier()
    A.release()
    if STOP == 2:
        A.release(); return
    A.mark()
    wq_s = [A.alloc(F32, 4 * 256) for _ in range(2)]; wq_b = [A.alloc(BF16, 4 * 256) for _ in range(2)]
    wk_s = [A.alloc(F32, 4 * 256) for _ in range(2)]; wk_b = [A.alloc(BF16, 4 * 256) for _ in range(2)]
    t_wqs = [Tok() for _ in range(2)]; t_wqb = [Tok() for _ in range(2)]
    t_wks = [Tok() for _ in range(2)]; t_wkb = [Tok() for _ in range(2)]
    ds_w = [P.dsem() for _ in range(2)]
    wuqv = wuq.rearrange("(c p) h f -> p c h f", p=128)
    wukvv = wukv.rearrange("(c p) h f -> p c h f", p=128)
    qn = A.alloc(BF16, T); qr = A.alloc(BF16, T); kn = A.alloc(BF16, T); kr = A.alloc(BF16, T)
    t_q, t_k = Tok(), Tok()
    P.memset("pool", qr, 0.0, (t_q,))
    P.memset("pool", kr, 0.0, (t_k,))
    P.memset("pool", sqb[1], 0.0, (t_sq[1],))
    Vb = A.alloc(BF16, 16 * 128); Vv = Vb.rearrange("p (s d) -> p s d", s=16); t_V = Tok()
    pT = [A.alloc(BF16, 512) for _ in range(3)]; t_pT = [Tok() for _ in range(3)]
    rs = [A.alloc(F32, 512) for _ in range(2)]; t_rs = [Tok() for _ in range(2)]
    t1 = [A.alloc(F32, 512) for _ in range(4)]; t2 = [A.alloc(F32, 512) for _ in range(4)]
    rst = [A.alloc(F32, 512) for _ in range(4)]
    sqn = [A.alloc(BF16, 512) for _ in range(4)]; sqp = [A.alloc(BF16, 512) for _ in range(4)]
    t_t1 = [Tok() for _ in range(4)]; t_t2 = [Tok() for _ in range(4)]; t_rst = [Tok() for _ in range(4)]
    t_sqn = [Tok() for _ in range(4)]; t_sqp = [Tok() for _ in range(4)]
    for tt in range(4):
        P.memset("pool", sqp[tt], 0.0, (t_sqp[tt],))
    Ob = [A.alloc(BF16, T) for _ in range(2)]; t_Ob = [Tok() for _ in range(2)]; ds_Ob = [P.dsem() for _ in range(2)]
    t_b = [Tok(True) for _ in range(8)]
    pj = 0
    sc = 0
    for h in range(MLA_H):
        s = h % 2
        P.dma("sp", wq_s[s].rearrange("p (c f) -> p c f", c=4), wuqv[:, :, h, :], ds_w[s], (), (t_wqs[s],))
        P.dma("sp", wk_s[s].rearrange("p (c f) -> p c f", c=4), wukvv[:, :, h, :], ds_w[s], (), (t_wks[s],))
        P.copy("act", wq_b[s], wq_s[s], (t_wqs[s],), (t_wqb[s],))
        P.copy("dve", wk_b[s], wk_s[s], (t_wks[s],), (t_wkb[s],))
        wqv = wq_b[s].rearrange("p (c f) -> p c f", c=4)
        wkv = wk_b[s].rearrange("p (c f) -> p c f", c=4)
        for g in range(4):
            b = 6 + (pj % 2); pj += 1
            for i in range(4):
                st = g * 4 + i
                for c in range(4):
                    P.mm(cx.bank(b)[:, i * 128:(i + 1) * 128], ckvnv[:, c, st * 128:(st + 1) * 128], wkv[:, c, 128:256],
                         c == 0, c == 3, (t_ckvn, t_wkb[s]), (t_b[b],), inc=(c == 3 and i == 3))
            P.copy("act", Vb[:, g * 512:(g + 1) * 512], cx.bank(b), (t_b[b],), (t_V,))
        if STOP == 31:
            continue
        TS = [slice(tt * 512, (tt + 1) * 512) for tt in range(4)]
        R4 = range(4)
        for tt in R4:
            for c in range(4):
                P.mm(cx.bank(tt)[0:64, :], wqv[:, c, 128:192], cqnv[:, c, TS[tt]], c == 0, c == 3, (t_wqb[s], t_cqn), (t_b[tt],))
        for tt in R4:
            P.actf(sqp[tt][0:64, :], cx.bank(tt)[0:64, :], AF.Square, (t_b[tt],), (t_sqp[tt],))
            P.tt("dve", t1[tt][0:64, :], cx.bank(tt)[0:64, :], Cq[0:64, TS[tt]], ALU.mult, (t_b[tt], t_tab), (t_t1[tt],))
        for tt in R4:
            for c in range(4):
                P.mm(cx.bank(4 + tt)[0:64, :], wqv[:, c, 192:256], cqnv[:, c, TS[tt]], c == 0, c == 3, (t_wqb[s], t_cqn), (t_b[4 + tt],))
        for tt in R4:
            P.tt("dve", t2[tt][0:64, :], cx.bank(4 + tt)[0:64, :], Sq[0:64, TS[tt]], ALU.mult, (t_b[4 + tt], t_tab), (t_t2[tt],))
            P.tt("dve", t1[tt][0:64, :], t1[tt][0:64, :], t2[tt][0:64, :], ALU.add, (t_t1[tt], t_t2[tt]), (t_t1[tt],))
        for tt in R4:
            for c in range(4):
                P.mm(cx.bank(tt), wqv[:, c, 0:128], cqnv[:, c, TS[tt]], c == 0, c == 3, (t_wqb[s], t_cqn), (t_b[tt],))
        for tt in R4:
            P.actf(sqn[tt], cx.bank(tt), AF.Square, (t_b[tt],), (t_sqn[tt],))
        for tt in R4:
            P.mm(cx.bank(4 + tt), cx.ones1_bf, sqn[tt], True, False, (t_sqn[tt], cx.t_const), (t_b[4 + tt],), inc=False)
            P.mm(cx.bank(4 + tt), cx.ones1_bf, sqp[tt], False, True, (t_sqp[tt], cx.t_const), (t_b[4 + tt],), inc=True)
        for tt in R4:
            P.ts("dve", rst[tt], cx.bank(4 + tt), 1.0 / 192.0, RMS_EPS, ALU.mult, ALU.add, (t_b[4 + tt],), (t_rst[tt],))
        for tt in R4:
            P.actf(rst[tt], rst[tt], AF.Ln, (t_rst[tt],), (t_rst[tt],))
            P.actf(rst[tt], rst[tt], AF.Exp, (t_rst[tt],), (t_rst[tt],), scale=-0.5)
        for tt in R4:
            P.stt("dve", qn[:, TS[tt]], cx.bank(tt), mc[:, 8:9], rst[tt], ALU.mult, ALU.mult, (t_b[tt], t_rst[tt], t_mc), (t_q,))
            P.tt("dve", qr[0:64, TS[tt]], t1[tt][0:64, :], rst[tt][0:64, :], ALU.mult, (t_t1[tt], t_rst[tt]), (t_q,))
        for tt in R4:
            for c in range(4):
                P.mm(cx.bank(tt), wkv[:, c, 0:128], ckvnv[:, c, TS[tt]], c == 0, c == 3, (t_wkb[s], t_ckvn), (t_b[tt],))
        for tt in R4:
            P.actf(sqn[tt], cx.bank(tt), AF.Square, (t_b[tt],), (t_sqn[tt],))
        for tt in R4:
            P.mm(cx.bank(4 + tt), cx.ones1_bf, sqn[tt], True, False, (t_sqn[tt], cx.t_const), (t_b[4 + tt],), inc=False)
            P.mm(cx.bank(4 + tt), cx.ones1_bf, sqkpe[:, TS[tt]], False, True, (t_kr, cx.t_const), (t_b[4 + tt],), inc=True)
        for tt in R4:
            P.ts("dve", rst[tt], cx.bank(4 + tt), 1.0 / 192.0, RMS_EPS, ALU.mult, ALU.add, (t_b[4 + tt],), (t_rst[tt],))
        for tt in R4:
            P.actf(rst[tt], rst[tt], AF.Ln, (t_rst[tt],), (t_rst[tt],))
            P.actf(rst[tt], rst[tt], AF.Exp, (t_rst[tt],), (t_rst[tt],), scale=-0.5)
        for tt in R4:
            P.stt("dve", kn[:, TS[tt]], cx.bank(tt), mc[:, 9:10], rst[tt], ALU.mult, ALU.mult, (t_b[tt], t_rst[tt], t_mc), (t_k,))
            P.tt("dve", kr[0:64, TS[tt]], kperot[0:64, TS[tt]], rst[tt][0:64, :], ALU.mult, (t_kr, t_rst[tt]), (t_k,))
        if STOP == 32:
            continue
        items = []
        for qt in range(4):
            qb0 = qt * 4
            nkt = qb0 + 4
            for kt in range(nkt):
                c0 = max(0, kt - qb0) * 128
                items.append((qt, kt, nkt, c0))
        sbank = {}

        def qk(i):
            nonlocal sc
            qt, kt, nkt, c0 = items[i]
            n = 512 - c0
            qsl = slice(qt * 512 + c0, (qt + 1) * 512)
            ksl = slice(kt * 128, (kt + 1) * 128)
            b = sc % 3; sc += 1
            sbank[i] = b
            P.mm(cx.bank(b)[:, 0:n], kn[:, ksl], qn[:, qsl], True, False, (t_k, t_q), (t_b[b],), inc=False)
            P.mm(cx.bank(b)[:, 0:n], kr[:, ksl], qr[:, qsl], False, True, (t_k, t_q), (t_b[b],))
            P.actf(pT[b][:, 0:n], cx.bank(b)[:, 0:n], AF.Exp, (t_b[b],), (t_pT[b],), scale=SM_SCALE)
            if kt >= qt * 4:
                P.memset("pool", pT[b][64:128, 0:64], 0.0, (t_pT[b],))

        def pv(i):
            qt, kt, nkt, c0 = items[i]
            n = 512 - c0
            b = sbank[i]
            bO, bS = (3, 4) if qt % 2 == 0 else (5, 6)
            P.mm(cx.bank(bO)[:, c0:512], Vv[:, kt, :], pT[b][:, 0:n], kt == 0, kt == nkt - 1, (t_V, t_pT[b]), (t_b[bO],), inc=False)
            P.mm(cx.bank(bS)[:, c0:512], cx.ones1_bf, pT[b][:, 0:n], kt == 0, kt == nkt - 1, (cx.t_const, t_pT[b]), (t_b[bS],), inc=True)
            if kt == nkt - 1:
                P.actf(rs[qt % 2], cx.bank(bS), AF.Ln, (t_b[bS],), (t_rs[qt % 2],))
                P.actf(rs[qt % 2], rs[qt % 2], AF.Exp, (t_rs[qt % 2],), (t_rs[qt % 2],), scale=-1.0)
                P.tt("dve", Ob[h % 2][:, qt * 512:(qt + 1) * 512], cx.bank(bO), rs[qt % 2], ALU.mult,
                     (t_b[bO], t_rs[qt % 2]), (t_Ob[h % 2],))

        qk(0)
        for i in range(len(items)):
            if i + 1 < len(items):
                qk(i + 1)
            pv(i)
        P.dma("pool", cx.ots[h], Ob[h % 2], ds_Ob[h % 2], (t_Ob[h % 2],), (t_OT,))
    P.barrier()
    A.release()
    if STOP == 3:
        A.release(); return
    OT = A.alloc(BF16, MLA_H * T); OTv = OT.rearrange("p (h t) -> p h t", h=MLA_H)
    for c4 in range(4):
        P.dma("sp", OTv[:, c4 * 4:(c4 + 1) * 4, :], cx.ots.rearrange("h p t -> p h t")[:, c4 * 4:(c4 + 1) * 4, :], dsm, (t_OT,), (t_OT,))
    wo_s = [A.alloc(F32, MLA_H * 128) for _ in range(2)]; wo_b = [A.alloc(BF16, MLA_H * 128) for _ in range(2)]
    t_wos = [Tok() for _ in range(2)]; t_wob = [Tok() for _ in range(2)]
    xold = [A.alloc(F32, 512) for _ in range(2)]; t_xold = [Tok() for _ in range(2)]; ds_xold = [P.dsem() for _ in range(2)]
    xnew = [A.alloc(F32, 512) for _ in range(2)]; t_xnew = [Tok() for _ in range(2)]; ds_xnew = [P.dsem() for _ in range(2)]
    wov = wo.rearrange("h p e -> p h e")
    def o_load(e):
        s = e % 2
        P.dma("sp", wo_s[s].rearrange("p (h e) -> p h e", h=MLA_H), wov[:, :, e * 128:(e + 1) * 128], ds_w[s], (), (t_wos[s],))

    def o_cast(e):
        s = e % 2
        P.copy("act" if e % 2 == 0 else "dve", wo_b[s], wo_s[s], (t_wos[s],), (t_wob[s],))

    o_load(0)
    o_load(1)
    o_cast(0)
    k = 0
    for e in range(NCH):
        s = e % 2
        if e + 1 < NCH:
            o_cast(e + 1)
        if e + 2 < NCH:
            o_load(e + 2)
        wb = wo_b[s].rearrange("p (h e) -> p h e", h=MLA_H)
        for tt in range(4):
            bi = k % 2; k += 1
            b = 6 + bi
            tsl = slice(tt * 512, (tt + 1) * 512)
            P.dma("act", xold[bi], src_v[:, e, tsl], ds_xold[bi], cx.xs_tok(e, tt), (t_xold[bi],))
            for h in range(MLA_H):
                P.mm(cx.bank(b), wb[:, h, :], OTv[:, h, tsl], h == 0, h == MLA_H - 1, (t_wob[s], t_OT), (t_b[b],))
            P.tt("dve", xnew[bi], cx.bank(b), xold[bi], ALU.add, (t_b[b], t_xold[bi]), (t_xnew[bi],))
            P.dma("pool", dst_v[:, e, tsl], xnew[bi], ds_xnew[bi], (t_xnew[bi],), cx.xs_tok(e, tt))
    P.barrier()
    for d in ds_w + ds_xold + ds_xnew + ds_Ob + [dsm]:
        P.free_dsems.append(d)
    A.release()


RW_L = 64
RW_NCK = T // RW_L
DEC_C = float(np.exp(-0.5))


def linear_fm(cx, actv, t_act, w_ap, n_in_chunks, out_cols, evac, bank0=0, M=128):
    P, A = cx.P, cx.A
    A.mark()
    stg = [A.alloc(F32, n_in_chunks * 128) for _ in range(2)]
    wb = [A.alloc(BF16, n_in_chunks * 128) for _ in range(2)]
    t_s = [Tok() for _ in range(2)]; t_w = [Tok() for _ in range(2)]; t_w2 = [Tok() for _ in range(2)]
    ds = [P.dsem() for _ in range(2)]
    t_bk = [Tok(True) for _ in range(2)]
    wv = w_ap.rearrange("(c p) f -> p c f", p=128)
    hc = n_in_chunks // 2

    def l_load(oi):
        c0, m = out_cols[oi]
        s = oi % 2
        sv = stg[s].rearrange("p (c f) -> p c f", c=n_in_chunks)
        P.dma("sp", sv[:, :, 0:m], wv[:, :, c0:c0 + m], ds[s], (), (t_s[s],))

    def l_cast(oi):
        c0, m = out_cols[oi]
        s = oi % 2
        sv = stg[s].rearrange("p (c f) -> p c f", c=n_in_chunks)
        bv = wb[s].rearrange("p (c f) -> p c f", c=n_in_chunks)
        P.copy("dve", bv[:, 0:hc, 0:m], sv[:, 0:hc, 0:m], (t_s[s],), (t_w[s],))
        P.copy("act", bv[:, hc:, 0:m], sv[:, hc:, 0:m], (t_s[s],), (t_w2[s],))

    n_out = len(out_cols)
    l_load(0)
    if n_out > 1:
        l_load(1)
    l_cast(0)
    k = 0
    for oi, (c0, m) in enumerate(out_cols):
        s = oi % 2
        bv = wb[s].rearrange("p (c f) -> p c f", c=n_in_chunks)
        if oi + 1 < n_out:
            l_cast(oi + 1)
        if oi + 2 < n_out:
            l_load(oi + 2)
        for tt in range(4):
            bi = k % 2; k += 1
            bank = cx.bank(bank0 + bi)
            for c in range(n_in_chunks):
                P.mm(bank[0:m, :], bv[:, c, 0:m], actv[:, c, tt * 512:(tt + 1) * 512], c == 0, c == n_in_chunks - 1,
                     (t_w[s] if c < hc else t_w2[s], t_act), (t_bk[bi],))
            evac(oi, tt, bank, t_bk[bi])
    P.barrier()
    for d in ds:
        P.free_dsems.append(d)
    A.release()


def phase_rwkv(cx, xs_src, xs_dst, W, gcol, rc_d):
    P, A = cx.P, cx.A
    A.mark()
    src_v = xs_src.rearrange("c p t -> p c t")
    dst_v = xs_dst.rearrange("c p t -> p c t")
    rc = A.alloc(F32, 13 * 16); t_rc = Tok(); dsm = P.dsem()
    P.dma("sp", rc, rc_d, dsm, (), (t_rc,))
    omka = A.alloc(F32, 16)
    P.ts("dve", omka, rc[:, 9 * 16:10 * 16], -1.0, 1.0, ALU.mult, ALU.add, (t_rc,), (t_rc,))
    col = lambda k, e: rc[:, k * 16 + e: k * 16 + e + 1]
    t_xm = cx.t_xmix
    A.mark()
    TT = 256
    xt = [A.alloc(F32, NCH * TT) for _ in range(2)]; t_xt = [Tok() for _ in range(2)]; ds_xt = [P.dsem() for _ in range(2)]
    hb = A.alloc(F32, NCH * (TT + 4)); hbv = hb.rearrange("p (c t) -> p c t", c=NCH); t_hb = Tok()
    xx = A.alloc(F32, NCH * TT); xxv = xx.rearrange("p (c t) -> p c t", c=NCH); t_xx = Tok()
    xm = [A.alloc(BF16, NCH * TT) for _ in range(6)]; t_xmb = [Tok() for _ in range(6)]; ds_xm = [P.dsem() for _ in range(6)]
    sqb = [A.alloc(BF16, 512) for _ in range(3)]; t_sq = [Tok() for _ in range(3)]
    rstd = A.alloc(F32, 512); t_rstd = Tok()
    t_bn = Tok(True)
    utmp = [A.alloc(F32, TT) for _ in range(4)]; t_utmp = [Tok() for _ in range(4)]
    P.memset("dve", hbv[:, :, 3:4], 0.0, (t_hb,))
    for ti in range(T // TT):
        s = ti % 2
        tsl = slice(ti * TT, (ti + 1) * TT)
        P.dma("sp", xt[s].rearrange("p (c t) -> p c t", c=NCH), src_v[:, :, tsl], ds_xt[s], cx.xs_tok(None, ti // 2), (t_xt[s],))
        if ti > 0:
            P.copy("dve", hbv[:, :, 3:4], hbv[:, :, TT + 3:TT + 4], (t_hb,), (t_hb,))
        rmsnorm_tile(cx, xt[s], t_xt[s], gcol, lambda c: hbv[:, c, 4:TT + 4], t_hb, TT, sqb, t_sq, rstd, t_rstd,
                     cx.bank(6), t_bn)
        P.tt("dve", xxv, hbv[:, :, 3:TT + 3], hbv[:, :, 4:TT + 4], ALU.subtract, (t_hb,), (t_xx,))
        for j in range(6):
            xmv = xm[j].rearrange("p (c t) -> p c t", c=NCH)
            for c in range(NCH):
                if (j * NCH + c) % 3 == 2:
                    P.stt("dve", xmv[:, c, :], xxv[:, c, :], col(j, c), hbv[:, c, 4:TT + 4], ALU.mult, ALU.add,
                          (t_xx, t_hb, t_rc), (t_xmb[j],))
                else:
                    u = (j * NCH + c) % 4
                    P.actf(utmp[u], xxv[:, c, :], AF.Copy, (t_xx, t_rc), (t_utmp[u],), scale=col(j, c))
                    P.tt("dve", xmv[:, c, :], utmp[u], hbv[:, c, 4:TT + 4], ALU.add, (t_utmp[u], t_hb), (t_xmb[j],))
            P.dma("sp", cx.xmix[j].rearrange("c p t -> p c t")[:, :, tsl], xmv, ds_xm[j], (t_xmb[j],), (t_xm,))
    P.barrier()
    for d in ds_xt + ds_xm:
        P.free_dsems.append(d)
    A.release()
    A.mark()
    lw = A.alloc(BF16, T); la = A.alloc(BF16, T); lg = A.alloc(BF16, 2 * T); lgv = lg.rearrange("p (c t) -> p c t", c=2)
    t_l = Tok()
    A.mark()
    acts = [A.alloc(BF16, NCH * T) for _ in range(2)]
    t_acts = [Tok() for _ in range(2)]; ds_act = [P.dsem() for _ in range(2)]
    ot = [A.alloc(F32, 512) for _ in range(4)]; t_ot = [Tok() for _ in range(4)]; ds_ot = [P.dsem() for _ in range(4)]
    cnt = [0]
    order = [(0, "r"), (2, "k"), (3, "v"), (1, "lw"), (4, "la"), (5, "lg")]

    def load_act(j):
        av = acts[j % 2].rearrange("p (c t) -> p c t", c=NCH)
        for c4 in range(4):
            P.dma("sp", av[:, c4 * 4:(c4 + 1) * 4, :], cx.xmix[order[j][0]].rearrange("c p t -> p c t")[:, c4 * 4:(c4 + 1) * 4, :],
                  ds_act[j % 2], (t_xm,), (t_acts[j % 2],))

    load_act(0)
    for j, (src_j, kind) in enumerate(order):
        if j + 1 < len(order):
            load_act(j + 1)
        actv = acts[j % 2].rearrange("p (c t) -> p c t", c=NCH)
        t_act = t_acts[j % 2]
        if kind in ("r", "k", "v"):
            ji = "rkv".index(kind)

            def ev(oi, tt, bank, tb, ji=ji):
                i = cnt[0] % 4; cnt[0] += 1
                P.copy("act" if i % 2 == 0 else "dve", ot[i], bank, (tb,), (t_ot[i],))
                P.dma("pool", cx.rkv[ji].rearrange("c p t -> p c t")[:, oi, tt * 512:(tt + 1) * 512], ot[i], ds_ot[i],
                      (t_ot[i],), (cx.t_rkv,))
            linear_fm(cx, actv, t_act, W["w_rkv"][ji], NCH, [(e * 128, 128) for e in range(NCH)], ev)
        elif kind == "lw":
            def ev(oi, tt, bank, tb):
                P.actf(lw[0:96, tt * 512:(tt + 1) * 512], bank[0:96, :], AF.Tanh, (tb,), (t_l,))
            linear_fm(cx, actv, t_act, W["w1"], NCH, [(0, 96)], ev)
        elif kind == "la":
            def ev(oi, tt, bank, tb):
                P.copy("act", la[0:96, tt * 512:(tt + 1) * 512], bank[0:96, :], (tb,), (t_l,))
            linear_fm(cx, actv, t_act, W["a1"], NCH, [(0, 96)], ev)
        else:
            def ev(oi, tt, bank, tb):
                P.actf(lgv[:, oi, tt * 512:(tt + 1) * 512], bank, AF.Sigmoid, (tb,), (t_l,))
            linear_fm(cx, actv, t_act, W["g1"], NCH, [(0, 128), (128, 128)], ev)
    P.free_dsems.extend(ds_act + ds_ot)
    A.release()
    w2b = A.alloc(BF16, D); a2b = A.alloc(BF16, D); g2b = A.alloc(BF16, 2 * D); g2bv = g2b.rearrange("p (c f) -> p c f", c=2)
    t_lw2 = Tok()
    st32 = A.alloc(F32, 2 * D)
    P.dma("sp", st32[0:96, 0:D], W["w2"], dsm, (), (t_lw2,))
    P.copy("dve", w2b[0:96, :], st32[0:96, 0:D], (t_lw2,), (t_lw2,))
    P.dma("sp", st32[0:96, 0:D], W["a2"], dsm, (t_lw2,), (t_lw2,))
    P.copy("dve", a2b[0:96, :], st32[0:96, 0:D], (t_lw2,), (t_lw2,))
    P.dma("sp", st32.rearrange("p (c f) -> p c f", c=2), W["g2"].rearrange("(c p) f -> p c f", p=128), dsm, (t_lw2,), (t_lw2,))
    P.copy("dve", g2b, st32, (t_lw2,), (t_lw2,))
    t_b2 = [Tok(True) for _ in range(6)]
    big = [[A.alloc(F32, T) for _ in range(3)] for _ in range(2)]
    t_big = [[Tok() for _ in range(3)] for _ in range(2)]
    ds_big = [P.dsem() for _ in range(2)]
    for e in range(NCH):
        esl = slice(e * 128, (e + 1) * 128)
        sb_ = e % 2
        for tt in range(4):
            tsl = slice(tt * 512, (tt + 1) * 512)
            b0 = (tt % 2) * 3
            P.mm(cx.bank(b0), w2b[0:96, esl], lw[0:96, tsl], True, True, (t_lw2, t_l), (t_b2[b0],))
            P.actf(big[sb_][0][:, tsl], cx.bank(b0), AF.Sigmoid, (t_b2[b0], t_rc), (t_big[sb_][0],), bias=col(6, e))
            P.mm(cx.bank(b0 + 1), a2b[0:96, esl], la[0:96, tsl], True, True, (t_lw2, t_l), (t_b2[b0 + 1],))
            P.actf(big[sb_][1][:, tsl], cx.bank(b0 + 1), AF.Sigmoid, (t_b2[b0 + 1], t_rc), (t_big[sb_][1],), bias=col(7, e))
            for c in range(2):
                P.mm(cx.bank(b0 + 2), g2bv[:, c, esl], lgv[:, c, tsl], c == 0, c == 1, (t_lw2, t_l), (t_b2[b0 + 2],))
            P.copy("dve", big[sb_][2][:, tsl], cx.bank(b0 + 2), (t_b2[b0 + 2],), (t_big[sb_][2],))
        for pl in range(3):
            P.dma("pool" if pl != 1 else "sp", cx.rkv[3 + pl][e], big[sb_][pl], ds_big[sb_], (t_big[sb_][pl],), (cx.t_rkv,))
    P.barrier()
    P.free_dsems.extend(ds_big)
    A.release()
    if STOP == 52:
        A.release(); return
    t_yg = Tok()
    A.mark()
    msk = A.alloc(F32, 3 * 512); t_k = Tok()
    P.dma("sp", msk, cx.rw_masks_d, dsm, (), (t_k,))
    ML_s, MU_s, MU_i = msk[:, 0:512], msk[:, 512:1024], msk[:, 1024:1536]
    id4 = A.alloc(BF16, 512)
    for q in range(4):
        P.copy("dve", id4[:, q * 128:(q + 1) * 128], cx.ident, (cx.t_const,), (t_k,))
    idb = id4[:, 0:128]
    bones = A.alloc(F32, 128)
    P.memset("pool", bones, 0.0, (t_k,))
    P.memset("pool", bones[0:64, 0:64], 1.0, (t_k,))
    P.memset("pool", bones[64:128, 64:128], 1.0, (t_k,))
    rmask = A.alloc(BF16, T)
    P.memset("pool", rmask, 1.0, (t_k,))
    P.memset("pool", rmask.rearrange("p (c t) -> p c t", t=RW_L)[:, :, 0:1], 0.0, (t_k,))
    xop = [A.alloc(BF16, RW_NCK * 128) for _ in range(7)]
    t_xop = Tok()
    for x_ in xop:
        P.memset("pool", x_, 0.0, (t_xop,))
    RTx, KTx, BTx, KHx, BHx, ATx, Vx = xop
    gam = A.alloc(F32, RW_NCK); t_gam = Tok()
    bon = A.alloc(F32, T); t_bon = Tok()
    psb = cx.ps_bf

    def xview(xo, h):
        return xo.rearrange("p (c i) -> p c i", i=128)[h * 64:(h + 1) * 64, :, h * 64:(h + 1) * 64]

    def hview(ap, h):
        return ap.rearrange("p (c t) -> p c t", t=RW_L)[h * 64:(h + 1) * 64, :, :]

    def ch(xo, c):
        return xo[:, c * 128:(c + 1) * 128]

    def q4(ap, q):
        return ap[:, q * 128:(q + 1) * 128]

    NG = RW_NCK // 4
    NSLOT = 3
    for e in range(NCH):
        A.mark()
        r_, k_, v_, a_, cum, sg_, tA, tB, tC, tD, tE, tF = [A.alloc(F32, T) for _ in range(12)]
        t_r, t_kk, t_v, t_a, t_cum, t_sg, t_tA, t_tB, t_tC, t_tD, t_tE, t_tF = [Tok() for _ in range(12)]
        t_bk = [Tok(True) for _ in range(8)]
        t_xop2 = [Tok(), Tok()]
        for ji, (dst, tk) in enumerate([(sg_, t_sg), (k_, t_kk), (a_, t_a), (r_, t_r), (v_, t_v)]):
            P.dma("sp", dst, cx.rkv[[3, 1, 4, 0, 2][ji]][e], dsm, (cx.t_rkv,), (tk,))
        cumv = cum.rearrange("p (c t) -> p c t", t=RW_L)
        TS = [slice(tt * 512, (tt + 1) * 512) for tt in range(4)]
        P.op("dve", lambda en, cum=cum, sg_=sg_: en.tensor_tensor_scan(cum, rmask, sg_, 0.0, ALU.mult, ALU.add),
             (t_sg, t_k), (t_cum,))
        P.actf(tA, k_, AF.Copy, (t_kk, t_rc), (t_tA,), scale=col(8, e))
        P.actf(tB, tA, AF.Square, (t_tA,), (t_tB,))
        P.actf(tC, cum, AF.Exp, (t_cum,), (t_tC,), scale=-DEC_C)
        for tt in range(4):
            P.mm(cx.bank(tt), bones, tB[:, TS[tt]], True, True, (t_k, t_tB), (t_bk[tt],))
        P.actf(tE, a_, AF.Identity, (t_a, t_rc), (t_tE,), scale=col(9, e), bias=omka[:, e:e + 1])
        P.tt("dve", k_, k_, tE, ALU.mult, (t_kk, t_tE), (t_kk,))
        for tt in range(4):
            P.ts("dve", tD[:, TS[tt]], cx.bank(tt), 1e-18, None, ALU.max, None, (t_bk[tt],), (t_tD,))
        P.actf(tD, tD, AF.Ln, (t_tD,), (t_tD,))
        P.actf(tD, tD, AF.Exp, (t_tD,), (t_tD,), scale=-0.5)
        P.copy("dve", gam, cumv[:, :, RW_L - 1], (t_cum,), (t_gam,))
        P.actf(gam, gam, AF.Exp, (t_gam,), (t_gam,), scale=-DEC_C)
        for h in range(2):
            P.tt("dve" if h == 0 else "pool", xview(RTx, h), hview(r_, h), hview(tC, h), ALU.mult, (t_r, t_tC), (t_xop2[h],))
        P.tt("dve", tE, cum, sg_, ALU.subtract, (t_cum, t_sg, t_tE), (t_tE,))
        P.actf(tE, tE, AF.Exp, (t_tE,), (t_tE,), scale=-DEC_C)
        P.stt("dve", tF, r_, col(10, e), k_, ALU.mult, ALU.mult, (t_r, t_kk, t_rc), (t_tF,))
        for tt in range(4):
            P.mm(cx.bank(4 + tt), bones, tF[:, TS[tt]], True, True, (t_k, t_tF), (t_bk[4 + tt],))
        P.tt("dve", tA, tA, tD, ALU.mult, (t_tA, t_tD), (t_tA,))
        P.tt("dve", tB, tA, a_, ALU.mult, (t_tA, t_a, t_tB), (t_tB,))
        P.actf(tC, cum, AF.Exp, (t_cum,), (t_tC,), scale=DEC_C)
        for tt in range(4):
            P.tt("dve", bon[:, TS[tt]], cx.bank(4 + tt), v_[:, TS[tt]], ALU.mult, (t_bk[4 + tt], t_v), (t_bon,))
        for h in range(2):
            P.stt("dve", xview(ATx, h), hview(tA, h), -1.0, hview(tE, h), ALU.mult, ALU.mult, (t_tA, t_tE), (t_xop,))
        P.tt("dve", r_.rearrange("p (c t) -> p c t", t=RW_L), cumv[:, :, RW_L - 1:RW_L].to_broadcast([128, RW_NCK, RW_L]),
             cumv, ALU.subtract, (t_cum, t_r), (t_r,))
        P.actf(r_, r_, AF.Exp, (t_r,), (t_r,), scale=-DEC_C)
        for h in range(2):
            P.copy("act", xview(Vx, h), hview(v_, h), (t_v,), (t_xop,))
        for h in range(2):
            eng = "dve" if h == 0 else "pool"
            P.tt(eng, xview(KTx, h), hview(k_, h), hview(tC, h), ALU.mult, (t_kk, t_tC), (t_xop2[h],))
            P.tt(eng, xview(BTx, h), hview(tB, h), hview(tC, h), ALU.mult, (t_tB, t_tC), (t_xop2[h],))
        for h in range(2):
            P.tt("dve", xview(KHx, h), hview(k_, h), hview(r_, h), ALU.mult, (t_kk, t_r), (t_xop,))
            P.tt("dve", xview(BHx, h), hview(tB, h), hview(r_, h), ALU.mult, (t_tB, t_r), (t_xop,))
        P.barrier()
        A.release()
        A.mark()
        GT = A.alloc(BF16, RW_NCK * 128); Hh = A.alloc(F32, RW_NCK * 128); Rb = A.alloc(BF16, RW_NCK * 128)
        y0 = A.alloc(F32, T); Sall = A.alloc(BF16, RW_NCK * 128); yT = A.alloc(F32, T)
        t_GT = [Tok() for _ in range(NG)]; t_H = [Tok() for _ in range(NG)]; t_Rb = [Tok() for _ in range(NG)]
        t_y0 = [Tok() for _ in range(NG)]
        gT = A.alloc(F32, T); t_g = Tok()
        P.dma("sp", gT, cx.rkv[5][e], dsm, (cx.t_rkv,), (t_g,))
        t_bk = [Tok(True) for _ in range(8)]
        slots = []
        for sl in range(NSLOT):
            d_ = {}
            d_["TMb"] = [A.alloc(BF16, 512) for _ in range(3)]
            d_["WUin"] = A.alloc(BF16, 4 * 256); d_["WU"] = A.alloc(BF16, 4 * 256)
            d_["Nb"] = [A.alloc(BF16, 512) for _ in range(2)]; d_["Qb"] = [A.alloc(BF16, 512) for _ in range(2)]
            d_["Pb"] = [A.alloc(BF16, 512) for _ in range(2)]
            d_["M"] = [A.alloc(BF16, 512) for _ in range(3)]
            d_["tok"] = {k: Tok() for k in ("TM", "WUin", "WU", "N", "Q", "P", "M")}
            d_["banks"] = (2 * sl, 2 * sl + 1)
            slots.append(d_)
        ev_i = [0]

        def group_steps(g, sd):
            cs = [g * 4 + q for q in range(4)]
            TMb, WUin, WU, Nb, Qb, Pb = sd["TMb"], sd["WUin"], sd["WU"], sd["Nb"], sd["Qb"], sd["Pb"]
            Mak, Mrb, Mrk = sd["M"]
            tk = sd["tok"]
            WUinv = WUin.rearrange("p (q f) -> p q f", q=4)
            WUv = WU.rearrange("p (q f) -> p q f", q=4)
            bi = [0]

            def nb():
                bi[0] ^= 1
                return sd["banks"][bi[0]]

            def eng2():
                ev_i[0] += 1
                return "dve" if ev_i[0] % 2 == 0 else "act"
            for half, ops_ in enumerate([(BHx, KHx), (Vx, ATx)]):
                b = nb()
                for oi, xo in enumerate(ops_):
                    for q in range(4):
                        P.tr(psb[:, b * 1024 + (oi * 4 + q) * 128: b * 1024 + (oi * 4 + q + 1) * 128], ch(xo, cs[q]), idb,
                             (t_xop, t_k), (t_bk[b],), inc=(oi == 1 and q == 3))
                if half == 0:
                    P.copy("act", TMb[0], psb[:, b * 1024: b * 1024 + 512], (t_bk[b],), (tk["TM"],))
                    P.copy("dve", TMb[1], psb[:, b * 1024 + 512: b * 1024 + 1024], (t_bk[b],), (tk["TM"],))
                else:
                    P.copy("act", TMb[2], psb[:, b * 1024: b * 1024 + 512], (t_bk[b],), (tk["TM"],))
                    P.copy("dve", WUinv[:, :, 0:128], psb[:, b * 1024 + 512: b * 1024 + 1024].rearrange("p (q f) -> p q f", q=4),
                           (t_bk[b],), (tk["WUin"],))
                yield
            b = nb()
            for q in range(4):
                P.mm(q4(cx.bank(b), q), ch(ATx, cs[q]), ch(BTx, cs[q]), True, True, (t_xop,), (t_bk[b],), inc=(q == 3))
            P.tt("dve", Nb[0], cx.bank(b), ML_s, ALU.mult, (t_bk[b], t_k), (tk["N"],))
            yield
            b = nb()
            for q in range(4):
                P.mm(q4(cx.bank(b), q), ch(BTx, cs[q]), ch(ATx, cs[q]), True, True, (t_xop,), (t_bk[b],), inc=(q == 3))
            P.tt("dve", Qb[0], cx.bank(b), MU_s, ALU.mult, (t_bk[b], t_k), (tk["Q"],))
            P.tt("pool", Pb[0], Qb[0], id4, ALU.add, (tk["Q"], t_k), (tk["P"],))
            yield
            for (lx, rx, mk, dst) in ((KTx, ATx, MU_s, Mak), (BTx, RTx, MU_i, Mrb), (KTx, RTx, MU_i, Mrk)):
                b = nb()
                for q in range(4):
                    P.mm(q4(cx.bank(b), q), ch(lx, cs[q]), ch(rx, cs[q]), True, True, (t_xop,), (t_bk[b],), inc=(q == 3))
                P.tt("dve", dst, cx.bank(b), mk, ALU.mult, (t_bk[b], t_k), (tk["M"],))
                yield
            pi = 0
            for j in range(1, 6):
                i0, i1 = (j - 1) % 2, j % 2
                b = nb()
                for q in range(4):
                    P.mm(q4(cx.bank(b), q), q4(Qb[i0], q), q4(Nb[i0], q), True, True, (tk["N"], tk["Q"]), (t_bk[b],), inc=(q == 3))
                if j < 5:
                    b2 = nb()
                    for q in range(4):
                        P.mm(q4(cx.bank(b2), q), q4(Nb[i0], q), q4(Qb[i0], q), True, True, (tk["N"], tk["Q"]), (t_bk[b2],), inc=(q == 3))
                P.copy("act", Nb[i1], cx.bank(b), (t_bk[b],), (tk["N"],))
                if j < 5:
                    P.copy(eng2(), Qb[i1], cx.bank(b2), (t_bk[b2],), (tk["Q"],))
                yield
                b = nb()
                for q in range(4):
                    P.mm(q4(cx.bank(b), q), idb, q4(Pb[pi], q), True, False, (t_k, tk["P"]), (t_bk[b],), inc=False)
                    P.mm(q4(cx.bank(b), q), q4(Nb[i1], q), q4(Pb[pi], q), False, True, (tk["N"], tk["P"]), (t_bk[b],), inc=(q == 3))
                P.copy(eng2(), Pb[1 - pi], cx.bank(b), (t_bk[b],), (tk["P"],))
                pi = 1 - pi
                yield
            TiT = Pb[pi]
            b = nb()
            for q in range(4):
                P.mm(q4(cx.bank(b), q), q4(Mak, q), q4(TMb[2], q), True, True, (tk["M"], tk["TM"]), (t_bk[b],), inc=(q == 3))
            P.copy("act", WUinv[:, :, 128:256], cx.bank(b).rearrange("p (q f) -> p q f", q=4), (t_bk[b],), (tk["WUin"],))
            yield
            for hf in range(2):
                b = nb()
                for qq in range(2):
                    q = hf * 2 + qq
                    P.mm(cx.bank(b)[:, qq * 256:(qq + 1) * 256], q4(TiT, q), WUinv[:, q, :], True, True, (tk["P"], tk["WUin"]),
                         (t_bk[b],), inc=(qq == 1))
                P.copy("act" if hf == 0 else "dve", WU[:, hf * 512:(hf + 1) * 512], cx.bank(b), (t_bk[b],), (tk["WU"],))
            yield
            b = nb()
            for q in range(4):
                P.mm(q4(cx.bank(b), q), WUv[:, q, 0:128], q4(TMb[0], q), True, True, (tk["WU"], tk["TM"]), (t_bk[b],), inc=(q == 3))
            P.copy("act", GT[:, g * 512:(g + 1) * 512], cx.bank(b), (t_bk[b],), (t_GT[g],))
            b = nb()
            for q in range(4):
                P.mm(q4(cx.bank(b), q), q4(TMb[0], q), WUv[:, q, 128:256], True, False, (tk["WU"], tk["TM"]), (t_bk[b],), inc=False)
                P.mm(q4(cx.bank(b), q), q4(TMb[1], q), q4(TMb[2], q), False, True, (tk["TM"],), (t_bk[b],), inc=(q == 3))
            P.copy("dve", Hh[:, g * 512:(g + 1) * 512], cx.bank(b), (t_bk[b],), (t_H[g],))
            yield
            b = nb()
            for q in range(4):
                P.mm(q4(cx.bank(b), q), idb, ch(RTx, cs[q]), True, False, (t_k, t_xop), (t_bk[b],), inc=False)
                P.mm(q4(cx.bank(b), q), WUv[:, q, 0:128], q4(Mrb, q), False, True, (tk["WU"], tk["M"]), (t_bk[b],), inc=(q == 3))
            P.copy("act", Rb[:, g * 512:(g + 1) * 512], cx.bank(b), (t_bk[b],), (t_Rb[g],))
            b = nb()
            for q in range(4):
                P.mm(q4(cx.bank(b), q), WUv[:, q, 128:256], q4(Mrb, q), True, False, (tk["WU"], tk["M"]), (t_bk[b],), inc=False)
                P.mm(q4(cx.bank(b), q), q4(TMb[2], q), q4(Mrk, q), False, True, (tk["TM"], tk["M"]), (t_bk[b],), inc=(q == 3))
            for h in range(2):
                bv = cx.bank(b).rearrange("p (q i) -> p q i", q=4)[h * 64:(h + 1) * 64, :, h * 64:(h + 1) * 64]
                P.copy("act", hview(y0, h)[:, g * 4:(g + 1) * 4, :], bv, (t_bk[b],), (t_y0[g],))
            yield

        pending = list(range(NG))
        active = []
        free_slots = list(range(NSLOT))
        while pending or active:
            while pending and free_slots:
                sl = free_slots.pop(0)
                active.append((group_steps(pending.pop(0), slots[sl]), sl))
            for item in list(active):
                gen, sl = item
                try:
                    next(gen)
                except StopIteration:
                    active.remove(item)
                    free_slots.append(sl)
        Sf = [A.alloc(F32, 128) for _ in range(2)]; tAq = [A.alloc(F32, 128) for _ in range(2)]
        t_Sf, t_tAq = [Tok() for _ in range(2)], [Tok() for _ in range(2)]
        t_Sg = [Tok() for _ in range(NG)]
        t_yq = [Tok() for _ in range(4)]

        def emit_y(g):
            b = 4 + (g % 2)
            for q in range(4):
                c = g * 4 + q
                P.mm(q4(cx.bank(b), q), ch(Sall, c), ch(Rb, c), True, True, (t_Sg[g], t_Rb[g]), (t_bk[b],), inc=(q == 3))
            for h in range(2):
                bv = cx.bank(b).rearrange("p (q i) -> p q i", q=4)[h * 64:(h + 1) * 64, :, h * 64:(h + 1) * 64]
                P.tt("dve", hview(yT, h)[:, g * 4:(g + 1) * 4, :], bv, hview(y0, h)[:, g * 4:(g + 1) * 4, :], ALU.add,
                     (t_bk[b], t_y0[g]), (t_yq[g // 2],))

        P.memset("pool", Sall[:, 0:128], 0.0, (t_Sg[0],))
        P.copy("dve", tAq[0], ch(Hh, 0), (t_H[0],), (t_tAq[0],))
        for c in range(RW_NCK - 1):
            si = c % 2
            b = 6 + (c % 2)
            P.mm(cx.bank(b)[:, 0:128], ch(GT, c), ch(Sall, c), True, True, (t_GT[c // 4], t_Sg[c // 4]), (t_bk[b],))
            P.tt("dve", ch(Sall, c + 1), cx.bank(b)[:, 0:128], tAq[si], ALU.add, (t_bk[b], t_tAq[si]), (t_Sg[(c + 1) // 4],))
            if c + 1 < RW_NCK - 1:
                P.tt("dve", Sf[si], cx.bank(b)[:, 0:128], tAq[si], ALU.add, (t_bk[b], t_tAq[si]), (t_Sf[si],))
                P.stt("dve", tAq[1 - si], Sf[si], gam[:, c + 1:c + 2], ch(Hh, c + 1), ALU.mult, ALU.add,
                      (t_Sf[si], t_gam, t_H[(c + 1) // 4]), (t_tAq[1 - si],))
            if (c + 1) % 4 == 3:
                emit_y((c + 1) // 4)
        if cx.dbg_y is not None:
            P.dma("sp", cx.dbg_y[e], yT, dsm, tuple(t_yq), (cx.t_out,))
        mean = [Hh[:, tt * 512:(tt + 1) * 512] for tt in range(4)]; t_mean = [Tok() for _ in range(4)]
        ygb = A.alloc(BF16, T); t_ygb = Tok()
        t_y0p = [Tok() for _ in range(4)]
        TS = [slice(tt * 512, (tt + 1) * 512) for tt in range(4)]
        for tt in range(4):
            P.mm(cx.bank(tt), bones, yT[:, TS[tt]], True, True, (t_k, t_yq[tt]), (t_bk[tt],))
        for tt in range(4):
            P.stt("dve", yT[:, TS[tt]], cx.bank(tt), -1.0 / 64.0, yT[:, TS[tt]], ALU.mult, ALU.add, (t_bk[tt], t_yq[tt]), (t_yq[tt],))
        for tt in range(4):
            P.actf(y0[:, TS[tt]], yT[:, TS[tt]], AF.Square, (t_yq[tt], t_y0[2 * tt], t_y0[2 * tt + 1]), (t_y0p[tt],))
        for tt in range(4):
            P.mm(cx.bank(4 + tt), bones, y0[:, TS[tt]], True, True, (t_k, t_y0p[tt]), (t_bk[4 + tt],))
        for tt in range(4):
            P.ts("dve", mean[tt], cx.bank(4 + tt), 1.0 / 64.0, 64e-5, ALU.mult, ALU.add, (t_bk[4 + tt],), (t_mean[tt],) + tuple(t_H))
        for tt in range(4):
            P.actf(mean[tt], mean[tt], AF.Ln, (t_mean[tt],), (t_mean[tt],))
            P.actf(mean[tt], mean[tt], AF.Exp, (t_mean[tt],), (t_mean[tt],), scale=-0.5)
        for tt in range(4):
            P.tt("dve", yT[:, TS[tt]], yT[:, TS[tt]], mean[tt], ALU.mult, (t_yq[tt], t_mean[tt]), (t_yq[tt],))
            P.ts("dve", yT[:, TS[tt]], yT[:, TS[tt]], col(11, e), col(12, e), ALU.mult, ALU.add, (t_yq[tt], t_rc), (t_yq[tt],))
            P.tt("dve", yT[:, TS[tt]], yT[:, TS[tt]], bon[:, TS[tt]], ALU.add, (t_yq[tt], t_bon), (t_yq[tt],))
            P.tt("dve", ygb[:, TS[tt]], yT[:, TS[tt]], gT[:, TS[tt]], ALU.mult, (t_yq[tt], t_g), (t_ygb,))
        P.dma("sp", cx.ygs[e], ygb, dsm, (t_ygb,), (t_yg,))
        P.barrier()
        A.release()
    A.release()
    xold = [A.alloc(F32, 512) for _ in range(2)]; t_xold = [Tok() for _ in range(2)]; ds_xold = [P.dsem() for _ in range(2)]
    xnew = [A.alloc(F32, 512) for _ in range(2)]; t_xnew = [Tok() for _ in range(2)]; ds_xnew = [P.dsem() for _ in range(2)]
    kk_ = [0]

    def ev_o(oi, tt, bank, tb):
        bi = kk_[0] % 2; kk_[0] += 1
        tsl = slice(tt * 512, (tt + 1) * 512)
        P.dma("act", xold[bi], src_v[:, oi, tsl], ds_xold[bi], cx.xs_tok(oi, tt), (t_xold[bi],))
        P.tt("dve", xnew[bi], bank, xold[bi], ALU.add, (tb, t_xold[bi]), (t_xnew[bi],))
        P.dma("pool", dst_v[:, oi, tsl], xnew[bi], ds_xnew[bi], (t_xnew[bi],), cx.xs_tok(oi, tt))
    ygT = A.alloc(BF16, NCH * T); ygv = ygT.rearrange("p (c t) -> p c t", c=NCH)
    for c4 in range(4):
        P.dma("sp", ygv[:, c4 * 4:(c4 + 1) * 4, :], cx.ygs.rearrange("c p t -> p c t")[:, c4 * 4:(c4 + 1) * 4, :], dsm, (t_yg,), (t_yg,))
    linear_fm(cx, ygv, t_yg, W["w_o"], NCH, [(e * 128, 128) for e in range(NCH)], ev_o, bank0=2)
    P.barrier()
    P.free_dsems.extend(ds_xold + ds_xnew + [dsm])
    A.release()


def pack_cols(vecs):
    return np.ascontiguousarray(
        np.concatenate([np.asarray(v, np.float32).reshape(NCH, 128).T for v in vecs], axis=1))


def build(phases):
    nc = bass.Bass("TRN2", target_bir_lowering=False)
    names = [p[0] for p in phases]
    dram = {}

    def din(name, shape, dt=F32):
        dram[name] = nc.dram_tensor(name, list(shape), dt, kind="ExternalInput").ap()
        return dram[name]

    ins = []
    if "tin" in names:
        x_tm = din("x", [T, D]); ins.append("x")
    else:
        xs_in = din("xs_in", [NCH, 128, T]); ins.append("xs_in")
    if "tout" in names:
        out_ap = nc.dram_tensor("out", [T, D], F32, kind="ExternalOutput").ap()
        out_name = "out"
    else:
        out_ap = nc.dram_tensor("xs_out", [NCH, 128, T], F32, kind="ExternalOutput").ap()
        out_name = "xs_out"
    ident_d = din("ident", [128, 128]); ins.append("ident")
    ncols = 16 * 8
    cols_d = din("cols", [128, ncols]); ins.append("cols")
    for p in phases:
        if p[0] == "ffn":
            l, s = p[1], p[2]
            din(f"w13_{l}{s}", [D, 2 * FF]); ins.append(f"w13_{l}{s}")
            din(f"w2_{l}{s}", [FF, D]); ins.append(f"w2_{l}{s}")
        if p[0] == "rwkv":
            din("rw_w_rkv", [3, D, D]); din("rw_w1", [D, 96]); din("rw_w2", [96, D]); din("rw_a1", [D, 96])
            din("rw_a2", [96, D]); din("rw_g1", [D, 256]); din("rw_g2", [256, D]); din("rw_w_o", [D, D])
            din("rw_cols", [128, 13 * 16]); din("rw_masks", [128, 1536])
            ins.extend(["rw_w_rkv", "rw_w1", "rw_w2", "rw_a1", "rw_a2", "rw_g1", "rw_g2", "rw_w_o", "rw_cols", "rw_masks"])
        if p[0] == "mla":
            din("mla_wd", [D, 1152]); din("mla_wuq", [512, 16, 256]); din("mla_wukv", [512, 16, 256])
            din("mla_wo", [16, 128, D]); din("mla_cols", [128, 16]); din("mla_pos", [64, T], I32)
            ins.extend(["mla_wd", "mla_wuq", "mla_wukv", "mla_wo", "mla_cols", "mla_pos"])
    xs_a = nc.dram_tensor("xs_a", [NCH, 128, T], F32, kind="Internal").ap()
    has_rw = "rwkv" in names
    ots_d = nc.dram_tensor("ots", [MLA_H, 128, T], BF16, kind="Internal").ap() if "mla" in names else None
    if has_rw:
        xmix_d = nc.dram_tensor("xmix", [6, NCH, 128, T], BF16, kind="Internal").ap()
        rkv_d = nc.dram_tensor("rkv", [6, NCH, 128, T], F32, kind="Internal").ap()
        ygs_d = nc.dram_tensor("ygs", [NCH, 128, T], BF16, kind="Internal").ap()
        dbg_d = nc.dram_tensor("dbg_y", [NCH, 128, T], F32, kind="ExternalOutput").ap() if DBG else None

    from contextlib import ExitStack
    with ExitStack() as es:
        sb = es.enter_context(nc.sbuf_tensor("sb", [128, SB_BYTES // 4], F32))
        ps = es.enter_context(nc.psum_tensor("ps", [128, 4096], F32))
        esems = {e: es.enter_context(nc.semaphore("s_" + e)) for e in Prog.CE}
        dsems = [es.enter_context(nc.semaphore(f"d{i}")) for i in range(40)]
        block = es.enter_context(nc.Block())
        P = Prog(nc, esems, dsems)
        A = Arena(sb, SB_BYTES)
        cx = Ctx()
        cx.P, cx.A, cx.nc = P, A, nc
        cx.bank = lambda b: ps[:, b * 512:(b + 1) * 512]
        cx.t_out, cx.t_const = Tok(), Tok()
        xs_toks = [[Tok() for _ in range(T // 512)] for _ in range(NCH)]

        def xs_tok(c=None, tt=None):
            cs = range(NCH) if c is None else [c]
            ts_ = range(T // 512) if tt is None else [tt]
            return tuple(xs_toks[ci][ti] for ci in cs for ti in ts_)
        cx.xs_tok = xs_tok
        cx.ps_bf = ps.bitcast(BF16)
        cx.ots = ots_d
        if has_rw:
            cx.xmix, cx.rkv, cx.ygs, cx.dbg_y = xmix_d, rkv_d, ygs_d, dbg_d
            cx.t_xmix, cx.t_rkv = Tok(), Tok()
            cx.rw_masks_d = dram["rw_masks"]
        cx.ident = A.alloc(F32, 128)
        cx.cols = A.alloc(F32, ncols)
        cx.ones_bf = A.alloc(BF16, 128)
        dsc = P.dsem()
        P.dma("sp", cx.ident, ident_d, dsc, (), (cx.t_const,))
        P.dma("sp", cx.cols, cols_d, dsc, (), (cx.t_const,))
        P.memset("pool", cx.ones_bf, 1.0 / D, (cx.t_const,))
        cx.ones1_bf = A.alloc(BF16, 128)
        P.memset("pool", cx.ones1_bf, 1.0, (cx.t_const,))
        P.barrier()
        cur = None if "tin" in names else xs_in
        n_ph = len(phases)
        for i, p in enumerate(phases):
            last = (i == n_ph - 1)
            if p[0] == "tin":
                dst = out_ap if last else xs_a
                phase_tin(cx, x_tm, dst)
                cur = dst
            elif p[0] == "tout":
                phase_tout(cx, cur, out_ap)
            elif p[0] == "ffn":
                l, s = p[1], p[2]
                nxt_is_out = last
                dst = out_ap if nxt_is_out else xs_a
                if cur is not xs_a and dst is xs_a:
                    pass
                k = (l * 2 + s)
                phase_ffn(cx, cur, dst, dram[f"w13_{l}{s}"], dram[f"w2_{l}{s}"], cx.cols[:, k * 16:(k + 1) * 16])
                cur = dst
            elif p[0] == "rwkv":
                dst = out_ap if last else xs_a
                Wd = {k: dram["rw_" + k] for k in ("w_rkv", "w1", "w2", "a1", "a2", "g1", "g2", "w_o")}
                phase_rwkv(cx, cur, dst, Wd, cx.cols[:, 4 * 16:5 * 16], dram["rw_cols"])
                cur = dst
            elif p[0] == "mla":
                dst = out_ap if last else xs_a
                phase_mla(cx, cur, dst, dram["mla_wd"], dram["mla_wuq"], dram["mla_wukv"], dram["mla_wo"],
                          cx.cols[:, 5 * 16:6 * 16], dram["mla_cols"], dram["mla_pos"])
                cur = dst
            else:
                raise ValueError(p)
        P.barrier()
        P.emit(block)
    return nc, ins, out_name, P


def host_consts(inputs):
    ident = np.eye(128, dtype=np.float32)
    fn = inputs["ffn_norm"]
    cols = pack_cols([fn[0, 0], fn[0, 1], fn[1, 0], fn[1, 1],
                      inputs["mix_norm"][0], inputs["mix_norm"][1], np.zeros(D), np.zeros(D)])
    return ident, cols


ROPE_PERM = np.concatenate([np.arange(32, 64), np.arange(0, 32)])


def mla_host(inputs, b):
    wd = inputs["mla_w_down"][0]
    wd_ext = np.ascontiguousarray(np.concatenate([wd, wd[:, 1024 + ROPE_PERM]], axis=1))
    wuq = inputs["mla_w_uq"][0]
    wuq_ext = np.ascontiguousarray(np.concatenate([wuq, wuq[:, :, 128 + ROPE_PERM]], axis=2))
    qn, kn = inputs["mla_q_norm"][0], inputs["mla_k_norm"][0]
    mc = np.zeros((128, 16), np.float32)
    mc[:, 0:4] = inputs["mla_q_a_norm"][0].reshape(4, 128).T
    mc[:, 4:8] = inputs["mla_kv_a_norm"][0].reshape(4, 128).T
    mc[:, 8] = qn[0:128]
    mc[:, 9] = kn[0:128]
    mc[0:64, 10] = qn[128:192]
    mc[0:64, 11] = qn[128 + ROPE_PERM]
    mc[0:64, 12] = kn[128:192]
    mc[0:64, 13] = kn[128 + ROPE_PERM]
    inv_freq = (np.float32(10000.0) ** (-np.arange(0, 64, 2, dtype=np.float32) / np.float32(64))).astype(np.float32)
    mc[0:64, 14] = np.concatenate([inv_freq, inv_freq])
    mc[0:32, 15] = -1.0
    mc[32:64, 15] = 1.0
    pos = np.ascontiguousarray(np.broadcast_to(inputs["positions"][b][None, :], (64, T))).astype(np.int32)
    return {"mla_wd": wd_ext, "mla_wuq": wuq_ext, "mla_wukv": np.ascontiguousarray(inputs["mla_w_ukv"][0]),
            "mla_wo": np.ascontiguousarray(inputs["mla_w_o"][0]), "mla_cols": mc, "mla_pos": pos}


def rwkv_host(inputs):
    g = lambda k: inputs["rwkv_" + k][0]
    vecs = [g("mu")[j] for j in range(6)] + [g("w0"), g("a0"), g("k_k"), g("k_a"), g("r_k").reshape(-1), g("ln_w"), g("ln_b")]
    idx = np.arange(128)
    same = (idx[:, None] // 64) == (idx[None, :] // 64)
    ti, tj = idx[:, None] % 64, idx[None, :] % 64
    ML_s = (same & (ti > tj)).astype(np.float32)
    MU_s = (same & (ti < tj)).astype(np.float32)
    MU_i = (same & (ti <= tj)).astype(np.float32)
    masks = np.ascontiguousarray(np.concatenate([np.tile(m, (1, 4)) for m in (ML_s, MU_s, MU_i)], axis=1))
    return {"rw_w_rkv": np.ascontiguousarray(g("w_rkv")), "rw_w1": g("w1"), "rw_w2": g("w2"), "rw_a1": g("a1"), "rw_a2": g("a2"),
            "rw_g1": g("g1"), "rw_g2": g("g2"), "rw_w_o": g("w_o"), "rw_cols": pack_cols(vecs), "rw_masks": masks}


PHASES = [("tin",), ("ffn", 0, 0), ("rwkv",), ("ffn", 0, 1), ("ffn", 1, 0), ("mla",), ("ffn", 1, 1), ("tout",)]


def make_feeds(inputs, b, shared=None):
    if shared is None:
        shared = {}
        ident, cols = host_consts(inputs)
        shared["ident"] = ident
        shared["cols"] = cols
        for l in range(2):
            for s_ in range(2):
                shared[f"w13_{l}{s_}"] = np.ascontiguousarray(np.asarray(inputs["ffn_w13"][l, s_], np.float32))
                shared[f"w2_{l}{s_}"] = np.ascontiguousarray(np.asarray(inputs["ffn_w2"][l, s_], np.float32))
        shared.update(rwkv_host(inputs))
        m = mla_host(inputs, 0)
        m.pop("mla_pos")
        shared.update(m)
    feeds = dict(shared)
    feeds["x"] = np.ascontiguousarray(np.asarray(inputs["x"][b], np.float32))
    feeds["mla_pos"] = np.ascontiguousarray(
        np.broadcast_to(np.asarray(inputs["positions"][b], np.int32)[None, :], (64, T)))
    return feeds, shared


def kernel(**inputs):
    inputs = {k: np.asarray(v) for k, v in inputs.items()}
    nb = inputs["x"].shape[0]
    nc, ins, out_name, _ = build(PHASES)
    in_maps = []
    shared = None
    for b in range(nb):
        feeds, shared = make_feeds(inputs, b, shared)
        in_maps.append({k: feeds[k] for k in ins})
    res = run_bass_kernel_spmd(nc, in_maps, core_ids=list(range(nb)))
    out = np.stack([np.asarray(res.results[b][out_name], np.float32) for b in range(nb)], axis=0)
    return out
```

```python
import numpy as np
import concourse.bass as bass
import concourse.mybir as mybir
from concourse.bass_utils import run_bass_kernel_spmd

F32 = mybir.dt.float32
BF16 = mybir.dt.bfloat16
I32 = mybir.dt.int32
AF = mybir.ActivationFunctionType
ALU = mybir.AluOpType
AX = mybir.AxisListType

T = 2048
D = 2048
FF = 5504
NCH = 16
NFC = 43
RMS_EPS = 1e-6
SB_BYTES = 207872


class Tok:
    __slots__ = ("w", "r", "excl")

    def __init__(self, excl=False):
        self.w = None
        self.r = {}
        self.excl = excl


class DSem:
    def __init__(self, h, idx):
        self.h = h
        self.idx = idx
        self.count = 0


class Prog:
    CE = ("pe", "act", "dve", "pool")

    def __init__(self, nc, esems, dsems):
        self.nc = nc
        self.code = {e: [] for e in ("pe", "act", "dve", "pool", "sp")}
        self.esem = esems
        self.ecnt = {e: 0 for e in self.CE}
        self.seen = {e: {} for e in self.code}
        self.free_dsems = [DSem(h, i) for i, h in enumerate(dsems)]
        self.all_dsems = list(self.free_dsems)
        self.ninstr = 0

    def dsem(self):
        return self.free_dsems.pop()

    def _need(self, eng, waits, ev):
        if ev is None:
            return
        kind, s, v = ev
        if kind == "d":
            v = s.count
            key = ("d", s.idx)
        else:
            if s == eng and eng == "pe":
                return
            key = ("e", s)
        if self.seen[eng].get(key, 0) >= v:
            return
        if waits.get(key, (None, 0))[1] < v:
            waits[key] = (s, v)

    def op(self, eng, fn, reads=(), writes=(), inc=True, dsem=None):
        if any(t.excl for t in reads):
            writes = tuple(writes) + tuple(t for t in reads if t.excl)
            reads = tuple(t for t in reads if not t.excl)
        waits = {}
        for t in reads:
            self._need(eng, waits, t.w)
        for t in writes:
            if t.w is not None and not (t.w[0] == "e" and t.w[1] == eng):
                self._need(eng, waits, t.w)
            for ev in t.r.values():
                if not (ev[0] == "e" and ev[1] == eng):
                    self._need(eng, waits, ev)
        wl = []
        for key, (s, v) in waits.items():
            self.seen[eng][key] = v
            wl.append((s.h if key[0] == "d" else self.esem[s], v))
        if dsem is not None:
            dsem.count += 16
            ev = ("d", dsem, dsem.count)
            incspec = (dsem.h, 16)
            rkey = ("d", dsem.idx)
        else:
            if inc:
                self.ecnt[eng] += 1
                ev = ("e", eng, self.ecnt[eng])
                incspec = (self.esem[eng], 1)
            else:
                ev = ("e", eng, self.ecnt[eng] + 1)
                incspec = None
            rkey = ("e", eng)
        for t in reads:
            t.r[rkey] = ev
        for t in writes:
            t.w = ev
            t.r = {}
        self.code[eng].append((wl, fn, incspec))
        self.ninstr += 1

    def barrier(self):
        evs = [("e", e, self.ecnt[e]) for e in self.CE if self.ecnt[e] > 0]
        evs += [("d", d, d.count) for d in self.all_dsems if d.count > 0]
        for eng in self.code:
            waits = {}
            for ev in evs:
                if ev[0] == "e" and ev[1] == eng and eng == "pe":
                    continue
                self._need(eng, waits, ev)
            wl = []
            for key, (s, v) in waits.items():
                self.seen[eng][key] = v
                wl.append((s.h if key[0] == "d" else self.esem[s], v))
            if wl:
                self.code[eng].append((wl, None, None))

    def emit(self, block):
        def mk(name):
            def body(e):
                for wl, fn, incspec in self.code[name]:
                    for h, v in wl:
                        e.wait_ge(h, v)
                    if fn is None:
                        continue
                    ins = fn(e)
                    if incspec is not None:
                        ins.then_inc(incspec[0], incspec[1])
            return body

        block.tensor(mk("pe"))
        block.scalar(mk("act"))
        block.vector(mk("dve"))
        block.gpsimd(mk("pool"))
        block.sync(mk("sp"))

    def mm(self, out, lhsT, rhs, start, stop, reads, writes, inc=None):
        self.op("pe", lambda e: e.matmul(out, lhsT, rhs, start=start, stop=stop),
                reads, writes, inc=(stop if inc is None else inc))

    def tr(self, out, in_, ident, reads, writes, inc=True):
        self.op("pe", lambda e: e.transpose(out, in_, ident), reads, writes, inc=inc)

    def dma(self, q, out, in_, dsem, reads, writes):
        self.op(q, lambda e: e.dma_start(out=out, in_=in_), reads, writes, dsem=dsem)

    def actf(self, out, in_, func, reads, writes, bias=None, scale=None, eng="act"):
        kw = {}
        if bias is not None:
            kw["bias"] = bias
        if scale is not None:
            kw["scale"] = scale
        self.op("act", lambda e: e.activation(out, in_, func, **kw), reads, writes)

    def copy(self, eng, out, in_, reads, writes):
        if eng == "act":
            self.op("act", lambda e: e.copy(out, in_), reads, writes)
        else:
            self.op(eng, lambda e: e.tensor_copy(out, in_), reads, writes)

    def tt(self, eng, out, in0, in1, op, reads, writes):
        self.op(eng, lambda e: e.tensor_tensor(out, in0, in1, op), reads, writes)

    def ts(self, eng, out, in0, s1, s2, op0, op1, reads, writes):
        if s2 is None:
            self.op(eng, lambda e: e.tensor_scalar(out, in0, s1, None, op0), reads, writes)
        else:
            self.op(eng, lambda e: e.tensor_scalar(out, in0, s1, s2, op0, op1), reads, writes)

    def stt(self, eng, out, in0, scalar, in1, op0, op1, reads, writes):
        self.op(eng, lambda e: e.scalar_tensor_tensor(out, in0, scalar, in1, op0, op1), reads, writes)

    def memset(self, eng, ap, val, writes):
        self.op(eng, lambda e: e.memset(ap, val), (), writes)


class Arena:
    def __init__(self, t32, nbytes):
        self.v = {F32: t32, BF16: t32.bitcast(BF16), I32: t32.bitcast(I32)}
        self.cap = nbytes
        self.top = 0
        self.marks = []

    def alloc(self, dtype, n, parts=128, p0=0):
        sz = 2 if dtype == BF16 else 4
        off = (self.top + 63) // 64 * 64
        self.top = off + n * sz
        assert self.top <= self.cap, f"SBUF arena overflow {self.top} > {self.cap}"
        return self.v[dtype][p0:p0 + parts, off // sz: off // sz + n]

    def mark(self):
        self.marks.append(self.top)

    def release(self):
        self.top = self.marks.pop()


class Ctx:
    pass


def phase_tin(cx, x_tm, xs_dst):
    P, A = cx.P, cx.A
    A.mark()
    xin = [A.alloc(F32, D) for _ in range(2)]
    xo = [A.alloc(F32, NCH * 128) for _ in range(2)]
    t_in = [Tok() for _ in range(2)]
    t_o = [Tok() for _ in range(2)]
    ds_in = [P.dsem() for _ in range(2)]
    ds_o = [P.dsem() for _ in range(2)]
    t_ps = [Tok(True) for _ in range(2)]
    dst_v = xs_dst.rearrange("c p t -> p c t")
    for tb in range(T // 128):
        s = tb % 2
        P.dma("sp", xin[s], x_tm[tb * 128:(tb + 1) * 128, :], ds_in[s], (), (t_in[s],))
        for q in range(4):
            b = (tb * 4 + q) % 2
            bank = cx.bank(b)
            for i in range(4):
                c = q * 4 + i
                P.tr(bank[:, i * 128:(i + 1) * 128], xin[s][:, c * 128:(c + 1) * 128], cx.ident,
                     (t_in[s],), (t_ps[b],), inc=(i == 3))
            eng = "dve" if q % 2 == 0 else "act"
            P.copy(eng, xo[s][:, q * 512:(q + 1) * 512], bank, (t_ps[b],), (t_o[s],))
        P.dma("sp", dst_v[:, :, tb * 128:(tb + 1) * 128],
              xo[s].rearrange("p (c t) -> p c t", c=NCH), ds_o[s], (t_o[s],), cx.xs_tok(None, tb // 4))
    P.barrier()
    for d in ds_in + ds_o:
        P.free_dsems.append(d)
    A.release()


def phase_tout(cx, xs_src, out_tm):
    P, A = cx.P, cx.A
    A.mark()
    xin = [A.alloc(F32, NCH * 128) for _ in range(2)]
    xo = [A.alloc(F32, D) for _ in range(2)]
    t_in = [Tok() for _ in range(2)]
    t_o = [Tok() for _ in range(2)]
    ds_in = [P.dsem() for _ in range(2)]
    ds_o = [P.dsem() for _ in range(2)]
    t_ps = [Tok(True) for _ in range(2)]
    src_v = xs_src.rearrange("c p t -> p c t")
    for tb in range(T // 128):
        s = tb % 2
        P.dma("sp", xin[s].rearrange("p (c t) -> p c t", c=NCH), src_v[:, :, tb * 128:(tb + 1) * 128],
              ds_in[s], cx.xs_tok(None, tb // 4), (t_in[s],))
        for q in range(4):
            b = (tb * 4 + q) % 2
            bank = cx.bank(b)
            for i in range(4):
                c = q * 4 + i
                P.tr(bank[:, i * 128:(i + 1) * 128], xin[s][:, c * 128:(c + 1) * 128], cx.ident,
                     (t_in[s],), (t_ps[b],), inc=(i == 3))
            eng = "dve" if q % 2 == 0 else "act"
            P.copy(eng, xo[s][:, q * 512:(q + 1) * 512], bank, (t_ps[b],), (t_o[s],))
        P.dma("sp", out_tm[tb * 128:(tb + 1) * 128, :], xo[s], ds_o[s], (t_o[s],), (cx.t_out,))
    P.barrier()
    for d in ds_in + ds_o:
        P.free_dsems.append(d)
    A.release()


def rmsnorm_tile(cx, xt, t_xt, gcol, hT_out, t_h, ntok, sqb, t_sq, rstd, t_rstd, bank, t_bank):
    P = cx.P
    xv = xt.rearrange("p (c t) -> p c t", c=NCH)
    for c in range(NCH):
        s = c % len(sqb)
        P.actf(sqb[s][:, :ntok], xv[:, c, :], AF.Square, (t_xt,), (t_sq[s],))
        P.mm(bank[:, :ntok], cx.ones_bf, sqb[s][:, :ntok], c == 0, c == NCH - 1,
             (t_sq[s], cx.t_const), (t_bank,), inc=True)
    P.ts("dve", rstd[:, :ntok], bank[:, :ntok], RMS_EPS, None, ALU.add, None, (t_bank,), (t_rstd,))
    P.actf(rstd[:, :ntok], rstd[:, :ntok], AF.Ln, (t_rstd,), (t_rstd,))
    P.actf(rstd[:, :ntok], rstd[:, :ntok], AF.Exp, (t_rstd,), (t_rstd,), scale=-0.5)
    for c in range(NCH):
        P.stt("dve", hT_out(c), xv[:, c, :], gcol[:, c:c + 1], rstd[:, :ntok], ALU.mult, ALU.mult,
              (t_xt, t_rstd, cx.t_const), (t_h,))


def phase_ffn(cx, xs_src, xs_dst, w13, w2, gcol):
    P, A = cx.P, cx.A
    A.mark()
    HALF = 1024
    NTT = HALF // 512
    hT = A.alloc(BF16, NCH * HALF)
    hTv = hT.rearrange("p (c t) -> p c t", c=NCH)
    actT = A.alloc(BF16, NFC * HALF)
    actTv = actT.rearrange("p (j t) -> p j t", j=NFC)
    t_h = Tok()
    t_act = [Tok() for _ in range(NFC)]
    sqb = [A.alloc(BF16, 512) for _ in range(3)]
    t_sq = [Tok() for _ in range(3)]
    rstd = A.alloc(F32, 512)
    t_rstd = Tok()
    sg = [A.alloc(F32, 512) for _ in range(2)]
    t_sg = [Tok() for _ in range(2)]
    xold = [A.alloc(F32, 512) for _ in range(2)]
    t_xold = [Tok() for _ in range(2)]
    ds_xold = [P.dsem() for _ in range(2)]
    xnew = [A.alloc(F32, 512) for _ in range(2)]
    t_xnew = [Tok() for _ in range(2)]
    ds_xnew = [P.dsem() for _ in range(2)]
    tops = []
    A.mark()
    xt = [A.alloc(F32, NCH * 512) for _ in range(2)]
    tops.append(A.top); A.release(); A.mark()
    w13s = [A.alloc(F32, 2 * NCH * 128) for _ in range(2)]
    w13b = [A.alloc(BF16, 2 * NCH * 128) for _ in range(2)]
    tops.append(A.top); A.release(); A.mark()
    w2s = [A.alloc(F32, NFC * 128) for _ in range(2)]
    w2b = [A.alloc(BF16, NFC * 128) for _ in range(2)]
    tops.append(A.top); A.release()
    A.top = max(tops)
    t_reg = [Tok() for _ in range(2)]
    t_regb = [Tok() for _ in range(2)]
    t_regb2 = [Tok() for _ in range(2)]
    ds_stage = [P.dsem() for _ in range(2)]
    src_v = xs_src.rearrange("c p t -> p c t")
    dst_v = xs_dst.rearrange("c p t -> p c t")
    w13v = w13.rearrange("(c p) f -> p c f", p=128)
    w2v = w2.rearrange("(j p) e -> p j e", p=128)
    t_bA = Tok(True)
    t_bB = [Tok(True) for _ in range(4)]
    t_bC = [Tok(True) for _ in range(2)]
    for th in range(T // HALF):
        tok0 = th * HALF
        for tt in range(NTT):
            s = tt % 2
            P.dma("sp", xt[s].rearrange("p (c t) -> p c t", c=NCH),
                  src_v[:, :, tok0 + tt * 512: tok0 + (tt + 1) * 512], ds_stage[s], cx.xs_tok(None, th * NTT + tt), (t_reg[s],))
            rmsnorm_tile(cx, xt[s], t_reg[s], gcol, lambda c, tt=tt: hTv[:, c, tt * 512:(tt + 1) * 512], t_h,
                         512, sqb, t_sq, rstd, t_rstd, cx.bank(6), t_bA)
        P.barrier()
        def b_load(j):
            if not DMACAST:
                s = j % 2
                stg = w13s[s].rearrange("p (g c f) -> p g c f", g=2, c=NCH)
                P.dma("sp", stg[:, 0], w13v[:, :, j * 128:(j + 1) * 128], ds_stage[s], (), (t_reg[s],))
                P.dma("sp", stg[:, 1], w13v[:, :, FF + j * 128: FF + (j + 1) * 128], ds_stage[s], (), (t_reg[s],))

        def b_cast(j):
            s = j % 2
            stg = w13s[s].rearrange("p (g c f) -> p g c f", g=2, c=NCH)
            stb = w13b[s].rearrange("p (g c f) -> p g c f", g=2, c=NCH)
            if DMACAST:
                P.dma("pool", stb[:, 0], w13v[:, :, j * 128:(j + 1) * 128], ds_stage[s], (), (t_regb[s],))
                P.dma("pool", stb[:, 1], w13v[:, :, FF + j * 128: FF + (j + 1) * 128], ds_stage[s], (), (t_regb2[s],))
            else:
                P.copy("dve", stb[:, 0], stg[:, 0], (t_reg[s],), (t_regb[s],))
                P.copy("act", stb[:, 1], stg[:, 1], (t_reg[s],), (t_regb2[s],))

        b_load(0)
        b_load(1)
        b_cast(0)
        for j in range(NFC):
            s = j % 2
            stb = w13b[s].rearrange("p (g c f) -> p g c f", g=2, c=NCH)
            if j + 1 < NFC:
                b_cast(j + 1)
            if j + 2 < NFC:
                b_load(j + 2)
            for tt in range(NTT):
                bi = (j * NTT + tt) % 2
                bg, bu = cx.bank(2 * bi), cx.bank(2 * bi + 1)
                tg, tu = t_bB[2 * bi], t_bB[2 * bi + 1]
                rhs_t = slice(tt * 512, (tt + 1) * 512)
                for c in range(NCH):
                    P.mm(bg, stb[:, 0, c, :], hTv[:, c, rhs_t], c == 0, c == NCH - 1, (t_regb[s], t_h), (tg,))
                for c in range(NCH):
                    P.mm(bu, stb[:, 1, c, :], hTv[:, c, rhs_t], c == 0, c == NCH - 1, (t_regb2[s], t_h), (tu,))
                P.actf(sg[bi], bg, AF.Silu, (tg,), (t_sg[bi],))
                P.tt("dve", actTv[:, j, rhs_t], sg[bi], bu, ALU.mult, (t_sg[bi], tu), (t_act[j],))
        P.barrier()
        def c_load(e):
            s = e % 2
            stg = w2s[s].rearrange("p (j e) -> p j e", j=NFC)
            P.dma("sp", stg[:, 0:22, :], w2v[:, 0:22, e * 128:(e + 1) * 128], ds_stage[s], (), (t_reg[s],))
            P.dma("sp", stg[:, 22:NFC, :], w2v[:, 22:NFC, e * 128:(e + 1) * 128], ds_stage[s], (), (t_reg[s],))

        def c_cast(e):
            s = e % 2
            stg = w2s[s].rearrange("p (j e) -> p j e", j=NFC)
            stb = w2b[s].rearrange("p (j e) -> p j e", j=NFC)
            P.copy("dve", stb[:, 0:22, :], stg[:, 0:22, :], (t_reg[s],), (t_regb[s],))
            P.copy("act", stb[:, 22:NFC, :], stg[:, 22:NFC, :], (t_reg[s],), (t_regb2[s],))

        c_load(0)
        c_load(1)
        c_cast(0)
        for e in range(NCH):
            s = e % 2
            stb = w2b[s].rearrange("p (j e) -> p j e", j=NFC)
            if e + 1 < NCH:
                c_cast(e + 1)
            if e + 2 < NCH:
                c_load(e + 2)
            for tt in range(NTT):
                bi = (e * NTT + tt) % 2
                bk, tb_ = cx.bank(4 + bi), t_bC[bi]
                rhs_t = slice(tt * 512, (tt + 1) * 512)
                tsl = slice(tok0 + tt * 512, tok0 + (tt + 1) * 512)
                P.dma("act", xold[bi], src_v[:, e, tsl], ds_xold[bi], cx.xs_tok(e, th * NTT + tt), (t_xold[bi],))
                for j in range(NFC):
                    P.mm(bk, stb[:, j, :], actTv[:, j, rhs_t], j == 0, j == NFC - 1,
                         (t_regb[s] if j < 22 else t_regb2[s], t_act[j]), (tb_,))
                P.stt("dve", xnew[bi], bk, 0.5, xold[bi], ALU.mult, ALU.add, (tb_, t_xold[bi]), (t_xnew[bi],))
                P.dma("pool", dst_v[:, e, tsl], xnew[bi], ds_xnew[bi], (t_xnew[bi],), cx.xs_tok(e, th * NTT + tt))
        P.barrier()
    for d in ds_xold + ds_xnew + ds_stage:
        P.free_dsems.append(d)
    A.release()


def phase_ffn2(cx, xs_src, xs_dst, w13, w2, gcol):
    P, A = cx.P, cx.A
    A.mark()
    HALF, NTT, TA = 1024, 2, 256
    hT = A.alloc(BF16, NCH * HALF); hTv = hT.rearrange("p (c t) -> p c t", c=NCH); t_h = Tok()
    actT = A.alloc(BF16, NFC * HALF); actTv = actT.rearrange("p (j t) -> p j t", j=NFC)
    t_act = [Tok() for _ in range(NFC)]
    w13b = [A.alloc(BF16, 2 * NCH * 128) for _ in range(2)]
    t_wg = [Tok() for _ in range(2)]; t_wu = [Tok() for _ in range(2)]; ds_w13 = [P.dsem() for _ in range(2)]
    w2b = [A.alloc(BF16, NFC * 128) for _ in range(2)]
    t_w2a = [Tok() for _ in range(2)]; t_w2b = [Tok() for _ in range(2)]; ds_w2 = [P.dsem() for _ in range(2)]
    xt = [A.alloc(F32, NCH * TA) for _ in range(2)]; t_xt = [Tok() for _ in range(2)]; ds_xt = [P.dsem() for _ in range(2)]
    sqb = [A.alloc(BF16, TA) for _ in range(3)]; t_sq = [Tok() for _ in range(3)]
    rstd = A.alloc(F32, TA); t_rstd = Tok()
    sg = [A.alloc(F32, 512) for _ in range(2)]; t_sg = [Tok() for _ in range(2)]
    xold = [A.alloc(F32, 512) for _ in range(2)]; t_xold = [Tok() for _ in range(2)]
    ds_xold = [P.dsem() for _ in range(2)]; ds_xnew = [P.dsem() for _ in range(2)]
    src_v = xs_src.rearrange("c p t -> p c t")
    dst_v = xs_dst.rearrange("c p t -> p c t")
    w13v = w13.rearrange("(c p) f -> p c f", p=128)
    w2v = w2.rearrange("(j p) e -> p j e", p=128)
    t_bA = Tok(True); t_bB = [Tok(True) for _ in range(4)]; t_bC = [Tok(True) for _ in range(2)]
    na = [0]

    def a_tile(th, i):
        s = na[0] % 2; na[0] += 1
        t0 = th * HALF + i * TA
        P.dma("sp", xt[s].rearrange("p (c t) -> p c t", c=NCH), src_v[:, :, t0:t0 + TA], ds_xt[s],
              cx.xs_tok(None, t0 // 512), (t_xt[s],))
        rmsnorm_tile(cx, xt[s], t_xt[s], gcol, lambda c: hTv[:, c, i * TA:(i + 1) * TA], t_h, TA, sqb, t_sq, rstd, t_rstd,
                     cx.bank(6), t_bA)

    def b_fetch(j):
        s = j % 2
        stb = w13b[s].rearrange("p (g c f) -> p g c f", g=2, c=NCH)
        P.dma("pool", stb[:, 0], w13v[:, :, j * 128:(j + 1) * 128], ds_w13[s], (), (t_wg[s],))
        P.dma("pool", stb[:, 1], w13v[:, :, FF + j * 128: FF + (j + 1) * 128], ds_w13[s], (), (t_wu[s],))

    def c_fetch(e):
        s = e % 2
        stb = w2b[s].rearrange("p (j e) -> p j e", j=NFC)
        P.dma("pool", stb[:, 0:22, :], w2v[:, 0:22, e * 128:(e + 1) * 128], ds_w2[s], (), (t_w2a[s],))
        P.dma("pool", stb[:, 22:NFC, :], w2v[:, 22:NFC, e * 128:(e + 1) * 128], ds_w2[s], (), (t_w2b[s],))

    b_fetch(0)
    b_fetch(1)
    for i in range(HALF // TA):
        a_tile(0, i)
    for th in range(T // HALF):
        tok0 = th * HALF
        for j in range(NFC):
            s = j % 2
            stb = w13b[s].rearrange("p (g c f) -> p g c f", g=2, c=NCH)
            for tt in range(NTT):
                bi = (j * NTT + tt) % 2
                bg, bu = cx.bank(2 * bi), cx.bank(2 * bi + 1)
                tg, tu = t_bB[2 * bi], t_bB[2 * bi + 1]
                rhs_t = slice(tt * 512, (tt + 1) * 512)
                for c in range(NCH):
                    P.mm(bg, stb[:, 0, c, :], hTv[:, c, rhs_t], c == 0, c == NCH - 1, (t_wg[s], t_h), (tg,))
                for c in range(NCH):
                    P.mm(bu, stb[:, 1, c, :], hTv[:, c, rhs_t], c == 0, c == NCH - 1, (t_wu[s], t_h), (tu,))
                P.actf(sg[bi], bg, AF.Silu, (tg,), (t_sg[bi],))
                P.tt("dve", actTv[:, j, rhs_t], sg[bi], bu, ALU.mult, (t_sg[bi], tu), (t_act[j],))
            if j + 2 < NFC:
                b_fetch(j + 2)
            if j == NFC - 3:
                c_fetch(0)
            if j == NFC - 2:
                c_fetch(1)
        if th + 1 < T // HALF:
            b_fetch(0)
            b_fetch(1)
        for e in range(NCH):
            s = e % 2
            stb = w2b[s].rearrange("p (j e) -> p j e", j=NFC)
            for tt in range(NTT):
                bi = (e * NTT + tt) % 2
                bk, tb_ = cx.bank(4 + bi), t_bC[bi]
                rhs_t = slice(tt * 512, (tt + 1) * 512)
                tsl = slice(tok0 + tt * 512, tok0 + (tt + 1) * 512)
                xtok = cx.xs_tok(e, th * NTT + tt)
                P.dma("act", xold[bi], src_v[:, e, tsl], ds_xold[bi], xtok, (t_xold[bi],))
                for j in range(NFC):
                    P.mm(bk, stb[:, j, :], actTv[:, j, rhs_t], j == 0, j == NFC - 1,
                         (t_w2a[s] if j < 22 else t_w2b[s], t_act[j]), (tb_,))
                P.stt("dve", xold[bi], bk, 0.5, xold[bi], ALU.mult, ALU.add, (tb_, t_xold[bi]), (t_xold[bi],))
                P.dma("sp", dst_v[:, e, tsl], xold[bi], ds_xnew[bi], (t_xold[bi],), xtok)
            if e + 2 < NCH:
                c_fetch(e + 2)
            if th + 1 < T // HALF and e % 4 == 3:
                a_tile(th + 1, e // 4)
    P.barrier()
    for d in ds_w13 + ds_w2 + ds_xt + ds_xold + ds_xnew:
        P.free_dsems.append(d)
    A.release()


def norm_from_bank(cx, rstd, t_rstd, bank, t_bank, n, mean_scale, eps, parts=128):
    P = cx.P
    P.ts("dve", rstd[:parts, :n], bank[:parts, :n], mean_scale, eps, ALU.mult, ALU.add, (t_bank,), (t_rstd,))
    P.actf(rstd[:parts, :n], rstd[:parts, :n], AF.Ln, (t_rstd,), (t_rstd,))
    P.actf(rstd[:parts, :n], rstd[:parts, :n], AF.Exp, (t_rstd,), (t_rstd,), scale=-0.5)


MLA_H = 16
STOP = 0
DBG = False
DMACAST = False
FFN2 = False
SM_SCALE = 1.0 / float(np.sqrt(192.0))


def phase_mla(cx, xs_src, xs_dst, wd, wuq, wukv, wo, gcol, mcols_d, pos_d):
    P, A = cx.P, cx.A
    A.mark()
    src_v = xs_src.rearrange("c p t -> p c t")
    dst_v = xs_dst.rearrange("c p t -> p c t")
    mc = A.alloc(F32, 16)
    t_mc = Tok()
    dsm = P.dsem()
    P.dma("sp", mc, mcols_d, dsm, (), (t_mc,))
    cqn = A.alloc(BF16, 4 * T); cqnv = cqn.rearrange("p (c t) -> p c t", c=4)
    ckvn = A.alloc(BF16, 4 * T); ckvnv = ckvn.rearrange("p (c t) -> p c t", c=4)
    kpe = A.alloc(F32, T)
    kpesw = A.alloc(F32, T)
    t_cqn, t_ckvn, t_kpe = Tok(), Tok(), Tok()
    rstd = A.alloc(F32, 512); t_rstd = Tok()
    sqb = [A.alloc(BF16, 512) for _ in range(3)]; t_sq = [Tok() for _ in range(3)]
    A.mark()
    wdb = A.alloc(BF16, NCH * 1152); wdbv = wdb.rearrange("p (c f) -> p c f", c=NCH)
    t_wdb = Tok()
    stg = [A.alloc(F32, NCH * 128) for _ in range(2)]; t_stg = [Tok() for _ in range(2)]
    ds_stg = [P.dsem() for _ in range(2)]
    wdv = wd.rearrange("(c p) f -> p c f", p=128)
    for i in range(9):
        s = i % 2
        P.dma("sp", stg[s].rearrange("p (c f) -> p c f", c=NCH), wdv[:, :, i * 128:(i + 1) * 128], ds_stg[s], (), (t_stg[s],))
        P.copy("act" if i % 2 == 0 else "dve", wdbv[:, :, i * 128:(i + 1) * 128], stg[s].rearrange("p (c f) -> p c f", c=NCH), (t_stg[s],), (t_wdb,))
    if STOP == 11:
        P.barrier(); A.release(); A.release(); return
    xt = A.alloc(F32, NCH * 512); t_xt = Tok(); ds_xt = P.dsem()
    hT = A.alloc(BF16, NCH * 512); hTv = hT.rearrange("p (c t) -> p c t", c=NCH); t_h = Tok()
    cT = A.alloc(F32, 8 * 512); cTv = cT.rearrange("p (c t) -> p c t", c=8); t_cT = Tok()
    t_b = [Tok(True) for _ in range(8)]
    for tt in range(4):
        tsl = slice(tt * 512, (tt + 1) * 512)
        P.dma("sp", xt.rearrange("p (c t) -> p c t", c=NCH), src_v[:, :, tsl], ds_xt, cx.xs_tok(None, tt), (t_xt,))
        rmsnorm_tile(cx, xt, t_xt, gcol, lambda c: hTv[:, c, :], t_h, 512, sqb, t_sq, rstd, t_rstd, cx.bank(6), t_b[6])
        for oc in range(10):
            if STOP == 12 or (STOP == 13 and oc >= 8):
                break
            b = oc % 2
            bank = cx.bank(b)
            if oc < 8:
                for c in range(NCH):
                    P.mm(bank, wdbv[:, c, oc * 128:(oc + 1) * 128], hTv[:, c, :], c == 0, c == NCH - 1, (t_wdb, t_h), (t_b[b],))
                eng = "act" if oc % 2 == 0 else "dve"
                P.copy(eng, cTv[:, oc, :], bank, (t_b[b],), (t_cT,))
            else:
                c0 = 1024 + (oc - 8) * 64
                for c in range(NCH):
                    P.mm(bank[0:64, :], wdbv[:, c, c0:c0 + 64], hTv[:, c, :], c == 0, c == NCH - 1, (t_wdb, t_h), (t_b[b],))
                dstb = kpe if oc == 8 else kpesw
                P.copy("act", dstb[0:64, tsl], bank[0:64, :], (t_b[b],), (t_kpe,))
        for which in range(2):
            if STOP in (12, 13, 14):
                break
            for c in range(4):
                s = c % 3
                P.actf(sqb[s], cTv[:, which * 4 + c, :], AF.Square, (t_cT,), (t_sq[s],))
                P.mm(cx.bank(6), cx.ones1_bf, sqb[s], c == 0, c == 3, (t_sq[s], cx.t_const), (t_b[6],), inc=True)
            norm_from_bank(cx, rstd, t_rstd, cx.bank(6), t_b[6], 512, 1.0 / 512.0, RMS_EPS)
            dstv, tk = (cqnv, t_cqn) if which == 0 else (ckvnv, t_ckvn)
            for c in range(4):
                P.stt("dve", dstv[:, c, tsl], cTv[:, which * 4 + c, :], mc[:, which * 4 + c: which * 4 + c + 1], rstd,
                      ALU.mult, ALU.mult, (t_cT, t_rstd, t_mc), (tk,))
    P.barrier()
    A.release()
    for d in ds_stg + [ds_xt]:
        P.free_dsems.append(d)
    if STOP == 1:
        A.release(); return
    Cq = A.alloc(F32, T); Sq = A.alloc(F32, T)
    t_tab = Tok()
    kperot = A.alloc(F32, T); sqkpe = A.alloc(BF16, T); t_kr = Tok()
    t_OT = Tok()
    A.mark()
    posi = A.alloc(I32, T); posf = A.alloc(F32, T); ang = posf
    t_pos, t_ang = Tok(), Tok()
    t_ang = t_pos
    Ck = A.alloc(F32, T); Sk = A.alloc(F32, T)
    tmp = A.alloc(F32, T); t_tmp = Tok()
    P.dma("sp", posi[0:64, :], pos_d, dsm, (), (t_pos,))
    P.copy("dve", posf[0:64, :], posi[0:64, :], (t_pos,), (t_pos,))
    TWO_PI = float(2.0 * np.pi)
    PI = float(np.pi)
    P.ts("dve", ang[0:64, :], posf[0:64, :], mc[0:64, 14:15], None, ALU.mult, None, (t_pos, t_mc), (t_ang,))
    ki = posi
    C1 = 6.28125
    C2 = float(2.0 * np.pi - 6.28125)

    def sin_of(out, shift):
        P.ts("dve", tmp[0:64, :], ang[0:64, :], shift, 1.0 / TWO_PI, ALU.add, ALU.mult, (t_ang,), (t_tmp,))
        P.copy("dve", ki[0:64, :], tmp[0:64, :], (t_tmp,), (t_ki,))
        P.copy("dve", tmp[0:64, :], ki[0:64, :], (t_ki,), (t_tmp,))
        P.ts("dve", out, ang[0:64, :], shift, None, ALU.add, None, (t_ang,), (t_tab,))
        P.stt("dve", out, tmp[0:64, :], -C1, out, ALU.mult, ALU.add, (t_tmp, t_tab), (t_tab,))
        P.stt("dve", out, tmp[0:64, :], -C2, out, ALU.mult, ALU.add, (t_tmp, t_tab), (t_tab,))
        P.ts("dve", tmp[0:64, :], out, PI, TWO_PI, ALU.is_gt, ALU.mult, (t_tab,), (t_tmp,))
        P.tt("dve", out, out, tmp[0:64, :], ALU.subtract, (t_tab, t_tmp), (t_tab,))
        P.ts("dve", out, out, -PI, PI, ALU.max, ALU.min, (t_tab,), (t_tab,))
        P.actf(out, out, AF.Sin, (t_tab,), (t_tab,))

    t_ki = Tok()
    sin_of(Sq[0:64, :], 0.0)
    sin_of(Cq[0:64, :], 0.5 * PI)
    P.ts("dve", Sq[0:64, :], Sq[0:64, :], mc[0:64, 15:16], None, ALU.mult, None, (t_tab, t_mc), (t_tab,))
    P.ts("dve", Ck[0:64, :], Cq[0:64, :], mc[0:64, 12:13], None, ALU.mult, None, (t_tab, t_mc), (t_tab,))
    P.ts("dve", Sk[0:64, :], Sq[0:64, :], mc[0:64, 13:14], None, ALU.mult, None, (t_tab, t_mc), (t_tab,))
    P.ts("dve", Cq[0:64, :], Cq[0:64, :], mc[0:64, 10:11], None, ALU.mult, None, (t_tab, t_mc), (t_tab,))
    P.ts("dve", Sq[0:64, :], Sq[0:64, :], mc[0:64, 11:12], None, ALU.mult, None, (t_tab, t_mc), (t_tab,))
    P.tt("dve", kperot[0:64, :], kpe[0:64, :], Ck[0:64, :], ALU.mult, (t_kpe, t_tab), (t_kr,))
    P.tt("dve", tmp[0:64, :], kpesw[0:64, :], Sk[0:64, :], ALU.mult, (t_kpe, t_tab), (t_tmp,))
    P.tt("dve", kperot[0:64, :], kperot[0:64, :], tmp[0:64, :], ALU.add, (t_kr, t_tmp), (t_kr,))
    P.memset("pool", sqkpe, 0.0, (t_kr,))
    P.actf(sqkpe[0:64, :], kpe[0:64, :], AF.Square, (t_kpe,), (t_kr,))
    P.barrier()
    A.release()
    if STOP == 2:
        A.release(); return
    A.mark()
    wq_s = [A.alloc(F32, 4 * 256) for _ in range(2)]; wq_b = [A.alloc(BF16, 4 * 256) for _ in range(2)]
    wk_s = [A.alloc(F32, 4 * 256) for _ in range(2)]; wk_b = [A.alloc(BF16, 4 * 256) for _ in range(2)]
    t_wqs = [Tok() for _ in range(2)]; t_wqb = [Tok() for _ in range(2)]
    t_wks = [Tok() for _ in range(2)]; t_wkb = [Tok() for _ in range(2)]
    ds_w = [P.dsem() for _ in range(2)]
    wuqv = wuq.rearrange("(c p) h f -> p c h f", p=128)
    wukvv = wukv.rearrange("(c p) h f -> p c h f", p=128)
    qn = A.alloc(BF16, T); qr = A.alloc(BF16, T); kn = A.alloc(BF16, T); kr = A.alloc(BF16, T)
    t_q, t_k = Tok(), Tok()
    P.memset("pool", qr, 0.0, (t_q,))
    P.memset("pool", kr, 0.0, (t_k,))
    P.memset("pool", sqb[1], 0.0, (t_sq[1],))
    Vb = A.alloc(BF16, 16 * 128); Vv = Vb.rearrange("p (s d) -> p s d", s=16); t_V = Tok()
    pT = [A.alloc(BF16, 512) for _ in range(3)]; t_pT = [Tok() for _ in range(3)]
    rs = [A.alloc(F32, 512) for _ in range(2)]; t_rs = [Tok() for _ in range(2)]
    t1 = [A.alloc(F32, 512) for _ in range(4)]; t2 = [A.alloc(F32, 512) for _ in range(4)]
    rst = [A.alloc(F32, 512) for _ in range(4)]
    sqn = [A.alloc(BF16, 512) for _ in range(4)]; sqp = [A.alloc(BF16, 512) for _ in range(4)]
    t_t1 = [Tok() for _ in range(4)]; t_t2 = [Tok() for _ in range(4)]; t_rst = [Tok() for _ in range(4)]
    t_sqn = [Tok() for _ in range(4)]; t_sqp = [Tok() for _ in range(4)]
    for tt in range(4):
        P.memset("pool", sqp[tt], 0.0, (t_sqp[tt],))
    Ob = [A.alloc(BF16, T) for _ in range(2)]; t_Ob = [Tok() for _ in range(2)]; ds_Ob = [P.dsem() for _ in range(2)]
    t_b = [Tok(True) for _ in range(8)]
    pj = 0
    sc = 0
    for h in range(MLA_H):
        s = h % 2
        P.dma("sp", wq_s[s].rearrange("p (c f) -> p c f", c=4), wuqv[:, :, h, :], ds_w[s], (), (t_wqs[s],))
        P.dma("sp", wk_s[s].rearrange("p (c f) -> p c f", c=4), wukvv[:, :, h, :], ds_w[s], (), (t_wks[s],))
        P.copy("act", wq_b[s], wq_s[s], (t_wqs[s],), (t_wqb[s],))
        P.copy("dve", wk_b[s], wk_s[s], (t_wks[s],), (t_wkb[s],))
        wqv = wq_b[s].rearrange("p (c f) -> p c f", c=4)
        wkv = wk_b[s].rearrange("p (c f) -> p c f", c=4)
        for g in range(4):
            b = 6 + (pj % 2); pj += 1
            for i in range(4):
                st = g * 4 + i
                for c in range(4):
                    P.mm(cx.bank(b)[:, i * 128:(i + 1) * 128], ckvnv[:, c, st * 128:(st + 1) * 128], wkv[:, c, 128:256],
                         c == 0, c == 3, (t_ckvn, t_wkb[s]), (t_b[b],), inc=(c == 3 and i == 3))
            P.copy("act", Vb[:, g * 512:(g + 1) * 512], cx.bank(b), (t_b[b],), (t_V,))
        if STOP == 31:
            continue
        TS = [slice(tt * 512, (tt + 1) * 512) for tt in range(4)]
        R4 = range(4)
        for tt in R4:
            for c in range(4):
                P.mm(cx.bank(tt)[0:64, :], wqv[:, c, 128:192], cqnv[:, c, TS[tt]], c == 0, c == 3, (t_wqb[s], t_cqn), (t_b[tt],))
        for tt in R4:
            P.actf(sqp[tt][0:64, :], cx.bank(tt)[0:64, :], AF.Square, (t_b[tt],), (t_sqp[tt],))
            P.tt("dve", t1[tt][0:64, :], cx.bank(tt)[0:64, :], Cq[0:64, TS[tt]], ALU.mult, (t_b[tt], t_tab), (t_t1[tt],))
        for tt in R4:
            for c in range(4):
                P.mm(cx.bank(4 + tt)[0:64, :], wqv[:, c, 192:256], cqnv[:, c, TS[tt]], c == 0, c == 3, (t_wqb[s], t_cqn), (t_b[4 + tt],))
        for tt in R4:
            P.tt("dve", t2[tt][0:64, :], cx.bank(4 + tt)[0:64, :], Sq[0:64, TS[tt]], ALU.mult, (t_b[4 + tt], t_tab), (t_t2[tt],))
            P.tt("dve", t1[tt][0:64, :], t1[tt][0:64, :], t2[tt][0:64, :], ALU.add, (t_t1[tt], t_t2[tt]), (t_t1[tt],))
        for tt in R4:
            for c in range(4):
                P.mm(cx.bank(tt), wqv[:, c, 0:128], cqnv[:, c, TS[tt]], c == 0, c == 3, (t_wqb[s], t_cqn), (t_b[tt],))
        for tt in R4:
            P.actf(sqn[tt], cx.bank(tt), AF.Square, (t_b[tt],), (t_sqn[tt],))
        for tt in R4:
            P.mm(cx.bank(4 + tt), cx.ones1_bf, sqn[tt], True, False, (t_sqn[tt], cx.t_const), (t_b[4 + tt],), inc=False)
            P.mm(cx.bank(4 + tt), cx.ones1_bf, sqp[tt], False, True, (t_sqp[tt], cx.t_const), (t_b[4 + tt],), inc=True)
        for tt in R4:
            P.ts("dve", rst[tt], cx.bank(4 + tt), 1.0 / 192.0, RMS_EPS, ALU.mult, ALU.add, (t_b[4 + tt],), (t_rst[tt],))
        for tt in R4:
            P.actf(rst[tt], rst[tt], AF.Ln, (t_rst[tt],), (t_rst[tt],))
            P.actf(rst[tt], rst[tt], AF.Exp, (t_rst[tt],), (t_rst[tt],), scale=-0.5)
        for tt in R4:
            P.stt("dve", qn[:, TS[tt]], cx.bank(tt), mc[:, 8:9], rst[tt], ALU.mult, ALU.mult, (t_b[tt], t_rst[tt], t_mc), (t_q,))
            P.tt("dve", qr[0:64, TS[tt]], t1[tt][0:64, :], rst[tt][0:64, :], ALU.mult, (t_t1[tt], t_rst[tt]), (t_q,))
        for tt in R4:
            for c in range(4):
                P.mm(cx.bank(tt), wkv[:, c, 0:128], ckvnv[:, c, TS[tt]], c == 0, c == 3, (t_wkb[s], t_ckvn), (t_b[tt],))
        for tt in R4:
            P.actf(sqn[tt], cx.bank(tt), AF.Square, (t_b[tt],), (t_sqn[tt],))
        for tt in R4:
            P.mm(cx.bank(4 + tt), cx.ones1_bf, sqn[tt], True, False, (t_sqn[tt], cx.t_const), (t_b[4 + tt],), inc=False)
            P.mm(cx.bank(4 + tt), cx.ones1_bf, sqkpe[:, TS[tt]], False, True, (t_kr, cx.t_const), (t_b[4 + tt],), inc=True)
        for tt in R4:
            P.ts("dve", rst[tt], cx.bank(4 + tt), 1.0 / 192.0, RMS_EPS, ALU.mult, ALU.add, (t_b[4 + tt],), (t_rst[tt],))
        for tt in R4:
            P.actf(rst[tt], rst[tt], AF.Ln, (t_rst[tt],), (t_rst[tt],))
            P.actf(rst[tt], rst[tt], AF.Exp, (t_rst[tt],), (t_rst[tt],), scale=-0.5)
        for tt in R4:
            P.stt("dve", kn[:, TS[tt]], cx.bank(tt), mc[:, 9:10], rst[tt], ALU.mult, ALU.mult, (t_b[tt], t_rst[tt], t_mc), (t_k,))
            P.tt("dve", kr[0:64, TS[tt]], kperot[0:64, TS[tt]], rst[tt][0:64, :], ALU.mult, (t_kr, t_rst[tt]), (t_k,))
        if STOP == 32:
            continue
        items = []
        for qt in range(4):
            qb0 = qt * 4
            nkt = qb0 + 4
            for kt in range(nkt):
                c0 = max(0, kt - qb0) * 128
                items.append((qt, kt, nkt, c0))
        sbank = {}

        def qk(i):
            nonlocal sc
            qt, kt, nkt, c0 = items[i]
            n = 512 - c0
            qsl = slice(qt * 512 + c0, (qt + 1) * 512)
            ksl = slice(kt * 128, (kt + 1) * 128)
            b = sc % 3; sc += 1
            sbank[i] = b
            P.mm(cx.bank(b)[:, 0:n], kn[:, ksl], qn[:, qsl], True, False, (t_k, t_q), (t_b[b],), inc=False)
            P.mm(cx.bank(b)[:, 0:n], kr[:, ksl], qr[:, qsl], False, True, (t_k, t_q), (t_b[b],))
            P.actf(pT[b][:, 0:n], cx.bank(b)[:, 0:n], AF.Exp, (t_b[b],), (t_pT[b],), scale=SM_SCALE)
            if kt >= qt * 4:
                P.memset("pool", pT[b][64:128, 0:64], 0.0, (t_pT[b],))

        def pv(i):
            qt, kt, nkt, c0 = items[i]
            n = 512 - c0
            b = sbank[i]
            bO, bS = (3, 4) if qt % 2 == 0 else (5, 6)
            P.mm(cx.bank(bO)[:, c0:512], Vv[:, kt, :], pT[b][:, 0:n], kt == 0, kt == nkt - 1, (t_V, t_pT[b]), (t_b[bO],), inc=False)
            P.mm(cx.bank(bS)[:, c0:512], cx.ones1_bf, pT[b][:, 0:n], kt == 0, kt == nkt - 1, (cx.t_const, t_pT[b]), (t_b[bS],), inc=True)
            if kt == nkt - 1:
                P.actf(rs[qt % 2], cx.bank(bS), AF.Ln, (t_b[bS],), (t_rs[qt % 2],))
                P.actf(rs[qt % 2], rs[qt % 2], AF.Exp, (t_rs[qt % 2],), (t_rs[qt % 2],), scale=-1.0)
                P.tt("dve", Ob[h % 2][:, qt * 512:(qt + 1) * 512], cx.bank(bO), rs[qt % 2], ALU.mult,
                     (t_b[bO], t_rs[qt % 2]), (t_Ob[h % 2],))

        qk(0)
        for i in range(len(items)):
            if i + 1 < len(items):
                qk(i + 1)
            pv(i)
        P.dma("pool", cx.ots[h], Ob[h % 2], ds_Ob[h % 2], (t_Ob[h % 2],), (t_OT,))
    P.barrier()
    A.release()
    if STOP == 3:
        A.release(); return
    OT = A.alloc(BF16, MLA_H * T); OTv = OT.rearrange("p (h t) -> p h t", h=MLA_H)
    for c4 in range(4):
        P.dma("sp", OTv[:, c4 * 4:(c4 + 1) * 4, :], cx.ots.rearrange("h p t -> p h t")[:, c4 * 4:(c4 + 1) * 4, :], dsm, (t_OT,), (t_OT,))
    wo_s = [A.alloc(F32, MLA_H * 128) for _ in range(2)]; wo_b = [A.alloc(BF16, MLA_H * 128) for _ in range(2)]
    t_wos = [Tok() for _ in range(2)]; t_wob = [Tok() for _ in range(2)]
    xold = [A.alloc(F32, 512) for _ in range(2)]; t_xold = [Tok() for _ in range(2)]; ds_xold = [P.dsem() for _ in range(2)]
    xnew = [A.alloc(F32, 512) for _ in range(2)]; t_xnew = [Tok() for _ in range(2)]; ds_xnew = [P.dsem() for _ in range(2)]
    wov = wo.rearrange("h p e -> p h e")
    def o_load(e):
        s = e % 2
        P.dma("sp", wo_s[s].rearrange("p (h e) -> p h e", h=MLA_H), wov[:, :, e * 128:(e + 1) * 128], ds_w[s], (), (t_wos[s],))

    def o_cast(e):
        s = e % 2
        P.copy("act" if e % 2 == 0 else "dve", wo_b[s], wo_s[s], (t_wos[s],), (t_wob[s],))

    o_load(0)
    o_load(1)
    o_cast(0)
    k = 0
    for e in range(NCH):
        s = e % 2
        if e + 1 < NCH:
            o_cast(e + 1)
        if e + 2 < NCH:
            o_load(e + 2)
        wb = wo_b[s].rearrange("p (h e) -> p h e", h=MLA_H)
        for tt in range(4):
            bi = k % 2; k += 1
            b = 6 + bi
            tsl = slice(tt * 512, (tt + 1) * 512)
            P.dma("act", xold[bi], src_v[:, e, tsl], ds_xold[bi], cx.xs_tok(e, tt), (t_xold[bi],))
            for h in range(MLA_H):
                P.mm(cx.bank(b), wb[:, h, :], OTv[:, h, tsl], h == 0, h == MLA_H - 1, (t_wob[s], t_OT), (t_b[b],))
            P.tt("dve", xnew[bi], cx.bank(b), xold[bi], ALU.add, (t_b[b], t_xold[bi]), (t_xnew[bi],))
            P.dma("pool", dst_v[:, e, tsl], xnew[bi], ds_xnew[bi], (t_xnew[bi],), cx.xs_tok(e, tt))
    P.barrier()
    for d in ds_w + ds_xold + ds_xnew + ds_Ob + [dsm]:
        P.free_dsems.append(d)
    A.release()


RW_L = 64
RW_NCK = T // RW_L
DEC_C = float(np.exp(-0.5))


def linear_bufs(cx, n_in_chunks):
    P, A = cx.P, cx.A
    return dict(stg=[A.alloc(F32, n_in_chunks * 128) for _ in range(2)],
                wb=[A.alloc(BF16, n_in_chunks * 128) for _ in range(2)],
                t_s=[Tok() for _ in range(2)], t_w=[Tok() for _ in range(2)], t_w2=[Tok() for _ in range(2)],
                ds=[P.dsem() for _ in range(2)], t_bk=[Tok(True) for _ in range(2)], k=[0])


def linear_fm(cx, actv, t_act, w_ap, n_in_chunks, out_cols, evac, bank0=0, M=128, bufs=None):
    P, A = cx.P, cx.A
    own = bufs is None
    if own:
        A.mark()
        bufs = linear_bufs(cx, n_in_chunks)
    stg, wb, t_s, t_w, t_w2, ds, t_bk = (bufs[k_] for k_ in ("stg", "wb", "t_s", "t_w", "t_w2", "ds", "t_bk"))
    wv = w_ap.rearrange("(c p) f -> p c f", p=128)
    hc = n_in_chunks // 2

    def l_load(oi):
        c0, m = out_cols[oi]
        s = oi % 2
        sv = stg[s].rearrange("p (c f) -> p c f", c=n_in_chunks)
        P.dma("sp", sv[:, :, 0:m], wv[:, :, c0:c0 + m], ds[s], (), (t_s[s],))

    def l_cast(oi):
        c0, m = out_cols[oi]
        s = oi % 2
        sv = stg[s].rearrange("p (c f) -> p c f", c=n_in_chunks)
        bv = wb[s].rearrange("p (c f) -> p c f", c=n_in_chunks)
        P.copy("dve", bv[:, 0:hc, 0:m], sv[:, 0:hc, 0:m], (t_s[s],), (t_w[s],))
        P.copy("act", bv[:, hc:, 0:m], sv[:, hc:, 0:m], (t_s[s],), (t_w2[s],))

    n_out = len(out_cols)
    l_load(0)
    if n_out > 1:
        l_load(1)
    l_cast(0)
    for oi, (c0, m) in enumerate(out_cols):
        s = oi % 2
        bv = wb[s].rearrange("p (c f) -> p c f", c=n_in_chunks)
        if oi + 1 < n_out:
            l_cast(oi + 1)
        if oi + 2 < n_out:
            l_load(oi + 2)
        for tt in range(4):
            bi = bufs["k"][0] % 2; bufs["k"][0] += 1
            bank = cx.bank(bank0 + bi)
            for c in range(n_in_chunks):
                P.mm(bank[0:m, :], bv[:, c, 0:m], actv[:, c, tt * 512:(tt + 1) * 512], c == 0, c == n_in_chunks - 1,
                     (t_w[s] if c < hc else t_w2[s], t_act), (t_bk[bi],))
            evac(oi, tt, bank, t_bk[bi])
    if own:
        P.barrier()
        for d in ds:
            P.free_dsems.append(d)
        A.release()


def phase_rwkv(cx, xs_src, xs_dst, W, gcol, rc_d):
    P, A = cx.P, cx.A
    A.mark()
    src_v = xs_src.rearrange("c p t -> p c t")
    dst_v = xs_dst.rearrange("c p t -> p c t")
    rc = A.alloc(F32, 13 * 16); t_rc = Tok(); dsm = P.dsem()
    P.dma("sp", rc, rc_d, dsm, (), (t_rc,))
    omka = A.alloc(F32, 16)
    P.ts("dve", omka, rc[:, 9 * 16:10 * 16], -1.0, 1.0, ALU.mult, ALU.add, (t_rc,), (t_rc,))
    col = lambda k, e: rc[:, k * 16 + e: k * 16 + e + 1]
    t_xm = cx.t_xmix
    A.mark()
    TT = 256
    xt = [A.alloc(F32, NCH * TT) for _ in range(2)]; t_xt = [Tok() for _ in range(2)]; ds_xt = [P.dsem() for _ in range(2)]
    hb = A.alloc(F32, NCH * (TT + 4)); hbv = hb.rearrange("p (c t) -> p c t", c=NCH); t_hb = Tok()
    xx = A.alloc(F32, NCH * TT); xxv = xx.rearrange("p (c t) -> p c t", c=NCH); t_xx = Tok()
    xm = [A.alloc(BF16, NCH * TT) for _ in range(6)]; t_xmb = [Tok() for _ in range(6)]; ds_xm = [P.dsem() for _ in range(6)]
    sqb = [A.alloc(BF16, 512) for _ in range(3)]; t_sq = [Tok() for _ in range(3)]
    rstd = A.alloc(F32, 512); t_rstd = Tok()
    t_bn = Tok(True)
    utmp = [A.alloc(F32, TT) for _ in range(4)]; t_utmp = [Tok() for _ in range(4)]
    P.memset("dve", hbv[:, :, 3:4], 0.0, (t_hb,))
    for ti in range(T // TT):
        s = ti % 2
        tsl = slice(ti * TT, (ti + 1) * TT)
        P.dma("sp", xt[s].rearrange("p (c t) -> p c t", c=NCH), src_v[:, :, tsl], ds_xt[s], cx.xs_tok(None, ti // 2), (t_xt[s],))
        if ti > 0:
            P.copy("dve", hbv[:, :, 3:4], hbv[:, :, TT + 3:TT + 4], (t_hb,), (t_hb,))
        rmsnorm_tile(cx, xt[s], t_xt[s], gcol, lambda c: hbv[:, c, 4:TT + 4], t_hb, TT, sqb, t_sq, rstd, t_rstd,
                     cx.bank(6), t_bn)
        P.tt("dve", xxv, hbv[:, :, 3:TT + 3], hbv[:, :, 4:TT + 4], ALU.subtract, (t_hb,), (t_xx,))
        for j in range(6):
            xmv = xm[j].rearrange("p (c t) -> p c t", c=NCH)
            for c in range(NCH):
                if (j * NCH + c) % 3 == 2:
                    P.stt("dve", xmv[:, c, :], xxv[:, c, :], col(j, c), hbv[:, c, 4:TT + 4], ALU.mult, ALU.add,
                          (t_xx, t_hb, t_rc), (t_xmb[j],))
                else:
                    u = (j * NCH + c) % 4
                    P.actf(utmp[u], xxv[:, c, :], AF.Copy, (t_xx, t_rc), (t_utmp[u],), scale=col(j, c))
                    P.tt("dve", xmv[:, c, :], utmp[u], hbv[:, c, 4:TT + 4], ALU.add, (t_utmp[u], t_hb), (t_xmb[j],))
            P.dma("sp", cx.xmix[j].rearrange("c p t -> p c t")[:, :, tsl], xmv, ds_xm[j], (t_xmb[j],), (t_xm,))
    P.barrier()
    for d in ds_xt + ds_xm:
        P.free_dsems.append(d)
    A.release()
    A.mark()
    lw = A.alloc(BF16, T); la = A.alloc(BF16, T); lg = A.alloc(BF16, 2 * T); lgv = lg.rearrange("p (c t) -> p c t", c=2)
    t_l = Tok()
    A.mark()
    acts = [A.alloc(BF16, NCH * T) for _ in range(2)]
    t_acts = [Tok() for _ in range(2)]; ds_act = [P.dsem() for _ in range(2)]
    ot = [A.alloc(F32, 512) for _ in range(4)]; t_ot = [Tok() for _ in range(4)]; ds_ot = [P.dsem() for _ in range(4)]
    cnt = [0]
    order = [(0, "r"), (2, "k"), (3, "v"), (1, "lw"), (4, "la"), (5, "lg")]
    lb = linear_bufs(cx, NCH)

    def load_act(j):
        av = acts[j % 2].rearrange("p (c t) -> p c t", c=NCH)
        for c4 in range(4):
            P.dma("sp", av[:, c4 * 4:(c4 + 1) * 4, :], cx.xmix[order[j][0]].rearrange("c p t -> p c t")[:, c4 * 4:(c4 + 1) * 4, :],
                  ds_act[j % 2], (t_xm,), (t_acts[j % 2],))

    load_act(0)
    for j, (src_j, kind) in enumerate(order):
        if j + 1 < len(order):
            load_act(j + 1)
        actv = acts[j % 2].rearrange("p (c t) -> p c t", c=NCH)
        t_act = t_acts[j % 2]
        if kind in ("r", "k", "v"):
            ji = "rkv".index(kind)

            def ev(oi, tt, bank, tb, ji=ji):
                i = cnt[0] % 4; cnt[0] += 1
                P.copy("act" if i % 2 == 0 else "dve", ot[i], bank, (tb,), (t_ot[i],))
                P.dma("pool", cx.rkv[ji].rearrange("c p t -> p c t")[:, oi, tt * 512:(tt + 1) * 512], ot[i], ds_ot[i],
                      (t_ot[i],), (cx.t_rkv,))
            linear_fm(cx, actv, t_act, W["w_rkv"][ji], NCH, [(e * 128, 128) for e in range(NCH)], ev, bufs=lb)
        elif kind == "lw":
            def ev(oi, tt, bank, tb):
                P.actf(lw[0:96, tt * 512:(tt + 1) * 512], bank[0:96, :], AF.Tanh, (tb,), (t_l,))
            linear_fm(cx, actv, t_act, W["w1"], NCH, [(0, 96)], ev, bufs=lb)
        elif kind == "la":
            def ev(oi, tt, bank, tb):
                P.copy("act", la[0:96, tt * 512:(tt + 1) * 512], bank[0:96, :], (tb,), (t_l,))
            linear_fm(cx, actv, t_act, W["a1"], NCH, [(0, 96)], ev, bufs=lb)
        else:
            def ev(oi, tt, bank, tb):
                P.actf(lgv[:, oi, tt * 512:(tt + 1) * 512], bank, AF.Sigmoid, (tb,), (t_l,))
            linear_fm(cx, actv, t_act, W["g1"], NCH, [(0, 128), (128, 128)], ev, bufs=lb)
    P.barrier()
    P.free_dsems.extend(ds_act + ds_ot + lb["ds"])
    A.release()
    w2b = A.alloc(BF16, D); a2b = A.alloc(BF16, D); g2b = A.alloc(BF16, 2 * D); g2bv = g2b.rearrange("p (c f) -> p c f", c=2)
    t_lw2 = Tok()
    st32 = A.alloc(F32, 2 * D)
    P.dma("sp", st32[0:96, 0:D], W["w2"], dsm, (), (t_lw2,))
    P.copy("dve", w2b[0:96, :], st32[0:96, 0:D], (t_lw2,), (t_lw2,))
    P.dma("sp", st32[0:96, 0:D], W["a2"], dsm, (t_lw2,), (t_lw2,))
    P.copy("dve", a2b[0:96, :], st32[0:96, 0:D], (t_lw2,), (t_lw2,))
    P.dma("sp", st32.rearrange("p (c f) -> p c f", c=2), W["g2"].rearrange("(c p) f -> p c f", p=128), dsm, (t_lw2,), (t_lw2,))
    P.copy("dve", g2b, st32, (t_lw2,), (t_lw2,))
    t_b2 = [Tok(True) for _ in range(6)]
    big = [[A.alloc(F32, T) for _ in range(3)] for _ in range(2)]
    t_big = [[Tok() for _ in range(3)] for _ in range(2)]
    ds_big = [P.dsem() for _ in range(2)]
    for e in range(NCH):
        esl = slice(e * 128, (e + 1) * 128)
        sb_ = e % 2
        for tt in range(4):
            tsl = slice(tt * 512, (tt + 1) * 512)
            b0 = (tt % 2) * 3
            P.mm(cx.bank(b0), w2b[0:96, esl], lw[0:96, tsl], True, True, (t_lw2, t_l), (t_b2[b0],))
            P.actf(big[sb_][0][:, tsl], cx.bank(b0), AF.Sigmoid, (t_b2[b0], t_rc), (t_big[sb_][0],), bias=col(6, e))
            P.mm(cx.bank(b0 + 1), a2b[0:96, esl], la[0:96, tsl], True, True, (t_lw2, t_l), (t_b2[b0 + 1],))
            P.actf(big[sb_][1][:, tsl], cx.bank(b0 + 1), AF.Sigmoid, (t_b2[b0 + 1], t_rc), (t_big[sb_][1],), bias=col(7, e))
            for c in range(2):
                P.mm(cx.bank(b0 + 2), g2bv[:, c, esl], lgv[:, c, tsl], c == 0, c == 1, (t_lw2, t_l), (t_b2[b0 + 2],))
            P.copy("dve", big[sb_][2][:, tsl], cx.bank(b0 + 2), (t_b2[b0 + 2],), (t_big[sb_][2],))
        for pl in range(3):
            P.dma("pool" if pl != 1 else "sp", cx.rkv[3 + pl][e], big[sb_][pl], ds_big[sb_], (t_big[sb_][pl],), (cx.t_rkv,))
    P.barrier()
    P.free_dsems.extend(ds_big)
    A.release()
    if STOP == 52:
        A.release(); return
    t_yg = Tok()
    A.mark()
    msk = A.alloc(F32, 3 * 512); t_k = Tok()
    P.dma("sp", msk, cx.rw_masks_d, dsm, (), (t_k,))
    ML_s, MU_s, MU_i = msk[:, 0:512], msk[:, 512:1024], msk[:, 1024:1536]
    id4 = A.alloc(BF16, 512)
    for q in range(4):
        P.copy("dve", id4[:, q * 128:(q + 1) * 128], cx.ident, (cx.t_const,), (t_k,))
    idb = id4[:, 0:128]
    bones = A.alloc(F32, 128)
    P.memset("pool", bones, 0.0, (t_k,))
    P.memset("pool", bones[0:64, 0:64], 1.0, (t_k,))
    P.memset("pool", bones[64:128, 64:128], 1.0, (t_k,))
    rmask = A.alloc(BF16, T)
    P.memset("pool", rmask, 1.0, (t_k,))
    P.memset("pool", rmask.rearrange("p (c t) -> p c t", t=RW_L)[:, :, 0:1], 0.0, (t_k,))
    xop = [A.alloc(BF16, RW_NCK * 128) for _ in range(7)]
    t_xop = Tok()
    for x_ in xop:
        P.memset("pool", x_, 0.0, (t_xop,))
    RTx, KTx, BTx, KHx, BHx, ATx, Vx = xop
    gam = A.alloc(F32, RW_NCK); t_gam = Tok()
    bon = A.alloc(F32, T); t_bon = Tok()
    psb = cx.ps_bf

    def xview(xo, h):
        return xo.rearrange("p (c i) -> p c i", i=128)[h * 64:(h + 1) * 64, :, h * 64:(h + 1) * 64]

    def hview(ap, h):
        return ap.rearrange("p (c t) -> p c t", t=RW_L)[h * 64:(h + 1) * 64, :, :]

    def ch(xo, c):
        return xo[:, c * 128:(c + 1) * 128]

    def q4(ap, q):
        return ap[:, q * 128:(q + 1) * 128]

    NG = RW_NCK // 4
    NSLOT = 3
    for e in range(NCH):
        A.mark()
        r_, k_, v_, a_, cum, sg_, tA, tB, tC, tD, tE, tF = [A.alloc(F32, T) for _ in range(12)]
        t_r, t_kk, t_v, t_a, t_cum, t_sg, t_tA, t_tB, t_tC, t_tD, t_tE, t_tF = [Tok() for _ in range(12)]
        t_bk = [Tok(True) for _ in range(8)]
        t_xop2 = [Tok(), Tok()]
        cB = [A.alloc(BF16, T) for _ in range(5)]; t_cB = [Tok() for _ in range(5)]
        for ji, (dst, tk) in enumerate([(sg_, t_sg), (k_, t_kk), (a_, t_a), (r_, t_r), (v_, t_v)]):
            P.dma("sp", dst, cx.rkv[[3, 1, 4, 0, 2][ji]][e], dsm, (cx.t_rkv,), (tk,))
        cumv = cum.rearrange("p (c t) -> p c t", t=RW_L)
        TS = [slice(tt * 512, (tt + 1) * 512) for tt in range(4)]
        P.op("dve", lambda en, cum=cum, sg_=sg_: en.tensor_tensor_scan(cum, rmask, sg_, 0.0, ALU.mult, ALU.add),
             (t_sg, t_k), (t_cum,))
        P.actf(tA, k_, AF.Copy, (t_kk, t_rc), (t_tA,), scale=col(8, e))
        P.actf(tB, tA, AF.Square, (t_tA,), (t_tB,))
        P.actf(tC, cum, AF.Exp, (t_cum,), (t_tC,), scale=-DEC_C)
        for tt in range(4):
            P.mm(cx.bank(tt), bones, tB[:, TS[tt]], True, True, (t_k, t_tB), (t_bk[tt],))
        P.actf(tE, a_, AF.Identity, (t_a, t_rc), (t_tE,), scale=col(9, e), bias=omka[:, e:e + 1])
        P.tt("dve", k_, k_, tE, ALU.mult, (t_kk, t_tE), (t_kk,))
        for tt in range(4):
            P.ts("dve", tD[:, TS[tt]], cx.bank(tt), 1e-18, None, ALU.max, None, (t_bk[tt],), (t_tD,))
        P.actf(tD, tD, AF.Ln, (t_tD,), (t_tD,))
        P.actf(tD, tD, AF.Exp, (t_tD,), (t_tD,), scale=-0.5)
        P.copy("dve", gam, cumv[:, :, RW_L - 1], (t_cum,), (t_gam,))
        P.actf(gam, gam, AF.Exp, (t_gam,), (t_gam,), scale=-DEC_C)
        def expand(xo, cb, t_cb):
            for h in range(2):
                P.copy("act", xview(xo, h), hview(cb, h), (t_cb,), (t_xop2[h],))

        P.tt("dve", cB[0], r_, tC, ALU.mult, (t_r, t_tC), (t_cB[0],))
        expand(RTx, cB[0], t_cB[0])
        P.tt("dve", tE, cum, sg_, ALU.subtract, (t_cum, t_sg, t_tE), (t_tE,))
        P.actf(tE, tE, AF.Exp, (t_tE,), (t_tE,), scale=-DEC_C)
        P.stt("dve", tF, r_, col(10, e), k_, ALU.mult, ALU.mult, (t_r, t_kk, t_rc), (t_tF,))
        for tt in range(4):
            P.mm(cx.bank(4 + tt), bones, tF[:, TS[tt]], True, True, (t_k, t_tF), (t_bk[4 + tt],))
        P.tt("dve", tA, tA, tD, ALU.mult, (t_tA, t_tD), (t_tA,))
        P.tt("dve", tB, tA, a_, ALU.mult, (t_tA, t_a, t_tB), (t_tB,))
        P.actf(tC, cum, AF.Exp, (t_cum, t_cB[0]), (t_tC,), scale=DEC_C)
        for tt in range(4):
            P.tt("dve", bon[:, TS[tt]], cx.bank(4 + tt), v_[:, TS[tt]], ALU.mult, (t_bk[4 + tt], t_v), (t_bon,))
        P.stt("dve", cB[1], tA, -1.0, tE, ALU.mult, ALU.mult, (t_tA, t_tE), (t_cB[1],))
        expand(ATx, cB[1], t_cB[1])
        P.tt("dve", r_.rearrange("p (c t) -> p c t", t=RW_L), cumv[:, :, RW_L - 1:RW_L].to_broadcast([128, RW_NCK, RW_L]),
             cumv, ALU.subtract, (t_cum, t_r, t_cB[0]), (t_r,))
        P.actf(r_, r_, AF.Exp, (t_r,), (t_r,), scale=-DEC_C)
        for h in range(2):
            P.copy("act", xview(Vx, h), hview(v_, h), (t_v,), (t_xop2[h],))
        P.tt("dve", cB[2], k_, tC, ALU.mult, (t_kk, t_tC), (t_cB[2],))
        expand(KTx, cB[2], t_cB[2])
        P.tt("dve", cB[3], tB, tC, ALU.mult, (t_tB, t_tC), (t_cB[3],))
        expand(BTx, cB[3], t_cB[3])
        P.tt("dve", cB[4], k_, r_, ALU.mult, (t_kk, t_r), (t_cB[4],))
        expand(KHx, cB[4], t_cB[4])
        P.tt("dve", cB[0], tB, r_, ALU.mult, (t_tB, t_r, t_cB[0]), (t_cB[0],))
        expand(BHx, cB[0], t_cB[0])
        P.barrier()
        A.release()
        A.mark()
        GT = A.alloc(BF16, RW_NCK * 128); Hh = A.alloc(F32, RW_NCK * 128); Rb = A.alloc(BF16, RW_NCK * 128)
        y0 = A.alloc(F32, T); Sall = A.alloc(BF16, RW_NCK * 128); yT = A.alloc(F32, T)
        t_GT = [Tok() for _ in range(NG)]; t_H = [Tok() for _ in range(NG)]; t_Rb = [Tok() for _ in range(NG)]
        t_y0 = [Tok() for _ in range(NG)]
        gT = A.alloc(F32, T); t_g = Tok()
        P.dma("sp", gT, cx.rkv[5][e], dsm, (cx.t_rkv,), (t_g,))
        t_bk = [Tok(True) for _ in range(8)]
        slots = []
        for sl in range(NSLOT):
            d_ = {}
            d_["TMb"] = [A.alloc(BF16, 512) for _ in range(3)]
            d_["WUin"] = A.alloc(BF16, 4 * 256); d_["WU"] = A.alloc(BF16, 4 * 256)
            d_["Nb"] = [A.alloc(BF16, 512) for _ in range(2)]; d_["Qb"] = [A.alloc(BF16, 512) for _ in range(2)]
            d_["Pb"] = [A.alloc(BF16, 512) for _ in range(2)]
            d_["M"] = [A.alloc(BF16, 512) for _ in range(3)]
            d_["tok"] = {k: Tok() for k in ("TM", "WUin", "WU", "N", "Q", "P", "M")}
            d_["banks"] = (2 * sl, 2 * sl + 1)
            slots.append(d_)
        ev_i = [0]

        def group_steps(g, sd):
            cs = [g * 4 + q for q in range(4)]
            TMb, WUin, WU, Nb, Qb, Pb = sd["TMb"], sd["WUin"], sd["WU"], sd["Nb"], sd["Qb"], sd["Pb"]
            Mak, Mrb, Mrk = sd["M"]
            tk = sd["tok"]
            WUinv = WUin.rearrange("p (q f) -> p q f", q=4)
            WUv = WU.rearrange("p (q f) -> p q f", q=4)
            bi = [0]

            def nb():
                bi[0] ^= 1
                return sd["banks"][bi[0]]

            def eng2():
                ev_i[0] += 1
                return "dve" if ev_i[0] % 2 == 0 else "act"
            for half, ops_ in enumerate([(BHx, KHx), (Vx, ATx)]):
                b = nb()
                for oi, xo in enumerate(ops_):
                    for q in range(4):
                        P.tr(psb[:, b * 1024 + (oi * 4 + q) * 128: b * 1024 + (oi * 4 + q + 1) * 128], ch(xo, cs[q]), idb,
                             (t_xop, t_k), (t_bk[b],), inc=(oi == 1 and q == 3))
                if half == 0:
                    P.copy("act", TMb[0], psb[:, b * 1024: b * 1024 + 512], (t_bk[b],), (tk["TM"],))
                    P.copy("dve", TMb[1], psb[:, b * 1024 + 512: b * 1024 + 1024], (t_bk[b],), (tk["TM"],))
                else:
                    P.copy("act", TMb[2], psb[:, b * 1024: b * 1024 + 512], (t_bk[b],), (tk["TM"],))
                    P.copy("dve", WUinv[:, :, 0:128], psb[:, b * 1024 + 512: b * 1024 + 1024].rearrange("p (q f) -> p q f", q=4),
                           (t_bk[b],), (tk["WUin"],))
                yield
            b = nb()
            for q in range(4):
                P.mm(q4(cx.bank(b), q), ch(ATx, cs[q]), ch(BTx, cs[q]), True, True, (t_xop,), (t_bk[b],), inc=(q == 3))
            P.tt("dve", Nb[0], cx.bank(b), ML_s, ALU.mult, (t_bk[b], t_k), (tk["N"],))
            yield
            b = nb()
            for q in range(4):
                P.mm(q4(cx.bank(b), q), ch(BTx, cs[q]), ch(ATx, cs[q]), True, True, (t_xop,), (t_bk[b],), inc=(q == 3))
            P.tt("dve", Qb[0], cx.bank(b), MU_s, ALU.mult, (t_bk[b], t_k), (tk["Q"],))
            P.tt("pool", Pb[0], Qb[0], id4, ALU.add, (tk["Q"], t_k), (tk["P"],))
            yield
            for (lx, rx, mk, dst) in ((KTx, ATx, MU_s, Mak), (BTx, RTx, MU_i, Mrb), (KTx, RTx, MU_i, Mrk)):
                b = nb()
                for q in range(4):
                    P.mm(q4(cx.bank(b), q), ch(lx, cs[q]), ch(rx, cs[q]), True, True, (t_xop,), (t_bk[b],), inc=(q == 3))
                P.tt("dve", dst, cx.bank(b), mk, ALU.mult, (t_bk[b], t_k), (tk["M"],))
                yield
            pi = 0
            for j in range(1, 6):
                i0, i1 = (j - 1) % 2, j % 2
                b = nb()
                for q in range(4):
                    P.mm(q4(cx.bank(b), q), q4(Qb[i0], q), q4(Nb[i0], q), True, True, (tk["N"], tk["Q"]), (t_bk[b],), inc=(q == 3))
                if j < 5:
                    b2 = nb()
                    for q in range(4):
                        P.mm(q4(cx.bank(b2), q), q4(Nb[i0], q), q4(Qb[i0], q), True, True, (tk["N"], tk["Q"]), (t_bk[b2],), inc=(q == 3))
                P.copy("act", Nb[i1], cx.bank(b), (t_bk[b],), (tk["N"],))
                if j < 5:
                    P.copy(eng2(), Qb[i1], cx.bank(b2), (t_bk[b2],), (tk["Q"],))
                yield
                b = nb()
                for q in range(4):
                    P.mm(q4(cx.bank(b), q), idb, q4(Pb[pi], q), True, False, (t_k, tk["P"]), (t_bk[b],), inc=False)
                    P.mm(q4(cx.bank(b), q), q4(Nb[i1], q), q4(Pb[pi], q), False, True, (tk["N"], tk["P"]), (t_bk[b],), inc=(q == 3))
                P.copy(eng2(), Pb[1 - pi], cx.bank(b), (t_bk[b],), (tk["P"],))
                pi = 1 - pi
                yield
            TiT = Pb[pi]
            b = nb()
            for q in range(4):
                P.mm(q4(cx.bank(b), q), q4(Mak, q), q4(TMb[2], q), True, True, (tk["M"], tk["TM"]), (t_bk[b],), inc=(q == 3))
            P.copy("act", WUinv[:, :, 128:256], cx.bank(b).rearrange("p (q f) -> p q f", q=4), (t_bk[b],), (tk["WUin"],))
            yield
            for hf in range(2):
                b = nb()
                for qq in range(2):
                    q = hf * 2 + qq
                    P.mm(cx.bank(b)[:, qq * 256:(qq + 1) * 256], q4(TiT, q), WUinv[:, q, :], True, True, (tk["P"], tk["WUin"]),
                         (t_bk[b],), inc=(qq == 1))
                P.copy("act" if hf == 0 else "dve", WU[:, hf * 512:(hf + 1) * 512], cx.bank(b), (t_bk[b],), (tk["WU"],))
            yield
            b = nb()
            for q in range(4):
                P.mm(q4(cx.bank(b), q), WUv[:, q, 0:128], q4(TMb[0], q), True, True, (tk["WU"], tk["TM"]), (t_bk[b],), inc=(q == 3))
            P.copy("act", GT[:, g * 512:(g + 1) * 512], cx.bank(b), (t_bk[b],), (t_GT[g],))
            b = nb()
            for q in range(4):
                P.mm(q4(cx.bank(b), q), q4(TMb[0], q), WUv[:, q, 128:256], True, False, (tk["WU"], tk["TM"]), (t_bk[b],), inc=False)
                P.mm(q4(cx.bank(b), q), q4(TMb[1], q), q4(TMb[2], q), False, True, (tk["TM"],), (t_bk[b],), inc=(q == 3))
            P.copy("dve", Hh[:, g * 512:(g + 1) * 512], cx.bank(b), (t_bk[b],), (t_H[g],))
            yield
            b = nb()
            for q in range(4):
                P.mm(q4(cx.bank(b), q), idb, ch(RTx, cs[q]), True, False, (t_k, t_xop), (t_bk[b],), inc=False)
                P.mm(q4(cx.bank(b), q), WUv[:, q, 0:128], q4(Mrb, q), False, True, (tk["WU"], tk["M"]), (t_bk[b],), inc=(q == 3))
            P.copy("act", Rb[:, g * 512:(g + 1) * 512], cx.bank(b), (t_bk[b],), (t_Rb[g],))
            b = nb()
            for q in range(4):
                P.mm(q4(cx.bank(b), q), WUv[:, q, 128:256], q4(Mrb, q), True, False, (tk["WU"], tk["M"]), (t_bk[b],), inc=False)
                P.mm(q4(cx.bank(b), q), q4(TMb[2], q), q4(Mrk, q), False, True, (tk["TM"], tk["M"]), (t_bk[b],), inc=(q == 3))
            for h in range(2):
                bv = cx.bank(b).rearrange("p (q i) -> p q i", q=4)[h * 64:(h + 1) * 64, :, h * 64:(h + 1) * 64]
                P.copy("act", hview(y0, h)[:, g * 4:(g + 1) * 4, :], bv, (t_bk[b],), (t_y0[g],))
            yield

        pending = list(range(NG))
        active = []
        free_slots = list(range(NSLOT))
        while pending or active:
            while pending and free_slots:
                sl = free_slots.pop(0)
                active.append((group_steps(pending.pop(0), slots[sl]), sl))
            for item in list(active):
                gen, sl = item
                try:
                    next(gen)
                except StopIteration:
                    active.remove(item)
                    free_slots.append(sl)
        Sf = [A.alloc(F32, 128) for _ in range(2)]; tAq = [A.alloc(F32, 128) for _ in range(2)]
        t_Sf, t_tAq = [Tok() for _ in range(2)], [Tok() for _ in range(2)]
        t_Sg = [Tok() for _ in range(NG)]
        t_yq = [Tok() for _ in range(4)]

        def emit_y(g):
            b = 4 + (g % 2)
            for q in range(4):
                c = g * 4 + q
                P.mm(q4(cx.bank(b), q), ch(Sall, c), ch(Rb, c), True, True, (t_Sg[g], t_Rb[g]), (t_bk[b],), inc=(q == 3))
            for h in range(2):
                bv = cx.bank(b).rearrange("p (q i) -> p q i", q=4)[h * 64:(h + 1) * 64, :, h * 64:(h + 1) * 64]
                P.tt("dve", hview(yT, h)[:, g * 4:(g + 1) * 4, :], bv, hview(y0, h)[:, g * 4:(g + 1) * 4, :], ALU.add,
                     (t_bk[b], t_y0[g]), (t_yq[g // 2],))

        P.memset("pool", Sall[:, 0:128], 0.0, (t_Sg[0],))
        P.copy("dve", tAq[0], ch(Hh, 0), (t_H[0],), (t_tAq[0],))
        for c in range(RW_NCK - 1):
            si = c % 2
            b = 6 + (c % 2)
            P.mm(cx.bank(b)[:, 0:128], ch(GT, c), ch(Sall, c), True, True, (t_GT[c // 4], t_Sg[c // 4]), (t_bk[b],))
            P.tt("dve", ch(Sall, c + 1), cx.bank(b)[:, 0:128], tAq[si], ALU.add, (t_bk[b], t_tAq[si]), (t_Sg[(c + 1) // 4],))
            if c + 1 < RW_NCK - 1:
                P.tt("dve", Sf[si], cx.bank(b)[:, 0:128], tAq[si], ALU.add, (t_bk[b], t_tAq[si]), (t_Sf[si],))
                P.stt("dve", tAq[1 - si], Sf[si], gam[:, c + 1:c + 2], ch(Hh, c + 1), ALU.mult, ALU.add,
                      (t_Sf[si], t_gam, t_H[(c + 1) // 4]), (t_tAq[1 - si],))
            if (c + 1) % 4 == 3:
                emit_y((c + 1) // 4)
        if cx.dbg_y is not None:
            P.dma("sp", cx.dbg_y[e], yT, dsm, tuple(t_yq), (cx.t_out,))
        mean = [Hh[:, tt * 512:(tt + 1) * 512] for tt in range(4)]; t_mean = [Tok() for _ in range(4)]
        ygb = A.alloc(BF16, T); t_ygb = Tok()
        t_y0p = [Tok() for _ in range(4)]
        TS = [slice(tt * 512, (tt + 1) * 512) for tt in range(4)]
        for tt in range(4):
            P.mm(cx.bank(tt), bones, yT[:, TS[tt]], True, True, (t_k, t_yq[tt]), (t_bk[tt],))
        for tt in range(4):
            P.stt("dve", yT[:, TS[tt]], cx.bank(tt), -1.0 / 64.0, yT[:, TS[tt]], ALU.mult, ALU.add, (t_bk[tt], t_yq[tt]), (t_yq[tt],))
        for tt in range(4):
            P.actf(y0[:, TS[tt]], yT[:, TS[tt]], AF.Square, (t_yq[tt], t_y0[2 * tt], t_y0[2 * tt + 1]), (t_y0p[tt],))
        for tt in range(4):
            P.mm(cx.bank(4 + tt), bones, y0[:, TS[tt]], True, True, (t_k, t_y0p[tt]), (t_bk[4 + tt],))
        for tt in range(4):
            P.ts("dve", mean[tt], cx.bank(4 + tt), 1.0 / 64.0, 64e-5, ALU.mult, ALU.add, (t_bk[4 + tt],), (t_mean[tt],) + tuple(t_H))
        for tt in range(4):
            P.actf(mean[tt], mean[tt], AF.Ln, (t_mean[tt],), (t_mean[tt],))
            P.actf(mean[tt], mean[tt], AF.Exp, (t_mean[tt],), (t_mean[tt],), scale=-0.5)
        for tt in range(4):
            P.tt("dve", yT[:, TS[tt]], yT[:, TS[tt]], mean[tt], ALU.mult, (t_yq[tt], t_mean[tt]), (t_yq[tt],))
            P.ts("dve", yT[:, TS[tt]], yT[:, TS[tt]], col(11, e), col(12, e), ALU.mult, ALU.add, (t_yq[tt], t_rc), (t_yq[tt],))
            P.tt("dve", yT[:, TS[tt]], yT[:, TS[tt]], bon[:, TS[tt]], ALU.add, (t_yq[tt], t_bon), (t_yq[tt],))
            P.tt("dve", ygb[:, TS[tt]], yT[:, TS[tt]], gT[:, TS[tt]], ALU.mult, (t_yq[tt], t_g), (t_ygb,))
        P.dma("sp", cx.ygs[e], ygb, dsm, (t_ygb,), (t_yg,))
        P.barrier()
        A.release()
    A.release()
    xold = [A.alloc(F32, 512) for _ in range(2)]; t_xold = [Tok() for _ in range(2)]; ds_xold = [P.dsem() for _ in range(2)]
    xnew = [A.alloc(F32, 512) for _ in range(2)]; t_xnew = [Tok() for _ in range(2)]; ds_xnew = [P.dsem() for _ in range(2)]
    kk_ = [0]

    def ev_o(oi, tt, bank, tb):
        bi = kk_[0] % 2; kk_[0] += 1
        tsl = slice(tt * 512, (tt + 1) * 512)
        P.dma("act", xold[bi], src_v[:, oi, tsl], ds_xold[bi], cx.xs_tok(oi, tt), (t_xold[bi],))
        P.tt("dve", xnew[bi], bank, xold[bi], ALU.add, (tb, t_xold[bi]), (t_xnew[bi],))
        P.dma("pool", dst_v[:, oi, tsl], xnew[bi], ds_xnew[bi], (t_xnew[bi],), cx.xs_tok(oi, tt))
    ygT = A.alloc(BF16, NCH * T); ygv = ygT.rearrange("p (c t) -> p c t", c=NCH)
    for c4 in range(4):
        P.dma("sp", ygv[:, c4 * 4:(c4 + 1) * 4, :], cx.ygs.rearrange("c p t -> p c t")[:, c4 * 4:(c4 + 1) * 4, :], dsm, (t_yg,), (t_yg,))
    linear_fm(cx, ygv, t_yg, W["w_o"], NCH, [(e * 128, 128) for e in range(NCH)], ev_o, bank0=2)
    P.barrier()
    P.free_dsems.extend(ds_xold + ds_xnew + [dsm])
    A.release()


def pack_cols(vecs):
    return np.ascontiguousarray(
        np.concatenate([np.asarray(v, np.float32).reshape(NCH, 128).T for v in vecs], axis=1))


def build(phases):
    nc = bass.Bass("TRN2", target_bir_lowering=False)
    names = [p[0] for p in phases]
    dram = {}

    def din(name, shape, dt=F32):
        dram[name] = nc.dram_tensor(name, list(shape), dt, kind="ExternalInput").ap()
        return dram[name]

    ins = []
    if "tin" in names:
        x_tm = din("x", [T, D]); ins.append("x")
    else:
        xs_in = din("xs_in", [NCH, 128, T]); ins.append("xs_in")
    if "tout" in names:
        out_ap = nc.dram_tensor("out", [T, D], F32, kind="ExternalOutput").ap()
        out_name = "out"
    else:
        out_ap = nc.dram_tensor("xs_out", [NCH, 128, T], F32, kind="ExternalOutput").ap()
        out_name = "xs_out"
    ident_d = din("ident", [128, 128]); ins.append("ident")
    ncols = 16 * 8
    cols_d = din("cols", [128, ncols]); ins.append("cols")
    for p in phases:
        if p[0] == "ffn":
            l, s = p[1], p[2]
            din(f"w13_{l}{s}", [D, 2 * FF]); ins.append(f"w13_{l}{s}")
            din(f"w2_{l}{s}", [FF, D]); ins.append(f"w2_{l}{s}")
        if p[0] == "rwkv":
            din("rw_w_rkv", [3, D, D]); din("rw_w1", [D, 96]); din("rw_w2", [96, D]); din("rw_a1", [D, 96])
            din("rw_a2", [96, D]); din("rw_g1", [D, 256]); din("rw_g2", [256, D]); din("rw_w_o", [D, D])
            din("rw_cols", [128, 13 * 16]); din("rw_masks", [128, 1536])
            ins.extend(["rw_w_rkv", "rw_w1", "rw_w2", "rw_a1", "rw_a2", "rw_g1", "rw_g2", "rw_w_o", "rw_cols", "rw_masks"])
        if p[0] == "mla":
            din("mla_wd", [D, 1152]); din("mla_wuq", [512, 16, 256]); din("mla_wukv", [512, 16, 256])
            din("mla_wo", [16, 128, D]); din("mla_cols", [128, 16]); din("mla_pos", [64, T], I32)
            ins.extend(["mla_wd", "mla_wuq", "mla_wukv", "mla_wo", "mla_cols", "mla_pos"])
    xs_a = nc.dram_tensor("xs_a", [NCH, 128, T], F32, kind="Internal").ap()
    has_rw = "rwkv" in names
    ots_d = nc.dram_tensor("ots", [MLA_H, 128, T], BF16, kind="Internal").ap() if "mla" in names else None
    if has_rw:
        xmix_d = nc.dram_tensor("xmix", [6, NCH, 128, T], BF16, kind="Internal").ap()
        rkv_d = nc.dram_tensor("rkv", [6, NCH, 128, T], F32, kind="Internal").ap()
        ygs_d = nc.dram_tensor("ygs", [NCH, 128, T], BF16, kind="Internal").ap()
        dbg_d = nc.dram_tensor("dbg_y", [NCH, 128, T], F32, kind="ExternalOutput").ap() if DBG else None

    from contextlib import ExitStack
    with ExitStack() as es:
        sb = es.enter_context(nc.sbuf_tensor("sb", [128, SB_BYTES // 4], F32))
        ps = es.enter_context(nc.psum_tensor("ps", [128, 4096], F32))
        esems = {e: es.enter_context(nc.semaphore("s_" + e)) for e in Prog.CE}
        dsems = [es.enter_context(nc.semaphore(f"d{i}")) for i in range(40)]
        block = es.enter_context(nc.Block())
        P = Prog(nc, esems, dsems)
        A = Arena(sb, SB_BYTES)
        cx = Ctx()
        cx.P, cx.A, cx.nc = P, A, nc
        cx.bank = lambda b: ps[:, b * 512:(b + 1) * 512]
        cx.t_out, cx.t_const = Tok(), Tok()
        xs_toks = [[Tok() for _ in range(T // 512)] for _ in range(NCH)]

        def xs_tok(c=None, tt=None):
            cs = range(NCH) if c is None else [c]
            ts_ = range(T // 512) if tt is None else [tt]
            return tuple(xs_toks[ci][ti] for ci in cs for ti in ts_)
        cx.xs_tok = xs_tok
        cx.ps_bf = ps.bitcast(BF16)
        cx.ots = ots_d
        if has_rw:
            cx.xmix, cx.rkv, cx.ygs, cx.dbg_y = xmix_d, rkv_d, ygs_d, dbg_d
            cx.t_xmix, cx.t_rkv = Tok(), Tok()
            cx.rw_masks_d = dram["rw_masks"]
        cx.ident = A.alloc(F32, 128)
        cx.cols = A.alloc(F32, ncols)
        cx.ones_bf = A.alloc(BF16, 128)
        dsc = P.dsem()
        P.dma("sp", cx.ident, ident_d, dsc, (), (cx.t_const,))
        P.dma("sp", cx.cols, cols_d, dsc, (), (cx.t_const,))
        P.memset("pool", cx.ones_bf, 1.0 / D, (cx.t_const,))
        cx.ones1_bf = A.alloc(BF16, 128)
        P.memset("pool", cx.ones1_bf, 1.0, (cx.t_const,))
        P.barrier()
        cur = None if "tin" in names else xs_in
        n_ph = len(phases)
        for i, p in enumerate(phases):
            last = (i == n_ph - 1)
            if p[0] == "tin":
                dst = out_ap if last else xs_a
                phase_tin(cx, x_tm, dst)
                cur = dst
            elif p[0] == "tout":
                phase_tout(cx, cur, out_ap)
            elif p[0] == "ffn":
                l, s = p[1], p[2]
                nxt_is_out = last
                dst = out_ap if nxt_is_out else xs_a
                if cur is not xs_a and dst is xs_a:
                    pass
                k = (l * 2 + s)
                (phase_ffn2 if FFN2 else phase_ffn)(cx, cur, dst, dram[f"w13_{l}{s}"], dram[f"w2_{l}{s}"], cx.cols[:, k * 16:(k + 1) * 16])
                cur = dst
            elif p[0] == "rwkv":
                dst = out_ap if last else xs_a
                Wd = {k: dram["rw_" + k] for k in ("w_rkv", "w1", "w2", "a1", "a2", "g1", "g2", "w_o")}
                phase_rwkv(cx, cur, dst, Wd, cx.cols[:, 4 * 16:5 * 16], dram["rw_cols"])
                cur = dst
            elif p[0] == "mla":
                dst = out_ap if last else xs_a
                phase_mla(cx, cur, dst, dram["mla_wd"], dram["mla_wuq"], dram["mla_wukv"], dram["mla_wo"],
                          cx.cols[:, 5 * 16:6 * 16], dram["mla_cols"], dram["mla_pos"])
                cur = dst
            else:
                raise ValueError(p)
        P.barrier()
        P.emit(block)
    return nc, ins, out_name, P


def host_consts(inputs):
    ident = np.eye(128, dtype=np.float32)
    fn = inputs["ffn_norm"]
    cols = pack_cols([fn[0, 0], fn[0, 1], fn[1, 0], fn[1, 1],
                      inputs["mix_norm"][0], inputs["mix_norm"][1], np.zeros(D), np.zeros(D)])
    return ident, cols


ROPE_PERM = np.concatenate([np.arange(32, 64), np.arange(0, 32)])


def mla_host(inputs, b):
    wd = inputs["mla_w_down"][0]
    wd_ext = np.ascontiguousarray(np.concatenate([wd, wd[:, 1024 + ROPE_PERM]], axis=1))
    wuq = inputs["mla_w_uq"][0]
    wuq_ext = np.ascontiguousarray(np.concatenate([wuq, wuq[:, :, 128 + ROPE_PERM]], axis=2))
    qn, kn = inputs["mla_q_norm"][0], inputs["mla_k_norm"][0]
    mc = np.zeros((128, 16), np.float32)
    mc[:, 0:4] = inputs["mla_q_a_norm"][0].reshape(4, 128).T
    mc[:, 4:8] = inputs["mla_kv_a_norm"][0].reshape(4, 128).T
    mc[:, 8] = qn[0:128]
    mc[:, 9] = kn[0:128]
    mc[0:64, 10] = qn[128:192]
    mc[0:64, 11] = qn[128 + ROPE_PERM]
    mc[0:64, 12] = kn[128:192]
    mc[0:64, 13] = kn[128 + ROPE_PERM]
    inv_freq = (np.float32(10000.0) ** (-np.arange(0, 64, 2, dtype=np.float32) / np.float32(64))).astype(np.float32)
    mc[0:64, 14] = np.concatenate([inv_freq, inv_freq])
    mc[0:32, 15] = -1.0
    mc[32:64, 15] = 1.0
    pos = np.ascontiguousarray(np.broadcast_to(inputs["positions"][b][None, :], (64, T))).astype(np.int32)
    return {"mla_wd": wd_ext, "mla_wuq": wuq_ext, "mla_wukv": np.ascontiguousarray(inputs["mla_w_ukv"][0]),
            "mla_wo": np.ascontiguousarray(inputs["mla_w_o"][0]), "mla_cols": mc, "mla_pos": pos}


def rwkv_host(inputs):
    g = lambda k: inputs["rwkv_" + k][0]
    vecs = [g("mu")[j] for j in range(6)] + [g("w0"), g("a0"), g("k_k"), g("k_a"), g("r_k").reshape(-1), g("ln_w"), g("ln_b")]
    idx = np.arange(128)
    same = (idx[:, None] // 64) == (idx[None, :] // 64)
    ti, tj = idx[:, None] % 64, idx[None, :] % 64
    ML_s = (same & (ti > tj)).astype(np.float32)
    MU_s = (same & (ti < tj)).astype(np.float32)
    MU_i = (same & (ti <= tj)).astype(np.float32)
    masks = np.ascontiguousarray(np.concatenate([np.tile(m, (1, 4)) for m in (ML_s, MU_s, MU_i)], axis=1))
    return {"rw_w_rkv": np.ascontiguousarray(g("w_rkv")), "rw_w1": g("w1"), "rw_w2": g("w2"), "rw_a1": g("a1"), "rw_a2": g("a2"),
            "rw_g1": g("g1"), "rw_g2": g("g2"), "rw_w_o": g("w_o"), "rw_cols": pack_cols(vecs), "rw_masks": masks}


PHASES = [("tin",), ("ffn", 0, 0), ("rwkv",), ("ffn", 0, 1), ("ffn", 1, 0), ("mla",), ("ffn", 1, 1), ("tout",)]


def make_feeds(inputs, b, shared=None):
    if shared is None:
        shared = {}
        ident, cols = host_consts(inputs)
        shared["ident"] = ident
        shared["cols"] = cols
        for l in range(2):
            for s_ in range(2):
                shared[f"w13_{l}{s_}"] = np.ascontiguousarray(np.asarray(inputs["ffn_w13"][l, s_], np.float32))
                shared[f"w2_{l}{s_}"] = np.ascontiguousarray(np.asarray(inputs["ffn_w2"][l, s_], np.float32))
        shared.update(rwkv_host(inputs))
        m = mla_host(inputs, 0)
        m.pop("mla_pos")
        shared.update(m)
    feeds = dict(shared)
    feeds["x"] = np.ascontiguousarray(np.asarray(inputs["x"][b], np.float32))
    feeds["mla_pos"] = np.ascontiguousarray(
        np.broadcast_to(np.asarray(inputs["positions"][b], np.int32)[None, :], (64, T)))
    return feeds, shared


def kernel(**inputs):
    inputs = {k: np.asarray(v) for k, v in inputs.items()}
    nb = inputs["x"].shape[0]
    nc, ins, out_name, _ = build(PHASES)
    in_maps = []
    shared = None
    for b in range(nb):
        feeds, shared = make_feeds(inputs, b, shared)
        in_maps.append({k: feeds[k] for k in ins})
    res = run_bass_kernel_spmd(nc, in_maps, core_ids=list(range(nb)))
    out = np.stack([np.asarray(res.results[b][out_name], np.float32) for b in range(nb)], axis=0)
    return out
```

```python
import numpy as np
import concourse.bass as bass
import concourse.mybir as mybir
from concourse.bass_utils import run_bass_kernel_spmd

F32 = mybir.dt.float32
BF16 = mybir.dt.bfloat16
I32 = mybir.dt.int32
AF = mybir.ActivationFunctionType
ALU = mybir.AluOpType
AX = mybir.AxisListType

T = 2048
D = 2048
FF = 5504
NCH = 16
NFC = 43
RMS_EPS = 1e-6
SB_BYTES = 207872


class Tok:
    __slots__ = ("w", "r", "excl")

    def __init__(self, excl=False):
        self.w = None
        self.r = {}
        self.excl = excl


class DSem:
    def __init__(self, h, idx):
        self.h = h
        self.idx = idx
        self.count = 0


class Prog:
    CE = ("pe", "act", "dve", "pool")

    def __init__(self, nc, esems, dsems):
        self.nc = nc
        self.code = {e: [] for e in ("pe", "act", "dve", "pool", "sp")}
        self.esem = esems
        self.ecnt = {e: 0 for e in self.CE}
        self.seen = {e: {} for e in self.code}
        self.free_dsems = [DSem(h, i) for i, h in enumerate(dsems)]
        self.all_dsems = list(self.free_dsems)
        self.ninstr = 0

    def dsem(self):
        return self.free_dsems.pop()

    def _need(self, eng, waits, ev):
        if ev is None:
            return
        kind, s, v = ev
        if kind == "d":
            v = s.count
            key = ("d", s.idx)
        else:
            if s == eng and eng == "pe":
                return
            key = ("e", s)
        if self.seen[eng].get(key, 0) >= v:
            return
        if waits.get(key, (None, 0))[1] < v:
            waits[key] = (s, v)

    def op(self, eng, fn, reads=(), writes=(), inc=True, dsem=None):
        if any(t.excl for t in reads):
            writes = tuple(writes) + tuple(t for t in reads if t.excl)
            reads = tuple(t for t in reads if not t.excl)
        waits = {}
        for t in reads:
            self._need(eng, waits, t.w)
        for t in writes:
            if t.w is not None and not (t.w[0] == "e" and t.w[1] == eng):
                self._need(eng, waits, t.w)
            for ev in t.r.values():
                if not (ev[0] == "e" and ev[1] == eng):
                    self._need(eng, waits, ev)
        wl = []
        for key, (s, v) in waits.items():
            self.seen[eng][key] = v
            wl.append((s.h if key[0] == "d" else self.esem[s], v))
        if dsem is not None:
            dsem.count += 16
            ev = ("d", dsem, dsem.count)
            incspec = (dsem.h, 16)
            rkey = ("d", dsem.idx)
        else:
            if inc:
                self.ecnt[eng] += 1
                ev = ("e", eng, self.ecnt[eng])
                incspec = (self.esem[eng], 1)
            else:
                ev = ("e", eng, self.ecnt[eng] + 1)
                incspec = None
            rkey = ("e", eng)
        for t in reads:
            t.r[rkey] = ev
        for t in writes:
            t.w = ev
            t.r = {}
        self.code[eng].append((wl, fn, incspec))
        self.ninstr += 1

    def barrier(self):
        evs = [("e", e, self.ecnt[e]) for e in self.CE if self.ecnt[e] > 0]
        evs += [("d", d, d.count) for d in self.all_dsems if d.count > 0]
        for eng in self.code:
            waits = {}
            for ev in evs:
                if ev[0] == "e" and ev[1] == eng and eng == "pe":
                    continue
                self._need(eng, waits, ev)
            wl = []
            for key, (s, v) in waits.items():
                self.seen[eng][key] = v
                wl.append((s.h if key[0] == "d" else self.esem[s], v))
            if wl:
                self.code[eng].append((wl, None, None))

    def emit(self, block):
        def mk(name):
            def body(e):
                for wl, fn, incspec in self.code[name]:
                    for h, v in wl:
                        e.wait_ge(h, v)
                    if fn is None:
                        continue
                    ins = fn(e)
                    if incspec is not None:
                        ins.then_inc(incspec[0], incspec[1])
            return body

        block.tensor(mk("pe"))
        block.scalar(mk("act"))
        block.vector(mk("dve"))
        block.gpsimd(mk("pool"))
        block.sync(mk("sp"))

    def mm(self, out, lhsT, rhs, start, stop, reads, writes, inc=None):
        self.op("pe", lambda e: e.matmul(out, lhsT, rhs, start=start, stop=stop),
                reads, writes, inc=(stop if inc is None else inc))

    def tr(self, out, in_, ident, reads, writes, inc=True):
        self.op("pe", lambda e: e.transpose(out, in_, ident), reads, writes, inc=inc)

    def dma(self, q, out, in_, dsem, reads, writes):
        self.op(q, lambda e: e.dma_start(out=out, in_=in_), reads, writes, dsem=dsem)

    def actf(self, out, in_, func, reads, writes, bias=None, scale=None, eng="act"):
        kw = {}
        if bias is not None:
            kw["bias"] = bias
        if scale is not None:
            kw["scale"] = scale
        self.op("act", lambda e: e.activation(out, in_, func, **kw), reads, writes)

    def copy(self, eng, out, in_, reads, writes):
        if eng == "act":
            self.op("act", lambda e: e.copy(out, in_), reads, writes)
        else:
            self.op(eng, lambda e: e.tensor_copy(out, in_), reads, writes)

    def tt(self, eng, out, in0, in1, op, reads, writes):
        self.op(eng, lambda e: e.tensor_tensor(out, in0, in1, op), reads, writes)

    def ts(self, eng, out, in0, s1, s2, op0, op1, reads, writes):
        if s2 is None:
            self.op(eng, lambda e: e.tensor_scalar(out, in0, s1, None, op0), reads, writes)
        else:
            self.op(eng, lambda e: e.tensor_scalar(out, in0, s1, s2, op0, op1), reads, writes)

    def stt(self, eng, out, in0, scalar, in1, op0, op1, reads, writes):
        self.op(eng, lambda e: e.scalar_tensor_tensor(out, in0, scalar, in1, op0, op1), reads, writes)

    def memset(self, eng, ap, val, writes):
        self.op(eng, lambda e: e.memset(ap, val), (), writes)


class Arena:
    def __init__(self, t32, nbytes):
        self.v = {F32: t32, BF16: t32.bitcast(BF16), I32: t32.bitcast(I32)}
        self.cap = nbytes
        self.top = 0
        self.marks = []

    def alloc(self, dtype, n, parts=128, p0=0):
        sz = 2 if dtype == BF16 else 4
        off = (self.top + 63) // 64 * 64
        self.top = off + n * sz
        assert self.top <= self.cap, f"SBUF arena overflow {self.top} > {self.cap}"
        return self.v[dtype][p0:p0 + parts, off // sz: off // sz + n]

    def mark(self):
        self.marks.append(self.top)

    def release(self):
        self.top = self.marks.pop()


class Ctx:
    pass


def phase_tin(cx, x_tm, xs_dst):
    P, A = cx.P, cx.A
    A.mark()
    xin = [A.alloc(F32, D) for _ in range(2)]
    xo = [A.alloc(F32, NCH * 128) for _ in range(2)]
    t_in = [Tok() for _ in range(2)]
    t_o = [Tok() for _ in range(2)]
    ds_in = [P.dsem() for _ in range(2)]
    ds_o = [P.dsem() for _ in range(2)]
    t_ps = [Tok(True) for _ in range(2)]
    dst_v = xs_dst.rearrange("c p t -> p c t")
    for tb in range(T // 128):
        s = tb % 2
        P.dma("sp", xin[s], x_tm[tb * 128:(tb + 1) * 128, :], ds_in[s], (), (t_in[s],))
        for q in range(4):
            b = (tb * 4 + q) % 2
            bank = cx.bank(b)
            for i in range(4):
                c = q * 4 + i
                P.tr(bank[:, i * 128:(i + 1) * 128], xin[s][:, c * 128:(c + 1) * 128], cx.ident,
                     (t_in[s],), (t_ps[b],), inc=(i == 3))
            eng = "dve" if q % 2 == 0 else "act"
            P.copy(eng, xo[s][:, q * 512:(q + 1) * 512], bank, (t_ps[b],), (t_o[s],))
        P.dma("sp", dst_v[:, :, tb * 128:(tb + 1) * 128],
              xo[s].rearrange("p (c t) -> p c t", c=NCH), ds_o[s], (t_o[s],), cx.xs_tok(None, tb // 4))
    P.barrier()
    for d in ds_in + ds_o:
        P.free_dsems.append(d)
    A.release()


def phase_tout(cx, xs_src, out_tm):
    P, A = cx.P, cx.A
    A.mark()
    xin = [A.alloc(F32, NCH * 128) for _ in range(2)]
    xo = [A.alloc(F32, D) for _ in range(2)]
    t_in = [Tok() for _ in range(2)]
    t_o = [Tok() for _ in range(2)]
    ds_in = [P.dsem() for _ in range(2)]
    ds_o = [P.dsem() for _ in range(2)]
    t_ps = [Tok(True) for _ in range(2)]
    src_v = xs_src.rearrange("c p t -> p c t")
    for tb in range(T // 128):
        s = tb % 2
        P.dma("sp", xin[s].rearrange("p (c t) -> p c t", c=NCH), src_v[:, :, tb * 128:(tb + 1) * 128],
              ds_in[s], cx.xs_tok(None, tb // 4), (t_in[s],))
        for q in range(4):
            b = (tb * 4 + q) % 2
            bank = cx.bank(b)
            for i in range(4):
                c = q * 4 + i
                P.tr(bank[:, i * 128:(i + 1) * 128], xin[s][:, c * 128:(c + 1) * 128], cx.ident,
                     (t_in[s],), (t_ps[b],), inc=(i == 3))
            eng = "dve" if q % 2 == 0 else "act"
            P.copy(eng, xo[s][:, q * 512:(q + 1) * 512], bank, (t_ps[b],), (t_o[s],))
        P.dma("sp", out_tm[tb * 128:(tb + 1) * 128, :], xo[s], ds_o[s], (t_o[s],), (cx.t_out,))
    P.barrier()
    for d in ds_in + ds_o:
        P.free_dsems.append(d)
    A.release()


def rmsnorm_tile(cx, xt, t_xt, gcol, hT_out, t_h, ntok, sqb, t_sq, rstd, t_rstd, bank, t_bank):
    P = cx.P
    xv = xt.rearrange("p (c t) -> p c t", c=NCH)
    for c in range(NCH):
        s = c % len(sqb)
        P.actf(sqb[s][:, :ntok], xv[:, c, :], AF.Square, (t_xt,), (t_sq[s],))
        P.mm(bank[:, :ntok], cx.ones_bf, sqb[s][:, :ntok], c == 0, c == NCH - 1,
             (t_sq[s], cx.t_const), (t_bank,), inc=True)
    P.ts("dve", rstd[:, :ntok], bank[:, :ntok], RMS_EPS, None, ALU.add, None, (t_bank,), (t_rstd,))
    P.actf(rstd[:, :ntok], rstd[:, :ntok], AF.Ln, (t_rstd,), (t_rstd,))
    P.actf(rstd[:, :ntok], rstd[:, :ntok], AF.Exp, (t_rstd,), (t_rstd,), scale=-0.5)
    for c in range(NCH):
        P.stt("dve", hT_out(c), xv[:, c, :], gcol[:, c:c + 1], rstd[:, :ntok], ALU.mult, ALU.mult,
              (t_xt, t_rstd, cx.t_const), (t_h,))


def phase_ffn(cx, xs_src, xs_dst, w13, w2, gcol):
    P, A = cx.P, cx.A
    A.mark()
    HALF = 1024
    NTT = HALF // 512
    hT = A.alloc(BF16, NCH * HALF)
    hTv = hT.rearrange("p (c t) -> p c t", c=NCH)
    actT = A.alloc(BF16, NFC * HALF)
    actTv = actT.rearrange("p (j t) -> p j t", j=NFC)
    t_h = Tok()
    t_act = [Tok() for _ in range(NFC)]
    sqb = [A.alloc(BF16, 512) for _ in range(3)]
    t_sq = [Tok() for _ in range(3)]
    rstd = A.alloc(F32, 512)
    t_rstd = Tok()
    sg = [A.alloc(F32, 512) for _ in range(2)]
    t_sg = [Tok() for _ in range(2)]
    xold = [A.alloc(F32, 512) for _ in range(2)]
    t_xold = [Tok() for _ in range(2)]
    ds_xold = [P.dsem() for _ in range(2)]
    xnew = [A.alloc(F32, 512) for _ in range(2)]
    t_xnew = [Tok() for _ in range(2)]
    ds_xnew = [P.dsem() for _ in range(2)]
    tops = []
    A.mark()
    xt = [A.alloc(F32, NCH * 512) for _ in range(2)]
    tops.append(A.top); A.release(); A.mark()
    w13s = [A.alloc(F32, 2 * NCH * 128) for _ in range(2)]
    w13b = [A.alloc(BF16, 2 * NCH * 128) for _ in range(2)]
    tops.append(A.top); A.release(); A.mark()
    w2s = [A.alloc(F32, NFC * 128) for _ in range(2)]
    w2b = [A.alloc(BF16, NFC * 128) for _ in range(2)]
    tops.append(A.top); A.release()
    A.top = max(tops)
    t_reg = [Tok() for _ in range(2)]
    t_regb = [Tok() for _ in range(2)]
    t_regb2 = [Tok() for _ in range(2)]
    ds_stage = [P.dsem() for _ in range(2)]
    src_v = xs_src.rearrange("c p t -> p c t")
    dst_v = xs_dst.rearrange("c p t -> p c t")
    w13v = w13.rearrange("(c p) f -> p c f", p=128)
    w2v = w2.rearrange("(j p) e -> p j e", p=128)
    t_bA = Tok(True)
    t_bB = [Tok(True) for _ in range(4)]
    t_bC = [Tok(True) for _ in range(2)]
    for th in range(T // HALF):
        tok0 = th * HALF
        for tt in range(NTT):
            s = tt % 2
            P.dma("sp", xt[s].rearrange("p (c t) -> p c t", c=NCH),
                  src_v[:, :, tok0 + tt * 512: tok0 + (tt + 1) * 512], ds_stage[s], cx.xs_tok(None, th * NTT + tt), (t_reg[s],))
            rmsnorm_tile(cx, xt[s], t_reg[s], gcol, lambda c, tt=tt: hTv[:, c, tt * 512:(tt + 1) * 512], t_h,
                         512, sqb, t_sq, rstd, t_rstd, cx.bank(6), t_bA)
        P.barrier()
        def b_load(j):
            if not DMACAST:
                s = j % 2
                stg = w13s[s].rearrange("p (g c f) -> p g c f", g=2, c=NCH)
                P.dma("sp", stg[:, 0], w13v[:, :, j * 128:(j + 1) * 128], ds_stage[s], (), (t_reg[s],))
                P.dma("sp", stg[:, 1], w13v[:, :, FF + j * 128: FF + (j + 1) * 128], ds_stage[s], (), (t_reg[s],))

        def b_cast(j):
            s = j % 2
            stg = w13s[s].rearrange("p (g c f) -> p g c f", g=2, c=NCH)
            stb = w13b[s].rearrange("p (g c f) -> p g c f", g=2, c=NCH)
            if DMACAST:
                P.dma("pool", stb[:, 0], w13v[:, :, j * 128:(j + 1) * 128], ds_stage[s], (), (t_regb[s],))
                P.dma("pool", stb[:, 1], w13v[:, :, FF + j * 128: FF + (j + 1) * 128], ds_stage[s], (), (t_regb2[s],))
            else:
                P.copy("dve", stb[:, 0], stg[:, 0], (t_reg[s],), (t_regb[s],))
                P.copy("act", stb[:, 1], stg[:, 1], (t_reg[s],), (t_regb2[s],))

        b_load(0)
        b_load(1)
        b_cast(0)
        for j in range(NFC):
            s = j % 2
            stb = w13b[s].rearrange("p (g c f) -> p g c f", g=2, c=NCH)
            if j + 1 < NFC:
                b_cast(j + 1)
            if j + 2 < NFC:
                b_load(j + 2)
            for tt in range(NTT):
                bi = (j * NTT + tt) % 2
                bg, bu = cx.bank(2 * bi), cx.bank(2 * bi + 1)
                tg, tu = t_bB[2 * bi], t_bB[2 * bi + 1]
                rhs_t = slice(tt * 512, (tt + 1) * 512)
                for c in range(NCH):
                    P.mm(bg, stb[:, 0, c, :], hTv[:, c, rhs_t], c == 0, c == NCH - 1, (t_regb[s], t_h), (tg,))
                for c in range(NCH):
                    P.mm(bu, stb[:, 1, c, :], hTv[:, c, rhs_t], c == 0, c == NCH - 1, (t_regb2[s], t_h), (tu,))
                P.actf(sg[bi], bg, AF.Silu, (tg,), (t_sg[bi],))
                P.tt("dve", actTv[:, j, rhs_t], sg[bi], bu, ALU.mult, (t_sg[bi], tu), (t_act[j],))
        P.barrier()
        def c_load(e):
            s = e % 2
            stg = w2s[s].rearrange("p (j e) -> p j e", j=NFC)
            P.dma("sp", stg[:, 0:22, :], w2v[:, 0:22, e * 128:(e + 1) * 128], ds_stage[s], (), (t_reg[s],))
            P.dma("sp", stg[:, 22:NFC, :], w2v[:, 22:NFC, e * 128:(e + 1) * 128], ds_stage[s], (), (t_reg[s],))

        def c_cast(e):
            s = e % 2
            stg = w2s[s].rearrange("p (j e) -> p j e", j=NFC)
            stb = w2b[s].rearrange("p (j e) -> p j e", j=NFC)
            P.copy("dve", stb[:, 0:22, :], stg[:, 0:22, :], (t_reg[s],), (t_regb[s],))
            P.copy("act", stb[:, 22:NFC, :], stg[:, 22:NFC, :], (t_reg[s],), (t_regb2[s],))

        c_load(0)
        c_load(1)
        c_cast(0)
        for e in range(NCH):
            s = e % 2
            stb = w2b[s].rearrange("p (j e) -> p j e", j=NFC)
            if e + 1 < NCH:
                c_cast(e + 1)
            if e + 2 < NCH:
                c_load(e + 2)
            for tt in range(NTT):
                bi = (e * NTT + tt) % 2
                bk, tb_ = cx.bank(4 + bi), t_bC[bi]
                rhs_t = slice(tt * 512, (tt + 1) * 512)
                tsl = slice(tok0 + tt * 512, tok0 + (tt + 1) * 512)
                P.dma("act", xold[bi], src_v[:, e, tsl], ds_xold[bi], cx.xs_tok(e, th * NTT + tt), (t_xold[bi],))
                for j in range(NFC):
                    P.mm(bk, stb[:, j, :], actTv[:, j, rhs_t], j == 0, j == NFC - 1,
                         (t_regb[s] if j < 22 else t_regb2[s], t_act[j]), (tb_,))
                P.stt("dve", xnew[bi], bk, 0.5, xold[bi], ALU.mult, ALU.add, (tb_, t_xold[bi]), (t_xnew[bi],))
                P.dma("pool", dst_v[:, e, tsl], xnew[bi], ds_xnew[bi], (t_xnew[bi],), cx.xs_tok(e, th * NTT + tt))
        P.barrier()
    for d in ds_xold + ds_xnew + ds_stage:
        P.free_dsems.append(d)
    A.release()


def phase_ffn2(cx, xs_src, xs_dst, w13, w2, gcol):
    P, A = cx.P, cx.A
    A.mark()
    HALF, NTT, TA = 1024, 2, 256
    hT = A.alloc(BF16, NCH * HALF); hTv = hT.rearrange("p (c t) -> p c t", c=NCH); t_h = Tok()
    actT = A.alloc(BF16, NFC * HALF); actTv = actT.rearrange("p (j t) -> p j t", j=NFC)
    t_act = [Tok() for _ in range(NFC)]
    w13b = [A.alloc(BF16, 2 * NCH * 128) for _ in range(2)]
    t_wg = [Tok() for _ in range(2)]; t_wu = [Tok() for _ in range(2)]; ds_w13 = [P.dsem() for _ in range(2)]
    w2b = [A.alloc(BF16, NFC * 128) for _ in range(2)]
    t_w2a = [Tok() for _ in range(2)]; t_w2b = [Tok() for _ in range(2)]; ds_w2 = [P.dsem() for _ in range(2)]
    xt = A.alloc(F32, NCH * TA); t_xt = Tok(); ds_xt = P.dsem()
    xtv = xt.rearrange("p (c t) -> p c t", c=NCH)
    sq = [A.alloc(BF16, TA) for _ in range(NCH)]; t_sqc = [Tok() for _ in range(NCH)]
    rstd = A.alloc(F32, TA); t_rstd = Tok()
    sg = [A.alloc(F32, 512) for _ in range(2)]; t_sg = [Tok() for _ in range(2)]
    xold = [A.alloc(F32, 512) for _ in range(2)]; t_xold = [Tok() for _ in range(2)]
    ds_xold = [P.dsem() for _ in range(2)]; ds_xnew = [P.dsem() for _ in range(2)]
    src_v = xs_src.rearrange("c p t -> p c t")
    dst_v = xs_dst.rearrange("c p t -> p c t")
    w13v = w13.rearrange("(c p) f -> p c f", p=128)
    w2v = w2.rearrange("(j p) e -> p j e", p=128)
    t_bA = Tok(True); t_bB = [Tok(True) for _ in range(4)]; t_bC = [Tok(True) for _ in range(2)]
    bankA = cx.bank(6)

    xt2v = actT[:, 0:2 * NCH * TA].bitcast(F32).rearrange("p (c t) -> p c t", c=NCH)
    t_xt2 = Tok(); ds_xt2 = P.dsem()
    sq2 = [actT[:, 2 * NCH * TA + c * TA: 2 * NCH * TA + (c + 1) * TA] for c in range(NCH)]
    t_sq2 = [Tok() for _ in range(NCH)]

    def a_s1(th, i, alt=False):
        t0 = th * HALF + i * TA
        xv, tx, dsx, sqs, tsq = (xt2v, t_xt2, ds_xt2, sq2, t_sq2) if alt else (xtv, t_xt, ds_xt, sq, t_sqc)
        P.dma("act", xv, src_v[:, :, t0:t0 + TA], dsx, cx.xs_tok(None, t0 // 512), (tx,))
        for c in range(NCH):
            P.actf(sqs[c], xv[:, c, :], AF.Square, (tx,), (tsq[c],))

    def a_s2(th, i, alt=False):
        sqs, tsq = (sq2, t_sq2) if alt else (sq, t_sqc)
        for c in range(NCH):
            P.mm(bankA[:, :TA], cx.ones_bf, sqs[c], c == 0, c == NCH - 1, (tsq[c], cx.t_const), (t_bA,), inc=(c == NCH - 1))

    def a_s3(th, i, alt=False):
        xv, tx = (xt2v, t_xt2) if alt else (xtv, t_xt)
        P.ts("dve", rstd, bankA[:, :TA], RMS_EPS, None, ALU.add, None, (t_bA,), (t_rstd,))
        P.actf(rstd, rstd, AF.Ln, (t_rstd,), (t_rstd,))
        P.actf(rstd, rstd, AF.Exp, (t_rstd,), (t_rstd,), scale=-0.5)
        for c in range(NCH):
            P.stt("dve", hTv[:, c, i * TA:(i + 1) * TA], xv[:, c, :], gcol[:, c:c + 1], rstd, ALU.mult, ALU.mult,
                  (tx, t_rstd, cx.t_const), (t_h,))

    def b_fetch(j):
        s = j % 2
        stb = w13b[s].rearrange("p (g c f) -> p g c f", g=2, c=NCH)
        P.dma("pool", stb[:, 0], w13v[:, :, j * 128:(j + 1) * 128], ds_w13[s], (), (t_wg[s],))
        P.dma("pool", stb[:, 1], w13v[:, :, FF + j * 128: FF + (j + 1) * 128], ds_w13[s], (), (t_wu[s],))

    def c_fetch(e):
        s = e % 2
        stb = w2b[s].rearrange("p (j e) -> p j e", j=NFC)
        P.dma("pool", stb[:, 0:22, :], w2v[:, 0:22, e * 128:(e + 1) * 128], ds_w2[s], (), (t_w2a[s],))
        P.dma("pool", stb[:, 22:NFC, :], w2v[:, 22:NFC, e * 128:(e + 1) * 128], ds_w2[s], (), (t_w2b[s],))

    b_fetch(0)
    b_fetch(1)
    nA = HALF // TA
    a_s1(0, 0, False)
    for i in range(nA):
        if i + 1 < nA:
            a_s1(0, i + 1, (i + 1) % 2 == 1)
        a_s2(0, i, i % 2 == 1)
        a_s3(0, i, i % 2 == 1)
    P.barrier()
    sched = {}
    for i in range(nA):
        sched[4 * i] = (a_s1, i)
        sched[4 * i + 2] = (a_s2, i)
        sched[4 * i + 3] = (a_s3, i)
    for th in range(T // HALF):
        tok0 = th * HALF
        for j in range(NFC):
            s = j % 2
            stb = w13b[s].rearrange("p (g c f) -> p g c f", g=2, c=NCH)
            for tt in range(NTT):
                bi = (j * NTT + tt) % 2
                bg, bu = cx.bank(2 * bi), cx.bank(2 * bi + 1)
                tg, tu = t_bB[2 * bi], t_bB[2 * bi + 1]
                rhs_t = slice(tt * 512, (tt + 1) * 512)
                for c in range(NCH):
                    P.mm(bg, stb[:, 0, c, :], hTv[:, c, rhs_t], c == 0, c == NCH - 1, (t_wg[s], t_h), (tg,))
                for c in range(NCH):
                    P.mm(bu, stb[:, 1, c, :], hTv[:, c, rhs_t], c == 0, c == NCH - 1, (t_wu[s], t_h), (tu,))
                P.actf(sg[bi], bg, AF.Silu, (tg,), (t_sg[bi],))
                P.tt("dve", actTv[:, j, rhs_t], sg[bi], bu, ALU.mult, (t_sg[bi], tu), (t_act[j],))
            if j + 2 < NFC:
                b_fetch(j + 2)
            if j == NFC - 3:
                c_fetch(0)
            if j == NFC - 2:
                c_fetch(1)
        if th + 1 < T // HALF:
            b_fetch(0)
            b_fetch(1)
        for e in range(NCH):
            s = e % 2
            stb = w2b[s].rearrange("p (j e) -> p j e", j=NFC)
            if th + 1 < T // HALF and e in sched:
                fn, i = sched[e]
                fn(th + 1, i)
            for tt in range(NTT):
                bi = (e * NTT + tt) % 2
                bk, tb_ = cx.bank(4 + bi), t_bC[bi]
                rhs_t = slice(tt * 512, (tt + 1) * 512)
                tsl = slice(tok0 + tt * 512, tok0 + (tt + 1) * 512)
                xtok = cx.xs_tok(e, th * NTT + tt)
                P.dma("act", xold[bi], src_v[:, e, tsl], ds_xold[bi], xtok, (t_xold[bi],))
                for j in range(NFC):
                    P.mm(bk, stb[:, j, :], actTv[:, j, rhs_t], j == 0, j == NFC - 1,
                         (t_w2a[s] if j < 22 else t_w2b[s], t_act[j]), (tb_,))
                P.stt("dve", xold[bi], bk, 0.5, xold[bi], ALU.mult, ALU.add, (tb_, t_xold[bi]), (t_xold[bi],))
                P.dma("sp", dst_v[:, e, tsl], xold[bi], ds_xnew[bi], (t_xold[bi],), xtok)
            if e + 2 < NCH:
                c_fetch(e + 2)
    P.barrier()
    for d in ds_w13 + ds_w2 + [ds_xt, ds_xt2] + ds_xold + ds_xnew:
        P.free_dsems.append(d)
    A.release()


def norm_from_bank(cx, rstd, t_rstd, bank, t_bank, n, mean_scale, eps, parts=128):
    P = cx.P
    P.ts("dve", rstd[:parts, :n], bank[:parts, :n], mean_scale, eps, ALU.mult, ALU.add, (t_bank,), (t_rstd,))
    P.actf(rstd[:parts, :n], rstd[:parts, :n], AF.Ln, (t_rstd,), (t_rstd,))
    P.actf(rstd[:parts, :n], rstd[:parts, :n], AF.Exp, (t_rstd,), (t_rstd,), scale=-0.5)


MLA_H = 16
STOP = 0
DBG = False
DMACAST = False
FFN2 = True
SM_SCALE = 1.0 / float(np.sqrt(192.0))


def phase_mla(cx, xs_src, xs_dst, wd, wuq, wukv, wo, gcol, mcols_d, pos_d):
    P, A = cx.P, cx.A
    A.mark()
    src_v = xs_src.rearrange("c p t -> p c t")
    dst_v = xs_dst.rearrange("c p t -> p c t")
    mc = A.alloc(F32, 16)
    t_mc = Tok()
    dsm = P.dsem()
    P.dma("sp", mc, mcols_d, dsm, (), (t_mc,))
    cqn = A.alloc(BF16, 4 * T); cqnv = cqn.rearrange("p (c t) -> p c t", c=4)
    ckvn = A.alloc(BF16, 4 * T); ckvnv = ckvn.rearrange("p (c t) -> p c t", c=4)
    kpe = A.alloc(F32, T)
    kpesw = A.alloc(F32, T)
    t_cqn, t_ckvn, t_kpe = Tok(), Tok(), Tok()
    rstd = A.alloc(F32, 512); t_rstd = Tok()
    sqb = [A.alloc(BF16, 512) for _ in range(3)]; t_sq = [Tok() for _ in range(3)]
    A.mark()
    wdb = A.alloc(BF16, NCH * 1152); wdbv = wdb.rearrange("p (c f) -> p c f", c=NCH)
    t_wdb = Tok()
    stg = [A.alloc(F32, NCH * 128) for _ in range(2)]; t_stg = [Tok() for _ in range(2)]
    ds_stg = [P.dsem() for _ in range(2)]
    wdv = wd.rearrange("(c p) f -> p c f", p=128)
    for i in range(9):
        s = i % 2
        P.dma("sp", stg[s].rearrange("p (c f) -> p c f", c=NCH), wdv[:, :, i * 128:(i + 1) * 128], ds_stg[s], (), (t_stg[s],))
        P.copy("act" if i % 2 == 0 else "dve", wdbv[:, :, i * 128:(i + 1) * 128], stg[s].rearrange("p (c f) -> p c f", c=NCH), (t_stg[s],), (t_wdb,))
    if STOP == 11:
        P.barrier(); A.release(); A.release(); return
    xt = A.alloc(F32, NCH * 512); t_xt = Tok(); ds_xt = P.dsem()
    hT = A.alloc(BF16, NCH * 512); hTv = hT.rearrange("p (c t) -> p c t", c=NCH); t_h = Tok()
    cT = A.alloc(F32, 8 * 512); cTv = cT.rearrange("p (c t) -> p c t", c=8); t_cT = Tok()
    t_b = [Tok(True) for _ in range(8)]
    for tt in range(4):
        tsl = slice(tt * 512, (tt + 1) * 512)
        P.dma("sp", xt.rearrange("p (c t) -> p c t", c=NCH), src_v[:, :, tsl], ds_xt, cx.xs_tok(None, tt), (t_xt,))
        rmsnorm_tile(cx, xt, t_xt, gcol, lambda c: hTv[:, c, :], t_h, 512, sqb, t_sq, rstd, t_rstd, cx.bank(6), t_b[6])
        for oc in range(10):
            if STOP == 12 or (STOP == 13 and oc >= 8):
                break
            b = oc % 2
            bank = cx.bank(b)
            if oc < 8:
                for c in range(NCH):
                    P.mm(bank, wdbv[:, c, oc * 128:(oc + 1) * 128], hTv[:, c, :], c == 0, c == NCH - 1, (t_wdb, t_h), (t_b[b],))
                eng = "act" if oc % 2 == 0 else "dve"
                P.copy(eng, cTv[:, oc, :], bank, (t_b[b],), (t_cT,))
            else:
                c0 = 1024 + (oc - 8) * 64
                for c in range(NCH):
                    P.mm(bank[0:64, :], wdbv[:, c, c0:c0 + 64], hTv[:, c, :], c == 0, c == NCH - 1, (t_wdb, t_h), (t_b[b],))
                dstb = kpe if oc == 8 else kpesw
                P.copy("act", dstb[0:64, tsl], bank[0:64, :], (t_b[b],), (t_kpe,))
        for which in range(2):
            if STOP in (12, 13, 14):
                break
            for c in range(4):
                s = c % 3
                P.actf(sqb[s], cTv[:, which * 4 + c, :], AF.Square, (t_cT,), (t_sq[s],))
                P.mm(cx.bank(6), cx.ones1_bf, sqb[s], c == 0, c == 3, (t_sq[s], cx.t_const), (t_b[6],), inc=True)
            norm_from_bank(cx, rstd, t_rstd, cx.bank(6), t_b[6], 512, 1.0 / 512.0, RMS_EPS)
            dstv, tk = (cqnv, t_cqn) if which == 0 else (ckvnv, t_ckvn)
            for c in range(4):
                P.stt("dve", dstv[:, c, tsl], cTv[:, which * 4 + c, :], mc[:, which * 4 + c: which * 4 + c + 1], rstd,
                      ALU.mult, ALU.mult, (t_cT, t_rstd, t_mc), (tk,))
    P.barrier()
    A.release()
    for d in ds_stg + [ds_xt]:
        P.free_dsems.append(d)
    if STOP == 1:
        A.release(); return
    Cq = A.alloc(F32, T); Sq = A.alloc(F32, T)
    t_tab = Tok()
    kperot = A.alloc(F32, T); sqkpe = A.alloc(BF16, T); t_kr = Tok()
    t_OT = Tok()
    A.mark()
    posi = A.alloc(I32, T); posf = A.alloc(F32, T); ang = posf
    t_pos, t_ang = Tok(), Tok()
    t_ang = t_pos
    Ck = A.alloc(F32, T); Sk = A.alloc(F32, T)
    tmp = A.alloc(F32, T); t_tmp = Tok()
    P.dma("sp", posi[0:64, :], pos_d, dsm, (), (t_pos,))
    P.copy("dve", posf[0:64, :], posi[0:64, :], (t_pos,), (t_pos,))
    TWO_PI = float(2.0 * np.pi)
    PI = float(np.pi)
    P.ts("dve", ang[0:64, :], posf[0:64, :], mc[0:64, 14:15], None, ALU.mult, None, (t_pos, t_mc), (t_ang,))
    ki = posi
    C1 = 6.28125
    C2 = float(2.0 * np.pi - 6.28125)

    def sin_of(out, shift):
        P.ts("dve", tmp[0:64, :], ang[0:64, :], shift, 1.0 / TWO_PI, ALU.add, ALU.mult, (t_ang,), (t_tmp,))
        P.copy("dve", ki[0:64, :], tmp[0:64, :], (t_tmp,), (t_ki,))
        P.copy("dve", tmp[0:64, :], ki[0:64, :], (t_ki,), (t_tmp,))
        P.ts("dve", out, ang[0:64, :], shift, None, ALU.add, None, (t_ang,), (t_tab,))
        P.stt("dve", out, tmp[0:64, :], -C1, out, ALU.mult, ALU.add, (t_tmp, t_tab), (t_tab,))
        P.stt("dve", out, tmp[0:64, :], -C2, out, ALU.mult, ALU.add, (t_tmp, t_tab), (t_tab,))
        P.ts("dve", tmp[0:64, :], out, PI, TWO_PI, ALU.is_gt, ALU.mult, (t_tab,), (t_tmp,))
        P.tt("dve", out, out, tmp[0:64, :], ALU.subtract, (t_tab, t_tmp), (t_tab,))
        P.ts("dve", out, out, -PI, PI, ALU.max, ALU.min, (t_tab,), (t_tab,))
        P.actf(out, out, AF.Sin, (t_tab,), (t_tab,))

    t_ki = Tok()
    sin_of(Sq[0:64, :], 0.0)
    sin_of(Cq[0:64, :], 0.5 * PI)
    P.ts("dve", Sq[0:64, :], Sq[0:64, :], mc[0:64, 15:16], None, ALU.mult, None, (t_tab, t_mc), (t_tab,))
    P.ts("dve", Ck[0:64, :], Cq[0:64, :], mc[0:64, 12:13], None, ALU.mult, None, (t_tab, t_mc), (t_tab,))
    P.ts("dve", Sk[0:64, :], Sq[0:64, :], mc[0:64, 13:14], None, ALU.mult, None, (t_tab, t_mc), (t_tab,))
    P.ts("dve", Cq[0:64, :], Cq[0:64, :], mc[0:64, 10:11], None, ALU.mult, None, (t_tab, t_mc), (t_tab,))
    P.ts("dve", Sq[0:64, :], Sq[0:64, :], mc[0:64, 11:12], None, ALU.mult, None, (t_tab, t_mc), (t_tab,))
    P.tt("dve", kperot[0:64, :], kpe[0:64, :], Ck[0:64, :], ALU.mult, (t_kpe, t_tab), (t_kr,))
    P.tt("dve", tmp[0:64, :], kpesw[0:64, :], Sk[0:64, :], ALU.mult, (t_kpe, t_tab), (t_tmp,))
    P.tt("dve", kperot[0:64, :], kperot[0:64, :], tmp[0:64, :], ALU.add, (t_kr, t_tmp), (t_kr,))
    P.memset("pool", sqkpe, 0.0, (t_kr,))
    P.actf(sqkpe[0:64, :], kpe[0:64, :], AF.Square, (t_kpe,), (t_kr,))
    P.barrier()
    A.release()
    if STOP == 2:
        A.release(); return
    A.mark()
    wq_s = [A.alloc(F32, 4 * 256) for _ in range(2)]; wq_b = [A.alloc(BF16, 4 * 256) for _ in range(2)]
    wk_s = [A.alloc(F32, 4 * 256) for _ in range(2)]; wk_b = [A.alloc(BF16, 4 * 256) for _ in range(2)]
    t_wqs = [Tok() for _ in range(2)]; t_wqb = [Tok() for _ in range(2)]
    t_wks = [Tok() for _ in range(2)]; t_wkb = [Tok() for _ in range(2)]
    ds_w = [P.dsem() for _ in range(2)]
    wuqv = wuq.rearrange("(c p) h f -> p c h f", p=128)
    wukvv = wukv.rearrange("(c p) h f -> p c h f", p=128)
    qn = A.alloc(BF16, T); qr = A.alloc(BF16, T); kn = A.alloc(BF16, T); kr = A.alloc(BF16, T)
    t_q, t_k = Tok(), Tok()
    P.memset("pool", qr, 0.0, (t_q,))
    P.memset("pool", kr, 0.0, (t_k,))
    P.memset("pool", sqb[1], 0.0, (t_sq[1],))
    Vb = A.alloc(BF16, 16 * 128); Vv = Vb.rearrange("p (s d) -> p s d", s=16); t_V = Tok()
    pT = [A.alloc(BF16, 512) for _ in range(3)]; t_pT = [Tok() for _ in range(3)]
    rs = [A.alloc(F32, 512) for _ in range(2)]; t_rs = [Tok() for _ in range(2)]
    t1 = [A.alloc(F32, 512) for _ in range(4)]; t2 = [A.alloc(F32, 512) for _ in range(4)]
    rst = [A.alloc(F32, 512) for _ in range(4)]
    sqn = [A.alloc(BF16, 512) for _ in range(4)]; sqp = [A.alloc(BF16, 512) for _ in range(4)]
    t_t1 = [Tok() for _ in range(4)]; t_t2 = [Tok() for _ in range(4)]; t_rst = [Tok() for _ in range(4)]
    t_sqn = [Tok() for _ in range(4)]; t_sqp = [Tok() for _ in range(4)]
    for tt in range(4):
        P.memset("pool", sqp[tt], 0.0, (t_sqp[tt],))
    Ob = [A.alloc(BF16, T) for _ in range(2)]; t_Ob = [Tok() for _ in range(2)]; ds_Ob = [P.dsem() for _ in range(2)]
    t_b = [Tok(True) for _ in range(8)]
    pj = 0
    sc = 0
    for h in range(MLA_H):
        s = h % 2
        P.dma("sp", wq_s[s].rearrange("p (c f) -> p c f", c=4), wuqv[:, :, h, :], ds_w[s], (), (t_wqs[s],))
        P.dma("sp", wk_s[s].rearrange("p (c f) -> p c f", c=4), wukvv[:, :, h, :], ds_w[s], (), (t_wks[s],))
        P.copy("act", wq_b[s], wq_s[s], (t_wqs[s],), (t_wqb[s],))
        P.copy("dve", wk_b[s], wk_s[s], (t_wks[s],), (t_wkb[s],))
        wqv = wq_b[s].rearrange("p (c f) -> p c f", c=4)
        wkv = wk_b[s].rearrange("p (c f) -> p c f", c=4)
        for g in range(4):
            b = 6 + (pj % 2); pj += 1
            for i in range(4):
                st = g * 4 + i
                for c in range(4):
                    P.mm(cx.bank(b)[:, i * 128:(i + 1) * 128], ckvnv[:, c, st * 128:(st + 1) * 128], wkv[:, c, 128:256],
                         c == 0, c == 3, (t_ckvn, t_wkb[s]), (t_b[b],), inc=(c == 3 and i == 3))
            P.copy("act", Vb[:, g * 512:(g + 1) * 512], cx.bank(b), (t_b[b],), (t_V,))
        if STOP == 31:
            continue
        TS = [slice(tt * 512, (tt + 1) * 512) for tt in range(4)]
        R4 = range(4)
        for tt in R4:
            for c in range(4):
                P.mm(cx.bank(tt)[0:64, :], wqv[:, c, 128:192], cqnv[:, c, TS[tt]], c == 0, c == 3, (t_wqb[s], t_cqn), (t_b[tt],))
        for tt in R4:
            P.actf(sqp[tt][0:64, :], cx.bank(tt)[0:64, :], AF.Square, (t_b[tt],), (t_sqp[tt],))
            P.tt("dve", t1[tt][0:64, :], cx.bank(tt)[0:64, :], Cq[0:64, TS[tt]], ALU.mult, (t_b[tt], t_tab), (t_t1[tt],))
        for tt in R4:
            for c in range(4):
                P.mm(cx.bank(4 + tt)[0:64, :], wqv[:, c, 192:256], cqnv[:, c, TS[tt]], c == 0, c == 3, (t_wqb[s], t_cqn), (t_b[4 + tt],))
        for tt in R4:
            P.tt("dve", t2[tt][0:64, :], cx.bank(4 + tt)[0:64, :], Sq[0:64, TS[tt]], ALU.mult, (t_b[4 + tt], t_tab), (t_t2[tt],))
            P.tt("dve", t1[tt][0:64, :], t1[tt][0:64, :], t2[tt][0:64, :], ALU.add, (t_t1[tt], t_t2[tt]), (t_t1[tt],))
        for tt in R4:
            for c in range(4):
                P.mm(cx.bank(tt), wqv[:, c, 0:128], cqnv[:, c, TS[tt]], c == 0, c == 3, (t_wqb[s], t_cqn), (t_b[tt],))
        for tt in R4:
            P.actf(sqn[tt], cx.bank(tt), AF.Square, (t_b[tt],), (t_sqn[tt],))
        for tt in R4:
            P.mm(cx.bank(4 + tt), cx.ones1_bf, sqn[tt], True, False, (t_sqn[tt], cx.t_const), (t_b[4 + tt],), inc=False)
            P.mm(cx.bank(4 + tt), cx.ones1_bf, sqp[tt], False, True, (t_sqp[tt], cx.t_const), (t_b[4 + tt],), inc=True)
        for tt in R4:
            P.ts("dve", rst[tt], cx.bank(4 + tt), 1.0 / 192.0, RMS_EPS, ALU.mult, ALU.add, (t_b[4 + tt],), (t_rst[tt],))
        for tt in R4:
            P.actf(rst[tt], rst[tt], AF.Ln, (t_rst[tt],), (t_rst[tt],))
            P.actf(rst[tt], rst[tt], AF.Exp, (t_rst[tt],), (t_rst[tt],), scale=-0.5)
        for tt in R4:
            P.stt("dve", qn[:, TS[tt]], cx.bank(tt), mc[:, 8:9], rst[tt], ALU.mult, ALU.mult, (t_b[tt], t_rst[tt], t_mc), (t_q,))
            P.tt("dve", qr[0:64, TS[tt]], t1[tt][0:64, :], rst[tt][0:64, :], ALU.mult, (t_t1[tt], t_rst[tt]), (t_q,))
        for tt in R4:
            for c in range(4):
                P.mm(cx.bank(tt), wkv[:, c, 0:128], ckvnv[:, c, TS[tt]], c == 0, c == 3, (t_wkb[s], t_ckvn), (t_b[tt],))
        for tt in R4:
            P.actf(sqn[tt], cx.bank(tt), AF.Square, (t_b[tt],), (t_sqn[tt],))
        for tt in R4:
            P.mm(cx.bank(4 + tt), cx.ones1_bf, sqn[tt], True, False, (t_sqn[tt], cx.t_const), (t_b[4 + tt],), inc=False)
            P.mm(cx.bank(4 + tt), cx.ones1_bf, sqkpe[:, TS[tt]], False, True, (t_kr, cx.t_const), (t_b[4 + tt],), inc=True)
        for tt in R4:
            P.ts("dve", rst[tt], cx.bank(4 + tt), 1.0 / 192.0, RMS_EPS, ALU.mult, ALU.add, (t_b[4 + tt],), (t_rst[tt],))
        for tt in R4:
            P.actf(rst[tt], rst[tt], AF.Ln, (t_rst[tt],), (t_rst[tt],))
            P.actf(rst[tt], rst[tt], AF.Exp, (t_rst[tt],), (t_rst[tt],), scale=-0.5)
        for tt in R4:
            P.stt("dve", kn[:, TS[tt]], cx.bank(tt), mc[:, 9:10], rst[tt], ALU.mult, ALU.mult, (t_b[tt], t_rst[tt], t_mc), (t_k,))
            P.tt("dve", kr[0:64, TS[tt]], kperot[0:64, TS[tt]], rst[tt][0:64, :], ALU.mult, (t_kr, t_rst[tt]), (t_k,))
        if STOP == 32:
            continue
        items = []
        for qt in range(4):
            qb0 = qt * 4
            nkt = qb0 + 4
            for kt in range(nkt):
                c0 = max(0, kt - qb0) * 128
                items.append((qt, kt, nkt, c0))
        sbank = {}

        def qk(i):
            nonlocal sc
            qt, kt, nkt, c0 = items[i]
            n = 512 - c0
            qsl = slice(qt * 512 + c0, (qt + 1) * 512)
            ksl = slice(kt * 128, (kt + 1) * 128)
            b = sc % 3; sc += 1
            sbank[i] = b
            P.mm(cx.bank(b)[:, 0:n], kn[:, ksl], qn[:, qsl], True, False, (t_k, t_q), (t_b[b],), inc=False)
            P.mm(cx.bank(b)[:, 0:n], kr[:, ksl], qr[:, qsl], False, True, (t_k, t_q), (t_b[b],))
            P.actf(pT[b][:, 0:n], cx.bank(b)[:, 0:n], AF.Exp, (t_b[b],), (t_pT[b],), scale=SM_SCALE)
            if kt >= qt * 4:
                P.memset("pool", pT[b][64:128, 0:64], 0.0, (t_pT[b],))

        def pv(i):
            qt, kt, nkt, c0 = items[i]
            n = 512 - c0
            b = sbank[i]
            bO, bS = (3, 4) if qt % 2 == 0 else (5, 6)
            P.mm(cx.bank(bO)[:, c0:512], Vv[:, kt, :], pT[b][:, 0:n], kt == 0, kt == nkt - 1, (t_V, t_pT[b]), (t_b[bO],), inc=False)
            P.mm(cx.bank(bS)[:, c0:512], cx.ones1_bf, pT[b][:, 0:n], kt == 0, kt == nkt - 1, (cx.t_const, t_pT[b]), (t_b[bS],), inc=True)
            if kt == nkt - 1:
                P.actf(rs[qt % 2], cx.bank(bS), AF.Ln, (t_b[bS],), (t_rs[qt % 2],))
                P.actf(rs[qt % 2], rs[qt % 2], AF.Exp, (t_rs[qt % 2],), (t_rs[qt % 2],), scale=-1.0)
                P.tt("dve", Ob[h % 2][:, qt * 512:(qt + 1) * 512], cx.bank(bO), rs[qt % 2], ALU.mult,
                     (t_b[bO], t_rs[qt % 2]), (t_Ob[h % 2],))

        qk(0)
        for i in range(len(items)):
            if i + 1 < len(items):
                qk(i + 1)
            pv(i)
        P.dma("pool", cx.ots[h], Ob[h % 2], ds_Ob[h % 2], (t_Ob[h % 2],), (t_OT,))
    P.barrier()
    A.release()
    if STOP == 3:
        A.release(); return
    OT = A.alloc(BF16, MLA_H * T); OTv = OT.rearrange("p (h t) -> p h t", h=MLA_H)
    for c4 in range(4):
        P.dma("sp", OTv[:, c4 * 4:(c4 + 1) * 4, :], cx.ots.rearrange("h p t -> p h t")[:, c4 * 4:(c4 + 1) * 4, :], dsm, (t_OT,), (t_OT,))
    wo_s = [A.alloc(F32, MLA_H * 128) for _ in range(2)]; wo_b = [A.alloc(BF16, MLA_H * 128) for _ in range(2)]
    t_wos = [Tok() for _ in range(2)]; t_wob = [Tok() for _ in range(2)]
    xold = [A.alloc(F32, 512) for _ in range(2)]; t_xold = [Tok() for _ in range(2)]; ds_xold = [P.dsem() for _ in range(2)]
    xnew = [A.alloc(F32, 512) for _ in range(2)]; t_xnew = [Tok() for _ in range(2)]; ds_xnew = [P.dsem() for _ in range(2)]
    wov = wo.rearrange("h p e -> p h e")
    def o_load(e):
        s = e % 2
        P.dma("sp", wo_s[s].rearrange("p (h e) -> p h e", h=MLA_H), wov[:, :, e * 128:(e + 1) * 128], ds_w[s], (), (t_wos[s],))

    def o_cast(e):
        s = e % 2
        P.copy("act" if e % 2 == 0 else "dve", wo_b[s], wo_s[s], (t_wos[s],), (t_wob[s],))

    o_load(0)
    o_load(1)
    o_cast(0)
    k = 0
    for e in range(NCH):
        s = e % 2
        if e + 1 < NCH:
            o_cast(e + 1)
        if e + 2 < NCH:
            o_load(e + 2)
        wb = wo_b[s].rearrange("p (h e) -> p h e", h=MLA_H)
        for tt in range(4):
            bi = k % 2; k += 1
            b = 6 + bi
            tsl = slice(tt * 512, (tt + 1) * 512)
            P.dma("act", xold[bi], src_v[:, e, tsl], ds_xold[bi], cx.xs_tok(e, tt), (t_xold[bi],))
            for h in range(MLA_H):
                P.mm(cx.bank(b), wb[:, h, :], OTv[:, h, tsl], h == 0, h == MLA_H - 1, (t_wob[s], t_OT), (t_b[b],))
            P.tt("dve", xnew[bi], cx.bank(b), xold[bi], ALU.add, (t_b[b], t_xold[bi]), (t_xnew[bi],))
            P.dma("pool", dst_v[:, e, tsl], xnew[bi], ds_xnew[bi], (t_xnew[bi],), cx.xs_tok(e, tt))
    P.barrier()
    for d in ds_w + ds_xold + ds_xnew + ds_Ob + [dsm]:
        P.free_dsems.append(d)
    A.release()


RW_L = 64
RW_NCK = T // RW_L
DEC_C = float(np.exp(-0.5))


def linear_bufs(cx, n_in_chunks):
    P, A = cx.P, cx.A
    return dict(stg=[A.alloc(F32, n_in_chunks * 128) for _ in range(2)],
                wb=[A.alloc(BF16, n_in_chunks * 128) for _ in range(2)],
                t_s=[Tok() for _ in range(2)], t_w=[Tok() for _ in range(2)], t_w2=[Tok() for _ in range(2)],
                ds=[P.dsem() for _ in range(2)], t_bk=[Tok(True) for _ in range(2)], k=[0])


def linear_fm(cx, actv, t_act, w_ap, n_in_chunks, out_cols, evac, bank0=0, M=128, bufs=None):
    P, A = cx.P, cx.A
    own = bufs is None
    if own:
        A.mark()
        bufs = linear_bufs(cx, n_in_chunks)
    stg, wb, t_s, t_w, t_w2, ds, t_bk = (bufs[k_] for k_ in ("stg", "wb", "t_s", "t_w", "t_w2", "ds", "t_bk"))
    wv = w_ap.rearrange("(c p) f -> p c f", p=128)
    hc = n_in_chunks // 2

    def l_load(oi):
        c0, m = out_cols[oi]
        s = oi % 2
        sv = stg[s].rearrange("p (c f) -> p c f", c=n_in_chunks)
        P.dma("sp", sv[:, :, 0:m], wv[:, :, c0:c0 + m], ds[s], (), (t_s[s],))

    def l_cast(oi):
        c0, m = out_cols[oi]
        s = oi % 2
        sv = stg[s].rearrange("p (c f) -> p c f", c=n_in_chunks)
        bv = wb[s].rearrange("p (c f) -> p c f", c=n_in_chunks)
        P.copy("dve", bv[:, 0:hc, 0:m], sv[:, 0:hc, 0:m], (t_s[s],), (t_w[s],))
        P.copy("act", bv[:, hc:, 0:m], sv[:, hc:, 0:m], (t_s[s],), (t_w2[s],))

    n_out = len(out_cols)
    l_load(0)
    if n_out > 1:
        l_load(1)
    l_cast(0)
    for oi, (c0, m) in enumerate(out_cols):
        s = oi % 2
        bv = wb[s].rearrange("p (c f) -> p c f", c=n_in_chunks)
        if oi + 1 < n_out:
            l_cast(oi + 1)
        if oi + 2 < n_out:
            l_load(oi + 2)
        for tt in range(4):
            bi = bufs["k"][0] % 2; bufs["k"][0] += 1
            bank = cx.bank(bank0 + bi)
            for c in range(n_in_chunks):
                P.mm(bank[0:m, :], bv[:, c, 0:m], actv[:, c, tt * 512:(tt + 1) * 512], c == 0, c == n_in_chunks - 1,
                     (t_w[s] if c < hc else t_w2[s], t_act), (t_bk[bi],))
            evac(oi, tt, bank, t_bk[bi])
    if own:
        P.barrier()
        for d in ds:
            P.free_dsems.append(d)
        A.release()


def phase_rwkv(cx, xs_src, xs_dst, W, gcol, rc_d):
    P, A = cx.P, cx.A
    A.mark()
    src_v = xs_src.rearrange("c p t -> p c t")
    dst_v = xs_dst.rearrange("c p t -> p c t")
    rc = A.alloc(F32, 13 * 16); t_rc = Tok(); dsm = P.dsem()
    P.dma("sp", rc, rc_d, dsm, (), (t_rc,))
    omka = A.alloc(F32, 16)
    P.ts("dve", omka, rc[:, 9 * 16:10 * 16], -1.0, 1.0, ALU.mult, ALU.add, (t_rc,), (t_rc,))
    col = lambda k, e: rc[:, k * 16 + e: k * 16 + e + 1]
    t_xm = cx.t_xmix
    A.mark()
    TT = 256
    xt = [A.alloc(F32, NCH * TT) for _ in range(2)]; t_xt = [Tok() for _ in range(2)]; ds_xt = [P.dsem() for _ in range(2)]
    hb = A.alloc(F32, NCH * (TT + 4)); hbv = hb.rearrange("p (c t) -> p c t", c=NCH); t_hb = Tok()
    xx = A.alloc(F32, NCH * TT); xxv = xx.rearrange("p (c t) -> p c t", c=NCH); t_xx = Tok()
    xm = [A.alloc(BF16, NCH * TT) for _ in range(6)]; t_xmb = [Tok() for _ in range(6)]; ds_xm = [P.dsem() for _ in range(6)]
    sqb = [A.alloc(BF16, 512) for _ in range(3)]; t_sq = [Tok() for _ in range(3)]
    rstd = A.alloc(F32, 512); t_rstd = Tok()
    t_bn = Tok(True)
    utmp = [A.alloc(F32, TT) for _ in range(4)]; t_utmp = [Tok() for _ in range(4)]
    P.memset("dve", hbv[:, :, 3:4], 0.0, (t_hb,))
    for ti in range(T // TT):
        s = ti % 2
        tsl = slice(ti * TT, (ti + 1) * TT)
        P.dma("sp", xt[s].rearrange("p (c t) -> p c t", c=NCH), src_v[:, :, tsl], ds_xt[s], cx.xs_tok(None, ti // 2), (t_xt[s],))
        if ti > 0:
            P.copy("dve", hbv[:, :, 3:4], hbv[:, :, TT + 3:TT + 4], (t_hb,), (t_hb,))
        rmsnorm_tile(cx, xt[s], t_xt[s], gcol, lambda c: hbv[:, c, 4:TT + 4], t_hb, TT, sqb, t_sq, rstd, t_rstd,
                     cx.bank(6), t_bn)
        P.tt("dve", xxv, hbv[:, :, 3:TT + 3], hbv[:, :, 4:TT + 4], ALU.subtract, (t_hb,), (t_xx,))
        for j in range(6):
            xmv = xm[j].rearrange("p (c t) -> p c t", c=NCH)
            for c in range(NCH):
                if (j * NCH + c) % 3 == 2:
                    P.stt("dve", xmv[:, c, :], xxv[:, c, :], col(j, c), hbv[:, c, 4:TT + 4], ALU.mult, ALU.add,
                          (t_xx, t_hb, t_rc), (t_xmb[j],))
                else:
                    u = (j * NCH + c) % 4
                    P.actf(utmp[u], xxv[:, c, :], AF.Copy, (t_xx, t_rc), (t_utmp[u],), scale=col(j, c))
                    P.tt("dve", xmv[:, c, :], utmp[u], hbv[:, c, 4:TT + 4], ALU.add, (t_utmp[u], t_hb), (t_xmb[j],))
            P.dma("sp", cx.xmix[j].rearrange("c p t -> p c t")[:, :, tsl], xmv, ds_xm[j], (t_xmb[j],), (t_xm,))
    P.barrier()
    for d in ds_xt + ds_xm:
        P.free_dsems.append(d)
    A.release()
    A.mark()
    lw = A.alloc(BF16, T); la = A.alloc(BF16, T); lg = A.alloc(BF16, 2 * T); lgv = lg.rearrange("p (c t) -> p c t", c=2)
    t_l = Tok()
    A.mark()
    acts = [A.alloc(BF16, NCH * T) for _ in range(2)]
    t_acts = [Tok() for _ in range(2)]; ds_act = [P.dsem() for _ in range(2)]
    ot = [A.alloc(F32, 512) for _ in range(4)]; t_ot = [Tok() for _ in range(4)]; ds_ot = [P.dsem() for _ in range(4)]
    cnt = [0]
    order = [(0, "r"), (2, "k"), (3, "v"), (1, "lw"), (4, "la"), (5, "lg")]
    lb = linear_bufs(cx, NCH)

    def load_act(j):
        av = acts[j % 2].rearrange("p (c t) -> p c t", c=NCH)
        for c4 in range(4):
            P.dma("sp", av[:, c4 * 4:(c4 + 1) * 4, :], cx.xmix[order[j][0]].rearrange("c p t -> p c t")[:, c4 * 4:(c4 + 1) * 4, :],
                  ds_act[j % 2], (t_xm,), (t_acts[j % 2],))

    load_act(0)
    for j, (src_j, kind) in enumerate(order):
        if j + 1 < len(order):
            load_act(j + 1)
        actv = acts[j % 2].rearrange("p (c t) -> p c t", c=NCH)
        t_act = t_acts[j % 2]
        if kind in ("r", "k", "v"):
            ji = "rkv".index(kind)

            def ev(oi, tt, bank, tb, ji=ji):
                i = cnt[0] % 4; cnt[0] += 1
                P.copy("act" if i % 2 == 0 else "dve", ot[i], bank, (tb,), (t_ot[i],))
                P.dma("pool", cx.rkv[ji].rearrange("c p t -> p c t")[:, oi, tt * 512:(tt + 1) * 512], ot[i], ds_ot[i],
                      (t_ot[i],), (cx.t_rkv,))
            linear_fm(cx, actv, t_act, W["w_rkv"][ji], NCH, [(e * 128, 128) for e in range(NCH)], ev, bufs=lb)
        elif kind == "lw":
            def ev(oi, tt, bank, tb):
                P.actf(lw[0:96, tt * 512:(tt + 1) * 512], bank[0:96, :], AF.Tanh, (tb,), (t_l,))
            linear_fm(cx, actv, t_act, W["w1"], NCH, [(0, 96)], ev, bufs=lb)
        elif kind == "la":
            def ev(oi, tt, bank, tb):
                P.copy("act", la[0:96, tt * 512:(tt + 1) * 512], bank[0:96, :], (tb,), (t_l,))
            linear_fm(cx, actv, t_act, W["a1"], NCH, [(0, 96)], ev, bufs=lb)
        else:
            def ev(oi, tt, bank, tb):
                P.actf(lgv[:, oi, tt * 512:(tt + 1) * 512], bank, AF.Sigmoid, (tb,), (t_l,))
            linear_fm(cx, actv, t_act, W["g1"], NCH, [(0, 128), (128, 128)], ev, bufs=lb)
    P.barrier()
    P.free_dsems.extend(ds_act + ds_ot + lb["ds"])
    A.release()
    w2b = A.alloc(BF16, D); a2b = A.alloc(BF16, D); g2b = A.alloc(BF16, 2 * D); g2bv = g2b.rearrange("p (c f) -> p c f", c=2)
    t_lw2 = Tok()
    st32 = A.alloc(F32, 2 * D)
    P.dma("sp", st32[0:96, 0:D], W["w2"], dsm, (), (t_lw2,))
    P.copy("dve", w2b[0:96, :], st32[0:96, 0:D], (t_lw2,), (t_lw2,))
    P.dma("sp", st32[0:96, 0:D], W["a2"], dsm, (t_lw2,), (t_lw2,))
    P.copy("dve", a2b[0:96, :], st32[0:96, 0:D], (t_lw2,), (t_lw2,))
    P.dma("sp", st32.rearrange("p (c f) -> p c f", c=2), W["g2"].rearrange("(c p) f -> p c f", p=128), dsm, (t_lw2,), (t_lw2,))
    P.copy("dve", g2b, st32, (t_lw2,), (t_lw2,))
    t_b2 = [Tok(True) for _ in range(6)]
    big = [[A.alloc(F32, T) for _ in range(3)] for _ in range(2)]
    t_big = [[Tok() for _ in range(3)] for _ in range(2)]
    ds_big = [P.dsem() for _ in range(2)]
    for e in range(NCH):
        esl = slice(e * 128, (e + 1) * 128)
        sb_ = e % 2
        for tt in range(4):
            tsl = slice(tt * 512, (tt + 1) * 512)
            b0 = (tt % 2) * 3
            P.mm(cx.bank(b0), w2b[0:96, esl], lw[0:96, tsl], True, True, (t_lw2, t_l), (t_b2[b0],))
            P.actf(big[sb_][0][:, tsl], cx.bank(b0), AF.Sigmoid, (t_b2[b0], t_rc), (t_big[sb_][0],), bias=col(6, e))
            P.mm(cx.bank(b0 + 1), a2b[0:96, esl], la[0:96, tsl], True, True, (t_lw2, t_l), (t_b2[b0 + 1],))
            P.actf(big[sb_][1][:, tsl], cx.bank(b0 + 1), AF.Sigmoid, (t_b2[b0 + 1], t_rc), (t_big[sb_][1],), bias=col(7, e))
            for c in range(2):
                P.mm(cx.bank(b0 + 2), g2bv[:, c, esl], lgv[:, c, tsl], c == 0, c == 1, (t_lw2, t_l), (t_b2[b0 + 2],))
            P.copy("dve", big[sb_][2][:, tsl], cx.bank(b0 + 2), (t_b2[b0 + 2],), (t_big[sb_][2],))
        for pl in range(3):
            P.dma("pool" if pl != 1 else "sp", cx.rkv[3 + pl][e], big[sb_][pl], ds_big[sb_], (t_big[sb_][pl],), (cx.t_rkv,))
    P.barrier()
    P.free_dsems.extend(ds_big)
    A.release()
    if STOP == 52:
        A.release(); return
    t_yg = Tok()
    A.mark()
    msk = A.alloc(F32, 3 * 512); t_k = Tok()
    P.dma("sp", msk, cx.rw_masks_d, dsm, (), (t_k,))
    ML_s, MU_s, MU_i = msk[:, 0:512], msk[:, 512:1024], msk[:, 1024:1536]
    id4 = A.alloc(BF16, 512)
    for q in range(4):
        P.copy("dve", id4[:, q * 128:(q + 1) * 128], cx.ident, (cx.t_const,), (t_k,))
    idb = id4[:, 0:128]
    bones = A.alloc(F32, 128)
    P.memset("pool", bones, 0.0, (t_k,))
    P.memset("pool", bones[0:64, 0:64], 1.0, (t_k,))
    P.memset("pool", bones[64:128, 64:128], 1.0, (t_k,))
    rmask = A.alloc(BF16, T)
    P.memset("pool", rmask, 1.0, (t_k,))
    P.memset("pool", rmask.rearrange("p (c t) -> p c t", t=RW_L)[:, :, 0:1], 0.0, (t_k,))
    xop = [A.alloc(BF16, RW_NCK * 128) for _ in range(7)]
    t_xop = Tok()
    for x_ in xop:
        P.memset("pool", x_, 0.0, (t_xop,))
    RTx, KTx, BTx, KHx, BHx, ATx, Vx = xop
    gam = A.alloc(F32, RW_NCK); t_gam = Tok()
    bon = A.alloc(F32, T); t_bon = Tok()
    psb = cx.ps_bf

    def xview(xo, h):
        return xo.rearrange("p (c i) -> p c i", i=128)[h * 64:(h + 1) * 64, :, h * 64:(h + 1) * 64]

    def hview(ap, h):
        return ap.rearrange("p (c t) -> p c t", t=RW_L)[h * 64:(h + 1) * 64, :, :]

    def ch(xo, c):
        return xo[:, c * 128:(c + 1) * 128]

    def q4(ap, q):
        return ap[:, q * 128:(q + 1) * 128]

    NG = RW_NCK // 4
    NSLOT = 3
    for e in range(NCH):
        A.mark()
        r_, k_, v_, a_, cum, sg_, tA, tB, tC, tD, tE, tF = [A.alloc(F32, T) for _ in range(12)]
        t_r, t_kk, t_v, t_a, t_cum, t_sg, t_tA, t_tB, t_tC, t_tD, t_tE, t_tF = [Tok() for _ in range(12)]
        t_bk = [Tok(True) for _ in range(8)]
        t_xop2 = [Tok(), Tok()]
        cB = [A.alloc(BF16, T) for _ in range(5)]; t_cB = [Tok() for _ in range(5)]
        for ji, (dst, tk) in enumerate([(sg_, t_sg), (k_, t_kk), (a_, t_a), (r_, t_r), (v_, t_v)]):
            P.dma("sp", dst, cx.rkv[[3, 1, 4, 0, 2][ji]][e], dsm, (cx.t_rkv,), (tk,))
        cumv = cum.rearrange("p (c t) -> p c t", t=RW_L)
        TS = [slice(tt * 512, (tt + 1) * 512) for tt in range(4)]
        P.op("dve", lambda en, cum=cum, sg_=sg_: en.tensor_tensor_scan(cum, rmask, sg_, 0.0, ALU.mult, ALU.add),
             (t_sg, t_k), (t_cum,))
        P.actf(tA, k_, AF.Copy, (t_kk, t_rc), (t_tA,), scale=col(8, e))
        P.actf(tB, tA, AF.Square, (t_tA,), (t_tB,))
        P.actf(tC, cum, AF.Exp, (t_cum,), (t_tC,), scale=-DEC_C)
        for tt in range(4):
            P.mm(cx.bank(tt), bones, tB[:, TS[tt]], True, True, (t_k, t_tB), (t_bk[tt],))
        P.actf(tE, a_, AF.Identity, (t_a, t_rc), (t_tE,), scale=col(9, e), bias=omka[:, e:e + 1])
        P.tt("dve", k_, k_, tE, ALU.mult, (t_kk, t_tE), (t_kk,))
        for tt in range(4):
            P.ts("dve", tD[:, TS[tt]], cx.bank(tt), 1e-18, None, ALU.max, None, (t_bk[tt],), (t_tD,))
        P.actf(tD, tD, AF.Ln, (t_tD,), (t_tD,))
        P.actf(tD, tD, AF.Exp, (t_tD,), (t_tD,), scale=-0.5)
        P.copy("dve", gam, cumv[:, :, RW_L - 1], (t_cum,), (t_gam,))
        P.actf(gam, gam, AF.Exp, (t_gam,), (t_gam,), scale=-DEC_C)
        def expand(xo, cb, t_cb):
            for h in range(2):
                P.copy("act", xview(xo, h), hview(cb, h), (t_cb,), (t_xop2[h],))

        P.tt("dve", cB[0], r_, tC, ALU.mult, (t_r, t_tC), (t_cB[0],))
        expand(RTx, cB[0], t_cB[0])
        P.tt("dve", tE, cum, sg_, ALU.subtract, (t_cum, t_sg, t_tE), (t_tE,))
        P.actf(tE, tE, AF.Exp, (t_tE,), (t_tE,), scale=-DEC_C)
        P.stt("dve", tF, r_, col(10, e), k_, ALU.mult, ALU.mult, (t_r, t_kk, t_rc), (t_tF,))
        for tt in range(4):
            P.mm(cx.bank(4 + tt), bones, tF[:, TS[tt]], True, True, (t_k, t_tF), (t_bk[4 + tt],))
        P.tt("dve", tA, tA, tD, ALU.mult, (t_tA, t_tD), (t_tA,))
        P.tt("dve", tB, tA, a_, ALU.mult, (t_tA, t_a, t_tB), (t_tB,))
        P.actf(tC, cum, AF.Exp, (t_cum, t_cB[0]), (t_tC,), scale=DEC_C)
        for tt in range(4):
            P.tt("dve", bon[:, TS[tt]], cx.bank(4 + tt), v_[:, TS[tt]], ALU.mult, (t_bk[4 + tt], t_v), (t_bon,))
        P.stt("dve", cB[1], tA, -1.0, tE, ALU.mult, ALU.mult, (t_tA, t_tE), (t_cB[1],))
        expand(ATx, cB[1], t_cB[1])
        P.tt("dve", r_.rearrange("p (c t) -> p c t", t=RW_L), cumv[:, :, RW_L - 1:RW_L].to_broadcast([128, RW_NCK, RW_L]),
             cumv, ALU.subtract, (t_cum, t_r, t_cB[0]), (t_r,))
        P.actf(r_, r_, AF.Exp, (t_r,), (t_r,), scale=-DEC_C)
        for h in range(2):
            P.copy("act", xview(Vx, h), hview(v_, h), (t_v,), (t_xop2[h],))
        P.tt("dve", cB[2], k_, tC, ALU.mult, (t_kk, t_tC), (t_cB[2],))
        expand(KTx, cB[2], t_cB[2])
        P.tt("dve", cB[3], tB, tC, ALU.mult, (t_tB, t_tC), (t_cB[3],))
        expand(BTx, cB[3], t_cB[3])
        P.tt("dve", cB[4], k_, r_, ALU.mult, (t_kk, t_r), (t_cB[4],))
        expand(KHx, cB[4], t_cB[4])
        P.tt("dve", cB[0], tB, r_, ALU.mult, (t_tB, t_r, t_cB[0]), (t_cB[0],))
        expand(BHx, cB[0], t_cB[0])
        P.barrier()
        A.release()
        A.mark()
        GT = A.alloc(BF16, RW_NCK * 128); Hh = A.alloc(F32, RW_NCK * 128); Rb = A.alloc(BF16, RW_NCK * 128)
        y0 = A.alloc(F32, T); Sall = A.alloc(BF16, RW_NCK * 128); yT = A.alloc(F32, T)
        t_GT = [Tok() for _ in range(NG)]; t_H = [Tok() for _ in range(NG)]; t_Rb = [Tok() for _ in range(NG)]
        t_y0 = [Tok() for _ in range(NG)]
        gT = A.alloc(F32, T); t_g = Tok()
        P.dma("sp", gT, cx.rkv[5][e], dsm, (cx.t_rkv,), (t_g,))
        t_bk = [Tok(True) for _ in range(8)]
        slots = []
        for sl in range(NSLOT):
            d_ = {}
            d_["TMb"] = [A.alloc(BF16, 512) for _ in range(3)]
            d_["WUin"] = A.alloc(BF16, 4 * 256); d_["WU"] = A.alloc(BF16, 4 * 256)
            d_["Nb"] = [A.alloc(BF16, 512) for _ in range(2)]; d_["Qb"] = [A.alloc(BF16, 512) for _ in range(2)]
            d_["Pb"] = [A.alloc(BF16, 512) for _ in range(2)]
            d_["M"] = [A.alloc(BF16, 512) for _ in range(3)]
            d_["tok"] = {k: Tok() for k in ("TM", "WUin", "WU", "N", "Q", "P", "M")}
            d_["banks"] = (2 * sl, 2 * sl + 1)
            slots.append(d_)
        ev_i = [0]

        def group_steps(g, sd):
            cs = [g * 4 + q for q in range(4)]
            TMb, WUin, WU, Nb, Qb, Pb = sd["TMb"], sd["WUin"], sd["WU"], sd["Nb"], sd["Qb"], sd["Pb"]
            Mak, Mrb, Mrk = sd["M"]
            tk = sd["tok"]
            WUinv = WUin.rearrange("p (q f) -> p q f", q=4)
            WUv = WU.rearrange("p (q f) -> p q f", q=4)
            bi = [0]

            def nb():
                bi[0] ^= 1
                return sd["banks"][bi[0]]

            def eng2():
                ev_i[0] += 1
                return "dve" if ev_i[0] % 2 == 0 else "act"
            for half, ops_ in enumerate([(BHx, KHx), (Vx, ATx)]):
                b = nb()
                for oi, xo in enumerate(ops_):
                    for q in range(4):
                        P.tr(psb[:, b * 1024 + (oi * 4 + q) * 128: b * 1024 + (oi * 4 + q + 1) * 128], ch(xo, cs[q]), idb,
                             (t_xop, t_k), (t_bk[b],), inc=(oi == 1 and q == 3))
                if half == 0:
                    P.copy("act", TMb[0], psb[:, b * 1024: b * 1024 + 512], (t_bk[b],), (tk["TM"],))
                    P.copy("dve", TMb[1], psb[:, b * 1024 + 512: b * 1024 + 1024], (t_bk[b],), (tk["TM"],))
                else:
                    P.copy("act", TMb[2], psb[:, b * 1024: b * 1024 + 512], (t_bk[b],), (tk["TM"],))
                    P.copy("dve", WUinv[:, :, 0:128], psb[:, b * 1024 + 512: b * 1024 + 1024].rearrange("p (q f) -> p q f", q=4),
                           (t_bk[b],), (tk["WUin"],))
                yield
            b = nb()
            for q in range(4):
                P.mm(q4(cx.bank(b), q), ch(ATx, cs[q]), ch(BTx, cs[q]), True, True, (t_xop,), (t_bk[b],), inc=(q == 3))
            P.tt("dve", Nb[0], cx.bank(b), ML_s, ALU.mult, (t_bk[b], t_k), (tk["N"],))
            yield
            b = nb()
            for q in range(4):
                P.mm(q4(cx.bank(b), q), ch(BTx, cs[q]), ch(ATx, cs[q]), True, True, (t_xop,), (t_bk[b],), inc=(q == 3))
            P.tt("dve", Qb[0], cx.bank(b), MU_s, ALU.mult, (t_bk[b], t_k), (tk["Q"],))
            P.tt("pool", Pb[0], Qb[0], id4, ALU.add, (tk["Q"], t_k), (tk["P"],))
            yield
            for (lx, rx, mk, dst) in ((KTx, ATx, MU_s, Mak), (BTx, RTx, MU_i, Mrb), (KTx, RTx, MU_i, Mrk)):
                b = nb()
                for q in range(4):
                    P.mm(q4(cx.bank(b), q), ch(lx, cs[q]), ch(rx, cs[q]), True, True, (t_xop,), (t_bk[b],), inc=(q == 3))
                P.tt("dve", dst, cx.bank(b), mk, ALU.mult, (t_bk[b], t_k), (tk["M"],))
                yield
            pi = 0
            for j in range(1, 6):
                i0, i1 = (j - 1) % 2, j % 2
                b = nb()
                for q in range(4):
                    P.mm(q4(cx.bank(b), q), q4(Qb[i0], q), q4(Nb[i0], q), True, True, (tk["N"], tk["Q"]), (t_bk[b],), inc=(q == 3))
                if j < 5:
                    b2 = nb()
                    for q in range(4):
                        P.mm(q4(cx.bank(b2), q), q4(Nb[i0], q), q4(Qb[i0], q), True, True, (tk["N"], tk["Q"]), (t_bk[b2],), inc=(q == 3))
                P.copy("act", Nb[i1], cx.bank(b), (t_bk[b],), (tk["N"],))
                if j < 5:
                    P.copy(eng2(), Qb[i1], cx.bank(b2), (t_bk[b2],), (tk["Q"],))
                yield
                b = nb()
                for q in range(4):
                    P.mm(q4(cx.bank(b), q), idb, q4(Pb[pi], q), True, False, (t_k, tk["P"]), (t_bk[b],), inc=False)
                    P.mm(q4(cx.bank(b), q), q4(Nb[i1], q), q4(Pb[pi], q), False, True, (tk["N"], tk["P"]), (t_bk[b],), inc=(q == 3))
                P.copy(eng2(), Pb[1 - pi], cx.bank(b), (t_bk[b],), (tk["P"],))
                pi = 1 - pi
                yield
            TiT = Pb[pi]
            b = nb()
            for q in range(4):
                P.mm(q4(cx.bank(b), q), q4(Mak, q), q4(TMb[2], q), True, True, (tk["M"], tk["TM"]), (t_bk[b],), inc=(q == 3))
            P.copy("act", WUinv[:, :, 128:256], cx.bank(b).rearrange("p (q f) -> p q f", q=4), (t_bk[b],), (tk["WUin"],))
            yield
            for hf in range(2):
                b = nb()
                for qq in range(2):
                    q = hf * 2 + qq
                    P.mm(cx.bank(b)[:, qq * 256:(qq + 1) * 256], q4(TiT, q), WUinv[:, q, :], True, True, (tk["P"], tk["WUin"]),
                         (t_bk[b],), inc=(qq == 1))
                P.copy("act" if hf == 0 else "dve", WU[:, hf * 512:(hf + 1) * 512], cx.bank(b), (t_bk[b],), (tk["WU"],))
            yield
            b = nb()
            for q in range(4):
                P.mm(q4(cx.bank(b), q), WUv[:, q, 0:128], q4(TMb[0], q), True, True, (tk["WU"], tk["TM"]), (t_bk[b],), inc=(q == 3))
            P.copy("act", GT[:, g * 512:(g + 1) * 512], cx.bank(b), (t_bk[b],), (t_GT[g],))
            b = nb()
            for q in range(4):
                P.mm(q4(cx.bank(b), q), q4(TMb[0], q), WUv[:, q, 128:256], True, False, (tk["WU"], tk["TM"]), (t_bk[b],), inc=False)
                P.mm(q4(cx.bank(b), q), q4(TMb[1], q), q4(TMb[2], q), False, True, (tk["TM"],), (t_bk[b],), inc=(q == 3))
            P.copy("dve", Hh[:, g * 512:(g + 1) * 512], cx.bank(b), (t_bk[b],), (t_H[g],))
            yield
            b = nb()
            for q in range(4):
                P.mm(q4(cx.bank(b), q), idb, ch(RTx, cs[q]), True, False, (t_k, t_xop), (t_bk[b],), inc=False)
                P.mm(q4(cx.bank(b), q), WUv[:, q, 0:128], q4(Mrb, q), False, True, (tk["WU"], tk["M"]), (t_bk[b],), inc=(q == 3))
            P.copy("act", Rb[:, g * 512:(g + 1) * 512], cx.bank(b), (t_bk[b],), (t_Rb[g],))
            b = nb()
            for q in range(4):
                P.mm(q4(cx.bank(b), q), WUv[:, q, 128:256], q4(Mrb, q), True, False, (tk["WU"], tk["M"]), (t_bk[b],), inc=False)
                P.mm(q4(cx.bank(b), q), q4(TMb[2], q), q4(Mrk, q), False, True, (tk["TM"], tk["M"]), (t_bk[b],), inc=(q == 3))
            for h in range(2):
                bv = cx.bank(b).rearrange("p (q i) -> p q i", q=4)[h * 64:(h + 1) * 64, :, h * 64:(h + 1) * 64]
                P.copy("act", hview(y0, h)[:, g * 4:(g + 1) * 4, :], bv, (t_bk[b],), (t_y0[g],))
            yield

        pending = list(range(NG))
        active = []
        free_slots = list(range(NSLOT))
        while pending or active:
            while pending and free_slots:
                sl = free_slots.pop(0)
                active.append((group_steps(pending.pop(0), slots[sl]), sl))
            for item in list(active):
                gen, sl = item
                try:
                    next(gen)
                except StopIteration:
                    active.remove(item)
                    free_slots.append(sl)
        Sf = [A.alloc(F32, 128) for _ in range(2)]; tAq = [A.alloc(F32, 128) for _ in range(2)]
        t_Sf, t_tAq = [Tok() for _ in range(2)], [Tok() for _ in range(2)]
        t_Sg = [Tok() for _ in range(NG)]
        t_yq = [Tok() for _ in range(4)]

        def emit_y(g):
            b = 4 + (g % 2)
            for q in range(4):
                c = g * 4 + q
                P.mm(q4(cx.bank(b), q), ch(Sall, c), ch(Rb, c), True, True, (t_Sg[g], t_Rb[g]), (t_bk[b],), inc=(q == 3))
            for h in range(2):
                bv = cx.bank(b).rearrange("p (q i) -> p q i", q=4)[h * 64:(h + 1) * 64, :, h * 64:(h + 1) * 64]
                P.tt("dve", hview(yT, h)[:, g * 4:(g + 1) * 4, :], bv, hview(y0, h)[:, g * 4:(g + 1) * 4, :], ALU.add,
                     (t_bk[b], t_y0[g]), (t_yq[g // 2],))

        P.memset("pool", Sall[:, 0:128], 0.0, (t_Sg[0],))
        P.copy("dve", tAq[0], ch(Hh, 0), (t_H[0],), (t_tAq[0],))
        for c in range(RW_NCK - 1):
            si = c % 2
            b = 6 + (c % 2)
            P.mm(cx.bank(b)[:, 0:128], ch(GT, c), ch(Sall, c), True, True, (t_GT[c // 4], t_Sg[c // 4]), (t_bk[b],))
            P.tt("dve", ch(Sall, c + 1), cx.bank(b)[:, 0:128], tAq[si], ALU.add, (t_bk[b], t_tAq[si]), (t_Sg[(c + 1) // 4],))
            if c + 1 < RW_NCK - 1:
                P.tt("dve", Sf[si], cx.bank(b)[:, 0:128], tAq[si], ALU.add, (t_bk[b], t_tAq[si]), (t_Sf[si],))
                P.stt("dve", tAq[1 - si], Sf[si], gam[:, c + 1:c + 2], ch(Hh, c + 1), ALU.mult, ALU.add,
                      (t_Sf[si], t_gam, t_H[(c + 1) // 4]), (t_tAq[1 - si],))
            if (c + 1) % 4 == 3:
                emit_y((c + 1) // 4)
        if cx.dbg_y is not None:
            P.dma("sp", cx.dbg_y[e], yT, dsm, tuple(t_yq), (cx.t_out,))
        mean = [Hh[:, tt * 512:(tt + 1) * 512] for tt in range(4)]; t_mean = [Tok() for _ in range(4)]
        ygb = A.alloc(BF16, T); t_ygb = Tok()
        t_y0p = [Tok() for _ in range(4)]
        TS = [slice(tt * 512, (tt + 1) * 512) for tt in range(4)]
        for tt in range(4):
            P.mm(cx.bank(tt), bones, yT[:, TS[tt]], True, True, (t_k, t_yq[tt]), (t_bk[tt],))
        for tt in range(4):
            P.stt("dve", yT[:, TS[tt]], cx.bank(tt), -1.0 / 64.0, yT[:, TS[tt]], ALU.mult, ALU.add, (t_bk[tt], t_yq[tt]), (t_yq[tt],))
        for tt in range(4):
            P.actf(y0[:, TS[tt]], yT[:, TS[tt]], AF.Square, (t_yq[tt], t_y0[2 * tt], t_y0[2 * tt + 1]), (t_y0p[tt],))
        for tt in range(4):
            P.mm(cx.bank(4 + tt), bones, y0[:, TS[tt]], True, True, (t_k, t_y0p[tt]), (t_bk[4 + tt],))
        for tt in range(4):
            P.ts("dve", mean[tt], cx.bank(4 + tt), 1.0 / 64.0, 64e-5, ALU.mult, ALU.add, (t_bk[4 + tt],), (t_mean[tt],) + tuple(t_H))
        for tt in range(4):
            P.actf(mean[tt], mean[tt], AF.Ln, (t_mean[tt],), (t_mean[tt],))
            P.actf(mean[tt], mean[tt], AF.Exp, (t_mean[tt],), (t_mean[tt],), scale=-0.5)
        for tt in range(4):
            P.tt("dve", yT[:, TS[tt]], yT[:, TS[tt]], mean[tt], ALU.mult, (t_yq[tt], t_mean[tt]), (t_yq[tt],))
            P.ts("dve", yT[:, TS[tt]], yT[:, TS[tt]], col(11, e), col(12, e), ALU.mult, ALU.add, (t_yq[tt], t_rc), (t_yq[tt],))
            P.tt("dve", yT[:, TS[tt]], yT[:, TS[tt]], bon[:, TS[tt]], ALU.add, (t_yq[tt], t_bon), (t_yq[tt],))
            P.tt("dve", ygb[:, TS[tt]], yT[:, TS[tt]], gT[:, TS[tt]], ALU.mult, (t_yq[tt], t_g), (t_ygb,))
        P.dma("sp", cx.ygs[e], ygb, dsm, (t_ygb,), (t_yg,))
        P.barrier()
        A.release()
    A.release()
    xold = [A.alloc(F32, 512) for _ in range(2)]; t_xold = [Tok() for _ in range(2)]; ds_xold = [P.dsem() for _ in range(2)]
    xnew = [A.alloc(F32, 512) for _ in range(2)]; t_xnew = [Tok() for _ in range(2)]; ds_xnew = [P.dsem() for _ in range(2)]
    kk_ = [0]

    def ev_o(oi, tt, bank, tb):
        bi = kk_[0] % 2; kk_[0] += 1
        tsl = slice(tt * 512, (tt + 1) * 512)
        P.dma("act", xold[bi], src_v[:, oi, tsl], ds_xold[bi], cx.xs_tok(oi, tt), (t_xold[bi],))
        P.tt("dve", xnew[bi], bank, xold[bi], ALU.add, (tb, t_xold[bi]), (t_xnew[bi],))
        P.dma("pool", dst_v[:, oi, tsl], xnew[bi], ds_xnew[bi], (t_xnew[bi],), cx.xs_tok(oi, tt))
    ygT = A.alloc(BF16, NCH * T); ygv = ygT.rearrange("p (c t) -> p c t", c=NCH)
    for c4 in range(4):
        P.dma("sp", ygv[:, c4 * 4:(c4 + 1) * 4, :], cx.ygs.rearrange("c p t -> p c t")[:, c4 * 4:(c4 + 1) * 4, :], dsm, (t_yg,), (t_yg,))
    linear_fm(cx, ygv, t_yg, W["w_o"], NCH, [(e * 128, 128) for e in range(NCH)], ev_o, bank0=2)
    P.barrier()
    P.free_dsems.extend(ds_xold + ds_xnew + [dsm])
    A.release()


def pack_cols(vecs):
    return np.ascontiguousarray(
        np.concatenate([np.asarray(v, np.float32).reshape(NCH, 128).T for v in vecs], axis=1))


def build(phases):
    nc = bass.Bass("TRN2", target_bir_lowering=False)
    names = [p[0] for p in phases]
    dram = {}

    def din(name, shape, dt=F32):
        dram[name] = nc.dram_tensor(name, list(shape), dt, kind="ExternalInput").ap()
        return dram[name]

    ins = []
    if "tin" in names:
        x_tm = din("x", [T, D]); ins.append("x")
    else:
        xs_in = din("xs_in", [NCH, 128, T]); ins.append("xs_in")
    if "tout" in names:
        out_ap = nc.dram_tensor("out", [T, D], F32, kind="ExternalOutput").ap()
        out_name = "out"
    else:
        out_ap = nc.dram_tensor("xs_out", [NCH, 128, T], F32, kind="ExternalOutput").ap()
        out_name = "xs_out"
    ident_d = din("ident", [128, 128]); ins.append("ident")
    ncols = 16 * 8
    cols_d = din("cols", [128, ncols]); ins.append("cols")
    for p in phases:
        if p[0] == "ffn":
            l, s = p[1], p[2]
            din(f"w13_{l}{s}", [D, 2 * FF]); ins.append(f"w13_{l}{s}")
            din(f"w2_{l}{s}", [FF, D]); ins.append(f"w2_{l}{s}")
        if p[0] == "rwkv":
            din("rw_w_rkv", [3, D, D]); din("rw_w1", [D, 96]); din("rw_w2", [96, D]); din("rw_a1", [D, 96])
            din("rw_a2", [96, D]); din("rw_g1", [D, 256]); din("rw_g2", [256, D]); din("rw_w_o", [D, D])
            din("rw_cols", [128, 13 * 16]); din("rw_masks", [128, 1536])
            ins.extend(["rw_w_rkv", "rw_w1", "rw_w2", "rw_a1", "rw_a2", "rw_g1", "rw_g2", "rw_w_o", "rw_cols", "rw_masks"])
        if p[0] == "mla":
            din("mla_wd", [D, 1152]); din("mla_wuq", [512, 16, 256]); din("mla_wukv", [512, 16, 256])
            din("mla_wo", [16, 128, D]); din("mla_cols", [128, 16]); din("mla_pos", [64, T], I32)
            ins.extend(["mla_wd", "mla_wuq", "mla_wukv", "mla_wo", "mla_cols", "mla_pos"])
    xs_a = nc.dram_tensor("xs_a", [NCH, 128, T], F32, kind="Internal").ap()
    has_rw = "rwkv" in names
    ots_d = nc.dram_tensor("ots", [MLA_H, 128, T], BF16, kind="Internal").ap() if "mla" in names else None
    if has_rw:
        xmix_d = nc.dram_tensor("xmix", [6, NCH, 128, T], BF16, kind="Internal").ap()
        rkv_d = nc.dram_tensor("rkv", [6, NCH, 128, T], F32, kind="Internal").ap()
        ygs_d = nc.dram_tensor("ygs", [NCH, 128, T], BF16, kind="Internal").ap()
        dbg_d = nc.dram_tensor("dbg_y", [NCH, 128, T], F32, kind="ExternalOutput").ap() if DBG else None

    from contextlib import ExitStack
    with ExitStack() as es:
        sb = es.enter_context(nc.sbuf_tensor("sb", [128, SB_BYTES // 4], F32))
        ps = es.enter_context(nc.psum_tensor("ps", [128, 4096], F32))
        esems = {e: es.enter_context(nc.semaphore("s_" + e)) for e in Prog.CE}
        dsems = [es.enter_context(nc.semaphore(f"d{i}")) for i in range(40)]
        block = es.enter_context(nc.Block())
        P = Prog(nc, esems, dsems)
        A = Arena(sb, SB_BYTES)
        cx = Ctx()
        cx.P, cx.A, cx.nc = P, A, nc
        cx.bank = lambda b: ps[:, b * 512:(b + 1) * 512]
        cx.t_out, cx.t_const = Tok(), Tok()
        xs_toks = [[Tok() for _ in range(T // 512)] for _ in range(NCH)]

        def xs_tok(c=None, tt=None):
            cs = range(NCH) if c is None else [c]
            ts_ = range(T // 512) if tt is None else [tt]
            return tuple(xs_toks[ci][ti] for ci in cs for ti in ts_)
        cx.xs_tok = xs_tok
        cx.ps_bf = ps.bitcast(BF16)
        cx.ots = ots_d
        if has_rw:
            cx.xmix, cx.rkv, cx.ygs, cx.dbg_y = xmix_d, rkv_d, ygs_d, dbg_d
            cx.t_xmix, cx.t_rkv = Tok(), Tok()
            cx.rw_masks_d = dram["rw_masks"]
        cx.ident = A.alloc(F32, 128)
        cx.cols = A.alloc(F32, ncols)
        cx.ones_bf = A.alloc(BF16, 128)
        dsc = P.dsem()
        P.dma("sp", cx.ident, ident_d, dsc, (), (cx.t_const,))
        P.dma("sp", cx.cols, cols_d, dsc, (), (cx.t_const,))
        P.memset("pool", cx.ones_bf, 1.0 / D, (cx.t_const,))
        cx.ones1_bf = A.alloc(BF16, 128)
        P.memset("pool", cx.ones1_bf, 1.0, (cx.t_const,))
        P.barrier()
        cur = None if "tin" in names else xs_in
        n_ph = len(phases)
        for i, p in enumerate(phases):
            last = (i == n_ph - 1)
            if p[0] == "tin":
                dst = out_ap if last else xs_a
                phase_tin(cx, x_tm, dst)
                cur = dst
            elif p[0] == "tout":
                phase_tout(cx, cur, out_ap)
            elif p[0] == "ffn":
                l, s = p[1], p[2]
                nxt_is_out = last
                dst = out_ap if nxt_is_out else xs_a
                if cur is not xs_a and dst is xs_a:
                    pass
                k = (l * 2 + s)
                (phase_ffn2 if FFN2 else phase_ffn)(cx, cur, dst, dram[f"w13_{l}{s}"], dram[f"w2_{l}{s}"], cx.cols[:, k * 16:(k + 1) * 16])
                cur = dst
            elif p[0] == "rwkv":
                dst = out_ap if last else xs_a
                Wd = {k: dram["rw_" + k] for k in ("w_rkv", "w1", "w2", "a1", "a2", "g1", "g2", "w_o")}
                phase_rwkv(cx, cur, dst, Wd, cx.cols[:, 4 * 16:5 * 16], dram["rw_cols"])
                cur = dst
            elif p[0] == "mla":
                dst = out_ap if last else xs_a
                phase_mla(cx, cur, dst, dram["mla_wd"], dram["mla_wuq"], dram["mla_wukv"], dram["mla_wo"],
                          cx.cols[:, 5 * 16:6 * 16], dram["mla_cols"], dram["mla_pos"])
                cur = dst
            else:
                raise ValueError(p)
        P.barrier()
        P.emit(block)
    return nc, ins, out_name, P


def host_consts(inputs):
    ident = np.eye(128, dtype=np.float32)
    fn = inputs["ffn_norm"]
    cols = pack_cols([fn[0, 0], fn[0, 1], fn[1, 0], fn[1, 1],
                      inputs["mix_norm"][0], inputs["mix_norm"][1], np.zeros(D), np.zeros(D)])
    return ident, cols


ROPE_PERM = np.concatenate([np.arange(32, 64), np.arange(0, 32)])


def mla_host(inputs, b):
    wd = inputs["mla_w_down"][0]
    wd_ext = np.ascontiguousarray(np.concatenate([wd, wd[:, 1024 + ROPE_PERM]], axis=1))
    wuq = inputs["mla_w_uq"][0]
    wuq_ext = np.ascontiguousarray(np.concatenate([wuq, wuq[:, :, 128 + ROPE_PERM]], axis=2))
    qn, kn = inputs["mla_q_norm"][0], inputs["mla_k_norm"][0]
    mc = np.zeros((128, 16), np.float32)
    mc[:, 0:4] = inputs["mla_q_a_norm"][0].reshape(4, 128).T
    mc[:, 4:8] = inputs["mla_kv_a_norm"][0].reshape(4, 128).T
    mc[:, 8] = qn[0:128]
    mc[:, 9] = kn[0:128]
    mc[0:64, 10] = qn[128:192]
    mc[0:64, 11] = qn[128 + ROPE_PERM]
    mc[0:64, 12] = kn[128:192]
    mc[0:64, 13] = kn[128 + ROPE_PERM]
    inv_freq = (np.float32(10000.0) ** (-np.arange(0, 64, 2, dtype=np.float32) / np.float32(64))).astype(np.float32)
    mc[0:64, 14] = np.concatenate([inv_freq, inv_freq])
    mc[0:32, 15] = -1.0
    mc[32:64, 15] = 1.0
    pos = np.ascontiguousarray(np.broadcast_to(inputs["positions"][b][None, :], (64, T))).astype(np.int32)
    return {"mla_wd": wd_ext, "mla_wuq": wuq_ext, "mla_wukv": np.ascontiguousarray(inputs["mla_w_ukv"][0]),
            "mla_wo": np.ascontiguousarray(inputs["mla_w_o"][0]), "mla_cols": mc, "mla_pos": pos}


def rwkv_host(inputs):
    g = lambda k: inputs["rwkv_" + k][0]
    vecs = [g("mu")[j] for j in range(6)] + [g("w0"), g("a0"), g("k_k"), g("k_a"), g("r_k").reshape(-1), g("ln_w"), g("ln_b")]
    idx = np.arange(128)
    same = (idx[:, None] // 64) == (idx[None, :] // 64)
    ti, tj = idx[:, None] % 64, idx[None, :] % 64
    ML_s = (same & (ti > tj)).astype(np.float32)
    MU_s = (same & (ti < tj)).astype(np.float32)
    MU_i = (same & (ti <= tj)).astype(np.float32)
    masks = np.ascontiguousarray(np.concatenate([np.tile(m, (1, 4)) for m in (ML_s, MU_s, MU_i)], axis=1))
    return {"rw_w_rkv": np.ascontiguousarray(g("w_rkv")), "rw_w1": g("w1"), "rw_w2": g("w2"), "rw_a1": g("a1"), "rw_a2": g("a2"),
            "rw_g1": g("g1"), "rw_g2": g("g2"), "rw_w_o": g("w_o"), "rw_cols": pack_cols(vecs), "rw_masks": masks}


PHASES = [("tin",), ("ffn", 0, 0), ("rwkv",), ("ffn", 0, 1), ("ffn", 1, 0), ("mla",), ("ffn", 1, 1), ("tout",)]


def make_feeds(inputs, b, shared=None):
    if shared is None:
        shared = {}
        ident, cols = host_consts(inputs)
        shared["ident"] = ident
        shared["cols"] = cols
        for l in range(2):
            for s_ in range(2):
                shared[f"w13_{l}{s_}"] = np.ascontiguousarray(np.asarray(inputs["ffn_w13"][l, s_], np.float32))
                shared[f"w2_{l}{s_}"] = np.ascontiguousarray(np.asarray(inputs["ffn_w2"][l, s_], np.float32))
        shared.update(rwkv_host(inputs))
        m = mla_host(inputs, 0)
        m.pop("mla_pos")
        shared.update(m)
    feeds = dict(shared)
    feeds["x"] = np.ascontiguousarray(np.asarray(inputs["x"][b], np.float32))
    feeds["mla_pos"] = np.ascontiguousarray(
        np.broadcast_to(np.asarray(inputs["positions"][b], np.int32)[None, :], (64, T)))
    return feeds, shared


def kernel(**inputs):
    inputs = {k: np.asarray(v) for k, v in inputs.items()}
    nb = inputs["x"].shape[0]
    nc, ins, out_name, _ = build(PHASES)
    in_maps = []
    shared = None
    for b in range(nb):
        feeds, shared = make_feeds(inputs, b, shared)
        in_maps.append({k: feeds[k] for k in ins})
    res = run_bass_kernel_spmd(nc, in_maps, core_ids=list(range(nb)))
    out = np.stack([np.asarray(res.results[b][out_name], np.float32) for b in range(nb)], axis=0)
    return out
```
